# Optimizing a Trainium2 kernel written in Bass

```python
import jax, jax.numpy as jnp
from jax import lax
import numpy as np

D_MODEL = 1024
BATCH = 2
SEQ = 8192
DEPTH = 2

CONV_W = D_MODEL // 2
CONV_K = 31
CONV_LN_EPS = 1e-5
RWKV_HEAD_DIM = 64
RWKV_HEADS = (D_MODEL // 2) // RWKV_HEAD_DIM
RWKV_W = RWKV_HEADS * RWKV_HEAD_DIM
LORA_DECAY = 64
LORA_ICLR = 64
LORA_GATE = 128
RWKV_GN_EPS = RWKV_HEAD_DIM * 1e-5
RWKV_IN = 3 * RWKV_W + LORA_DECAY + LORA_ICLR + LORA_GATE
RWKV_SPLITS = (RWKV_W, 2 * RWKV_W, 3 * RWKV_W, 3 * RWKV_W + LORA_DECAY, 3 * RWKV_W + LORA_DECAY + LORA_ICLR)
AB_IN = 2 * CONV_W + RWKV_IN
AB_OUT = CONV_W + RWKV_W
HEAD_DIM = 64
N_HEADS = D_MODEL // HEAD_DIM
N_KV_HEADS = 4
GROUP = N_HEADS // N_KV_HEADS
WINDOW = 128
BLOCK = 128
ROT_DIM = HEAD_DIM // 4
ROPE_THETA = 500000.0
QKV_W = (N_HEADS + 2 * N_KV_HEADS) * HEAD_DIM
D_FF = 2816
FFN_CONV_K = 3
NORM_EPS = 1e-6
N_EVEN = (DEPTH + 1) // 2
N_ODD = DEPTH // 2

kernel_name = 'hybrid_conformer_rwkv7_swa_sink_convffn'


def rms_norm(x, g):
    xf = x.astype(jnp.float32)
    y = xf * lax.rsqrt(jnp.mean(xf * xf, axis=-1, keepdims=True) + NORM_EPS)
    return (y * g.astype(jnp.float32)).astype(x.dtype)


def layer_norm(x, g, b, eps):
    xf = x.astype(jnp.float32)
    mu = jnp.mean(xf, axis=-1, keepdims=True)
    xc = xf - mu
    var = jnp.mean(xc * xc, axis=-1, keepdims=True)
    return (xc * lax.rsqrt(var + eps) * g.astype(jnp.float32) + b.astype(jnp.float32)).astype(x.dtype)


def causal_dwconv(x, w, b):
    k = w.shape[0]
    y = lax.conv_general_dilated(x, w[:, None, :].astype(x.dtype), window_strides=(1,),
                                 padding=[(k - 1, 0)], dimension_numbers=('NWC', 'WIO', 'NWC'),
                                 feature_group_count=x.shape[-1])
    return y + b


def token_shift(p):
    return jnp.pad(p, ((0, 0), (1, 0), (0, 0)))[:, :-1]


def partial_rope(x, positions):
    half = ROT_DIM // 2
    inv_freq = ROPE_THETA ** (-(jnp.arange(half, dtype=jnp.float32) * 2.0) / ROT_DIM)
    ang = positions.astype(jnp.float32)[..., None] * inv_freq
    cos = jnp.cos(ang)[:, :, None, :]
    sin = jnp.sin(ang)[:, :, None, :]
    xf = x.astype(jnp.float32)
    x1 = xf[..., :half]
    x2 = xf[..., half:ROT_DIM]
    out = jnp.concatenate([x1 * cos - x2 * sin, x2 * cos + x1 * sin, xf[..., ROT_DIM:]], axis=-1)
    return out.astype(x.dtype)


def rwkv7_recurrence(r, w, k, v, a, b):
    bsz, _, h, n = r.shape

    def step(s, inp):
        r_t, w_t, k_t, v_t, a_t, b_t = inp
        sa = jnp.einsum('bhij,bhj->bhi', s, a_t)
        s = s * w_t[:, :, None, :] + sa[..., None] * b_t[:, :, None, :] + v_t[..., None] * k_t[:, :, None, :]
        return s, jnp.einsum('bhij,bhj->bhi', s, r_t)

    xs = tuple(jnp.moveaxis(t, 1, 0) for t in (r, w, k, v, a, b))
    s0 = jnp.zeros((bsz, h, n, n), jnp.float32)
    _, ys = lax.scan(step, s0, xs)
    return jnp.moveaxis(ys, 0, 1)


def conv_rwkv_mixer(x, norm_g, w_in, conv_in_b, conv_dw_w, conv_dw_b, conv_ln_g, conv_ln_b,
                    rwkv_mu, rwkv_w0, rwkv_w2, rwkv_a0, rwkv_a2, rwkv_g2, rwkv_k_k, rwkv_k_a,
                    rwkv_r_k, rwkv_ln_g, rwkv_ln_b, w_out):
    bsz, t, _ = x.shape
    f32 = jnp.float32
    h = rms_norm(x, norm_g)
    p = h @ w_in
    c = p[..., :2 * CONV_W] + conv_in_b
    c = c[..., :CONV_W] * jax.nn.sigmoid(c[..., CONV_W:])
    c = causal_dwconv(c, conv_dw_w, conv_dw_b)
    c = jax.nn.silu(layer_norm(c, conv_ln_g, conv_ln_b, CONV_LN_EPS))
    rw = p[..., 2 * CONV_W:]
    rw = (rw + (token_shift(rw) - rw) * rwkv_mu).astype(f32)
    r, k, v, wd, ad, gd = jnp.split(rw, RWKV_SPLITS, axis=-1)
    w = -jax.nn.softplus(-(rwkv_w0 + jnp.tanh(wd) @ rwkv_w2)) - 0.5
    decay = jnp.exp(-jnp.exp(w))
    a = jax.nn.sigmoid(rwkv_a0 + ad @ rwkv_a2)
    gate = jax.nn.sigmoid(gd) @ rwkv_g2
    hs = lambda z: z.reshape(bsz, t, RWKV_HEADS, RWKV_HEAD_DIM)
    kk = hs(k * rwkv_k_k)
    kk = kk / jnp.maximum(jnp.sqrt(jnp.sum(kk * kk, axis=-1, keepdims=True)), 1e-12)
    k = k * (1.0 + (a - 1.0) * rwkv_k_a)
    rh, kh, vh = hs(r), hs(k), hs(v)
    y = rwkv7_recurrence(rh, hs(decay), kh, vh, -kk, kk * hs(a))
    mu = jnp.mean(y, axis=-1, keepdims=True)
    yc = y - mu
    y = yc * lax.rsqrt(jnp.mean(yc * yc, axis=-1, keepdims=True) + RWKV_GN_EPS)
    ln_g = rwkv_ln_g.astype(f32).reshape(RWKV_HEADS, RWKV_HEAD_DIM)
    ln_b = rwkv_ln_b.astype(f32).reshape(RWKV_HEADS, RWKV_HEAD_DIM)
    y = y * ln_g + ln_b
    y = y + jnp.sum(rh * kh * rwkv_r_k, axis=-1, keepdims=True) * vh
    y = y.reshape(bsz, t, RWKV_W) * gate
    return jnp.concatenate([c, y.astype(c.dtype)], axis=-1) @ w_out


def sliding_window_sink_attention(x, positions, norm_g, w_qkv, b_qkv, q_norm_g, k_norm_g, sinks, w_o, b_o):
    bsz, t, _ = x.shape
    nb = t // BLOCK
    h = rms_norm(x, norm_g)
    qkv = h @ w_qkv + b_qkv
    qd, kd = N_HEADS * HEAD_DIM, N_KV_HEADS * HEAD_DIM
    q = qkv[..., :qd].reshape(bsz, t, N_HEADS, HEAD_DIM)
    k = qkv[..., qd:qd + kd].reshape(bsz, t, N_KV_HEADS, HEAD_DIM)
    v = qkv[..., qd + kd:].reshape(bsz, t, N_KV_HEADS, HEAD_DIM)
    q = partial_rope(rms_norm(q, q_norm_g), positions)
    k = partial_rope(rms_norm(k, k_norm_g), positions)
    q = q.reshape(bsz, nb, BLOCK, N_KV_HEADS, GROUP, HEAD_DIM)

    def with_prev(z):
        zb = z.reshape(bsz, nb, BLOCK, N_KV_HEADS, HEAD_DIM)
        prev = jnp.pad(zb, ((0, 0), (1, 0), (0, 0), (0, 0), (0, 0)))[:, :-1]
        return jnp.concatenate([prev, zb], axis=2)

    kc, vc = with_prev(k), with_prev(v)
    s = jnp.einsum('bnqhgd,bnkhd->bnhgqk', q, kc).astype(jnp.float32) * (HEAD_DIM ** -0.5)
    qi = jnp.arange(BLOCK)[:, None]
    kj = jnp.arange(2 * BLOCK)[None, :]
    rel = qi + BLOCK - kj
    band = (rel >= 0) & (rel < WINDOW)
    kpos = jnp.arange(nb)[:, None, None] * BLOCK - BLOCK + kj[None]
    valid = band[None] & (kpos >= 0)
    s = jnp.where(valid[None, :, None, None], s, -jnp.inf)
    sink = sinks.astype(jnp.float32).reshape(N_KV_HEADS, GROUP)[None, None, :, :, None, None]
    m = jnp.maximum(jnp.max(s, axis=-1, keepdims=True), sink)
    pr = jnp.exp(s - m)
    pr = pr / (jnp.sum(pr, axis=-1, keepdims=True) + jnp.exp(sink - m))
    o = jnp.einsum('bnhgqk,bnkhd->bnqhgd', pr.astype(vc.dtype), vc)
    return o.reshape(bsz, t, N_HEADS * HEAD_DIM) @ w_o + b_o


def conv_glu_ffn(x, norm_g, w_up, conv_w, conv_b, w_down):
    h = rms_norm(x, norm_g)
    u = h @ w_up
    gate = causal_dwconv(u[..., :D_FF], conv_w, conv_b)
    return (jax.nn.silu(gate) * u[..., D_FF:]) @ w_down


def setup_inputs(seed: int = 0) -> dict:
    key = jax.random.key(seed)
    ks = iter(jax.random.split(key, 40))
    f32 = jnp.float32

    def nrm(shape, scale):
        return scale * jax.random.normal(next(ks), shape, f32)

    def gain(shape):
        return 1.0 + nrm(shape, 0.02)

    e, o, l = N_EVEN, N_ODD, DEPTH
    return {
        'x': jax.random.normal(next(ks), (BATCH, SEQ, D_MODEL), f32),
        'positions': jnp.broadcast_to(jnp.arange(SEQ, dtype=jnp.int32), (BATCH, SEQ)),
        'ab_norm_g': gain((e, D_MODEL)),
        'ab_w_in': nrm((e, D_MODEL, AB_IN), D_MODEL ** -0.5),
        'conv_in_b': nrm((e, 2 * CONV_W), 0.02),
        'conv_dw_w': nrm((e, CONV_K, CONV_W), CONV_K ** -0.5),
        'conv_dw_b': nrm((e, CONV_W), 0.02),
        'conv_ln_g': gain((e, CONV_W)),
        'conv_ln_b': nrm((e, CONV_W), 0.02),
        'rwkv_mu': jax.random.uniform(next(ks), (e, RWKV_IN), f32),
        'rwkv_w0': jax.random.uniform(next(ks), (e, RWKV_W), f32, -6.0, -1.0),
        'rwkv_w2': nrm((e, LORA_DECAY, RWKV_W), 0.1 * LORA_DECAY ** -0.5),
        'rwkv_a0': nrm((e, RWKV_W), 0.1),
        'rwkv_a2': nrm((e, LORA_ICLR, RWKV_W), 0.3 * LORA_ICLR ** -0.5),
        'rwkv_g2': nrm((e, LORA_GATE, RWKV_W), LORA_GATE ** -0.5),
        'rwkv_k_k': 0.85 + nrm((e, RWKV_W), 0.1),
        'rwkv_k_a': 1.0 + nrm((e, RWKV_W), 0.1),
        'rwkv_r_k': nrm((e, RWKV_HEADS, RWKV_HEAD_DIM), 0.1),
        'rwkv_ln_g': gain((e, RWKV_W)),
        'rwkv_ln_b': nrm((e, RWKV_W), 0.02),
        'ab_w_out': nrm((e, AB_OUT, D_MODEL), 0.5 * AB_OUT ** -0.5),
        'attn_norm_g': gain((o, D_MODEL)),
        'attn_w_qkv': nrm((o, D_MODEL, QKV_W), D_MODEL ** -0.5),
        'attn_b_qkv': nrm((o, QKV_W), 0.02),
        'attn_q_norm_g': gain((o, HEAD_DIM)),
        'attn_k_norm_g': gain((o, HEAD_DIM)),
        'attn_sinks': nrm((o, N_HEADS), 0.5),
        'attn_w_o': nrm((o, N_HEADS * HEAD_DIM, D_MODEL), 0.5 * (N_HEADS * HEAD_DIM) ** -0.5),
        'attn_b_o': nrm((o, D_MODEL), 0.02),
        'ffn_norm_g': gain((l, D_MODEL)),
        'ffn_w_up': nrm((l, D_MODEL, 2 * D_FF), D_MODEL ** -0.5),
        'ffn_conv_w': nrm((l, FFN_CONV_K, D_FF), FFN_CONV_K ** -0.5),
        'ffn_conv_b': nrm((l, D_FF), 0.02),
        'ffn_w_down': nrm((l, D_FF, D_MODEL), 0.5 * D_FF ** -0.5),
    }


def reference(x, positions, ab_norm_g, ab_w_in, conv_in_b, conv_dw_w, conv_dw_b, conv_ln_g, conv_ln_b,
              rwkv_mu, rwkv_w0, rwkv_w2, rwkv_a0, rwkv_a2, rwkv_g2, rwkv_k_k, rwkv_k_a, rwkv_r_k,
              rwkv_ln_g, rwkv_ln_b, ab_w_out, attn_norm_g, attn_w_qkv, attn_b_qkv, attn_q_norm_g,
              attn_k_norm_g, attn_sinks, attn_w_o, attn_b_o, ffn_norm_g, ffn_w_up, ffn_conv_w,
              ffn_conv_b, ffn_w_down):
    for layer in range(DEPTH):
        i = layer // 2
        if layer % 2 == 0:
            mix = conv_rwkv_mixer(x, ab_norm_g[i], ab_w_in[i], conv_in_b[i], conv_dw_w[i], conv_dw_b[i],
                                  conv_ln_g[i], conv_ln_b[i], rwkv_mu[i], rwkv_w0[i], rwkv_w2[i],
                                  rwkv_a0[i], rwkv_a2[i], rwkv_g2[i], rwkv_k_k[i], rwkv_k_a[i],
                                  rwkv_r_k[i], rwkv_ln_g[i], rwkv_ln_b[i], ab_w_out[i])
        else:
            mix = sliding_window_sink_attention(x, positions, attn_norm_g[i], attn_w_qkv[i], attn_b_qkv[i],
                                                attn_q_norm_g[i], attn_k_norm_g[i], attn_sinks[i],
                                                attn_w_o[i], attn_b_o[i])
        x = x + mix.astype(x.dtype)
        x = x + conv_glu_ffn(x, ffn_norm_g[layer], ffn_w_up[layer], ffn_conv_w[layer],
                             ffn_conv_b[layer], ffn_w_down[layer]).astype(x.dtype)
    return x
```

```python
import contextlib
import numpy as np
import concourse.bass as bass
import concourse.mybir as mybir
from concourse.bass_utils import run_bass_kernel_spmd

F32 = mybir.dt.float32
BF16 = mybir.dt.bfloat16
I32 = mybir.dt.int32
ALU = mybir.AluOpType
AF = mybir.ActivationFunctionType

ENGS = ["pe", "act", "dve", "pool", "sp"]
NDMASEM = 6
D = 1024
DFF = 2816
PI = float(np.pi)
DBG = {'att': 9, 'mix': 9, 'skip': ''}


class Sched:
    def __init__(self, nc, stack):
        self.nc = nc
        self.ops = []
        self.per_eng = {e: [] for e in ENGS}
        self.acc = {}
        self.dma_count = {e: 0 for e in ENGS}
        Sched.count = getattr(Sched, "count", 0) + 1
        sp_ = "q%d" % Sched.count
        self.sems = {e: stack.enter_context(nc.semaphore(sp_ + "s_" + e)) for e in ENGS if e != "sp"}
        self.dsems = {q: [stack.enter_context(nc.semaphore(sp_ + "d_%s%d" % (q, i))) for i in range(NDMASEM)]
                      for q in ("sp", "act", "pool")}

    def _deps(self, reads, writes, opid):
        deps = set()
        for (name, lo, hi) in reads:
            lst = self.acc.setdefault(name, [])
            for (l, h, k, o) in lst:
                if k == "w" and l < hi and lo < h:
                    deps.add(o)
            lst.append((lo, hi, "r", opid))
        for (name, lo, hi) in writes:
            lst = self.acc.setdefault(name, [])
            keep = []
            for (l, h, k, o) in lst:
                if l < hi and lo < h:
                    if o != opid:
                        deps.add(o)
                    if lo <= l and h <= hi:
                        continue
                keep.append((l, h, k, o))
            keep.append((lo, hi, "w", opid))
            self.acc[name] = keep
        return deps

    def op(self, eng, fn, reads=(), writes=()):
        opid = len(self.ops)
        deps = self._deps(reads, writes, opid)
        rec = dict(id=opid, eng=eng, fn=fn, deps=deps, dma=None, sig=False)
        self.ops.append(rec)
        self.per_eng[eng].append(rec)
        return opid

    def dma(self, q, out, in_, reads=(), writes=(), **kw):
        opid = len(self.ops)
        deps = self._deps(reads, writes, opid)
        n = self.dma_count[q]
        self.dma_count[q] += 1
        rec = dict(id=opid, eng=q, fn=None, deps=deps, dma=(q, n, out, in_, kw), sig=True)
        self.ops.append(rec)
        self.per_eng[q].append(rec)
        return opid

    def _finalize(self):
        for rec in self.ops:
            for d in rec["deps"]:
                p = self.ops[d]
                if p["dma"] is None and not (p["eng"] == "pe" and rec["eng"] == "pe" and rec["dma"] is None):
                    p["sig"] = True
        cnt = {e: 0 for e in ENGS}
        for rec in self.ops:
            if rec["dma"] is not None:
                q, n, _, _, _ = rec["dma"]
                rec["sem"] = self.dsems[q][n % NDMASEM]
                rec["val"] = 16 * (n // NDMASEM + 1)
            elif rec["sig"]:
                cnt[rec["eng"]] += 1
                rec["sem"] = self.sems[rec["eng"]]
                rec["val"] = cnt[rec["eng"]]

    def _emit_engine(self, ename, eng):
        seen = {}

        def wait(sem, val):
            key = id(sem)
            if seen.get(key, 0) >= val:
                return
            seen[key] = val
            eng.wait_ge(sem, val)

        for rec in self.per_eng[ename]:
            for d in sorted(rec["deps"]):
                p = self.ops[d]
                if p["dma"] is None and p["eng"] == "pe" and ename == "pe" and rec["dma"] is None:
                    continue
                wait(p["sem"], p["val"])
            if rec["dma"] is not None:
                q, n, out, in_, kw = rec["dma"]
                if n >= NDMASEM:
                    wait(rec["sem"], rec["val"] - 16)
                eng.dma_start(out=out, in_=in_, **kw).then_inc(rec["sem"], 16)
            else:
                ins = rec["fn"](eng)
                if rec["sig"]:
                    ins.then_inc(rec["sem"], 1)
        return wait

    def emit(self):
        self._finalize()
        with self.nc.Block() as block:
            @block.tensor
            def _(e):
                self._emit_engine("pe", e)

            @block.scalar
            def _(e):
                self._emit_engine("act", e)

            @block.vector
            def _(e):
                self._emit_engine("dve", e)

            @block.gpsimd
            def _(e):
                w = self._emit_engine("pool", e)
                n = self.dma_count["pool"]
                for i in range(NDMASEM):
                    k = len(range(i, n, NDMASEM))
                    if k:
                        w(self.dsems["pool"][i], 16 * k)

            @block.sync
            def _(e):
                w = self._emit_engine("sp", e)
                n = self.dma_count["sp"]
                for i in range(NDMASEM):
                    k = len(range(i, n, NDMASEM))
                    if k:
                        w(self.dsems["sp"][i], 16 * k)


def colvec(v):
    v = np.asarray(v, np.float32).reshape(-1, 128)
    return np.ascontiguousarray(v.T)


class Pack:
    def __init__(self):
        self.parts = []
        self.off = {}
        self.n = 0

    def add(self, name, arr):
        arr = np.asarray(arr, np.float32)
        if arr.ndim == 1:
            arr = arr[:, None]
        assert arr.shape[0] == 128, (name, arr.shape)
        arr = arr.reshape(128, -1)
        self.off[name] = self.n
        self.parts.append(arr)
        self.n += arr.shape[1]

    def build(self):
        return np.ascontiguousarray(np.concatenate(self.parts, axis=1))


def bd_mask(fn):
    m = np.zeros((128, 128), np.float32)
    i = np.arange(64)
    blk = fn(i[:, None], i[None, :]).astype(np.float32)
    m[:64, :64] = blk
    m[64:, 64:] = blk
    return m


def make_consts(inp):
    P = Pack()
    g = lambda n, l=0: np.asarray(inp[n][l], np.float32)
    P.add("eps6", np.full(128, 1e-6)); P.add("eps5", np.full(128, 1e-5)); P.add("epsgn", np.full(128, 64e-5))
    P.add("halfpi", np.full(128, PI / 2)); P.add("zero", np.zeros(128))
    P.add("ab_g", colvec(g("ab_norm_g")))
    P.add("cin_b", colvec(g("conv_in_b")))
    P.add("dw_w", np.stack([colvec(g("conv_dw_w")[j]) for j in range(31)], axis=2))
    P.add("dw_b", colvec(g("conv_dw_b"))); P.add("ln_g", colvec(g("conv_ln_g"))); P.add("ln_b", colvec(g("conv_ln_b")))
    P.add("mu", colvec(g("rwkv_mu"))); P.add("omm", colvec(1.0 - 0.0 * g("rwkv_mu")) * 0 + 0)
    P.add("w0", colvec(g("rwkv_w0"))); P.add("a0", colvec(g("rwkv_a0")))
    P.add("k_k", colvec(g("rwkv_k_k"))); P.add("k_a", colvec(g("rwkv_k_a")))
    P.add("r_k", colvec(g("rwkv_r_k").reshape(-1)))
    P.add("gn_g", colvec(g("rwkv_ln_g"))); P.add("gn_b", colvec(g("rwkv_ln_b")))
    for l in range(2):
        P.add("ffn_g%d" % l, colvec(g("ffn_norm_g", l)))
        P.add("fc_w%d" % l, np.stack([colvec(g("ffn_conv_w", l)[j]) for j in range(3)], axis=2))
        P.add("fc_b%d" % l, colvec(g("ffn_conv_b", l)))
    P.add("at_g", colvec(g("attn_norm_g")))
    bq = g("attn_b_qkv")
    P.add("b_q", colvec(bq[:1024]))
    P.add("b_kd", np.stack([np.concatenate([bq[1024 + h * 64:1088 + h * 64]] * 2) for h in range(4)], axis=1))
    P.add("qn_g", np.concatenate([g("attn_q_norm_g")] * 2)); P.add("kn_g", np.concatenate([g("attn_k_norm_g")] * 2))
    P.add("b_o", colvec(g("attn_b_o")))
    half = 8
    invf = (500000.0 ** (-(np.arange(half, dtype=np.float32) * 2.0) / 16)).astype(np.float32)
    iv = np.zeros(64, np.float32); iv[:8] = invf; iv[8:16] = invf
    P.add("invf", np.concatenate([iv, iv]))
    P.add("ident", np.eye(128, dtype=np.float32))
    P.add("bones", bd_mask(lambda a, b: a * 0 + b * 0 + 1))
    P.add("ones", np.ones((128, 128), np.float32))
    P.add("m_su", bd_mask(lambda s, t: s < t)); P.add("m_iu", bd_mask(lambda s, t: s <= t)); P.add("m_sl", bd_mask(lambda t, s: s < t))
    rm = np.zeros((64, 64), np.float32)
    for m in range(8):
        rm[m + 8, m] = -1.0
        rm[m, m + 8] = 1.0
    R2 = np.zeros((128, 128), np.float32); R2[:64, :64] = rm; R2[64:, 64:] = rm
    P.add("rmT", R2)
    kq = np.arange(128)
    mP = (kq[:, None] > kq[None, :]).astype(np.float32)
    mC = (kq[:, None] <= kq[None, :]).astype(np.float32)
    P.add("mP", np.tile(mP, (1, 4))); P.add("mC", np.tile(mC, (1, 4)))
    sk = g("attn_sinks")
    se = np.zeros((4, 2, 2, 128), np.float32)
    for hk in range(4):
        for par in range(2):
            for pair in range(2):
                se[hk, par, pair, :] = sk[4 * hk + 2 * pair + par]
    P.add("sinks", np.broadcast_to(se.reshape(1, -1), (128, 2048)))
    return P


class Phase:
    count = 0

    def __init__(self, nc, cdram, coff, need):
        self.nc = nc
        Phase.count += 1
        self.pfx = "p%d_" % Phase.count
        self.st = contextlib.ExitStack()
        self.S = Sched(nc, self.st)
        self.ps = [self.st.enter_context(nc.psum_tensor(self.pfx + "ps%d" % i, [128, 512], F32)) for i in range(8)]
        self.pi = 0
        self.coff = coff
        self.cmap = {}
        n = 0
        for name, w in need:
            self.cmap[name] = n
            n += w
        self.C = self.sb("C", [128, n], F32)
        for name, w in need:
            o = self.cmap[name]
            self.S.dma("sp", self.C[:, o:o + w], cdram[:, coff[name]:coff[name] + w], writes=[("C", o, o + w)], allow_slow_non_contiguous=True)

    def sb(self, name, shape, dt):
        return self.st.enter_context(self.nc.sbuf_tensor(self.pfx + name, shape, dt))

    def c(self, name, lo=0, w=1):
        o = self.cmap[name] + lo
        return self.C[:, o:o + w]

    def cr(self, name, lo=0, w=1):
        o = self.cmap[name] + lo
        return ("C", o, o + w)

    def newps(self):
        i = self.pi
        self.pi = (self.pi + 1) % 8
        return self.ps[i], "ps%d" % i

    def close(self):
        self.S.emit()
        self.st.close()


def rmsnorm(ph, x, xname, hT, hname, sq, sqname, rstd, ones_bf, gname, T):
    S = ph.S
    for k in range(8):
        S.op("act", lambda e, k=k: e.activation(out=sq[:, k, :T], in_=x[:, k, :T], func=AF.Square),
             reads=[(xname, k, k + 1)], writes=[(sqname, k, k + 1)])
    ps, pn = ph.newps()
    for k in range(8):
        S.op("pe", lambda e, k=k: e.matmul(ps[:, :T], lhsT=ones_bf[:], rhs=sq[:, k, :T], start=(k == 0), stop=(k == 7)),
             reads=[("ones_bf", 0, 1), (sqname, k, k + 1)], writes=[(pn, 0, 1)])
    S.op("act", lambda e: e.activation(out=rstd[:, :T], in_=ps[:, :T], func=AF.Ln, scale=1.0 / D, bias=ph.c("eps6")),
         reads=[(pn, 0, 1), ph.cr("eps6")], writes=[("rstd", 0, 1)])
    S.op("act", lambda e: e.activation(out=rstd[:, :T], in_=rstd[:, :T], func=AF.Exp, scale=-0.5),
         reads=[("rstd", 0, 1)], writes=[("rstd", 0, 1)])
    for k in range(8):
        S.op("dve",
             lambda e, k=k: e.scalar_tensor_tensor(out=hT[:, k, :T], in0=x[:, k, :T], scalar=ph.c(gname, k), in1=rstd[:, :T],
                                                   op0=ALU.mult, op1=ALU.mult),
             reads=[(xname, k, k + 1), ("rstd", 0, 1), ph.cr(gname, k)], writes=[(hname, k, k + 1)])


def load_w(S, q, dst, dname, src, nchunk):
    for k in range(nchunk):
        S.dma(q, dst[:, k, :], src[k * 128:(k + 1) * 128, :], writes=[(dname, k, k + 1)])


def ffn_phase(nc, src, dst, w_up, w_dn, cdram, coff, L, TL):
    T = 512
    NT = TL // T
    need = [("eps6", 1), ("ffn_g%d" % L, 8), ("fc_w%d" % L, 66), ("fc_b%d" % L, 22), ("ones", 128)]
    ph = Phase(nc, cdram, coff, need)
    S = ph.S
    wup = ph.sb("wup", [128, 8, 2 * DFF], BF16)
    wdn = ph.sb("wdn", [128, 22, D], BF16)
    x = ph.sb("x", [128, 8, T], F32)
    hT = ph.sb("hT", [128, 8, T], BF16)
    act = ph.sb("act", [128, 22, T], BF16)
    rstd = ph.sb("rstd", [128, T], F32)
    ones_bf = ph.sb("ones_bf", [128, 128], BF16)
    G = [ph.sb("G%d" % i, [128, T + 2], F32) for i in range(2)]
    acc = [ph.sb("acc%d" % i, [128, T], F32) for i in range(2)]
    sg = [ph.sb("sg%d" % i, [128, T], F32) for i in range(2)]
    carry = ph.sb("carry", [128, 22, 2], F32)
    load_w(S, "pool", wup, "wup", w_up, 8)
    load_w(S, "pool", wdn, "wdn", w_dn, 22)
    S.op("dve", lambda e: e.tensor_copy(out=ones_bf[:], in_=ph.c("ones", 0, 128)), reads=[ph.cr("ones", 0, 128)], writes=[("ones_bf", 0, 1)])
    S.op("pool", lambda e: e.memset(carry[:], 0.0), writes=[("carry", 0, 22)])
    fw, fb, gname = "fc_w%d" % L, "fc_b%d" % L, "ffn_g%d" % L
    srcv = src.rearrange("(c p) t -> p c t", p=128)
    dstv = dst.rearrange("(c p) t -> p c t", p=128)
    for t in range(NT):
        S.dma("sp", x[:], srcv[:, :, t * T:(t + 1) * T], writes=[("x", 0, 8)])
        rmsnorm(ph, x, "x", hT, "hT", act, "act", rstd, ones_bf, gname, T)
        for c in range(22):
            i = c % 2
            Gt, at, st_ = G[i], acc[i], sg[i]
            gn, an, sn = "G%d" % i, "acc%d" % i, "sg%d" % i
            psg, pgn = ph.newps()
            psu, pun = ph.newps()
            for k in range(8):
                S.op("pe", lambda e, k=k, c=c, psg=psg: e.matmul(psg[:, :T], lhsT=wup[:, k, c * 128:(c + 1) * 128], rhs=hT[:, k, :],
                                                                 start=(k == 0), stop=(k == 7)),
                     reads=[("wup", k, k + 1), ("hT", k, k + 1)], writes=[(pgn, 0, 1)])
            for k in range(8):
                S.op("pe", lambda e, k=k, c=c, psu=psu: e.matmul(psu[:, :T], lhsT=wup[:, k, DFF + c * 128:DFF + (c + 1) * 128], rhs=hT[:, k, :],
                                                                 start=(k == 0), stop=(k == 7)),
                     reads=[("wup", k, k + 1), ("hT", k, k + 1)], writes=[(pun, 0, 1)])
            S.op("pool", lambda e, c=c, Gt=Gt: e.tensor_copy(out=Gt[:, 0:2], in_=carry[:, c, :]),
                 reads=[("carry", c, c + 1)], writes=[(gn, 0, 2)])
            S.op("act", lambda e, Gt=Gt, psg=psg: e.activation(out=Gt[:, 2:T + 2], in_=psg[:, :T], func=AF.Identity),
                 reads=[(pgn, 0, 1)], writes=[(gn, 2, T + 2)])
            S.op("pool", lambda e, c=c, Gt=Gt: e.tensor_copy(out=carry[:, c, :], in_=Gt[:, T:T + 2]),
                 reads=[(gn, T, T + 2)], writes=[("carry", c, c + 1)])
            S.op("dve", lambda e, c=c, Gt=Gt, at=at: e.tensor_scalar(out=at[:], in0=Gt[:, 2:T + 2], scalar1=ph.c(fw, c * 3 + 2), scalar2=ph.c(fb, c),
                                                                    op0=ALU.mult, op1=ALU.add),
                 reads=[(gn, 2, T + 2), ph.cr(fw, c * 3 + 2), ph.cr(fb, c)], writes=[(an, 0, 1)])
            S.op("dve", lambda e, c=c, Gt=Gt, at=at: e.scalar_tensor_tensor(out=at[:], in0=Gt[:, 1:T + 1], scalar=ph.c(fw, c * 3 + 1), in1=at[:],
                                                                            op0=ALU.mult, op1=ALU.add),
                 reads=[(gn, 1, T + 1), (an, 0, 1), ph.cr(fw, c * 3 + 1)], writes=[(an, 0, 1)])
            S.op("dve", lambda e, c=c, Gt=Gt, at=at: e.scalar_tensor_tensor(out=at[:], in0=Gt[:, 0:T], scalar=ph.c(fw, c * 3), in1=at[:],
                                                                           op0=ALU.mult, op1=ALU.add),
                 reads=[(gn, 0, T), (an, 0, 1), ph.cr(fw, c * 3)], writes=[(an, 0, 1)])
            S.op("act", lambda e, at=at, st_=st_: e.activation(out=st_[:], in_=at[:], func=AF.Silu),
                 reads=[(an, 0, 1)], writes=[(sn, 0, 1)])
            S.op("dve", lambda e, c=c, st_=st_, psu=psu: e.tensor_tensor(out=act[:, c, :], in0=psu[:, :T], in1=st_[:], op=ALU.mult),
                 reads=[(pun, 0, 1), (sn, 0, 1)], writes=[("act", c, c + 1)])
        for oc in range(8):
            pso, pon = ph.newps()
            for c in range(22):
                S.op("pe", lambda e, c=c, oc=oc, pso=pso: e.matmul(pso[:, :T], lhsT=wdn[:, c, oc * 128:(oc + 1) * 128], rhs=act[:, c, :],
                                                                   start=(c == 0), stop=(c == 21)),
                     reads=[("wdn", c, c + 1), ("act", c, c + 1)], writes=[(pon, 0, 1)])
            S.op("dve", lambda e, oc=oc, pso=pso: e.tensor_tensor(out=x[:, oc, :], in0=pso[:, :T], in1=x[:, oc, :], op=ALU.add),
                 reads=[(pon, 0, 1), ("x", oc, oc + 1)], writes=[("x", oc, oc + 1)])
        S.dma("sp", dstv[:, :, t * T:(t + 1) * T], x[:], reads=[("x", 0, 8)])
    ph.close()


def attn_phase(nc, src, dst, pos, w_qkv, w_o, cdram, coff, TL):
    T = 512
    NT = TL // T
    need = [("eps6", 1), ("halfpi", 1), ("zero", 1), ("at_g", 8), ("b_q", 8), ("b_kd", 4), ("qn_g", 1), ("kn_g", 1), ("b_o", 8),
            ("invf", 1), ("bones", 128), ("ones", 128), ("rmT", 128), ("mP", 512), ("mC", 512), ("sinks", 2048)]
    ph = Phase(nc, cdram, coff, need)
    S = ph.S
    wq = ph.sb("wq", [128, 8, 1024], BF16)
    wkd = ph.sb("wkd", [128, 8, 4, 128], BF16)
    wvd = ph.sb("wvd", [128, 8, 4, 128], BF16)
    wo = ph.sb("wo", [128, 8, D], BF16)
    x = ph.sb("x", [128, 8, T], F32)
    hT = ph.sb("hT", [128, 8, T], BF16)
    sq = ph.sb("sq", [128, 8, T], BF16)
    rstd = ph.sb("rstd", [128, T], F32)
    ones_bf = ph.sb("ones_bf", [128, 128], BF16)
    mPb = ph.sb("mPb", [128, 512], BF16)
    mCb = ph.sb("mCb", [128, 512], BF16)
    esink = ph.sb("esink", [128, 2048], F32)
    COS = ph.sb("COS", [128, T], F32)
    SIN = ph.sb("SIN", [128, T], F32)
    posi = ph.sb("posi", [128, T], I32)
    ang = ph.sb("ang", [128, T], F32)
    nf = ph.sb("nf", [128, T], F32)
    QT = ph.sb("QT", [128, 8, T], BF16)
    KT = ph.sb("KT", [128, 4, 128 + T], BF16)
    VB = ph.sb("VB", [128, 5, 512], BF16)
    OT = ph.sb("OT", [128, 8, T], BF16)
    bv = ph.sb("bv", [1, 512], BF16)
    qraw = [ph.sb("qraw%d" % i, [128, T], F32) for i in range(2)]
    qsq = [ph.sb("qsq%d" % i, [128, T], F32) for i in range(2)]
    qr = [ph.sb("qr%d" % i, [128, T], F32) for i in range(2)]
    qn = [ph.sb("qn%d" % i, [128, T], F32) for i in range(2)]
    t1 = [ph.sb("t1%d" % i, [128, T], F32) for i in range(2)]
    t2 = [ph.sb("t2%d" % i, [128, T], F32) for i in range(2)]
    E = [ph.sb("E%d" % i, [128, 512], BF16) for i in range(4)]
    den = [ph.sb("den%d" % i, [128, 512], F32) for i in range(2)]
    load_w(S, "pool", wq, "wq", w_qkv[:, 0:1024], 8)
    load_w(S, "pool", wo, "wo", w_o, 8)
    wkv = ph.sb("wkv", [128, 8, 512], BF16)
    load_w(S, "pool", wkv, "wkv", w_qkv[:, 1024:1536], 8)
    for hk in range(4):
        for cp in range(2):
            S.op("pool", lambda e, hk=hk, cp=cp: e.tensor_copy(out=wkd[:, :, hk, cp * 64:(cp + 1) * 64], in_=wkv[:, :, hk * 64:(hk + 1) * 64]),
                 reads=[("wkv", 0, 8)], writes=[("wkd", hk * 2 + cp, hk * 2 + cp + 1)])
            S.op("dve", lambda e, hk=hk, cp=cp: e.tensor_copy(out=wvd[:, :, hk, cp * 64:(cp + 1) * 64], in_=wkv[:, :, 256 + hk * 64:256 + (hk + 1) * 64]),
                 reads=[("wkv", 0, 8)], writes=[("wvd", hk * 2 + cp, hk * 2 + cp + 1)])
    if 'bv' not in DBG['skip']:
        S.dma("pool", bv[:], nc_bv_dram[0], writes=[("bv", 0, 1)])
    S.op("dve", lambda e: e.tensor_copy(out=ones_bf[:], in_=ph.c("ones", 0, 128)), reads=[ph.cr("ones", 0, 128)], writes=[("ones_bf", 0, 1)])
    S.op("dve", lambda e: e.tensor_copy(out=mPb[:], in_=ph.c("mP", 0, 512)), reads=[ph.cr("mP", 0, 512)], writes=[("mPb", 0, 1)])
    S.op("dve", lambda e: e.tensor_copy(out=mCb[:], in_=ph.c("mC", 0, 512)), reads=[ph.cr("mC", 0, 512)], writes=[("mCb", 0, 1)])
    if 'esink' not in DBG['skip']:
      S.op("act", lambda e: e.activation(out=esink[:], in_=ph.c("sinks", 0, 2048), func=AF.Exp), reads=[ph.cr("sinks", 0, 2048)], writes=[("esink", 0, 1)])
    srcv = src.rearrange("(c p) t -> p c t", p=128)
    dstv = dst.rearrange("(c p) t -> p c t", p=128)
    ei = 0
    for t in range(NT):
        tc0 = t * T
        S.dma("sp", x[:], srcv[:, :, tc0:tc0 + T], writes=[("x", 0, 8)])
        rmsnorm(ph, x, "x", hT, "hT", sq, "sq", rstd, ones_bf, "at_g", T)
        S.dma("sp", posi[:], pos[0:1, tc0:tc0 + T].to_broadcast([128, T]), writes=[("posi", 0, 1)])
        S.op("dve", lambda e: e.tensor_copy(out=ang[:], in_=posi[:]), reads=[("posi", 0, 1)], writes=[("ang", 0, 1)])
        S.op("dve", lambda e: e.tensor_scalar(out=ang[:], in0=ang[:], scalar1=ph.c("invf"), scalar2=None, op0=ALU.mult),
             reads=[("ang", 0, 1), ph.cr("invf")], writes=[("ang", 0, 1)])
        for which, tab, tn in ((0, SIN, "SIN"), (1, COS, "COS")):
            if which == 1:
                S.op("dve", lambda e: e.tensor_scalar(out=ang[:], in0=ang[:], scalar1=PI / 2, scalar2=None, op0=ALU.add),
                     reads=[("ang", 0, 1)], writes=[("ang", 0, 1)])
            S.op("dve", lambda e: e.tensor_scalar(out=posi[:], in0=ang[:], scalar1=float(1.0 / (2 * PI)), scalar2=None, op0=ALU.mult),
                 reads=[("ang", 0, 1)], writes=[("posi", 0, 1)])
            S.op("dve", lambda e: e.tensor_copy(out=nf[:], in_=posi[:]), reads=[("posi", 0, 1)], writes=[("nf", 0, 1)])
            S.op("dve", lambda e: e.scalar_tensor_tensor(out=nf[:], in0=nf[:], scalar=float(-2 * PI), in1=ang[:], op0=ALU.mult, op1=ALU.add),
                 reads=[("nf", 0, 1), ("ang", 0, 1)], writes=[("nf", 0, 1)])
            S.op("dve", lambda e: e.tensor_scalar(out=nf[:], in0=nf[:], scalar1=PI, scalar2=-PI, op0=ALU.min, op1=ALU.max),
                 reads=[("nf", 0, 1)], writes=[("nf", 0, 1)])
            S.op("act", lambda e, tab=tab: e.activation(out=tab[:], in_=nf[:], func=AF.Sin), reads=[("nf", 0, 1)], writes=[(tn, 0, 1)])
        if t > 0:
            S.op("pool", lambda e: e.tensor_copy(out=KT[:, :, 0:128], in_=KT[:, :, T:T + 128]), reads=[("KT", 4, 5)], writes=[("KT", 0, 1)])
            S.op("pool", lambda e: e.tensor_copy(out=VB[:, 0, :], in_=VB[:, 4, :]), reads=[("VB", 4, 5)], writes=[("VB", 0, 1)])
        for b in range(4 if DBG['att'] >= 1 else 0):
            ps, pn = ph.newps()
            for k in range(8):
                S.op("pe", lambda e, k=k, b=b, ps=ps: e.matmul(ps[:, :], lhsT=hT[:, k, b * 128:(b + 1) * 128], rhs=wvd[:, k, :, :],
                                                               start=(k == 0), stop=False),
                     reads=[("hT", k, k + 1), ("wvd", 0, 8)], writes=[(pn, 0, 1)])
            S.op("pe", lambda e, ps=ps: e.matmul(ps[:, :], lhsT=ones_bf[0:1, :], rhs=bv[0:1, :], start=False, stop=True),
                 reads=[("ones_bf", 0, 1), ("bv", 0, 1)], writes=[(pn, 0, 1)])
            S.op("act", lambda e, b=b, ps=ps: e.activation(out=VB[:, 1 + b, :], in_=ps[:, :], func=AF.Identity),
                 reads=[(pn, 0, 1)], writes=[("VB", 1 + b, 2 + b)])
        for j in range(12 if DBG['att'] >= 2 else 0):
            i = j % 2
            isq = j < 8
            ps, pn = ph.newps()
            for k in range(8):
                if isq:
                    S.op("pe", lambda e, k=k, j=j, ps=ps: e.matmul(ps[:, :T], lhsT=wq[:, k, j * 128:(j + 1) * 128], rhs=hT[:, k, :],
                                                                   start=(k == 0), stop=(k == 7)),
                         reads=[("wq", k, k + 1), ("hT", k, k + 1)], writes=[(pn, 0, 1)])
                else:
                    S.op("pe", lambda e, k=k, j=j, ps=ps: e.matmul(ps[:, :T], lhsT=wkd[:, k, j - 8, :], rhs=hT[:, k, :],
                                                                   start=(k == 0), stop=(k == 7)),
                         reads=[("wkd", 0, 8), ("hT", k, k + 1)], writes=[(pn, 0, 1)])
            bias = ph.c("b_q", j) if isq else ph.c("b_kd", j - 8)
            bres = ph.cr("b_q", j) if isq else ph.cr("b_kd", j - 8)
            gcol, gres = (ph.c("qn_g"), ph.cr("qn_g")) if isq else (ph.c("kn_g"), ph.cr("kn_g"))
            qa, qs_, qrr, qnn, ta, tb = qraw[i], qsq[i], qr[i], qn[i], t1[i], t2[i]
            S.op("act", lambda e, ps=ps, qa=qa, bias=bias: e.activation(out=qa[:], in_=ps[:, :T], func=AF.Identity, bias=bias),
                 reads=[(pn, 0, 1), bres], writes=[("qraw%d" % i, 0, 1)])
            S.op("pool", lambda e, qa=qa, qs_=qs_: e.tensor_tensor(out=qs_[:], in0=qa[:], in1=qa[:], op=ALU.mult),
                 reads=[("qraw%d" % i, 0, 1)], writes=[("qsq%d" % i, 0, 1)])
            ps2, pn2 = ph.newps()
            S.op("pe", lambda e, ps2=ps2, qs_=qs_: e.matmul(ps2[:, :T], lhsT=ph.c("bones", 0, 128), rhs=qs_[:], start=True, stop=True),
                 reads=[ph.cr("bones", 0, 128), ("qsq%d" % i, 0, 1)], writes=[(pn2, 0, 1)])
            S.op("act", lambda e, ps2=ps2, qrr=qrr: e.activation(out=qrr[:], in_=ps2[:, :T], func=AF.Ln, scale=1.0 / 64, bias=ph.c("eps6")),
                 reads=[(pn2, 0, 1), ph.cr("eps6")], writes=[("qr%d" % i, 0, 1)])
            S.op("act", lambda e, qrr=qrr: e.activation(out=qrr[:], in_=qrr[:], func=AF.Exp, scale=-0.5),
                 reads=[("qr%d" % i, 0, 1)], writes=[("qr%d" % i, 0, 1)])
            S.op("dve", lambda e, qa=qa, qrr=qrr, qnn=qnn, gcol=gcol: e.scalar_tensor_tensor(out=qnn[:], in0=qa[:], scalar=gcol, in1=qrr[:],
                                                                                         op0=ALU.mult, op1=ALU.mult),
                 reads=[("qraw%d" % i, 0, 1), ("qr%d" % i, 0, 1), gres], writes=[("qn%d" % i, 0, 1)])
            ps3, pn3 = ph.newps()
            S.op("pe", lambda e, ps3=ps3, qnn=qnn: e.matmul(ps3[:, :T], lhsT=ph.c("rmT", 0, 128), rhs=qnn[:], start=True, stop=True),
                 reads=[ph.cr("rmT", 0, 128), ("qn%d" % i, 0, 1)], writes=[(pn3, 0, 1)])
            S.op("pool", lambda e, qnn=qnn, ta=ta: e.tensor_tensor(out=ta[:], in0=qnn[:], in1=COS[:, :], op=ALU.mult),
                 reads=[("qn%d" % i, 0, 1), ("COS", 0, 1)], writes=[("t1%d" % i, 0, 1)])
            S.op("dve", lambda e, ps3=ps3, tb=tb: e.tensor_tensor(out=tb[:], in0=ps3[:, :T], in1=SIN[:, :], op=ALU.mult),
                 reads=[(pn3, 0, 1), ("SIN", 0, 1)], writes=[("t2%d" % i, 0, 1)])
            if isq:
                S.op("pool", lambda e, ta=ta, tb=tb, j=j: e.tensor_tensor(out=QT[:, j, :], in0=ta[:], in1=tb[:], op=ALU.add),
                     reads=[("t1%d" % i, 0, 1), ("t2%d" % i, 0, 1)], writes=[("QT", j, j + 1)])
            else:
                S.op("pool", lambda e, ta=ta, tb=tb, j=j: e.tensor_tensor(out=KT[:, j - 8, 128:128 + T], in0=ta[:], in1=tb[:], op=ALU.add),
                     reads=[("t1%d" % i, 0, 1), ("t2%d" % i, 0, 1)], writes=[("KT", 1, 5)])
        for qb in range(4 if DBG['att'] >= 3 else 0):
            first = (t == 0 and qb == 0)
            for hk in range(4):
                kbs = [1] if first else [0, 1]
                Es = []
                for kb in kbs:
                    kc0 = (qb + kb) * 128
                    Et = E[ei % 4]
                    en = "E%d" % (ei % 4)
                    ei += 1
                    for par in range(2):
                        pss, psn = ph.newps()
                        hp = slice(par * 64, par * 64 + 64)
                        for pair in range(2):
                            col = pair * 128
                            S.op("pe", lambda e, pss=pss, hp=hp, col=col, kc0=kc0, hk=hk, pair=pair, qb=qb:
                                 e.matmul(pss[:, col:col + 128], lhsT=KT[hp, hk, kc0:kc0 + 128], rhs=QT[hp, 2 * hk + pair, qb * 128:(qb + 1) * 128],
                                          start=True, stop=True),
                                 reads=[("KT", 0, 5), ("QT", 2 * hk + pair, 2 * hk + pair + 1)], writes=[(psn, 0, 1)])
                        S.op("act", lambda e, Et=Et, pss=pss, par=par: e.activation(out=Et[:, par * 256:(par + 1) * 256], in_=pss[:, 0:256], func=AF.Exp, scale=0.125),
                             reads=[(psn, 0, 1)], writes=[(en, par, par + 1)])
                    mk, mkn = (mPb, "mPb") if kb == 0 else (mCb, "mCb")
                    S.op("pool", lambda e, Et=Et, mk=mk: e.tensor_tensor(out=Et[:], in0=Et[:], in1=mk[:], op=ALU.mult),
                         reads=[(en, 0, 2), (mkn, 0, 1)], writes=[(en, 0, 2)])
                    Es.append((Et, en, kb))
                psd, pdn = ph.newps()
                pso, pon = ph.newps()
                for n_, (Et, en, kb) in enumerate(Es):
                    S.op("pe", lambda e, psd=psd, Et=Et, n_=n_: e.matmul(psd[:, :], lhsT=ones_bf[:], rhs=Et[:], start=(n_ == 0), stop=(n_ == len(Es) - 1)),
                         reads=[("ones_bf", 0, 1), (en, 0, 2)], writes=[(pdn, 0, 1)])
                for n_, (Et, en, kb) in enumerate(Es):
                    vb = qb + kb
                    S.op("pe", lambda e, pso=pso, Et=Et, n_=n_, vb=vb, hk=hk: e.matmul(pso[:, :], lhsT=VB[:, vb, hk * 128:(hk + 1) * 128], rhs=Et[:],
                                                                                     start=(n_ == 0), stop=(n_ == len(Es) - 1)),
                         reads=[("VB", vb, vb + 1), (en, 0, 2)], writes=[(pon, 0, 1)])
                dn = den[hk % 2]
                dnn = "den%d" % (hk % 2)
                S.op("dve", lambda e, dn=dn, psd=psd, hk=hk: e.tensor_tensor(out=dn[:], in0=psd[:, :], in1=esink[:, hk * 512:(hk + 1) * 512], op=ALU.add),
                     reads=[(pdn, 0, 1), ("esink", 0, 1)], writes=[(dnn, 0, 1)])
                S.op("dve", lambda e, dn=dn: e.reciprocal(out=dn[:], in_=dn[:]), reads=[(dnn, 0, 1)], writes=[(dnn, 0, 1)])
                for par in range(2):
                    hp = slice(par * 64, par * 64 + 64)
                    S.op("dve", lambda e, dn=dn, pso=pso, hp=hp, par=par, hk=hk, qb=qb:
                         e.tensor_tensor(out=OT[hp, 2 * hk:2 * hk + 2, qb * 128:(qb + 1) * 128],
                                         in0=pso[hp, par * 256:(par + 1) * 256].rearrange("p (a q) -> p a q", a=2),
                                         in1=dn[hp, par * 256:(par + 1) * 256].rearrange("p (a q) -> p a q", a=2), op=ALU.mult),
                         reads=[(pon, 0, 1), (dnn, 0, 1)], writes=[("OT", 2 * hk, 2 * hk + 2)])
        for oc in range(8):
            pso, pon = ph.newps()
            for k in range(8):
                S.op("pe", lambda e, k=k, oc=oc, pso=pso: e.matmul(pso[:, :T], lhsT=wo[:, k, oc * 128:(oc + 1) * 128], rhs=OT[:, k, :],
                                                                   start=(k == 0), stop=(k == 7)),
                     reads=[("wo", k, k + 1), ("OT", k, k + 1)], writes=[(pon, 0, 1)])
            S.op("dve", lambda e, oc=oc, pso=pso: e.scalar_tensor_tensor(out=x[:, oc, :], in0=pso[:, :T], scalar=ph.c("b_o", oc), in1=x[:, oc, :],
                                                                        op0=ALU.add, op1=ALU.add),
                 reads=[(pon, 0, 1), ("x", oc, oc + 1), ph.cr("b_o", oc)], writes=[("x", oc, oc + 1)])
        S.dma("sp", dstv[:, :, tc0:tc0 + T], x[:], reads=[("x", 0, 8)])
    ph.close()


nc_bv_dram = [None]


def mixer_phase(nc, src, dst, w_in, w_out, w2a2_d, g2_d, cdram, coff, TL):
    T = 256
    NT = TL // T
    NCH = T // 64
    need = [("eps6", 1), ("eps5", 1), ("epsgn", 1), ("ab_g", 8), ("cin_b", 8), ("dw_w", 124), ("dw_b", 4), ("ln_g", 4), ("ln_b", 4),
            ("mu", 14), ("w0", 4), ("a0", 4), ("k_k", 4), ("k_a", 4), ("r_k", 4), ("gn_g", 4), ("gn_b", 4),
            ("ident", 128), ("bones", 128), ("ones", 128), ("m_su", 128), ("m_iu", 128), ("m_sl", 128)]
    ph = Phase(nc, cdram, coff, need)
    S = ph.S
    sb = ph.sb
    win = sb("win", [128, 8, 2816], BF16)
    wout = sb("wout", [128, 8, D], BF16)
    w2a2 = sb("w2a2", [128, 512], BF16)
    g2b = sb("g2b", [128, 512], BF16)
    x = sb("x", [128, 8, T], F32)
    hT = sb("hT", [128, 8, T], BF16)
    sq = sb("sq", [128, 8, T], BF16)
    rstd = sb("rstd", [128, T], F32)
    ones_bf = sb("ones_bf", [128, 128], BF16)
    omm = sb("omm", [128, 14], F32)
    GL = sb("GL", [128, 4, 30 + T], F32)
    cacc = sb("cacc", [128, 4, T], F32)
    csq = sb("csq", [128, 4, T], F32)
    sig = sb("sig", [128, T], F32)
    crs = sb("crs", [128, T], F32)
    catT = sb("catT", [128, 8, T], BF16)
    Pb = [sb("Pb%d" % i, [128, T + 1], F32) for i in range(2)]
    ptmp = [sb("ptmp%d" % i, [128, T], F32) for i in range(2)]
    pcarry = sb("pcarry", [128, 14], F32)
    rw12 = sb("rw12", [128, T], F32)
    rw13 = sb("rw13", [128, T], F32)
    twad = sb("twad", [128, T], BF16)
    sgd = sb("sgd", [128, T], BF16)
    lw4 = sb("lw4", [128, 4, T], F32)
    a4 = sb("a4", [128, 4, T], F32)
    gate4 = sb("gate4", [128, 4, T], F32)
    onesrow = sb("onesrow", [128, 64], F32)
    names = ["r", "k", "v", "kk", "q1", "rn", "kkn", "k2", "bb", "bonus", "cum", "P", "Pinv", "Pprev", "y", "yc"]
    st_ = {n: [sb("s_%s%d" % (n, i), [128, T], F32) for i in range(2)] for n in names}
    bdn = ["a", "r", "b", "k", "v"]
    BD = {n: [sb("bd_%s%d" % (n, i), [128, NCH, 128], F32) for i in range(2)] for n in bdn}
    mats = ["AT", "A", "YrbT", "XakT", "YrkT", "TT", "Vt", "Bt", "Kt", "W", "U"]
    M = {n: [sb("m_%s%d" % (n, i), [128, 128], F32) for i in range(2)] for n in mats}
    A2 = [sb("A2_%d" % i, [128, 128], F32) for i in range(2)]
    A2T = [sb("A2T_%d" % i, [128, 128], F32) for i in range(2)]
    TT2 = [sb("TT2_%d" % i, [128, 128], F32) for i in range(2)]
    H = [[sb("H%d_%d" % (cc, i), [128, 128], F32) for i in range(2)] for cc in range(4)]
    hcur = [0, 0, 0, 0]

    load_w(S, "pool", win, "win", w_in, 8)
    load_w(S, "pool", wout, "wout", w_out, 8)
    S.dma("pool", w2a2[:], w2a2_d, writes=[("w2a2", 0, 1)])
    S.dma("pool", g2b[:], g2_d, writes=[("g2b", 0, 1)])
    S.op("dve", lambda e: e.tensor_copy(out=ones_bf[:], in_=ph.c("ones", 0, 128)), reads=[ph.cr("ones", 0, 128)], writes=[("ones_bf", 0, 1)])
    S.op("dve", lambda e: e.tensor_copy(out=onesrow[:], in_=ph.c("ones", 0, 64)), reads=[ph.cr("ones", 0, 64)], writes=[("onesrow", 0, 1)])
    S.op("dve", lambda e: e.tensor_scalar(out=omm[:], in0=ph.c("mu", 0, 14), scalar1=-1.0, scalar2=1.0, op0=ALU.mult, op1=ALU.add),
         reads=[ph.cr("mu", 0, 14)], writes=[("omm", 0, 14)])
    S.op("pool", lambda e: e.memset(GL[:], 0.0), writes=[("GL", 0, 4 * 1000)])
    S.op("pool", lambda e: e.memset(pcarry[:], 0.0), writes=[("pcarry", 0, 14)])
    for n in bdn:
        for i in range(2):
            S.op("pool", lambda e, n=n, i=i: e.memset(BD[n][i][:], 0.0), writes=[("bd_%s%d" % (n, i), 0, NCH)])
    for cc in range(4):
        S.op("pool", lambda e, cc=cc: e.memset(H[cc][0][:], 0.0), writes=[("H%d_0" % cc, 0, 1)])

    ident = ph.c("ident", 0, 128)
    bones = ph.c("bones", 0, 128)
    ones32 = ph.c("ones", 0, 128)
    srcv = src.rearrange("(c p) t -> p c t", p=128)
    dstv = dst.rearrange("(c p) t -> p c t", p=128)

    def proj(pc):
        ps, pn = ph.newps()
        for k in range(8):
            S.op("pe", lambda e, k=k, ps=ps: e.matmul(ps[:, :T], lhsT=win[:, k, pc * 128:(pc + 1) * 128], rhs=hT[:, k, :],
                                                      start=(k == 0), stop=(k == 7)),
                 reads=[("win", k, k + 1), ("hT", k, k + 1)], writes=[(pn, 0, 1)])
        return ps, pn

    def shifted(ch, out, outname):
        i = ch % 2
        ps, pn = proj(8 + ch)
        pb, pbn, tm, tmn = Pb[i], "Pb%d" % i, ptmp[i], "ptmp%d" % i
        S.op("pool", lambda e: e.tensor_copy(out=pb[:, 0:1], in_=pcarry[:, ch:ch + 1]), reads=[("pcarry", ch, ch + 1)], writes=[(pbn, 0, 1)])
        S.op("act", lambda e: e.activation(out=pb[:, 1:T + 1], in_=ps[:, :T], func=AF.Identity), reads=[(pn, 0, 1)], writes=[(pbn, 1, T + 1)])
        S.op("pool", lambda e: e.tensor_copy(out=pcarry[:, ch:ch + 1], in_=pb[:, T:T + 1]), reads=[(pbn, T, T + 1)], writes=[("pcarry", ch, ch + 1)])
        S.op("pool", lambda e: e.tensor_scalar(out=tm[:], in0=pb[:, 0:T], scalar1=ph.c("mu", ch), scalar2=None, op0=ALU.mult),
             reads=[(pbn, 0, T), ph.cr("mu", ch)], writes=[(tmn, 0, 1)])
        S.op("dve", lambda e: e.scalar_tensor_tensor(out=out[:], in0=pb[:, 1:T + 1], scalar=omm[:, ch:ch + 1], in1=tm[:], op0=ALU.mult, op1=ALU.add),
             reads=[(pbn, 1, T + 1), ("omm", ch, ch + 1), (tmn, 0, 1)], writes=[(outname, 0, 1)])

    def mm1(ps, pn, lhsT, rhs, reads, start=True, stop=True, n=128):
        S.op("pe", lambda e: e.matmul(ps[:, :n], lhsT=lhsT, rhs=rhs, start=start, stop=stop), reads=reads, writes=[(pn, 0, 1)])

    def rsqrt_ps(ps, pn, out, outn, scale, epsname, n=T):
        S.op("act", lambda e: e.activation(out=out, in_=ps[:, :n], func=AF.Ln, scale=scale, bias=ph.c(epsname)),
             reads=[(pn, 0, 1), ph.cr(epsname)], writes=[(outn, 0, 1)])
        S.op("act", lambda e: e.activation(out=out, in_=out, func=AF.Exp, scale=-0.5), reads=[(outn, 0, 1)], writes=[(outn, 0, 1)])

    for t in range(NT):
        S.dma("sp", x[:], srcv[:, :, t * T:(t + 1) * T], writes=[("x", 0, 8)])
        rmsnorm(ph, x, "x", hT, "hT", sq, "sq", rstd, ones_bf, "ab_g", T)
        for cc in range(4):
            psa, pan = proj(cc)
            psg, pgn = proj(4 + cc)
            S.op("act", lambda e, psg=psg, cc=cc: e.activation(out=sig[:], in_=psg[:, :T], func=AF.Sigmoid, bias=ph.c("cin_b", 4 + cc)),
                 reads=[(pgn, 0, 1), ph.cr("cin_b", 4 + cc)], writes=[("sig", 0, 1)])
            S.op("dve", lambda e, psa=psa, cc=cc: e.scalar_tensor_tensor(out=GL[:, cc, 30:30 + T], in0=psa[:, :T], scalar=ph.c("cin_b", cc), in1=sig[:],
                                                                        op0=ALU.add, op1=ALU.mult),
                 reads=[(pan, 0, 1), ("sig", 0, 1), ph.cr("cin_b", cc)], writes=[("GL", cc * 1000 + 30, cc * 1000 + 30 + T)])
            eng = "dve"
            S.op(eng, lambda e, cc=cc: e.tensor_scalar(out=cacc[:, cc, :], in0=GL[:, cc, 30:30 + T], scalar1=ph.c("dw_w", cc * 31 + 30), scalar2=ph.c("dw_b", cc),
                                                       op0=ALU.mult, op1=ALU.add),
                 reads=[("GL", cc * 1000, cc * 1000 + 30 + T), ph.cr("dw_w", cc * 31, 31), ph.cr("dw_b", cc)], writes=[("cacc", cc, cc + 1)])
            for j in range(30):
                S.op(eng, lambda e, cc=cc, j=j: e.scalar_tensor_tensor(out=cacc[:, cc, :], in0=GL[:, cc, j:j + T], scalar=ph.c("dw_w", cc * 31 + j), in1=cacc[:, cc, :],
                                                                      op0=ALU.mult, op1=ALU.add),
                     reads=[("GL", cc * 1000, cc * 1000 + 30 + T), ("cacc", cc, cc + 1)], writes=[("cacc", cc, cc + 1)])
            S.op(eng, lambda e, cc=cc: e.tensor_copy(out=GL[:, cc, 0:30], in_=GL[:, cc, T:T + 30]),
                 reads=[("GL", cc * 1000 + T, cc * 1000 + T + 30)], writes=[("GL", cc * 1000, cc * 1000 + 30)])
        psm, pmn = ph.newps()
        for cc in range(4):
            mm1(psm, pmn, ones32, cacc[:, cc, :], [ph.cr("ones", 0, 128), ("cacc", cc, cc + 1)], start=(cc == 0), stop=(cc == 3), n=T)
        for cc in range(4):
            S.op("dve", lambda e, cc=cc, psm=psm: e.scalar_tensor_tensor(out=cacc[:, cc, :], in0=psm[:, :T], scalar=-1.0 / 512, in1=cacc[:, cc, :],
                                                                        op0=ALU.mult, op1=ALU.add),
                 reads=[(pmn, 0, 1), ("cacc", cc, cc + 1)], writes=[("cacc", cc, cc + 1)])
            S.op("pool", lambda e, cc=cc: e.tensor_tensor(out=csq[:, cc, :], in0=cacc[:, cc, :], in1=cacc[:, cc, :], op=ALU.mult),
                 reads=[("cacc", cc, cc + 1)], writes=[("csq", cc, cc + 1)])
        psv, pvn = ph.newps()
        for cc in range(4):
            mm1(psv, pvn, ones32, csq[:, cc, :], [ph.cr("ones", 0, 128), ("csq", cc, cc + 1)], start=(cc == 0), stop=(cc == 3), n=T)
        rsqrt_ps(psv, pvn, crs[:], "crs", 1.0 / 512, "eps5")
        for cc in range(4):
            S.op("dve", lambda e, cc=cc: e.tensor_tensor(out=cacc[:, cc, :], in0=cacc[:, cc, :], in1=crs[:], op=ALU.mult),
                 reads=[("cacc", cc, cc + 1), ("crs", 0, 1)], writes=[("cacc", cc, cc + 1)])
            S.op("act", lambda e, cc=cc: e.activation(out=catT[:, cc, :], in_=cacc[:, cc, :], func=AF.Silu, scale=ph.c("ln_g", cc), bias=ph.c("ln_b", cc)),
                 reads=[("cacc", cc, cc + 1), ph.cr("ln_g", cc), ph.cr("ln_b", cc)], writes=[("catT", cc, cc + 1)])
        shifted(12, rw12, "rw12")
        shifted(13, rw13, "rw13")
        S.op("act", lambda e: e.activation(out=twad[0:64, :], in_=rw12[0:64, :], func=AF.Tanh), reads=[("rw12", 0, 1)], writes=[("twad", 0, 1)])
        S.op("dve", lambda e: e.tensor_copy(out=twad[64:128, :], in_=rw12[64:128, :]), reads=[("rw12", 0, 1)], writes=[("twad", 1, 2)])
        S.op("act", lambda e: e.activation(out=sgd[:], in_=rw13[:], func=AF.Sigmoid), reads=[("rw13", 0, 1)], writes=[("sgd", 0, 1)])
        for cc in range(4):
            ps, pn = ph.newps()
            mm1(ps, pn, w2a2[0:64, cc * 128:(cc + 1) * 128], twad[0:64, :], [("w2a2", 0, 1), ("twad", 0, 1)], n=T)
            S.op("act", lambda e, ps=ps, cc=cc: e.activation(out=lw4[:, cc, :], in_=ps[:, :T], func=AF.Sigmoid, bias=ph.c("w0", cc)),
                 reads=[(pn, 0, 1), ph.cr("w0", cc)], writes=[("lw4", cc, cc + 1)])
            S.op("pool", lambda e, cc=cc: e.tensor_scalar(out=lw4[:, cc, :], in0=lw4[:, cc, :], scalar1=-float(np.exp(-0.5)), scalar2=None, op0=ALU.mult),
                 reads=[("lw4", cc, cc + 1)], writes=[("lw4", cc, cc + 1)])
            ps, pn = ph.newps()
            mm1(ps, pn, w2a2[64:128, cc * 128:(cc + 1) * 128], twad[64:128, :], [("w2a2", 0, 1), ("twad", 1, 2)], n=T)
            S.op("act", lambda e, ps=ps, cc=cc: e.activation(out=a4[:, cc, :], in_=ps[:, :T], func=AF.Sigmoid, bias=ph.c("a0", cc)),
                 reads=[(pn, 0, 1), ph.cr("a0", cc)], writes=[("a4", cc, cc + 1)])
            ps, pn = ph.newps()
            mm1(ps, pn, g2b[:, cc * 128:(cc + 1) * 128], sgd[:], [("g2b", 0, 1), ("sgd", 0, 1)], n=T)
            S.op("act", lambda e, ps=ps, cc=cc: e.activation(out=gate4[:, cc, :], in_=ps[:, :T], func=AF.Identity),
                 reads=[(pn, 0, 1)], writes=[("gate4", cc, cc + 1)])
        for cc in range(4):
            i = cc % 2
            s = {n: st_[n][i] for n in names}
            sn = {n: "s_%s%d" % (n, i) for n in names}
            bd = {n: BD[n][i] for n in bdn}
            bn = {n: "bd_%s%d" % (n, i) for n in bdn}
            shifted(cc, s["r"], sn["r"])
            shifted(4 + cc, s["k"], sn["k"])
            shifted(8 + cc, s["v"], sn["v"])
            lw = lw4[:, cc, :]
            av = a4[:, cc, :]

            def ew(eng, f, reads, writes):
                S.op(eng, f, reads=[(sn[r], 0, 1) if r in sn else r for r in reads], writes=[(sn[w], 0, 1) if w in sn else w for w in writes])
            ew("pool", lambda e, s=s, cc=cc: e.tensor_scalar(out=s["kk"][:], in0=s["k"][:], scalar1=ph.c("k_k", cc), scalar2=None, op0=ALU.mult),
               ["k", ph.cr("k_k", cc)], ["kk"])
            ew("pool", lambda e, s=s: e.tensor_tensor(out=s["q1"][:], in0=s["kk"][:], in1=s["kk"][:], op=ALU.mult), ["kk"], ["q1"])
            ps, pn = ph.newps()
            mm1(ps, pn, bones, s["q1"][:], [ph.cr("bones", 0, 128), (sn["q1"], 0, 1)], n=T)
            ew("dve", lambda e, s=s, ps=ps: e.tensor_scalar(out=s["rn"][:], in0=ps[:, :T], scalar1=1e-24, scalar2=None, op0=ALU.max), [(pn, 0, 1)], ["rn"])
            ew("act", lambda e, s=s: e.activation(out=s["rn"][:], in_=s["rn"][:], func=AF.Ln), ["rn"], ["rn"])
            ew("act", lambda e, s=s: e.activation(out=s["rn"][:], in_=s["rn"][:], func=AF.Exp, scale=-0.5), ["rn"], ["rn"])
            ew("dve", lambda e, s=s: e.tensor_tensor(out=s["kkn"][:], in0=s["kk"][:], in1=s["rn"][:], op=ALU.mult), ["kk", "rn"], ["kkn"])
            ew("pool", lambda e, s=s, av=av, cc=cc: e.tensor_scalar(out=s["q1"][:], in0=av, scalar1=-1.0, scalar2=ph.c("k_a", cc), op0=ALU.add, op1=ALU.mult),
               [("a4", cc, cc + 1), ph.cr("k_a", cc)], ["q1"])
            ew("dve", lambda e, s=s: e.scalar_tensor_tensor(out=s["k2"][:], in0=s["q1"][:], scalar=1.0, in1=s["k"][:], op0=ALU.add, op1=ALU.mult),
               ["q1", "k"], ["k2"])
            ew("dve", lambda e, s=s, av=av: e.tensor_tensor(out=s["bb"][:], in0=s["kkn"][:], in1=av, op=ALU.mult), ["kkn", ("a4", cc, cc + 1)], ["bb"])
            ew("dve", lambda e, s=s, cc=cc: e.scalar_tensor_tensor(out=s["q1"][:], in0=s["r"][:], scalar=ph.c("r_k", cc), in1=s["k2"][:], op0=ALU.mult, op1=ALU.mult),
               ["r", "k2", ph.cr("r_k", cc)], ["q1"])
            ps, pn = ph.newps()
            mm1(ps, pn, bones, s["q1"][:], [ph.cr("bones", 0, 128), (sn["q1"], 0, 1)], n=T)
            ew("dve", lambda e, s=s, ps=ps: e.tensor_tensor(out=s["bonus"][:], in0=ps[:, :T], in1=s["v"][:], op=ALU.mult), [(pn, 0, 1), "v"], ["bonus"])
            for n in range(NCH):
                ew("dve", lambda e, s=s, n=n, lw=lw: e.tensor_tensor_scan(out=s["cum"][:, n * 64:(n + 1) * 64], data0=onesrow[:], data1=lw[:, n * 64:(n + 1) * 64],
                                                                         initial=0.0, op0=ALU.mult, op1=ALU.add),
                   [("lw4", cc, cc + 1), ("onesrow", 0, 1)], ["cum"])
            ew("act", lambda e, s=s: e.activation(out=s["P"][:], in_=s["cum"][:], func=AF.Exp), ["cum"], ["P"])
            ew("act", lambda e, s=s: e.activation(out=s["Pinv"][:], in_=s["cum"][:], func=AF.Exp, scale=-1.0), ["cum"], ["Pinv"])
            ew("pool", lambda e, s=s, lw=lw: e.tensor_tensor(out=s["Pprev"][:], in0=s["cum"][:], in1=lw, op=ALU.subtract), ["cum", ("lw4", cc, cc + 1)], ["Pprev"])
            ew("act", lambda e, s=s: e.activation(out=s["Pprev"][:], in_=s["Pprev"][:], func=AF.Exp), ["Pprev"], ["Pprev"])
            v3 = lambda ap: ap.rearrange("p (n t) -> p n t", t=64)
            for hh in range(2):
                hp = slice(hh * 64, hh * 64 + 64)
                hc = slice(hh * 64, hh * 64 + 64)
                eng = "dve" if hh == 0 else "pool"
                ew("dve", lambda e, s=s, bd=bd, hp=hp, hc=hc: e.scalar_tensor_tensor(out=bd["a"][hp, :, hc], in0=v3(s["kkn"][hp, :]), scalar=-1.0, in1=v3(s["Pprev"][hp, :]),
                                                                                   op0=ALU.mult, op1=ALU.mult), ["kkn", "Pprev"], [(bn["a"], 0, NCH)])
                ew(eng, lambda e, s=s, bd=bd, hp=hp, hc=hc: e.tensor_tensor(out=bd["r"][hp, :, hc], in0=v3(s["r"][hp, :]), in1=v3(s["P"][hp, :]), op=ALU.mult),
                   ["r", "P"], [(bn["r"], 0, NCH)])
                ew(eng, lambda e, s=s, bd=bd, hp=hp, hc=hc: e.tensor_tensor(out=bd["b"][hp, :, hc], in0=v3(s["bb"][hp, :]), in1=v3(s["Pinv"][hp, :]), op=ALU.mult),
                   ["bb", "Pinv"], [(bn["b"], 0, NCH)])
                ew(eng, lambda e, s=s, bd=bd, hp=hp, hc=hc: e.tensor_tensor(out=bd["k"][hp, :, hc], in0=v3(s["k2"][hp, :]), in1=v3(s["Pinv"][hp, :]), op=ALU.mult),
                   ["k2", "Pinv"], [(bn["k"], 0, NCH)])
                ew(eng, lambda e, s=s, bd=bd, hp=hp, hc=hc: e.tensor_copy(out=bd["v"][hp, :, hc], in_=v3(s["v"][hp, :])), ["v"], [(bn["v"], 0, NCH)])
            for n in range(NCH):
                m = {k_: M[k_][n % 2] for k_ in mats}
                mn = {k_: "m_%s%d" % (k_, n % 2) for k_ in mats}
                ba, br, bb_, bk, bv_ = (bd[q][:, n, :] for q in bdn)
                R = lambda q: (bn[q], n, n + 1)

                def sc(lq, rq, lhs, rhs, out, mask):
                    ps, pn = ph.newps()
                    mm1(ps, pn, lhs, rhs, [R(lq), R(rq)])
                    S.op("dve", lambda e, ps=ps, m=m: e.tensor_tensor(out=m[out][:], in0=ps[:, :128], in1=ph.c(mask, 0, 128), op=ALU.mult),
                         reads=[(pn, 0, 1), ph.cr(mask, 0, 128)], writes=[(mn[out], 0, 1)])
                sc("b", "a", bb_, ba, "AT", "m_su")
                sc("a", "b", ba, bb_, "A", "m_sl")
                sc("b", "r", bb_, br, "YrbT", "m_iu")
                sc("k", "a", bk, ba, "XakT", "m_su")
                sc("k", "r", bk, br, "YrkT", "m_iu")
                S.op("pool", lambda e, m=m: e.tensor_tensor(out=m["TT"][:], in0=m["AT"][:], in1=ident, op=ALU.add),
                     reads=[(mn["AT"], 0, 1), ph.cr("ident", 0, 128)], writes=[(mn["TT"], 0, 1)])
                cA, cAn, cAT, cATn, cTT, cTTn = m["A"], mn["A"], m["AT"], mn["AT"], m["TT"], mn["TT"]
                for d in range(5):
                    nA, nAn = A2[d % 2], "A2_%d" % (d % 2)
                    nAT, nATn = A2T[d % 2], "A2T_%d" % (d % 2)
                    nTT, nTTn = TT2[d % 2], "TT2_%d" % (d % 2)
                    ps, pn = ph.newps()
                    mm1(ps, pn, cAT[:], cA[:], [(cATn, 0, 1), (cAn, 0, 1)])
                    S.op("act", lambda e, ps=ps, nA=nA: e.activation(out=nA[:], in_=ps[:, :128], func=AF.Identity), reads=[(pn, 0, 1)], writes=[(nAn, 0, 1)])
                    if d < 4:
                        ps2, pn2 = ph.newps()
                        mm1(ps2, pn2, cA[:], cAT[:], [(cATn, 0, 1), (cAn, 0, 1)])
                        S.op("dve", lambda e, ps2=ps2, nAT=nAT: e.tensor_copy(out=nAT[:], in_=ps2[:, :128]), reads=[(pn2, 0, 1)], writes=[(nATn, 0, 1)])
                    ps3, pn3 = ph.newps()
                    mm1(ps3, pn3, nA[:], cTT[:], [(nAn, 0, 1), (cTTn, 0, 1)])
                    S.op("dve", lambda e, ps3=ps3, nTT=nTT, cTT=cTT: e.tensor_tensor(out=nTT[:], in0=ps3[:, :128], in1=cTT[:], op=ALU.add),
                         reads=[(pn3, 0, 1), (cTTn, 0, 1)], writes=[(nTTn, 0, 1)])
                    cA, cAn, cAT, cATn, cTT, cTTn = nA, nAn, nAT, nATn, nTT, nTTn
                for q, dst_ in (("v", "Vt"), ("b", "Bt"), ("k", "Kt")):
                    ps, pn = ph.newps()
                    mm1(ps, pn, bd[q][:, n, :], ident, [R(q), ph.cr("ident", 0, 128)])
                    S.op("act", lambda e, ps=ps, dst_=dst_, m=m: e.activation(out=m[dst_][:], in_=ps[:, :128], func=AF.Identity),
                         reads=[(pn, 0, 1)], writes=[(mn[dst_], 0, 1)])
                Hc, Hcn = H[cc][hcur[cc]], "H%d_%d" % (cc, hcur[cc])
                Hn, Hnn = H[cc][1 - hcur[cc]], "H%d_%d" % (cc, 1 - hcur[cc])
                hcur[cc] = 1 - hcur[cc]
                ps, pn = ph.newps()
                mm1(ps, pn, ba, Hc[:], [R("a"), (Hcn, 0, 1)], start=True, stop=False)
                mm1(ps, pn, m["XakT"][:], m["Vt"][:], [(mn["XakT"], 0, 1), (mn["Vt"], 0, 1)], start=False, stop=True)
                S.op("act", lambda e, ps=ps, m=m: e.activation(out=m["W"][:], in_=ps[:, :128], func=AF.Identity), reads=[(pn, 0, 1)], writes=[(mn["W"], 0, 1)])
                ps, pn = ph.newps()
                mm1(ps, pn, cTT[:], m["W"][:], [(cTTn, 0, 1), (mn["W"], 0, 1)])
                S.op("dve", lambda e, ps=ps, m=m: e.tensor_copy(out=m["U"][:], in_=ps[:, :128]), reads=[(pn, 0, 1)], writes=[(mn["U"], 0, 1)])
                ps, pn = ph.newps()
                mm1(ps, pn, Hc[:], br, [(Hcn, 0, 1), R("r")], start=True, stop=False)
                mm1(ps, pn, m["U"][:], m["YrbT"][:], [(mn["U"], 0, 1), (mn["YrbT"], 0, 1)], start=False, stop=False)
                mm1(ps, pn, m["Vt"][:], m["YrkT"][:], [(mn["Vt"], 0, 1), (mn["YrkT"], 0, 1)], start=False, stop=True)
                for hh in range(2):
                    hp = slice(hh * 64, hh * 64 + 64)
                    S.op("act" if hh == 0 else "dve",
                         (lambda e, ps=ps, hp=hp, s=s, n=n: e.activation(out=s["y"][hp, n * 64:(n + 1) * 64], in_=ps[hp, hp], func=AF.Identity)) if hh == 0 else
                         (lambda e, ps=ps, hp=hp, s=s, n=n: e.tensor_copy(out=s["y"][hp, n * 64:(n + 1) * 64], in_=ps[hp, hp])),
                         reads=[(pn, 0, 1)], writes=[(sn["y"], 0, 1)])
                ps, pn = ph.newps()
                mm1(ps, pn, m["Bt"][:], m["U"][:], [(mn["Bt"], 0, 1), (mn["U"], 0, 1)], start=True, stop=False)
                mm1(ps, pn, m["Kt"][:], m["Vt"][:], [(mn["Kt"], 0, 1), (mn["Vt"], 0, 1)], start=False, stop=True)
                pc = s["P"][:, n * 64 + 63:n * 64 + 64]
                S.op("pool", lambda e, Hn=Hn, Hc=Hc, pc=pc: e.tensor_scalar(out=Hn[:], in0=Hc[:], scalar1=pc, scalar2=None, op0=ALU.mult),
                     reads=[(Hcn, 0, 1), (sn["P"], 0, 1)], writes=[(Hnn, 0, 1)])
                S.op("dve", lambda e, ps=ps, Hn=Hn, pc=pc: e.scalar_tensor_tensor(out=Hn[:], in0=ps[:, :128], scalar=pc, in1=Hn[:], op0=ALU.mult, op1=ALU.add),
                     reads=[(pn, 0, 1), (Hnn, 0, 1), (sn["P"], 0, 1)], writes=[(Hnn, 0, 1)])
            ps, pn = ph.newps()
            mm1(ps, pn, bones, s["y"][:], [ph.cr("bones", 0, 128), (sn["y"], 0, 1)], n=T)
            ew("dve", lambda e, s=s, ps=ps: e.scalar_tensor_tensor(out=s["yc"][:], in0=ps[:, :T], scalar=-1.0 / 64, in1=s["y"][:], op0=ALU.mult, op1=ALU.add),
               [(pn, 0, 1), "y"], ["yc"])
            ew("pool", lambda e, s=s: e.tensor_tensor(out=s["q1"][:], in0=s["yc"][:], in1=s["yc"][:], op=ALU.mult), ["yc"], ["q1"])
            ps, pn = ph.newps()
            mm1(ps, pn, bones, s["q1"][:], [ph.cr("bones", 0, 128), (sn["q1"], 0, 1)], n=T)
            rsqrt_ps(ps, pn, s["rn"][:], sn["rn"], 1.0 / 64, "epsgn")
            ew("dve", lambda e, s=s: e.tensor_tensor(out=s["yc"][:], in0=s["yc"][:], in1=s["rn"][:], op=ALU.mult), ["yc", "rn"], ["yc"])
            ew("pool", lambda e, s=s, cc=cc: e.tensor_scalar(out=s["yc"][:], in0=s["yc"][:], scalar1=ph.c("gn_g", cc), scalar2=ph.c("gn_b", cc), op0=ALU.mult, op1=ALU.add),
               ["yc", ph.cr("gn_g", cc), ph.cr("gn_b", cc)], ["yc"])
            ew("pool", lambda e, s=s: e.tensor_tensor(out=s["yc"][:], in0=s["yc"][:], in1=s["bonus"][:], op=ALU.add), ["yc", "bonus"], ["yc"])
            ew("dve", lambda e, s=s, cc=cc: e.tensor_tensor(out=catT[:, 4 + cc, :], in0=s["yc"][:], in1=gate4[:, cc, :], op=ALU.mult),
               ["yc", ("gate4", cc, cc + 1)], [("catT", 4 + cc, 5 + cc)])
        if DBG['mix'] == 1:
            for k in range(8):
                S.op("dve", lambda e, k=k: e.tensor_copy(out=x[:, k, :], in_=catT[:, k, :]), reads=[("catT", k, k + 1)], writes=[("x", k, k + 1)])
        for oc in range(8 if DBG['mix'] != 1 else 0):
            pso, pon = ph.newps()
            for k in range(8):
                S.op("pe", lambda e, k=k, oc=oc, pso=pso: e.matmul(pso[:, :T], lhsT=wout[:, k, oc * 128:(oc + 1) * 128], rhs=catT[:, k, :],
                                                                   start=(k == 0), stop=(k == 7)),
                     reads=[("wout", k, k + 1), ("catT", k, k + 1)], writes=[(pon, 0, 1)])
            S.op("dve", lambda e, oc=oc, pso=pso: e.tensor_tensor(out=x[:, oc, :], in0=pso[:, :T], in1=x[:, oc, :], op=ALU.add),
                 reads=[(pon, 0, 1), ("x", oc, oc + 1)], writes=[("x", oc, oc + 1)])
        S.dma("sp", dstv[:, :, t * T:(t + 1) * T], x[:], reads=[("x", 0, 8)])
    ph.close()


def build(TL, NC, phases=(0, 1, 2, 3)):
    nc = bass.Bass("TRN2", target_bir_lowering=False)
    dt = lambda n, s, d=F32, kind="ExternalInput": nc.dram_tensor(n, s, d, kind=kind).ap()
    xT = dt("xT", [D, TL]); pos = dt("pos", [1, TL], I32); cst = dt("cst", [128, NC])
    w_in = dt("w_in", [D, 2816]); w_out = dt("w_out", [D, D]); w2a2 = dt("w2a2", [128, 512]); g2 = dt("g2", [128, 512])
    w_up = [dt("w_up%d" % l, [D, 2 * DFF]) for l in range(2)]
    w_dn = [dt("w_dn%d" % l, [DFF, D]) for l in range(2)]
    w_qkv = dt("w_qkv", [D, 1536]); w_o = dt("w_o", [D, D]); bvd = dt("bvd", [1, 512])
    nc_bv_dram[0] = bvd
    yT = dt("yT", [D, TL], kind="ExternalOutput")
    scr = [dt("scr%d" % i, [D, TL], kind="Internal") for i in range(3)]
    coff = build.coff
    chain = [xT] + scr[:len(phases) - 1] + [yT]
    ci = 0
    for p in phases:
        s_, d_ = chain[ci], chain[ci + 1]
        ci += 1
        if p == 0:
            mixer_phase(nc, s_, d_, w_in, w_out, w2a2, g2, cst, coff, TL)
        elif p == 1:
            ffn_phase(nc, s_, d_, w_up[0], w_dn[0], cst, coff, 0, TL)
        elif p == 2:
            attn_phase(nc, s_, d_, pos, w_qkv, w_o, cst, coff, TL)
        else:
            ffn_phase(nc, s_, d_, w_up[1], w_dn[1], cst, coff, 1, TL)
    return nc


def host_inputs(inp, xb, posb):
    P = make_consts(inp)
    build.coff = P.off
    cst = P.build()
    f = lambda a: np.ascontiguousarray(np.asarray(a, np.float32))
    bq = np.asarray(inp["attn_b_qkv"][0], np.float32)
    bvd = np.concatenate([np.concatenate([bq[1280 + h * 64:1344 + h * 64]] * 2) for h in range(4)])[None, :]
    m = {"xT": f(np.asarray(xb).T), "pos": np.ascontiguousarray(np.asarray(posb, np.int32)[None, :]), "cst": cst,
         "w_in": f(inp["ab_w_in"][0]), "w_out": f(inp["ab_w_out"][0]),
         "w2a2": f(np.concatenate([inp["rwkv_w2"][0], inp["rwkv_a2"][0]], axis=0)), "g2": f(inp["rwkv_g2"][0]),
         "w_up0": f(inp["ffn_w_up"][0]), "w_up1": f(inp["ffn_w_up"][1]), "w_dn0": f(inp["ffn_w_down"][0]), "w_dn1": f(inp["ffn_w_down"][1]),
         "w_qkv": f(inp["attn_w_qkv"][0]), "w_o": f(inp["attn_w_o"][0]), "bvd": f(bvd)}
    return m, cst.shape[1]


def kernel(**inputs):
    x = np.asarray(inputs["x"], np.float32)
    pos = np.asarray(inputs["positions"])
    B, TL, _ = x.shape
    maps = []
    for c in range(8):
        m, NC = host_inputs(inputs, x[c % B], pos[c % B])
        maps.append(m)
    nc = build(TL, NC)
    res = run_bass_kernel_spmd(nc, maps, core_ids=list(range(8)))
    out = np.stack([np.ascontiguousarray(res.results[b]["yT"].T) for b in range(B)], axis=0)
    return out.astype(np.float32)
```

```python
import contextlib
import numpy as np
import concourse.bass as bass
import concourse.mybir as mybir
from concourse.bass_utils import run_bass_kernel_spmd

F32 = mybir.dt.float32
BF16 = mybir.dt.bfloat16
I32 = mybir.dt.int32
ALU = mybir.AluOpType
AF = mybir.ActivationFunctionType

ENGS = ["pe", "act", "dve", "pool", "sp"]
NDMASEM = 6
D = 1024
DFF = 2816
PI = float(np.pi)
DBG = {'att': 9, 'mix': 9, 'skip': ''}


class Sched:
    def __init__(self, nc, stack):
        self.nc = nc
        self.ops = []
        self.per_eng = {e: [] for e in ENGS}
        self.acc = {}
        self.dma_count = {e: 0 for e in ENGS}
        Sched.count = getattr(Sched, "count", 0) + 1
        sp_ = "q%d" % Sched.count
        self.sems = {e: stack.enter_context(nc.semaphore(sp_ + "s_" + e)) for e in ENGS if e != "sp"}
        self.dsems = {q: [stack.enter_context(nc.semaphore(sp_ + "d_%s%d" % (q, i))) for i in range(NDMASEM)]
                      for q in ("sp", "act", "pool")}

    def _deps(self, reads, writes, opid):
        deps = set()
        for (name, lo, hi) in reads:
            lst = self.acc.setdefault(name, [])
            for (l, h, k, o) in lst:
                if k == "w" and l < hi and lo < h:
                    deps.add(o)
            lst.append((lo, hi, "r", opid))
        for (name, lo, hi) in writes:
            lst = self.acc.setdefault(name, [])
            keep = []
            for (l, h, k, o) in lst:
                if l < hi and lo < h:
                    if o != opid:
                        deps.add(o)
                    if lo <= l and h <= hi:
                        continue
                keep.append((l, h, k, o))
            keep.append((lo, hi, "w", opid))
            self.acc[name] = keep
        return deps

    def op(self, eng, fn, reads=(), writes=()):
        opid = len(self.ops)
        deps = self._deps(reads, writes, opid)
        rec = dict(id=opid, eng=eng, fn=fn, deps=deps, dma=None, sig=False)
        self.ops.append(rec)
        self.per_eng[eng].append(rec)
        return opid

    def dma(self, q, out, in_, reads=(), writes=(), **kw):
        opid = len(self.ops)
        deps = self._deps(reads, writes, opid)
        n = self.dma_count[q]
        self.dma_count[q] += 1
        rec = dict(id=opid, eng=q, fn=None, deps=deps, dma=(q, n, out, in_, kw), sig=True)
        self.ops.append(rec)
        self.per_eng[q].append(rec)
        return opid

    def _finalize(self):
        for rec in self.ops:
            for d in rec["deps"]:
                p = self.ops[d]
                if p["dma"] is None and not (p["eng"] == "pe" and rec["eng"] == "pe" and rec["dma"] is None):
                    p["sig"] = True
        cnt = {e: 0 for e in ENGS}
        for rec in self.ops:
            if rec["dma"] is not None:
                q, n, _, _, _ = rec["dma"]
                rec["sem"] = self.dsems[q][n % NDMASEM]
                rec["val"] = 16 * (n // NDMASEM + 1)
            elif rec["sig"]:
                cnt[rec["eng"]] += 1
                rec["sem"] = self.sems[rec["eng"]]
                rec["val"] = cnt[rec["eng"]]

    def _emit_engine(self, ename, eng):
        seen = {}

        def wait(sem, val):
            key = id(sem)
            if seen.get(key, 0) >= val:
                return
            seen[key] = val
            eng.wait_ge(sem, val)

        for rec in self.per_eng[ename]:
            for d in sorted(rec["deps"]):
                p = self.ops[d]
                if p["dma"] is None and p["eng"] == "pe" and ename == "pe" and rec["dma"] is None:
                    continue
                wait(p["sem"], p["val"])
            if rec["dma"] is not None:
                q, n, out, in_, kw = rec["dma"]
                if n >= NDMASEM:
                    wait(rec["sem"], rec["val"] - 16)
                eng.dma_start(out=out, in_=in_, **kw).then_inc(rec["sem"], 16)
            else:
                ins = rec["fn"](eng)
                if rec["sig"]:
                    ins.then_inc(rec["sem"], 1)
        return wait

    def emit(self):
        self._finalize()
        with self.nc.Block() as block:
            @block.tensor
            def _(e):
                self._emit_engine("pe", e)

            @block.scalar
            def _(e):
                self._emit_engine("act", e)

            @block.vector
            def _(e):
                self._emit_engine("dve", e)

            @block.gpsimd
            def _(e):
                w = self._emit_engine("pool", e)
                n = self.dma_count["pool"]
                for i in range(NDMASEM):
                    k = len(range(i, n, NDMASEM))
                    if k:
                        w(self.dsems["pool"][i], 16 * k)

            @block.sync
            def _(e):
                w = self._emit_engine("sp", e)
                n = self.dma_count["sp"]
                for i in range(NDMASEM):
                    k = len(range(i, n, NDMASEM))
                    if k:
                        w(self.dsems["sp"][i], 16 * k)


def colvec(v):
    v = np.asarray(v, np.float32).reshape(-1, 128)
    return np.ascontiguousarray(v.T)


class Pack:
    def __init__(self):
        self.parts = []
        self.off = {}
        self.n = 0

    def add(self, name, arr):
        arr = np.asarray(arr, np.float32)
        if arr.ndim == 1:
            arr = arr[:, None]
        assert arr.shape[0] == 128, (name, arr.shape)
        arr = arr.reshape(128, -1)
        self.off[name] = self.n
        self.parts.append(arr)
        self.n += arr.shape[1]

    def build(self):
        return np.ascontiguousarray(np.concatenate(self.parts, axis=1))


def bd_mask(fn):
    m = np.zeros((128, 128), np.float32)
    i = np.arange(64)
    blk = fn(i[:, None], i[None, :]).astype(np.float32)
    m[:64, :64] = blk
    m[64:, 64:] = blk
    return m


def make_consts(inp, hmask=1.0):
    P = Pack()
    g = lambda n, l=0: np.asarray(inp[n][l], np.float32)
    P.add("eps6", np.full(128, 1e-6)); P.add("eps5", np.full(128, 1e-5)); P.add("epsgn", np.full(128, 64e-5))
    P.add("halfpi", np.full(128, PI / 2)); P.add("zero", np.zeros(128)); P.add("hmask", np.full(128, float(hmask)))
    P.add("ab_g", colvec(g("ab_norm_g")))
    P.add("cin_b", colvec(g("conv_in_b")))
    P.add("dw_w", np.stack([colvec(g("conv_dw_w")[j]) for j in range(31)], axis=2))
    P.add("dw_b", colvec(g("conv_dw_b"))); P.add("ln_g", colvec(g("conv_ln_g"))); P.add("ln_b", colvec(g("conv_ln_b")))
    P.add("mu", colvec(g("rwkv_mu"))); P.add("omm", colvec(1.0 - 0.0 * g("rwkv_mu")) * 0 + 0)
    P.add("w0", colvec(g("rwkv_w0"))); P.add("a0", colvec(g("rwkv_a0")))
    P.add("k_k", colvec(g("rwkv_k_k"))); P.add("k_a", colvec(g("rwkv_k_a")))
    P.add("r_k", colvec(g("rwkv_r_k").reshape(-1)))
    P.add("gn_g", colvec(g("rwkv_ln_g"))); P.add("gn_b", colvec(g("rwkv_ln_b")))
    for l in range(2):
        P.add("ffn_g%d" % l, colvec(g("ffn_norm_g", l)))
        P.add("fc_w%d" % l, np.stack([colvec(g("ffn_conv_w", l)[j]) for j in range(3)], axis=2))
        P.add("fc_b%d" % l, colvec(g("ffn_conv_b", l)))
    P.add("at_g", colvec(g("attn_norm_g")))
    bq = g("attn_b_qkv")
    P.add("b_q", colvec(bq[:1024]))
    P.add("b_kd", np.stack([np.concatenate([bq[1024 + h * 64:1088 + h * 64]] * 2) for h in range(4)], axis=1))
    P.add("qn_g", np.concatenate([g("attn_q_norm_g")] * 2)); P.add("kn_g", np.concatenate([g("attn_k_norm_g")] * 2))
    P.add("b_o", colvec(g("attn_b_o")))
    half = 8
    invf = (500000.0 ** (-(np.arange(half, dtype=np.float32) * 2.0) / 16)).astype(np.float32)
    iv = np.zeros(64, np.float32); iv[:8] = invf; iv[8:16] = invf
    P.add("invf", np.concatenate([iv, iv]))
    P.add("ident", np.eye(128, dtype=np.float32))
    P.add("bones", bd_mask(lambda a, b: a * 0 + b * 0 + 1))
    P.add("ones", np.ones((128, 128), np.float32))
    P.add("m_su", bd_mask(lambda s, t: s < t)); P.add("m_iu", bd_mask(lambda s, t: s <= t)); P.add("m_sl", bd_mask(lambda t, s: s < t))
    rm = np.zeros((64, 64), np.float32)
    for m in range(8):
        rm[m + 8, m] = -1.0
        rm[m, m + 8] = 1.0
    R2 = np.zeros((128, 128), np.float32); R2[:64, :64] = rm; R2[64:, 64:] = rm
    P.add("rmT", R2)
    kq = np.arange(128)
    mP = (kq[:, None] > kq[None, :]).astype(np.float32)
    mC = (kq[:, None] <= kq[None, :]).astype(np.float32)
    P.add("mP", np.tile(mP, (1, 4))); P.add("mC", np.tile(mC, (1, 4)))
    sk = g("attn_sinks")
    se = np.zeros((4, 2, 2, 128), np.float32)
    for hk in range(4):
        for par in range(2):
            for pair in range(2):
                se[hk, par, pair, :] = sk[4 * hk + 2 * pair + par]
    P.add("sinks", np.broadcast_to(se.reshape(1, -1), (128, 2048)))
    return P


class Phase:
    count = 0

    def __init__(self, nc, cdram, coff, need):
        self.nc = nc
        Phase.count += 1
        self.pfx = "p%d_" % Phase.count
        self.st = contextlib.ExitStack()
        self.S = Sched(nc, self.st)
        self.ps = [self.st.enter_context(nc.psum_tensor(self.pfx + "ps%d" % i, [128, 512], F32)) for i in range(8)]
        self.pi = 0
        self.coff = coff
        self.cmap = {}
        n = 0
        for name, w in need:
            self.cmap[name] = n
            n += w
        self.C = self.sb("C", [128, n], F32)
        for name, w in need:
            o = self.cmap[name]
            self.S.dma("sp", self.C[:, o:o + w], cdram[:, coff[name]:coff[name] + w], writes=[("C", o, o + w)], allow_slow_non_contiguous=True)

    def sb(self, name, shape, dt):
        return self.st.enter_context(self.nc.sbuf_tensor(self.pfx + name, shape, dt))

    def c(self, name, lo=0, w=1):
        o = self.cmap[name] + lo
        return self.C[:, o:o + w]

    def cr(self, name, lo=0, w=1):
        o = self.cmap[name] + lo
        return ("C", o, o + w)

    def newps(self):
        i = self.pi
        self.pi = (self.pi + 1) % 8
        return self.ps[i], "ps%d" % i

    def close(self):
        self.S.emit()
        self.st.close()


def rmsnorm(ph, x, xname, hT, hname, sq, sqname, rstd, ones_bf, gname, T):
    S = ph.S
    for k in range(8):
        S.op("act", lambda e, k=k: e.activation(out=sq[:, k, :T], in_=x[:, k, :T], func=AF.Square),
             reads=[(xname, k, k + 1)], writes=[(sqname, k, k + 1)])
    ps, pn = ph.newps()
    for k in range(8):
        S.op("pe", lambda e, k=k: e.matmul(ps[:, :T], lhsT=ones_bf[:], rhs=sq[:, k, :T], start=(k == 0), stop=(k == 7)),
             reads=[("ones_bf", 0, 1), (sqname, k, k + 1)], writes=[(pn, 0, 1)])
    S.op("act", lambda e: e.activation(out=rstd[:, :T], in_=ps[:, :T], func=AF.Ln, scale=1.0 / D, bias=ph.c("eps6")),
         reads=[(pn, 0, 1), ph.cr("eps6")], writes=[("rstd", 0, 1)])
    S.op("act", lambda e: e.activation(out=rstd[:, :T], in_=rstd[:, :T], func=AF.Exp, scale=-0.5),
         reads=[("rstd", 0, 1)], writes=[("rstd", 0, 1)])
    for k in range(8):
        S.op("dve",
             lambda e, k=k: e.scalar_tensor_tensor(out=hT[:, k, :T], in0=x[:, k, :T], scalar=ph.c(gname, k), in1=rstd[:, :T],
                                                   op0=ALU.mult, op1=ALU.mult),
             reads=[(xname, k, k + 1), ("rstd", 0, 1), ph.cr(gname, k)], writes=[(hname, k, k + 1)])


def load_w(S, q, dst, dname, src, nchunk):
    for k in range(nchunk):
        S.dma(q, dst[:, k, :], src[k * 128:(k + 1) * 128, :], writes=[(dname, k, k + 1)])


def ffn_phase(nc, src, dst, w_up, w_dn, cdram, coff, L, TL, skip=0):
    T = 512
    NT = TL // T
    need = [("eps6", 1), ("ffn_g%d" % L, 8), ("fc_w%d" % L, 66), ("fc_b%d" % L, 22), ("ones", 128)]
    ph = Phase(nc, cdram, coff, need)
    S = ph.S
    wup = ph.sb("wup", [128, 8, 2 * DFF], BF16)
    wdn = ph.sb("wdn", [128, 22, D], BF16)
    x = ph.sb("x", [128, 8, T], F32)
    hT = ph.sb("hT", [128, 8, T], BF16)
    act = ph.sb("act", [128, 22, T], BF16)
    rstd = ph.sb("rstd", [128, T], F32)
    ones_bf = ph.sb("ones_bf", [128, 128], BF16)
    G = [ph.sb("G%d" % i, [128, T + 2], F32) for i in range(2)]
    acc = [ph.sb("acc%d" % i, [128, T], F32) for i in range(2)]
    sg = [ph.sb("sg%d" % i, [128, T], F32) for i in range(2)]
    carry = ph.sb("carry", [128, 22, 2], F32)
    load_w(S, "pool", wup, "wup", w_up, 8)
    load_w(S, "pool", wdn, "wdn", w_dn, 22)
    S.op("dve", lambda e: e.tensor_copy(out=ones_bf[:], in_=ph.c("ones", 0, 128)), reads=[ph.cr("ones", 0, 128)], writes=[("ones_bf", 0, 1)])
    S.op("pool", lambda e: e.memset(carry[:], 0.0), writes=[("carry", 0, 22)])
    fw, fb, gname = "fc_w%d" % L, "fc_b%d" % L, "ffn_g%d" % L
    srcv = src.rearrange("(c p) t -> p c t", p=128)
    dstv = dst.rearrange("(c p) t -> p c t", p=128)
    for t in range(NT):
        S.dma("sp", x[:], srcv[:, :, t * T:(t + 1) * T], writes=[("x", 0, 8)])
        rmsnorm(ph, x, "x", hT, "hT", act, "act", rstd, ones_bf, gname, T)
        for c in range(22):
            i = c % 2
            Gt, at, st_ = G[i], acc[i], sg[i]
            gn, an, sn = "G%d" % i, "acc%d" % i, "sg%d" % i
            psg, pgn = ph.newps()
            psu, pun = ph.newps()
            for k in range(8):
                S.op("pe", lambda e, k=k, c=c, psg=psg: e.matmul(psg[:, :T], lhsT=wup[:, k, c * 128:(c + 1) * 128], rhs=hT[:, k, :],
                                                                 start=(k == 0), stop=(k == 7)),
                     reads=[("wup", k, k + 1), ("hT", k, k + 1)], writes=[(pgn, 0, 1)])
            for k in range(8):
                S.op("pe", lambda e, k=k, c=c, psu=psu: e.matmul(psu[:, :T], lhsT=wup[:, k, DFF + c * 128:DFF + (c + 1) * 128], rhs=hT[:, k, :],
                                                                 start=(k == 0), stop=(k == 7)),
                     reads=[("wup", k, k + 1), ("hT", k, k + 1)], writes=[(pun, 0, 1)])
            S.op("pool", lambda e, c=c, Gt=Gt: e.tensor_copy(out=Gt[:, 0:2], in_=carry[:, c, :]),
                 reads=[("carry", c, c + 1)], writes=[(gn, 0, 2)])
            S.op("act", lambda e, Gt=Gt, psg=psg: e.activation(out=Gt[:, 2:T + 2], in_=psg[:, :T], func=AF.Identity),
                 reads=[(pgn, 0, 1)], writes=[(gn, 2, T + 2)])
            S.op("pool", lambda e, c=c, Gt=Gt: e.tensor_copy(out=carry[:, c, :], in_=Gt[:, T:T + 2]),
                 reads=[(gn, T, T + 2)], writes=[("carry", c, c + 1)])
            S.op("dve", lambda e, c=c, Gt=Gt, at=at: e.tensor_scalar(out=at[:], in0=Gt[:, 2:T + 2], scalar1=ph.c(fw, c * 3 + 2), scalar2=ph.c(fb, c),
                                                                    op0=ALU.mult, op1=ALU.add),
                 reads=[(gn, 2, T + 2), ph.cr(fw, c * 3 + 2), ph.cr(fb, c)], writes=[(an, 0, 1)])
            S.op("dve", lambda e, c=c, Gt=Gt, at=at: e.scalar_tensor_tensor(out=at[:], in0=Gt[:, 1:T + 1], scalar=ph.c(fw, c * 3 + 1), in1=at[:],
                                                                            op0=ALU.mult, op1=ALU.add),
                 reads=[(gn, 1, T + 1), (an, 0, 1), ph.cr(fw, c * 3 + 1)], writes=[(an, 0, 1)])
            S.op("dve", lambda e, c=c, Gt=Gt, at=at: e.scalar_tensor_tensor(out=at[:], in0=Gt[:, 0:T], scalar=ph.c(fw, c * 3), in1=at[:],
                                                                           op0=ALU.mult, op1=ALU.add),
                 reads=[(gn, 0, T), (an, 0, 1), ph.cr(fw, c * 3)], writes=[(an, 0, 1)])
            S.op("act", lambda e, at=at, st_=st_: e.activation(out=st_[:], in_=at[:], func=AF.Silu),
                 reads=[(an, 0, 1)], writes=[(sn, 0, 1)])
            S.op("dve", lambda e, c=c, st_=st_, psu=psu: e.tensor_tensor(out=act[:, c, :], in0=psu[:, :T], in1=st_[:], op=ALU.mult),
                 reads=[(pun, 0, 1), (sn, 0, 1)], writes=[("act", c, c + 1)])
        for oc in range(8):
            pso, pon = ph.newps()
            for c in range(22):
                S.op("pe", lambda e, c=c, oc=oc, pso=pso: e.matmul(pso[:, :T], lhsT=wdn[:, c, oc * 128:(oc + 1) * 128], rhs=act[:, c, :],
                                                                   start=(c == 0), stop=(c == 21)),
                     reads=[("wdn", c, c + 1), ("act", c, c + 1)], writes=[(pon, 0, 1)])
            S.op("dve", lambda e, oc=oc, pso=pso: e.tensor_tensor(out=x[:, oc, :], in0=pso[:, :T], in1=x[:, oc, :], op=ALU.add),
                 reads=[(pon, 0, 1), ("x", oc, oc + 1)], writes=[("x", oc, oc + 1)])
        if t >= skip:
            S.dma("sp", dstv[:, :, (t - skip) * T:(t - skip + 1) * T], x[:], reads=[("x", 0, 8)])
    ph.close()


def attn_phase(nc, src, dst, pos, w_qkv, w_o, cdram, coff, TL, masked=False):
    T = 512
    NT = TL // T
    need = [("eps6", 1), ("halfpi", 1), ("zero", 1), ("hmask", 1), ("at_g", 8), ("b_q", 8), ("b_kd", 4), ("qn_g", 1), ("kn_g", 1), ("b_o", 8),
            ("invf", 1), ("bones", 128), ("ones", 128), ("rmT", 128), ("mP", 512), ("mC", 512), ("sinks", 2048)]
    ph = Phase(nc, cdram, coff, need)
    S = ph.S
    wq = ph.sb("wq", [128, 8, 1024], BF16)
    wkd = ph.sb("wkd", [128, 8, 4, 128], BF16)
    wvd = ph.sb("wvd", [128, 8, 4, 128], BF16)
    wo = ph.sb("wo", [128, 8, D], BF16)
    x = ph.sb("x", [128, 8, T], F32)
    hT = ph.sb("hT", [128, 8, T], BF16)
    sq = ph.sb("sq", [128, 8, T], BF16)
    rstd = ph.sb("rstd", [128, T], F32)
    ones_bf = ph.sb("ones_bf", [128, 128], BF16)
    mPb = ph.sb("mPb", [128, 512], BF16)
    mCb = ph.sb("mCb", [128, 512], BF16)
    esink = ph.sb("esink", [128, 2048], F32)
    COS = ph.sb("COS", [128, T], F32)
    SIN = ph.sb("SIN", [128, T], F32)
    posi = ph.sb("posi", [128, T], I32)
    ang = ph.sb("ang", [128, T], F32)
    nf = ph.sb("nf", [128, T], F32)
    QT = ph.sb("QT", [128, 8, T], BF16)
    KT = ph.sb("KT", [128, 4, 128 + T], BF16)
    VB = ph.sb("VB", [128, 5, 512], BF16)
    OT = ph.sb("OT", [128, 8, T], BF16)
    bv = ph.sb("bv", [1, 512], BF16)
    qraw = [ph.sb("qraw%d" % i, [128, T], F32) for i in range(2)]
    qsq = [ph.sb("qsq%d" % i, [128, T], F32) for i in range(2)]
    qr = [ph.sb("qr%d" % i, [128, T], F32) for i in range(2)]
    qn = [ph.sb("qn%d" % i, [128, T], F32) for i in range(2)]
    t1 = [ph.sb("t1%d" % i, [128, T], F32) for i in range(2)]
    t2 = [ph.sb("t2%d" % i, [128, T], F32) for i in range(2)]
    E = [ph.sb("E%d" % i, [128, 512], BF16) for i in range(4)]
    den = [ph.sb("den%d" % i, [128, 512], F32) for i in range(2)]
    load_w(S, "pool", wq, "wq", w_qkv[:, 0:1024], 8)
    load_w(S, "pool", wo, "wo", w_o, 8)
    wkv = ph.sb("wkv", [128, 8, 512], BF16)
    load_w(S, "pool", wkv, "wkv", w_qkv[:, 1024:1536], 8)
    for hk in range(4):
        for cp in range(2):
            S.op("pool", lambda e, hk=hk, cp=cp: e.tensor_copy(out=wkd[:, :, hk, cp * 64:(cp + 1) * 64], in_=wkv[:, :, hk * 64:(hk + 1) * 64]),
                 reads=[("wkv", 0, 8)], writes=[("wkd", hk * 2 + cp, hk * 2 + cp + 1)])
            S.op("dve", lambda e, hk=hk, cp=cp: e.tensor_copy(out=wvd[:, :, hk, cp * 64:(cp + 1) * 64], in_=wkv[:, :, 256 + hk * 64:256 + (hk + 1) * 64]),
                 reads=[("wkv", 0, 8)], writes=[("wvd", hk * 2 + cp, hk * 2 + cp + 1)])
    if 'bv' not in DBG['skip']:
        S.dma("pool", bv[:], nc_bv_dram[0], writes=[("bv", 0, 1)])
    S.op("dve", lambda e: e.tensor_copy(out=ones_bf[:], in_=ph.c("ones", 0, 128)), reads=[ph.cr("ones", 0, 128)], writes=[("ones_bf", 0, 1)])
    S.op("dve", lambda e: e.tensor_copy(out=mPb[:], in_=ph.c("mP", 0, 512)), reads=[ph.cr("mP", 0, 512)], writes=[("mPb", 0, 1)])
    S.op("dve", lambda e: e.tensor_copy(out=mCb[:], in_=ph.c("mC", 0, 512)), reads=[ph.cr("mC", 0, 512)], writes=[("mCb", 0, 1)])
    if 'esink' not in DBG['skip']:
      S.op("act", lambda e: e.activation(out=esink[:], in_=ph.c("sinks", 0, 2048), func=AF.Exp), reads=[ph.cr("sinks", 0, 2048)], writes=[("esink", 0, 1)])
    srcv = src.rearrange("(c p) t -> p c t", p=128)
    dstv = dst.rearrange("(c p) t -> p c t", p=128)
    ei = 0
    for t in range(NT):
        tc0 = t * T
        S.dma("sp", x[:], srcv[:, :, tc0:tc0 + T], writes=[("x", 0, 8)])
        rmsnorm(ph, x, "x", hT, "hT", sq, "sq", rstd, ones_bf, "at_g", T)
        S.dma("sp", posi[:], pos[0:1, tc0:tc0 + T].to_broadcast([128, T]), writes=[("posi", 0, 1)])
        S.op("dve", lambda e: e.tensor_copy(out=ang[:], in_=posi[:]), reads=[("posi", 0, 1)], writes=[("ang", 0, 1)])
        S.op("dve", lambda e: e.tensor_scalar(out=ang[:], in0=ang[:], scalar1=ph.c("invf"), scalar2=None, op0=ALU.mult),
             reads=[("ang", 0, 1), ph.cr("invf")], writes=[("ang", 0, 1)])
        for which, tab, tn in ((0, SIN, "SIN"), (1, COS, "COS")):
            if which == 1:
                S.op("dve", lambda e: e.tensor_scalar(out=ang[:], in0=ang[:], scalar1=PI / 2, scalar2=None, op0=ALU.add),
                     reads=[("ang", 0, 1)], writes=[("ang", 0, 1)])
            S.op("dve", lambda e: e.tensor_scalar(out=posi[:], in0=ang[:], scalar1=float(1.0 / (2 * PI)), scalar2=None, op0=ALU.mult),
                 reads=[("ang", 0, 1)], writes=[("posi", 0, 1)])
            S.op("dve", lambda e: e.tensor_copy(out=nf[:], in_=posi[:]), reads=[("posi", 0, 1)], writes=[("nf", 0, 1)])
            S.op("dve", lambda e: e.scalar_tensor_tensor(out=nf[:], in0=nf[:], scalar=float(-2 * PI), in1=ang[:], op0=ALU.mult, op1=ALU.add),
                 reads=[("nf", 0, 1), ("ang", 0, 1)], writes=[("nf", 0, 1)])
            S.op("dve", lambda e: e.tensor_scalar(out=nf[:], in0=nf[:], scalar1=PI, scalar2=-PI, op0=ALU.min, op1=ALU.max),
                 reads=[("nf", 0, 1)], writes=[("nf", 0, 1)])
            S.op("act", lambda e, tab=tab: e.activation(out=tab[:], in_=nf[:], func=AF.Sin), reads=[("nf", 0, 1)], writes=[(tn, 0, 1)])
        if t > 0:
            S.op("pool", lambda e: e.tensor_copy(out=KT[:, :, 0:128], in_=KT[:, :, T:T + 128]), reads=[("KT", 4, 5)], writes=[("KT", 0, 1)])
            S.op("pool", lambda e: e.tensor_copy(out=VB[:, 0, :], in_=VB[:, 4, :]), reads=[("VB", 4, 5)], writes=[("VB", 0, 1)])
        for b in range(4 if DBG['att'] >= 1 else 0):
            ps, pn = ph.newps()
            for k in range(8):
                S.op("pe", lambda e, k=k, b=b, ps=ps: e.matmul(ps[:, :], lhsT=hT[:, k, b * 128:(b + 1) * 128], rhs=wvd[:, k, :, :],
                                                               start=(k == 0), stop=False),
                     reads=[("hT", k, k + 1), ("wvd", 0, 8)], writes=[(pn, 0, 1)])
            S.op("pe", lambda e, ps=ps: e.matmul(ps[:, :], lhsT=ones_bf[0:1, :], rhs=bv[0:1, :], start=False, stop=True),
                 reads=[("ones_bf", 0, 1), ("bv", 0, 1)], writes=[(pn, 0, 1)])
            S.op("act", lambda e, b=b, ps=ps: e.activation(out=VB[:, 1 + b, :], in_=ps[:, :], func=AF.Identity),
                 reads=[(pn, 0, 1)], writes=[("VB", 1 + b, 2 + b)])
        for j in range(12 if DBG['att'] >= 2 else 0):
            i = j % 2
            isq = j < 8
            ps, pn = ph.newps()
            for k in range(8):
                if isq:
                    S.op("pe", lambda e, k=k, j=j, ps=ps: e.matmul(ps[:, :T], lhsT=wq[:, k, j * 128:(j + 1) * 128], rhs=hT[:, k, :],
                                                                   start=(k == 0), stop=(k == 7)),
                         reads=[("wq", k, k + 1), ("hT", k, k + 1)], writes=[(pn, 0, 1)])
                else:
                    S.op("pe", lambda e, k=k, j=j, ps=ps: e.matmul(ps[:, :T], lhsT=wkd[:, k, j - 8, :], rhs=hT[:, k, :],
                                                                   start=(k == 0), stop=(k == 7)),
                         reads=[("wkd", 0, 8), ("hT", k, k + 1)], writes=[(pn, 0, 1)])
            bias = ph.c("b_q", j) if isq else ph.c("b_kd", j - 8)
            bres = ph.cr("b_q", j) if isq else ph.cr("b_kd", j - 8)
            gcol, gres = (ph.c("qn_g"), ph.cr("qn_g")) if isq else (ph.c("kn_g"), ph.cr("kn_g"))
            qa, qs_, qrr, qnn, ta, tb = qraw[i], qsq[i], qr[i], qn[i], t1[i], t2[i]
            S.op("act", lambda e, ps=ps, qa=qa, bias=bias: e.activation(out=qa[:], in_=ps[:, :T], func=AF.Identity, bias=bias),
                 reads=[(pn, 0, 1), bres], writes=[("qraw%d" % i, 0, 1)])
            S.op("pool", lambda e, qa=qa, qs_=qs_: e.tensor_tensor(out=qs_[:], in0=qa[:], in1=qa[:], op=ALU.mult),
                 reads=[("qraw%d" % i, 0, 1)], writes=[("qsq%d" % i, 0, 1)])
            ps2, pn2 = ph.newps()
            S.op("pe", lambda e, ps2=ps2, qs_=qs_: e.matmul(ps2[:, :T], lhsT=ph.c("bones", 0, 128), rhs=qs_[:], start=True, stop=True),
                 reads=[ph.cr("bones", 0, 128), ("qsq%d" % i, 0, 1)], writes=[(pn2, 0, 1)])
            S.op("act", lambda e, ps2=ps2, qrr=qrr: e.activation(out=qrr[:], in_=ps2[:, :T], func=AF.Ln, scale=1.0 / 64, bias=ph.c("eps6")),
                 reads=[(pn2, 0, 1), ph.cr("eps6")], writes=[("qr%d" % i, 0, 1)])
            S.op("act", lambda e, qrr=qrr: e.activation(out=qrr[:], in_=qrr[:], func=AF.Exp, scale=-0.5),
                 reads=[("qr%d" % i, 0, 1)], writes=[("qr%d" % i, 0, 1)])
            S.op("dve", lambda e, qa=qa, qrr=qrr, qnn=qnn, gcol=gcol: e.scalar_tensor_tensor(out=qnn[:], in0=qa[:], scalar=gcol, in1=qrr[:],
                                                                                         op0=ALU.mult, op1=ALU.mult),
                 reads=[("qraw%d" % i, 0, 1), ("qr%d" % i, 0, 1), gres], writes=[("qn%d" % i, 0, 1)])
            ps3, pn3 = ph.newps()
            S.op("pe", lambda e, ps3=ps3, qnn=qnn: e.matmul(ps3[:, :T], lhsT=ph.c("rmT", 0, 128), rhs=qnn[:], start=True, stop=True),
                 reads=[ph.cr("rmT", 0, 128), ("qn%d" % i, 0, 1)], writes=[(pn3, 0, 1)])
            S.op("pool", lambda e, qnn=qnn, ta=ta: e.tensor_tensor(out=ta[:], in0=qnn[:], in1=COS[:, :], op=ALU.mult),
                 reads=[("qn%d" % i, 0, 1), ("COS", 0, 1)], writes=[("t1%d" % i, 0, 1)])
            S.op("dve", lambda e, ps3=ps3, tb=tb: e.tensor_tensor(out=tb[:], in0=ps3[:, :T], in1=SIN[:, :], op=ALU.mult),
                 reads=[(pn3, 0, 1), ("SIN", 0, 1)], writes=[("t2%d" % i, 0, 1)])
            if isq:
                S.op("pool", lambda e, ta=ta, tb=tb, j=j: e.tensor_tensor(out=QT[:, j, :], in0=ta[:], in1=tb[:], op=ALU.add),
                     reads=[("t1%d" % i, 0, 1), ("t2%d" % i, 0, 1)], writes=[("QT", j, j + 1)])
            else:
                S.op("pool", lambda e, ta=ta, tb=tb, j=j: e.tensor_tensor(out=KT[:, j - 8, 128:128 + T], in0=ta[:], in1=tb[:], op=ALU.add),
                     reads=[("t1%d" % i, 0, 1), ("t2%d" % i, 0, 1)], writes=[("KT", 1, 5)])
        for qb in range(4 if DBG['att'] >= 3 else 0):
            first = (t == 0 and qb == 0)
            for hk in range(4):
                kbs = [1] if first else [0, 1]
                Es = []
                for kb in kbs:
                    kc0 = (qb + kb) * 128
                    Et = E[ei % 4]
                    en = "E%d" % (ei % 4)
                    ei += 1
                    for par in range(2):
                        pss, psn = ph.newps()
                        hp = slice(par * 64, par * 64 + 64)
                        for pair in range(2):
                            col = pair * 128
                            S.op("pe", lambda e, pss=pss, hp=hp, col=col, kc0=kc0, hk=hk, pair=pair, qb=qb:
                                 e.matmul(pss[:, col:col + 128], lhsT=KT[hp, hk, kc0:kc0 + 128], rhs=QT[hp, 2 * hk + pair, qb * 128:(qb + 1) * 128],
                                          start=True, stop=True),
                                 reads=[("KT", 0, 5), ("QT", 2 * hk + pair, 2 * hk + pair + 1)], writes=[(psn, 0, 1)])
                        S.op("act", lambda e, Et=Et, pss=pss, par=par: e.activation(out=Et[:, par * 256:(par + 1) * 256], in_=pss[:, 0:256], func=AF.Exp, scale=0.125),
                             reads=[(psn, 0, 1)], writes=[(en, par, par + 1)])
                    mk, mkn = (mPb, "mPb") if kb == 0 else (mCb, "mCb")
                    S.op("pool", lambda e, Et=Et, mk=mk: e.tensor_tensor(out=Et[:], in0=Et[:], in1=mk[:], op=ALU.mult),
                         reads=[(en, 0, 2), (mkn, 0, 1)], writes=[(en, 0, 2)])
                    if masked and t == 1 and qb == 0 and kb == 0:
                        S.op("act", lambda e, Et=Et: e.activation(out=Et[:], in_=Et[:], func=AF.Identity, scale=ph.c("hmask")),
                             reads=[(en, 0, 2), ph.cr("hmask")], writes=[(en, 0, 2)])
                    Es.append((Et, en, kb))
                psd, pdn = ph.newps()
                pso, pon = ph.newps()
                for n_, (Et, en, kb) in enumerate(Es):
                    S.op("pe", lambda e, psd=psd, Et=Et, n_=n_: e.matmul(psd[:, :], lhsT=ones_bf[:], rhs=Et[:], start=(n_ == 0), stop=(n_ == len(Es) - 1)),
                         reads=[("ones_bf", 0, 1), (en, 0, 2)], writes=[(pdn, 0, 1)])
                for n_, (Et, en, kb) in enumerate(Es):
                    vb = qb + kb
                    S.op("pe", lambda e, pso=pso, Et=Et, n_=n_, vb=vb, hk=hk: e.matmul(pso[:, :], lhsT=VB[:, vb, hk * 128:(hk + 1) * 128], rhs=Et[:],
                                                                                     start=(n_ == 0), stop=(n_ == len(Es) - 1)),
                         reads=[("VB", vb, vb + 1), (en, 0, 2)], writes=[(pon, 0, 1)])
                dn = den[hk % 2]
                dnn = "den%d" % (hk % 2)
                S.op("dve", lambda e, dn=dn, psd=psd, hk=hk: e.tensor_tensor(out=dn[:], in0=psd[:, :], in1=esink[:, hk * 512:(hk + 1) * 512], op=ALU.add),
                     reads=[(pdn, 0, 1), ("esink", 0, 1)], writes=[(dnn, 0, 1)])
                S.op("dve", lambda e, dn=dn: e.reciprocal(out=dn[:], in_=dn[:]), reads=[(dnn, 0, 1)], writes=[(dnn, 0, 1)])
                for par in range(2):
                    hp = slice(par * 64, par * 64 + 64)
                    S.op("dve", lambda e, dn=dn, pso=pso, hp=hp, par=par, hk=hk, qb=qb:
                         e.tensor_tensor(out=OT[hp, 2 * hk:2 * hk + 2, qb * 128:(qb + 1) * 128],
                                         in0=pso[hp, par * 256:(par + 1) * 256].rearrange("p (a q) -> p a q", a=2),
                                         in1=dn[hp, par * 256:(par + 1) * 256].rearrange("p (a q) -> p a q", a=2), op=ALU.mult),
                         reads=[(pon, 0, 1), (dnn, 0, 1)], writes=[("OT", 2 * hk, 2 * hk + 2)])
        for oc in range(8):
            pso, pon = ph.newps()
            for k in range(8):
                S.op("pe", lambda e, k=k, oc=oc, pso=pso: e.matmul(pso[:, :T], lhsT=wo[:, k, oc * 128:(oc + 1) * 128], rhs=OT[:, k, :],
                                                                   start=(k == 0), stop=(k == 7)),
                     reads=[("wo", k, k + 1), ("OT", k, k + 1)], writes=[(pon, 0, 1)])
            S.op("dve", lambda e, oc=oc, pso=pso: e.scalar_tensor_tensor(out=x[:, oc, :], in0=pso[:, :T], scalar=ph.c("b_o", oc), in1=x[:, oc, :],
                                                                        op0=ALU.add, op1=ALU.add),
                 reads=[(pon, 0, 1), ("x", oc, oc + 1), ph.cr("b_o", oc)], writes=[("x", oc, oc + 1)])
        if masked and t == 0:
            for k in range(8):
                S.op("act", lambda e, k=k: e.activation(out=x[:, k, :], in_=x[:, k, :], func=AF.Identity, scale=ph.c("hmask")),
                     reads=[("x", k, k + 1), ph.cr("hmask")], writes=[("x", k, k + 1)])
        S.dma("sp", dstv[:, :, tc0:tc0 + T], x[:], reads=[("x", 0, 8)])
    ph.close()


nc_bv_dram = [None]


def mixer_phase(nc, src, dst, w_in, w_out, w2a2_d, g2_d, cdram, coff, TL, NPRE=0, masked=False):
    T = 256
    NT = TL // T
    NCH = T // 64
    need = [("eps6", 1), ("eps5", 1), ("epsgn", 1), ("hmask", 1), ("ab_g", 8), ("cin_b", 8), ("dw_w", 124), ("dw_b", 4), ("ln_g", 4), ("ln_b", 4),
            ("mu", 14), ("w0", 4), ("a0", 4), ("k_k", 4), ("k_a", 4), ("r_k", 4), ("gn_g", 4), ("gn_b", 4),
            ("ident", 128), ("bones", 128), ("ones", 128), ("m_su", 128), ("m_iu", 128), ("m_sl", 128)]
    ph = Phase(nc, cdram, coff, need)
    S = ph.S
    sb = ph.sb
    win = sb("win", [128, 8, 2816], BF16)
    wout = sb("wout", [128, 8, D], BF16)
    w2a2 = sb("w2a2", [128, 512], BF16)
    g2b = sb("g2b", [128, 512], BF16)
    x = sb("x", [128, 8, T], F32)
    hT = sb("hT", [128, 8, T], BF16)
    sq = sb("sq", [128, 8, T], BF16)
    rstd = sb("rstd", [128, T], F32)
    ones_bf = sb("ones_bf", [128, 128], BF16)
    omm = sb("omm", [128, 14], F32)
    GL = sb("GL", [128, 4, 30 + T], F32)
    cacc = sb("cacc", [128, 4, T], F32)
    csq = sb("csq", [128, 4, T], F32)
    sig = sb("sig", [128, T], F32)
    crs = sb("crs", [128, T], F32)
    catT = sb("catT", [128, 8, T], BF16)
    Pb = [sb("Pb%d" % i, [128, T + 1], F32) for i in range(2)]
    ptmp = [sb("ptmp%d" % i, [128, T], F32) for i in range(2)]
    pcarry = sb("pcarry", [128, 14], F32)
    rw12 = sb("rw12", [128, T], F32)
    rw13 = sb("rw13", [128, T], F32)
    twad = sb("twad", [128, T], BF16)
    sgd = sb("sgd", [128, T], BF16)
    lw4 = sb("lw4", [128, 4, T], F32)
    a4 = sb("a4", [128, 4, T], F32)
    gate4 = sb("gate4", [128, 4, T], F32)
    onesrow = sb("onesrow", [128, 64], F32)
    names = ["r", "k", "v", "kk", "q1", "rn", "kkn", "k2", "bb", "bonus", "cum", "P", "Pinv", "Pprev", "y", "yc"]
    st_ = {n: [sb("s_%s%d" % (n, i), [128, T], F32) for i in range(2)] for n in names}
    bdn = ["a", "r", "b", "k", "v"]
    BD = {n: [sb("bd_%s%d" % (n, i), [128, NCH, 128], BF16) for i in range(2)] for n in bdn}
    mats = ["AT", "A", "YrbT", "XakT", "YrkT", "TT", "Vt", "Bt", "Kt", "W", "U"]
    M = {n: [sb("m_%s%d" % (n, i), [128, 128], BF16) for i in range(2)] for n in mats}
    A2 = [[sb("A2_%d_%d" % (j, i), [128, 128], BF16) for i in range(2)] for j in range(2)]
    A2T = [[sb("A2T_%d_%d" % (j, i), [128, 128], BF16) for i in range(2)] for j in range(2)]
    TT2 = [[sb("TT2_%d_%d" % (j, i), [128, 128], BF16) for i in range(2)] for j in range(2)]
    H = [[sb("H%d_%d" % (cc, i), [128, 128], F32) for i in range(2)] for cc in range(4)]
    hcur = [0, 0, 0, 0]
    Hb = [[sb("Hb%d_%d" % (cc, i), [128, 128], BF16) for i in range(2)] for cc in range(4)]
    ident_bf = sb("ident_bf", [128, 128], BF16)

    load_w(S, "pool", win, "win", w_in, 8)
    load_w(S, "pool", wout, "wout", w_out, 8)
    S.dma("pool", w2a2[:], w2a2_d, writes=[("w2a2", 0, 1)])
    S.dma("pool", g2b[:], g2_d, writes=[("g2b", 0, 1)])
    S.op("dve", lambda e: e.tensor_copy(out=ones_bf[:], in_=ph.c("ones", 0, 128)), reads=[ph.cr("ones", 0, 128)], writes=[("ones_bf", 0, 1)])
    S.op("dve", lambda e: e.tensor_copy(out=onesrow[:], in_=ph.c("ones", 0, 64)), reads=[ph.cr("ones", 0, 64)], writes=[("onesrow", 0, 1)])
    S.op("dve", lambda e: e.tensor_scalar(out=omm[:], in0=ph.c("mu", 0, 14), scalar1=-1.0, scalar2=1.0, op0=ALU.mult, op1=ALU.add),
         reads=[ph.cr("mu", 0, 14)], writes=[("omm", 0, 14)])
    S.op("pool", lambda e: e.memset(GL[:], 0.0), writes=[("GL", 0, 4 * 1000)])
    S.op("pool", lambda e: e.memset(pcarry[:], 0.0), writes=[("pcarry", 0, 14)])
    for n in bdn:
        for i in range(2):
            S.op("pool", lambda e, n=n, i=i: e.memset(BD[n][i][:], 0.0), writes=[("bd_%s%d" % (n, i), 0, NCH)])
    for cc in range(4):
        S.op("pool", lambda e, cc=cc: e.memset(H[cc][0][:], 0.0), writes=[("H%d_0" % cc, 0, 1)])
        S.op("pool", lambda e, cc=cc: e.memset(Hb[cc][0][:], 0.0), writes=[("Hb%d_0" % cc, 0, 1)])
    S.op("dve", lambda e: e.tensor_copy(out=ident_bf[:], in_=ph.c("ident", 0, 128)), reads=[ph.cr("ident", 0, 128)], writes=[("ident_bf", 0, 1)])

    ident = ph.c("ident", 0, 128)
    bones = ph.c("bones", 0, 128)
    ones32 = ph.c("ones", 0, 128)
    srcv = src.rearrange("(c p) t -> p c t", p=128)
    dstv = dst.rearrange("(c p) t -> p c t", p=128)

    def proj(pc):
        ps, pn = ph.newps()
        for k in range(8):
            S.op("pe", lambda e, k=k, ps=ps: e.matmul(ps[:, :T], lhsT=win[:, k, pc * 128:(pc + 1) * 128], rhs=hT[:, k, :],
                                                      start=(k == 0), stop=(k == 7)),
                 reads=[("win", k, k + 1), ("hT", k, k + 1)], writes=[(pn, 0, 1)])
        return ps, pn

    def shifted(ch, out, outname):
        i = ch % 2
        ps, pn = proj(8 + ch)
        pb, pbn, tm, tmn = Pb[i], "Pb%d" % i, ptmp[i], "ptmp%d" % i
        S.op("pool", lambda e: e.tensor_copy(out=pb[:, 0:1], in_=pcarry[:, ch:ch + 1]), reads=[("pcarry", ch, ch + 1)], writes=[(pbn, 0, 1)])
        S.op("act", lambda e: e.activation(out=pb[:, 1:T + 1], in_=ps[:, :T], func=AF.Identity), reads=[(pn, 0, 1)], writes=[(pbn, 1, T + 1)])
        S.op("pool", lambda e: e.tensor_copy(out=pcarry[:, ch:ch + 1], in_=pb[:, T:T + 1]), reads=[(pbn, T, T + 1)], writes=[("pcarry", ch, ch + 1)])
        S.op("act", lambda e: e.activation(out=tm[:], in_=pb[:, 0:T], func=AF.Identity, scale=ph.c("mu", ch)),
             reads=[(pbn, 0, T), ph.cr("mu", ch)], writes=[(tmn, 0, 1)])
        S.op("dve", lambda e: e.scalar_tensor_tensor(out=out[:], in0=pb[:, 1:T + 1], scalar=omm[:, ch:ch + 1], in1=tm[:], op0=ALU.mult, op1=ALU.add),
             reads=[(pbn, 1, T + 1), ("omm", ch, ch + 1), (tmn, 0, 1)], writes=[(outname, 0, 1)])

    def mm1(ps, pn, lhsT, rhs, reads, start=True, stop=True, n=128):
        S.op("pe", lambda e: e.matmul(ps[:, :n], lhsT=lhsT, rhs=rhs, start=start, stop=stop), reads=reads, writes=[(pn, 0, 1)])

    def rsqrt_ps(ps, pn, out, outn, scale, epsname, n=T):
        S.op("act", lambda e: e.activation(out=out, in_=ps[:, :n], func=AF.Ln, scale=scale, bias=ph.c(epsname)),
             reads=[(pn, 0, 1), ph.cr(epsname)], writes=[(outn, 0, 1)])
        S.op("act", lambda e: e.activation(out=out, in_=out, func=AF.Exp, scale=-0.5), reads=[(outn, 0, 1)], writes=[(outn, 0, 1)])

    for t in range(NT):
        full = t >= NPRE
        plast = (t == NPRE - 1)
        mtile = masked and (NPRE - 1 <= t < NPRE + 2)
        S.dma("sp", x[:], srcv[:, :, t * T:(t + 1) * T], writes=[("x", 0, 8)])
        rmsnorm(ph, x, "x", hT, "hT", sq, "sq", rstd, ones_bf, "ab_g", T)
        for cc in range(4 if (full or plast) else 0):
            psa, pan = proj(cc)
            psg, pgn = proj(4 + cc)
            S.op("act", lambda e, psg=psg, cc=cc: e.activation(out=sig[:], in_=psg[:, :T], func=AF.Sigmoid, bias=ph.c("cin_b", 4 + cc)),
                 reads=[(pgn, 0, 1), ph.cr("cin_b", 4 + cc)], writes=[("sig", 0, 1)])
            S.op("dve", lambda e, psa=psa, cc=cc: e.scalar_tensor_tensor(out=GL[:, cc, 30:30 + T], in0=psa[:, :T], scalar=ph.c("cin_b", cc), in1=sig[:],
                                                                        op0=ALU.add, op1=ALU.mult),
                 reads=[(pan, 0, 1), ("sig", 0, 1), ph.cr("cin_b", cc)], writes=[("GL", cc * 1000 + 30, cc * 1000 + 30 + T)])
            eng = "dve"
            if mtile:
                S.op("act", lambda e, cc=cc: e.activation(out=GL[:, cc, 30:30 + T], in_=GL[:, cc, 30:30 + T], func=AF.Identity, scale=ph.c("hmask")),
                     reads=[("GL", cc * 1000 + 30, cc * 1000 + 30 + T), ph.cr("hmask")], writes=[("GL", cc * 1000 + 30, cc * 1000 + 30 + T)])
            if full:
              S.op(eng, lambda e, cc=cc: e.tensor_scalar(out=cacc[:, cc, :], in0=GL[:, cc, 30:30 + T], scalar1=ph.c("dw_w", cc * 31 + 30), scalar2=ph.c("dw_b", cc),
                                                       op0=ALU.mult, op1=ALU.add),
                 reads=[("GL", cc * 1000, cc * 1000 + 30 + T), ph.cr("dw_w", cc * 31, 31), ph.cr("dw_b", cc)], writes=[("cacc", cc, cc + 1)])
            for j in range(30 if full else 0):
                S.op(eng, lambda e, cc=cc, j=j: e.scalar_tensor_tensor(out=cacc[:, cc, :], in0=GL[:, cc, j:j + T], scalar=ph.c("dw_w", cc * 31 + j), in1=cacc[:, cc, :],
                                                                      op0=ALU.mult, op1=ALU.add),
                     reads=[("GL", cc * 1000, cc * 1000 + 30 + T), ("cacc", cc, cc + 1)], writes=[("cacc", cc, cc + 1)])
            S.op(eng, lambda e, cc=cc: e.tensor_copy(out=GL[:, cc, 0:30], in_=GL[:, cc, T:T + 30]),
                 reads=[("GL", cc * 1000 + T, cc * 1000 + T + 30)], writes=[("GL", cc * 1000, cc * 1000 + 30)])
        psm, pmn = ph.newps()
        for cc in range(4 if full else 0):
            mm1(psm, pmn, ones32, cacc[:, cc, :], [ph.cr("ones", 0, 128), ("cacc", cc, cc + 1)], start=(cc == 0), stop=(cc == 3), n=T)
        for cc in range(4 if full else 0):
            S.op("dve", lambda e, cc=cc, psm=psm: e.scalar_tensor_tensor(out=cacc[:, cc, :], in0=psm[:, :T], scalar=-1.0 / 512, in1=cacc[:, cc, :],
                                                                        op0=ALU.mult, op1=ALU.add),
                 reads=[(pmn, 0, 1), ("cacc", cc, cc + 1)], writes=[("cacc", cc, cc + 1)])
            S.op("pool", lambda e, cc=cc: e.tensor_tensor(out=csq[:, cc, :], in0=cacc[:, cc, :], in1=cacc[:, cc, :], op=ALU.mult),
                 reads=[("cacc", cc, cc + 1)], writes=[("csq", cc, cc + 1)])
        psv, pvn = ph.newps()
        for cc in range(4 if full else 0):
            mm1(psv, pvn, ones32, csq[:, cc, :], [ph.cr("ones", 0, 128), ("csq", cc, cc + 1)], start=(cc == 0), stop=(cc == 3), n=T)
        if full:
            rsqrt_ps(psv, pvn, crs[:], "crs", 1.0 / 512, "eps5")
        for cc in range(4 if full else 0):
            S.op("dve", lambda e, cc=cc: e.tensor_tensor(out=cacc[:, cc, :], in0=cacc[:, cc, :], in1=crs[:], op=ALU.mult),
                 reads=[("cacc", cc, cc + 1), ("crs", 0, 1)], writes=[("cacc", cc, cc + 1)])
            S.op("act", lambda e, cc=cc: e.activation(out=catT[:, cc, :], in_=cacc[:, cc, :], func=AF.Silu, scale=ph.c("ln_g", cc), bias=ph.c("ln_b", cc)),
                 reads=[("cacc", cc, cc + 1), ph.cr("ln_g", cc), ph.cr("ln_b", cc)], writes=[("catT", cc, cc + 1)])
        shifted(12, rw12, "rw12")
        if full or plast:
            shifted(13, rw13, "rw13")
        S.op("act", lambda e: e.activation(out=twad[0:64, :], in_=rw12[0:64, :], func=AF.Tanh), reads=[("rw12", 0, 1)], writes=[("twad", 0, 1)])
        S.op("dve", lambda e: e.tensor_copy(out=twad[64:128, :], in_=rw12[64:128, :]), reads=[("rw12", 0, 1)], writes=[("twad", 1, 2)])
        if full:
            S.op("act", lambda e: e.activation(out=sgd[:], in_=rw13[:], func=AF.Sigmoid), reads=[("rw13", 0, 1)], writes=[("sgd", 0, 1)])
        for cc in range(4):
            ps, pn = ph.newps()
            mm1(ps, pn, w2a2[0:64, cc * 128:(cc + 1) * 128], twad[0:64, :], [("w2a2", 0, 1), ("twad", 0, 1)], n=T)
            S.op("act", lambda e, ps=ps, cc=cc: e.activation(out=lw4[:, cc, :], in_=ps[:, :T], func=AF.Sigmoid, bias=ph.c("w0", cc)),
                 reads=[(pn, 0, 1), ph.cr("w0", cc)], writes=[("lw4", cc, cc + 1)])
            S.op("act", lambda e, cc=cc: e.activation(out=lw4[:, cc, :], in_=lw4[:, cc, :], func=AF.Identity, scale=-float(np.exp(-0.5))),
                 reads=[("lw4", cc, cc + 1)], writes=[("lw4", cc, cc + 1)])
            ps, pn = ph.newps()
            mm1(ps, pn, w2a2[64:128, cc * 128:(cc + 1) * 128], twad[64:128, :], [("w2a2", 0, 1), ("twad", 1, 2)], n=T)
            S.op("act", lambda e, ps=ps, cc=cc: e.activation(out=a4[:, cc, :], in_=ps[:, :T], func=AF.Sigmoid, bias=ph.c("a0", cc)),
                 reads=[(pn, 0, 1), ph.cr("a0", cc)], writes=[("a4", cc, cc + 1)])
            if full:
                ps, pn = ph.newps()
                mm1(ps, pn, g2b[:, cc * 128:(cc + 1) * 128], sgd[:], [("g2b", 0, 1), ("sgd", 0, 1)], n=T)
                S.op("act", lambda e, ps=ps, cc=cc: e.activation(out=gate4[:, cc, :], in_=ps[:, :T], func=AF.Identity),
                     reads=[(pn, 0, 1)], writes=[("gate4", cc, cc + 1)])
        def prep(cc):
            i = cc % 2
            s = {n: st_[n][i] for n in names}
            sn = {n: "s_%s%d" % (n, i) for n in names}
            bd = {n: BD[n][i] for n in bdn}
            bn = {n: "bd_%s%d" % (n, i) for n in bdn}
            if full or plast:
                shifted(cc, s["r"], sn["r"])
            shifted(4 + cc, s["k"], sn["k"])
            shifted(8 + cc, s["v"], sn["v"])
            lw = lw4[:, cc, :]
            av = a4[:, cc, :]

            def ew(eng, f, reads, writes):
                S.op(eng, f, reads=[(sn[r], 0, 1) if r in sn else r for r in reads], writes=[(sn[w], 0, 1) if w in sn else w for w in writes])
            ew("act", lambda e, s=s, cc=cc: e.activation(out=s["kk"][:], in_=s["k"][:], func=AF.Identity, scale=ph.c("k_k", cc)),
               ["k", ph.cr("k_k", cc)], ["kk"])
            ew("pool", lambda e, s=s: e.tensor_tensor(out=s["q1"][:], in0=s["kk"][:], in1=s["kk"][:], op=ALU.mult), ["kk"], ["q1"])
            ps, pn = ph.newps()
            mm1(ps, pn, bones, s["q1"][:], [ph.cr("bones", 0, 128), (sn["q1"], 0, 1)], n=T)
            ew("dve", lambda e, s=s, ps=ps: e.tensor_scalar(out=s["rn"][:], in0=ps[:, :T], scalar1=1e-24, scalar2=None, op0=ALU.max), [(pn, 0, 1)], ["rn"])
            ew("act", lambda e, s=s: e.activation(out=s["rn"][:], in_=s["rn"][:], func=AF.Ln), ["rn"], ["rn"])
            ew("act", lambda e, s=s: e.activation(out=s["rn"][:], in_=s["rn"][:], func=AF.Exp, scale=-0.5), ["rn"], ["rn"])
            ew("dve", lambda e, s=s: e.tensor_tensor(out=s["kkn"][:], in0=s["kk"][:], in1=s["rn"][:], op=ALU.mult), ["kk", "rn"], ["kkn"])
            ew("pool", lambda e, s=s, av=av, cc=cc: e.tensor_scalar(out=s["q1"][:], in0=av, scalar1=-1.0, scalar2=ph.c("k_a", cc), op0=ALU.add, op1=ALU.mult),
               [("a4", cc, cc + 1), ph.cr("k_a", cc)], ["q1"])
            ew("dve", lambda e, s=s: e.scalar_tensor_tensor(out=s["k2"][:], in0=s["q1"][:], scalar=1.0, in1=s["k"][:], op0=ALU.add, op1=ALU.mult),
               ["q1", "k"], ["k2"])
            ew("dve", lambda e, s=s, av=av: e.tensor_tensor(out=s["bb"][:], in0=s["kkn"][:], in1=av, op=ALU.mult), ["kkn", ("a4", cc, cc + 1)], ["bb"])
            if full:
                ew("dve", lambda e, s=s, cc=cc: e.scalar_tensor_tensor(out=s["q1"][:], in0=s["r"][:], scalar=ph.c("r_k", cc), in1=s["k2"][:], op0=ALU.mult, op1=ALU.mult),
                   ["r", "k2", ph.cr("r_k", cc)], ["q1"])
                ps, pn = ph.newps()
                mm1(ps, pn, bones, s["q1"][:], [ph.cr("bones", 0, 128), (sn["q1"], 0, 1)], n=T)
                ew("dve", lambda e, s=s, ps=ps: e.tensor_tensor(out=s["bonus"][:], in0=ps[:, :T], in1=s["v"][:], op=ALU.mult), [(pn, 0, 1), "v"], ["bonus"])
            for n in range(NCH):
                ew("dve", lambda e, s=s, n=n, lw=lw: e.tensor_tensor_scan(out=s["cum"][:, n * 64:(n + 1) * 64], data0=onesrow[:], data1=lw[:, n * 64:(n + 1) * 64],
                                                                         initial=0.0, op0=ALU.mult, op1=ALU.add),
                   [("lw4", cc, cc + 1), ("onesrow", 0, 1)], ["cum"])
            ew("act", lambda e, s=s: e.activation(out=s["P"][:], in_=s["cum"][:], func=AF.Exp), ["cum"], ["P"])
            ew("act", lambda e, s=s: e.activation(out=s["Pinv"][:], in_=s["cum"][:], func=AF.Exp, scale=-1.0), ["cum"], ["Pinv"])
            ew("pool", lambda e, s=s, lw=lw: e.tensor_tensor(out=s["Pprev"][:], in0=s["cum"][:], in1=lw, op=ALU.subtract), ["cum", ("lw4", cc, cc + 1)], ["Pprev"])
            ew("act", lambda e, s=s: e.activation(out=s["Pprev"][:], in_=s["Pprev"][:], func=AF.Exp), ["Pprev"], ["Pprev"])
            v3 = lambda ap: ap.rearrange("p (n t) -> p n t", t=64)
            for hh in range(2):
                hp = slice(hh * 64, hh * 64 + 64)
                hc = slice(hh * 64, hh * 64 + 64)
                eng = "dve" if hh == 0 else "pool"
                ew("dve", lambda e, s=s, bd=bd, hp=hp, hc=hc: e.scalar_tensor_tensor(out=bd["a"][hp, :, hc], in0=v3(s["kkn"][hp, :]), scalar=-1.0, in1=v3(s["Pprev"][hp, :]),
                                                                                   op0=ALU.mult, op1=ALU.mult), ["kkn", "Pprev"], [(bn["a"], 0, NCH)])
                if full:
                    ew(eng, lambda e, s=s, bd=bd, hp=hp, hc=hc: e.tensor_tensor(out=bd["r"][hp, :, hc], in0=v3(s["r"][hp, :]), in1=v3(s["P"][hp, :]), op=ALU.mult),
                       ["r", "P"], [(bn["r"], 0, NCH)])
                ew(eng, lambda e, s=s, bd=bd, hp=hp, hc=hc: e.tensor_tensor(out=bd["b"][hp, :, hc], in0=v3(s["bb"][hp, :]), in1=v3(s["Pinv"][hp, :]), op=ALU.mult),
                   ["bb", "Pinv"], [(bn["b"], 0, NCH)])
                ew(eng, lambda e, s=s, bd=bd, hp=hp, hc=hc: e.tensor_tensor(out=bd["k"][hp, :, hc], in0=v3(s["k2"][hp, :]), in1=v3(s["Pinv"][hp, :]), op=ALU.mult),
                   ["k2", "Pinv"], [(bn["k"], 0, NCH)])
                ew(eng, lambda e, s=s, bd=bd, hp=hp, hc=hc: e.tensor_copy(out=bd["v"][hp, :, hc], in_=v3(s["v"][hp, :])), ["v"], [(bn["v"], 0, NCH)])
            return dict(cc=cc, s=s, sn=sn, bd=bd, bn=bn, ew=ew)

        def unit(cx, n):
            cc, s, sn, bd, bn = cx["cc"], cx["s"], cx["sn"], cx["bd"], cx["bn"]
            i = cc % 2
            if True:
                m = {k_: M[k_][i] for k_ in mats}
                mn = {k_: "m_%s%d" % (k_, i) for k_ in mats}
                ba, br, bb_, bk, bv_ = (bd[q][:, n, :] for q in bdn)
                R = lambda q: (bn[q], n, n + 1)

                def sc(lq, rq, lhs, rhs, out, mask):
                    ps, pn = ph.newps()
                    mm1(ps, pn, lhs, rhs, [R(lq), R(rq)])
                    S.op("dve", lambda e, ps=ps, m=m: e.tensor_tensor(out=m[out][:], in0=ps[:, :128], in1=ph.c(mask, 0, 128), op=ALU.mult),
                         reads=[(pn, 0, 1), ph.cr(mask, 0, 128)], writes=[(mn[out], 0, 1)])
                sc("b", "a", bb_, ba, "AT", "m_su")
                sc("a", "b", ba, bb_, "A", "m_sl")
                if full:
                    sc("b", "r", bb_, br, "YrbT", "m_iu")
                sc("k", "a", bk, ba, "XakT", "m_su")
                if full:
                    sc("k", "r", bk, br, "YrkT", "m_iu")
                S.op("pool", lambda e, m=m: e.tensor_tensor(out=m["TT"][:], in0=m["AT"][:], in1=ident, op=ALU.add),
                     reads=[(mn["AT"], 0, 1), ph.cr("ident", 0, 128)], writes=[(mn["TT"], 0, 1)])
                cA, cAn, cAT, cATn, cTT, cTTn = m["A"], mn["A"], m["AT"], mn["AT"], m["TT"], mn["TT"]
                yield
                for d in range(5):
                    nA, nAn = A2[i][d % 2], "A2_%d_%d" % (i, d % 2)
                    nAT, nATn = A2T[i][d % 2], "A2T_%d_%d" % (i, d % 2)
                    nTT, nTTn = TT2[i][d % 2], "TT2_%d_%d" % (i, d % 2)
                    ps, pn = ph.newps()
                    mm1(ps, pn, cAT[:], cA[:], [(cATn, 0, 1), (cAn, 0, 1)])
                    S.op("act", lambda e, ps=ps, nA=nA: e.activation(out=nA[:], in_=ps[:, :128], func=AF.Identity), reads=[(pn, 0, 1)], writes=[(nAn, 0, 1)])
                    if d < 4:
                        ps2, pn2 = ph.newps()
                        mm1(ps2, pn2, cA[:], cAT[:], [(cATn, 0, 1), (cAn, 0, 1)])
                        S.op("dve", lambda e, ps2=ps2, nAT=nAT: e.tensor_copy(out=nAT[:], in_=ps2[:, :128]), reads=[(pn2, 0, 1)], writes=[(nATn, 0, 1)])
                    ps3, pn3 = ph.newps()
                    mm1(ps3, pn3, nA[:], cTT[:], [(nAn, 0, 1), (cTTn, 0, 1)])
                    S.op("dve", lambda e, ps3=ps3, nTT=nTT, cTT=cTT: e.tensor_tensor(out=nTT[:], in0=ps3[:, :128], in1=cTT[:], op=ALU.add),
                         reads=[(pn3, 0, 1), (cTTn, 0, 1)], writes=[(nTTn, 0, 1)])
                    cA, cAn, cAT, cATn, cTT, cTTn = nA, nAn, nAT, nATn, nTT, nTTn
                    yield
                for q, dst_ in (("v", "Vt"), ("b", "Bt"), ("k", "Kt")):
                    ps, pn = ph.newps()
                    mm1(ps, pn, bd[q][:, n, :], ident_bf[:], [R(q), ("ident_bf", 0, 1)])
                    S.op("act", lambda e, ps=ps, dst_=dst_, m=m: e.activation(out=m[dst_][:], in_=ps[:, :128], func=AF.Identity),
                         reads=[(pn, 0, 1)], writes=[(mn[dst_], 0, 1)])
                yield
                Hc, Hcn = H[cc][hcur[cc]], "H%d_%d" % (cc, hcur[cc])
                Hn, Hnn = H[cc][1 - hcur[cc]], "H%d_%d" % (cc, 1 - hcur[cc])
                Hbc, Hbcn = Hb[cc][hcur[cc]], "Hb%d_%d" % (cc, hcur[cc])
                Hbn, Hbnn = Hb[cc][1 - hcur[cc]], "Hb%d_%d" % (cc, 1 - hcur[cc])
                hcur[cc] = 1 - hcur[cc]
                ps, pn = ph.newps()
                mm1(ps, pn, ba, Hbc[:], [R("a"), (Hbcn, 0, 1)], start=True, stop=False)
                mm1(ps, pn, m["XakT"][:], m["Vt"][:], [(mn["XakT"], 0, 1), (mn["Vt"], 0, 1)], start=False, stop=True)
                S.op("act", lambda e, ps=ps, m=m: e.activation(out=m["W"][:], in_=ps[:, :128], func=AF.Identity), reads=[(pn, 0, 1)], writes=[(mn["W"], 0, 1)])
                yield
                ps, pn = ph.newps()
                mm1(ps, pn, cTT[:], m["W"][:], [(cTTn, 0, 1), (mn["W"], 0, 1)])
                S.op("dve", lambda e, ps=ps, m=m: e.tensor_copy(out=m["U"][:], in_=ps[:, :128]), reads=[(pn, 0, 1)], writes=[(mn["U"], 0, 1)])
                yield
                if full:
                    ps, pn = ph.newps()
                    mm1(ps, pn, Hbc[:], br, [(Hbcn, 0, 1), R("r")], start=True, stop=False)
                    mm1(ps, pn, m["U"][:], m["YrbT"][:], [(mn["U"], 0, 1), (mn["YrbT"], 0, 1)], start=False, stop=False)
                    mm1(ps, pn, m["Vt"][:], m["YrkT"][:], [(mn["Vt"], 0, 1), (mn["YrkT"], 0, 1)], start=False, stop=True)
                    for hh in range(2):
                        hp = slice(hh * 64, hh * 64 + 64)
                        S.op("act" if hh == 0 else "dve",
                             (lambda e, ps=ps, hp=hp, s=s, n=n: e.activation(out=s["y"][hp, n * 64:(n + 1) * 64], in_=ps[hp, hp], func=AF.Identity)) if hh == 0 else
                             (lambda e, ps=ps, hp=hp, s=s, n=n: e.tensor_copy(out=s["y"][hp, n * 64:(n + 1) * 64], in_=ps[hp, hp])),
                             reads=[(pn, 0, 1)], writes=[(sn["y"], 0, 1)])
                ps, pn = ph.newps()
                mm1(ps, pn, m["Bt"][:], m["U"][:], [(mn["Bt"], 0, 1), (mn["U"], 0, 1)], start=True, stop=False)
                mm1(ps, pn, m["Kt"][:], m["Vt"][:], [(mn["Kt"], 0, 1), (mn["Vt"], 0, 1)], start=False, stop=True)
                pc = s["P"][:, n * 64 + 63:n * 64 + 64]
                S.op("act", lambda e, Hn=Hn, Hc=Hc, pc=pc: e.activation(out=Hn[:], in_=Hc[:], func=AF.Identity, scale=pc),
                     reads=[(Hcn, 0, 1), (sn["P"], 0, 1)], writes=[(Hnn, 0, 1)])
                S.op("dve", lambda e, ps=ps, Hn=Hn, pc=pc: e.scalar_tensor_tensor(out=Hn[:], in0=ps[:, :128], scalar=pc, in1=Hn[:], op0=ALU.mult, op1=ALU.add),
                     reads=[(pn, 0, 1), (Hnn, 0, 1), (sn["P"], 0, 1)], writes=[(Hnn, 0, 1)])
                S.op("act", lambda e, Hn=Hn, Hbn=Hbn: e.activation(out=Hbn[:], in_=Hn[:], func=AF.Identity), reads=[(Hnn, 0, 1)], writes=[(Hbnn, 0, 1)])

        def post(cx):
            cc, s, sn, ew = cx["cc"], cx["s"], cx["sn"], cx["ew"]
            ps, pn = ph.newps()
            mm1(ps, pn, bones, s["y"][:], [ph.cr("bones", 0, 128), (sn["y"], 0, 1)], n=T)
            ew("dve", lambda e, s=s, ps=ps: e.scalar_tensor_tensor(out=s["yc"][:], in0=ps[:, :T], scalar=-1.0 / 64, in1=s["y"][:], op0=ALU.mult, op1=ALU.add),
               [(pn, 0, 1), "y"], ["yc"])
            ew("pool", lambda e, s=s: e.tensor_tensor(out=s["q1"][:], in0=s["yc"][:], in1=s["yc"][:], op=ALU.mult), ["yc"], ["q1"])
            ps, pn = ph.newps()
            mm1(ps, pn, bones, s["q1"][:], [ph.cr("bones", 0, 128), (sn["q1"], 0, 1)], n=T)
            rsqrt_ps(ps, pn, s["rn"][:], sn["rn"], 1.0 / 64, "epsgn")
            ew("dve", lambda e, s=s: e.tensor_tensor(out=s["yc"][:], in0=s["yc"][:], in1=s["rn"][:], op=ALU.mult), ["yc", "rn"], ["yc"])
            ew("act", lambda e, s=s, cc=cc: e.activation(out=s["yc"][:], in_=s["yc"][:], func=AF.Identity, scale=ph.c("gn_g", cc), bias=ph.c("gn_b", cc)),
               ["yc", ph.cr("gn_g", cc), ph.cr("gn_b", cc)], ["yc"])
            ew("pool", lambda e, s=s: e.tensor_tensor(out=s["yc"][:], in0=s["yc"][:], in1=s["bonus"][:], op=ALU.add), ["yc", "bonus"], ["yc"])
            ew("dve", lambda e, s=s, cc=cc: e.tensor_tensor(out=catT[:, 4 + cc, :], in0=s["yc"][:], in1=gate4[:, cc, :], op=ALU.mult),
               ["yc", ("gate4", cc, cc + 1)], [("catT", 4 + cc, 5 + cc)])
        for pair in range(2):
            cxs = [prep(2 * pair), prep(2 * pair + 1)]
            for n in range(NCH):
                gens = [unit(cx, n) for cx in cxs]
                alive = True
                while alive:
                    alive = False
                    for g in gens:
                        try:
                            next(g)
                            alive = True
                        except StopIteration:
                            pass
            for cx in cxs:
                if full:
                    post(cx)
        if DBG['mix'] == 1:
            for k in range(8):
                S.op("dve", lambda e, k=k: e.tensor_copy(out=x[:, k, :], in_=catT[:, k, :]), reads=[("catT", k, k + 1)], writes=[("x", k, k + 1)])
        for oc in range(8 if (DBG['mix'] != 1 and full) else 0):
            pso, pon = ph.newps()
            for k in range(8):
                S.op("pe", lambda e, k=k, oc=oc, pso=pso: e.matmul(pso[:, :T], lhsT=wout[:, k, oc * 128:(oc + 1) * 128], rhs=catT[:, k, :],
                                                                   start=(k == 0), stop=(k == 7)),
                     reads=[("wout", k, k + 1), ("catT", k, k + 1)], writes=[(pon, 0, 1)])
            S.op("dve", lambda e, oc=oc, pso=pso: e.tensor_tensor(out=x[:, oc, :], in0=pso[:, :T], in1=x[:, oc, :], op=ALU.add),
                 reads=[(pon, 0, 1), ("x", oc, oc + 1)], writes=[("x", oc, oc + 1)])
        if full and mtile:
            for k in range(8):
                S.op("act", lambda e, k=k: e.activation(out=x[:, k, :], in_=x[:, k, :], func=AF.Identity, scale=ph.c("hmask")),
                     reads=[("x", k, k + 1), ph.cr("hmask")], writes=[("x", k, k + 1)])
        if full:
            S.dma("sp", dstv[:, :, (t - NPRE) * T:(t - NPRE + 1) * T], x[:], reads=[("x", 0, 8)])
    ph.close()


def build(TL, NC, phases=(0, 1, 2, 3), halo=False):
    nc = bass.Bass("TRN2", target_bir_lowering=False)
    dt = lambda n, s, d=F32, kind="ExternalInput": nc.dram_tensor(n, s, d, kind=kind).ap()
    TW, TH, TO = (8192, 2560, 2048) if halo else (TL, TL, TL)
    xT = dt("xT", [D, TW]); pos = dt("pos", [1, TH], I32); cst = dt("cst", [128, NC])
    w_in = dt("w_in", [D, 2816]); w_out = dt("w_out", [D, D]); w2a2 = dt("w2a2", [128, 512]); g2 = dt("g2", [128, 512])
    w_up = [dt("w_up%d" % l, [D, 2 * DFF]) for l in range(2)]
    w_dn = [dt("w_dn%d" % l, [DFF, D]) for l in range(2)]
    w_qkv = dt("w_qkv", [D, 1536]); w_o = dt("w_o", [D, D]); bvd = dt("bvd", [1, 512])
    nc_bv_dram[0] = bvd
    yT = dt("yT", [D, TO], kind="ExternalOutput")
    scr = [dt("scr%d" % i, [D, TH], kind="Internal") for i in range(3)]
    coff = build.coff
    chain = [xT] + scr[:len(phases) - 1] + [yT]
    ci = 0
    for p in phases:
        s_, d_ = chain[ci], chain[ci + 1]
        ci += 1
        if p == 0:
            if halo:
                mixer_phase(nc, s_, d_, w_in, w_out, w2a2, g2, cst, coff, TW, NPRE=(TW - TH) // 256, masked=True)
            else:
                mixer_phase(nc, s_, d_, w_in, w_out, w2a2, g2, cst, coff, TL)
        elif p == 1:
            ffn_phase(nc, s_, d_, w_up[0], w_dn[0], cst, coff, 0, TH)
        elif p == 2:
            attn_phase(nc, s_, d_, pos, w_qkv, w_o, cst, coff, TH, masked=halo)
        else:
            ffn_phase(nc, s_, d_, w_up[1], w_dn[1], cst, coff, 1, TH, skip=(TH - TO) // 512)
    return nc


def host_inputs(inp, xb, posb, hmask=1.0, cache={}):
    key = id(inp)
    if key not in cache:
        P = make_consts(inp, 1.0)
        f = lambda a: np.ascontiguousarray(np.asarray(a, np.float32))
        bq = np.asarray(inp["attn_b_qkv"][0], np.float32)
        bvd = np.concatenate([np.concatenate([bq[1280 + h * 64:1344 + h * 64]] * 2) for h in range(4)])[None, :]
        shared = {"w_in": f(inp["ab_w_in"][0]), "w_out": f(inp["ab_w_out"][0]),
                  "w2a2": f(np.concatenate([inp["rwkv_w2"][0], inp["rwkv_a2"][0]], axis=0)), "g2": f(inp["rwkv_g2"][0]),
                  "w_up0": f(inp["ffn_w_up"][0]), "w_up1": f(inp["ffn_w_up"][1]), "w_dn0": f(inp["ffn_w_down"][0]), "w_dn1": f(inp["ffn_w_down"][1]),
                  "w_qkv": f(inp["attn_w_qkv"][0]), "w_o": f(inp["attn_w_o"][0]), "bvd": f(bvd)}
        cache.clear()
        cache[key] = (P.off, P.build(), shared)
    off, cst0, shared = cache[key]
    build.coff = off
    cst = cst0.copy()
    cst[:, off["hmask"]] = hmask
    m = dict(shared)
    m["xT"] = np.ascontiguousarray(np.asarray(xb, np.float32).T)
    m["pos"] = np.ascontiguousarray(np.asarray(posb, np.int32)[None, :])
    m["cst"] = cst
    return m, cst.shape[1]


def kernel(**inputs):
    x = np.asarray(inputs["x"], np.float32)
    pos = np.asarray(inputs["positions"])
    B, SEQ, _ = x.shape
    TW, TH, TO = 8192, 2560, 2048
    NQ = SEQ // TO
    maps = []
    for c in range(8):
        b, q = c // NQ, c % NQ
        end = (q + 1) * TO
        start = end - TW
        xw = np.zeros((TW, D), np.float32)
        xw[max(0, -start):] = x[b, max(0, start):end]
        hs = end - TH
        pw = np.zeros((TH,), np.int32)
        pw[max(0, -hs):] = pos[b, max(0, hs):end]
        m, NC = host_inputs(inputs, xw, pw, hmask=(0.0 if q == 0 else 1.0))
        maps.append(m)
    nc = build(SEQ, NC, halo=True)
    res = run_bass_kernel_spmd(nc, maps, core_ids=list(range(8)))
    out = np.zeros((B, SEQ, D), np.float32)
    for c in range(8):
        b, q = c // NQ, c % NQ
        out[b, q * TO:(q + 1) * TO] = res.results[c]["yT"].T
    return out
```

```python
import contextlib
import numpy as np
import concourse.bass as bass
import concourse.mybir as mybir
from concourse.bass_utils import run_bass_kernel_spmd

F32 = mybir.dt.float32
BF16 = mybir.dt.bfloat16
I32 = mybir.dt.int32
ALU = mybir.AluOpType
AF = mybir.ActivationFunctionType

ENGS = ["pe", "act", "dve", "pool", "sp"]
NDMASEM = 6
D = 1024
DFF = 2816
PI = float(np.pi)
DBG = {'att': 9, 'mix': 9, 'skip': ''}


class Sched:
    def __init__(self, nc, stack):
        self.nc = nc
        self.ops = []
        self.per_eng = {e: [] for e in ENGS}
        self.acc = {}
        self.dma_count = {e: 0 for e in ENGS}
        Sched.count = getattr(Sched, "count", 0) + 1
        sp_ = "q%d" % Sched.count
        self.sems = {e: stack.enter_context(nc.semaphore(sp_ + "s_" + e)) for e in ENGS if e != "sp"}
        self.dsems = {q: [stack.enter_context(nc.semaphore(sp_ + "d_%s%d" % (q, i))) for i in range(NDMASEM)]
                      for q in ("sp", "act", "pool")}

    def _deps(self, reads, writes, opid):
        deps = set()
        for (name, lo, hi) in reads:
            lst = self.acc.setdefault(name, [])
            for (l, h, k, o) in lst:
                if k == "w" and l < hi and lo < h:
                    deps.add(o)
            lst.append((lo, hi, "r", opid))
        for (name, lo, hi) in writes:
            lst = self.acc.setdefault(name, [])
            keep = []
            for (l, h, k, o) in lst:
                if l < hi and lo < h:
                    if o != opid:
                        deps.add(o)
                    if lo <= l and h <= hi:
                        continue
                keep.append((l, h, k, o))
            keep.append((lo, hi, "w", opid))
            self.acc[name] = keep
        return deps

    def op(self, eng, fn, reads=(), writes=()):
        opid = len(self.ops)
        deps = self._deps(reads, writes, opid)
        rec = dict(id=opid, eng=eng, fn=fn, deps=deps, dma=None, sig=False)
        self.ops.append(rec)
        self.per_eng[eng].append(rec)
        return opid

    def dma(self, q, out, in_, reads=(), writes=(), **kw):
        opid = len(self.ops)
        deps = self._deps(reads, writes, opid)
        n = self.dma_count[q]
        self.dma_count[q] += 1
        rec = dict(id=opid, eng=q, fn=None, deps=deps, dma=(q, n, out, in_, kw), sig=True)
        self.ops.append(rec)
        self.per_eng[q].append(rec)
        return opid

    def _finalize(self):
        for rec in self.ops:
            for d in rec["deps"]:
                p = self.ops[d]
                if p["dma"] is None and not (p["eng"] == "pe" and rec["eng"] == "pe" and rec["dma"] is None):
                    p["sig"] = True
        cnt = {e: 0 for e in ENGS}
        for rec in self.ops:
            if rec["dma"] is not None:
                q, n, _, _, _ = rec["dma"]
                rec["sem"] = self.dsems[q][n % NDMASEM]
                rec["val"] = 16 * (n // NDMASEM + 1)
            elif rec["sig"]:
                cnt[rec["eng"]] += 1
                rec["sem"] = self.sems[rec["eng"]]
                rec["val"] = cnt[rec["eng"]]

    def _emit_engine(self, ename, eng):
        seen = {}

        def wait(sem, val):
            key = id(sem)
            if seen.get(key, 0) >= val:
                return
            seen[key] = val
            eng.wait_ge(sem, val)

        for rec in self.per_eng[ename]:
            for d in sorted(rec["deps"]):
                p = self.ops[d]
                if p["dma"] is None and p["eng"] == "pe" and ename == "pe" and rec["dma"] is None:
                    continue
                wait(p["sem"], p["val"])
            if rec["dma"] is not None:
                q, n, out, in_, kw = rec["dma"]
                if n >= NDMASEM:
                    wait(rec["sem"], rec["val"] - 16)
                eng.dma_start(out=out, in_=in_, **kw).then_inc(rec["sem"], 16)
            else:
                ins = rec["fn"](eng)
                if rec["sig"]:
                    ins.then_inc(rec["sem"], 1)
        return wait

    def emit(self):
        self._finalize()
        with self.nc.Block() as block:
            @block.tensor
            def _(e):
                self._emit_engine("pe", e)

            @block.scalar
            def _(e):
                self._emit_engine("act", e)

            @block.vector
            def _(e):
                self._emit_engine("dve", e)

            @block.gpsimd
            def _(e):
                w = self._emit_engine("pool", e)
                n = self.dma_count["pool"]
                for i in range(NDMASEM):
                    k = len(range(i, n, NDMASEM))
                    if k:
                        w(self.dsems["pool"][i], 16 * k)

            @block.sync
            def _(e):
                w = self._emit_engine("sp", e)
                n = self.dma_count["sp"]
                for i in range(NDMASEM):
                    k = len(range(i, n, NDMASEM))
                    if k:
                        w(self.dsems["sp"][i], 16 * k)


def colvec(v):
    v = np.asarray(v, np.float32).reshape(-1, 128)
    return np.ascontiguousarray(v.T)


class Pack:
    def __init__(self):
        self.parts = []
        self.off = {}
        self.n = 0

    def add(self, name, arr):
        arr = np.asarray(arr, np.float32)
        if arr.ndim == 1:
            arr = arr[:, None]
        assert arr.shape[0] == 128, (name, arr.shape)
        arr = arr.reshape(128, -1)
        self.off[name] = self.n
        self.parts.append(arr)
        self.n += arr.shape[1]

    def build(self):
        return np.ascontiguousarray(np.concatenate(self.parts, axis=1))


def bd_mask(fn):
    m = np.zeros((128, 128), np.float32)
    i = np.arange(64)
    blk = fn(i[:, None], i[None, :]).astype(np.float32)
    m[:64, :64] = blk
    m[64:, 64:] = blk
    return m


def make_consts(inp, hmask=1.0):
    P = Pack()
    g = lambda n, l=0: np.asarray(inp[n][l], np.float32)
    P.add("eps6", np.full(128, 1e-6)); P.add("eps5", np.full(128, 1e-5)); P.add("epsgn", np.full(128, 64e-5))
    P.add("halfpi", np.full(128, PI / 2)); P.add("zero", np.zeros(128)); P.add("hmask", np.full(128, float(hmask)))
    P.add("ab_g", colvec(g("ab_norm_g")))
    P.add("cin_b", colvec(g("conv_in_b")))
    P.add("dw_w", np.stack([colvec(g("conv_dw_w")[j]) for j in range(31)], axis=2))
    P.add("dw_b", colvec(g("conv_dw_b"))); P.add("ln_g", colvec(g("conv_ln_g"))); P.add("ln_b", colvec(g("conv_ln_b")))
    P.add("mu", colvec(g("rwkv_mu"))); P.add("omm", colvec(1.0 - 0.0 * g("rwkv_mu")) * 0 + 0)
    P.add("w0", colvec(g("rwkv_w0"))); P.add("a0", colvec(g("rwkv_a0")))
    P.add("k_k", colvec(g("rwkv_k_k"))); P.add("k_a", colvec(g("rwkv_k_a")))
    P.add("r_k", colvec(g("rwkv_r_k").reshape(-1)))
    P.add("gn_g", colvec(g("rwkv_ln_g"))); P.add("gn_b", colvec(g("rwkv_ln_b")))
    for l in range(2):
        P.add("ffn_g%d" % l, colvec(g("ffn_norm_g", l)))
        P.add("fc_w%d" % l, np.stack([colvec(g("ffn_conv_w", l)[j]) for j in range(3)], axis=2))
        P.add("fc_b%d" % l, colvec(g("ffn_conv_b", l)))
    P.add("at_g", colvec(g("attn_norm_g")))
    bq = g("attn_b_qkv")
    P.add("b_q", colvec(bq[:1024]))
    P.add("b_kd", np.stack([np.concatenate([bq[1024 + h * 64:1088 + h * 64]] * 2) for h in range(4)], axis=1))
    P.add("qn_g", np.concatenate([g("attn_q_norm_g")] * 2)); P.add("kn_g", np.concatenate([g("attn_k_norm_g")] * 2))
    P.add("b_o", colvec(g("attn_b_o")))
    half = 8
    invf = (500000.0 ** (-(np.arange(half, dtype=np.float32) * 2.0) / 16)).astype(np.float32)
    iv = np.zeros(64, np.float32); iv[:8] = invf; iv[8:16] = invf
    P.add("invf", np.concatenate([iv, iv]))
    P.add("ident", np.eye(128, dtype=np.float32))
    P.add("bones", bd_mask(lambda a, b: a * 0 + b * 0 + 1))
    P.add("ones", np.ones((128, 128), np.float32))
    P.add("m_su", bd_mask(lambda s, t: s < t)); P.add("m_iu", bd_mask(lambda s, t: s <= t)); P.add("m_sl", bd_mask(lambda t, s: s < t))
    rm = np.zeros((64, 64), np.float32)
    for m in range(8):
        rm[m + 8, m] = -1.0
        rm[m, m + 8] = 1.0
    R2 = np.zeros((128, 128), np.float32); R2[:64, :64] = rm; R2[64:, 64:] = rm
    P.add("rmT", R2)
    kq = np.arange(128)
    mP = (kq[:, None] > kq[None, :]).astype(np.float32)
    mC = (kq[:, None] <= kq[None, :]).astype(np.float32)
    P.add("mP", np.tile(mP, (1, 4))); P.add("mC", np.tile(mC, (1, 4)))
    sk = g("attn_sinks")
    se = np.zeros((4, 2, 2, 128), np.float32)
    for hk in range(4):
        for par in range(2):
            for pair in range(2):
                se[hk, par, pair, :] = sk[4 * hk + 2 * pair + par]
    P.add("sinks", np.broadcast_to(se.reshape(1, -1), (128, 2048)))
    return P


class Phase:
    count = 0

    def __init__(self, nc, cdram, coff, need):
        self.nc = nc
        Phase.count += 1
        self.pfx = "p%d_" % Phase.count
        self.st = contextlib.ExitStack()
        self.S = Sched(nc, self.st)
        self.ps = [self.st.enter_context(nc.psum_tensor(self.pfx + "ps%d" % i, [128, 512], F32)) for i in range(8)]
        self.pi = 0
        self.coff = coff
        self.cmap = {}
        n = 0
        for name, w in need:
            self.cmap[name] = n
            n += w
        self.C = self.sb("C", [128, n], F32)
        for name, w in need:
            o = self.cmap[name]
            self.S.dma("sp", self.C[:, o:o + w], cdram[:, coff[name]:coff[name] + w], writes=[("C", o, o + w)], allow_slow_non_contiguous=True)

    def sb(self, name, shape, dt):
        return self.st.enter_context(self.nc.sbuf_tensor(self.pfx + name, shape, dt))

    def c(self, name, lo=0, w=1):
        o = self.cmap[name] + lo
        return self.C[:, o:o + w]

    def cr(self, name, lo=0, w=1):
        o = self.cmap[name] + lo
        return ("C", o, o + w)

    def newps(self):
        i = self.pi
        self.pi = (self.pi + 1) % 8
        return self.ps[i], "ps%d" % i

    def close(self):
        self.S.emit()
        self.st.close()


def rmsnorm(ph, x, xname, hT, hname, sq, sqname, rstd, ones_bf, gname, T):
    S = ph.S
    for k in range(8):
        S.op("act", lambda e, k=k: e.activation(out=sq[:, k, :T], in_=x[:, k, :T], func=AF.Square),
             reads=[(xname, k, k + 1)], writes=[(sqname, k, k + 1)])
    ps, pn = ph.newps()
    for k in range(8):
        S.op("pe", lambda e, k=k: e.matmul(ps[:, :T], lhsT=ones_bf[:], rhs=sq[:, k, :T], start=(k == 0), stop=(k == 7)),
             reads=[("ones_bf", 0, 1), (sqname, k, k + 1)], writes=[(pn, 0, 1)])
    S.op("act", lambda e: e.activation(out=rstd[:, :T], in_=ps[:, :T], func=AF.Ln, scale=1.0 / D, bias=ph.c("eps6")),
         reads=[(pn, 0, 1), ph.cr("eps6")], writes=[("rstd", 0, 1)])
    S.op("act", lambda e: e.activation(out=rstd[:, :T], in_=rstd[:, :T], func=AF.Exp, scale=-0.5),
         reads=[("rstd", 0, 1)], writes=[("rstd", 0, 1)])
    for k in range(8):
        S.op("dve",
             lambda e, k=k: e.scalar_tensor_tensor(out=hT[:, k, :T], in0=x[:, k, :T], scalar=ph.c(gname, k), in1=rstd[:, :T],
                                                   op0=ALU.mult, op1=ALU.mult),
             reads=[(xname, k, k + 1), ("rstd", 0, 1), ph.cr(gname, k)], writes=[(hname, k, k + 1)])


def load_w(S, q, dst, dname, src, nchunk):
    for k in range(nchunk):
        S.dma(q, dst[:, k, :], src[k * 128:(k + 1) * 128, :], writes=[(dname, k, k + 1)])


def ffn_phase(nc, src, dst, w_up, w_dn, cdram, coff, L, TL, skip=0):
    T = 512
    NT = TL // T
    need = [("eps6", 1), ("ffn_g%d" % L, 8), ("fc_w%d" % L, 66), ("fc_b%d" % L, 22), ("ones", 128)]
    ph = Phase(nc, cdram, coff, need)
    S = ph.S
    wup = ph.sb("wup", [128, 8, 2 * DFF], BF16)
    wdn = ph.sb("wdn", [128, 22, D], BF16)
    x = ph.sb("x", [128, 8, T], F32)
    hT = ph.sb("hT", [128, 8, T], BF16)
    act = ph.sb("act", [128, 22, T], BF16)
    rstd = ph.sb("rstd", [128, T], F32)
    ones_bf = ph.sb("ones_bf", [128, 128], BF16)
    G = [ph.sb("G%d" % i, [128, T + 2], F32) for i in range(2)]
    acc = [ph.sb("acc%d" % i, [128, T], F32) for i in range(2)]
    sg = [ph.sb("sg%d" % i, [128, T], F32) for i in range(2)]
    carry = ph.sb("carry", [128, 22, 2], F32)
    load_w(S, "pool", wup, "wup", w_up, 8)
    load_w(S, "pool", wdn, "wdn", w_dn, 22)
    S.op("dve", lambda e: e.tensor_copy(out=ones_bf[:], in_=ph.c("ones", 0, 128)), reads=[ph.cr("ones", 0, 128)], writes=[("ones_bf", 0, 1)])
    S.op("pool", lambda e: e.memset(carry[:], 0.0), writes=[("carry", 0, 22)])
    fw, fb, gname = "fc_w%d" % L, "fc_b%d" % L, "ffn_g%d" % L
    srcv = src.rearrange("(c p) t -> p c t", p=128)
    dstv = dst.rearrange("(c p) t -> p c t", p=128)
    for t in range(NT):
        S.dma("sp", x[:], srcv[:, :, t * T:(t + 1) * T], writes=[("x", 0, 8)])
        rmsnorm(ph, x, "x", hT, "hT", act, "act", rstd, ones_bf, gname, T)
        for c in range(22):
            i = c % 2
            Gt, at, st_ = G[i], acc[i], sg[i]
            gn, an, sn = "G%d" % i, "acc%d" % i, "sg%d" % i
            psg, pgn = ph.newps()
            psu, pun = ph.newps()
            for k in range(8):
                S.op("pe", lambda e, k=k, c=c, psg=psg: e.matmul(psg[:, :T], lhsT=wup[:, k, c * 128:(c + 1) * 128], rhs=hT[:, k, :],
                                                                 start=(k == 0), stop=(k == 7)),
                     reads=[("wup", k, k + 1), ("hT", k, k + 1)], writes=[(pgn, 0, 1)])
            for k in range(8):
                S.op("pe", lambda e, k=k, c=c, psu=psu: e.matmul(psu[:, :T], lhsT=wup[:, k, DFF + c * 128:DFF + (c + 1) * 128], rhs=hT[:, k, :],
                                                                 start=(k == 0), stop=(k == 7)),
                     reads=[("wup", k, k + 1), ("hT", k, k + 1)], writes=[(pun, 0, 1)])
            S.op("pool", lambda e, c=c, Gt=Gt: e.tensor_copy(out=Gt[:, 0:2], in_=carry[:, c, :]),
                 reads=[("carry", c, c + 1)], writes=[(gn, 0, 2)])
            S.op("act", lambda e, Gt=Gt, psg=psg: e.activation(out=Gt[:, 2:T + 2], in_=psg[:, :T], func=AF.Identity),
                 reads=[(pgn, 0, 1)], writes=[(gn, 2, T + 2)])
            S.op("pool", lambda e, c=c, Gt=Gt: e.tensor_copy(out=carry[:, c, :], in_=Gt[:, T:T + 2]),
                 reads=[(gn, T, T + 2)], writes=[("carry", c, c + 1)])
            S.op("dve", lambda e, c=c, Gt=Gt, at=at: e.tensor_scalar(out=at[:], in0=Gt[:, 2:T + 2], scalar1=ph.c(fw, c * 3 + 2), scalar2=ph.c(fb, c),
                                                                    op0=ALU.mult, op1=ALU.add),
                 reads=[(gn, 2, T + 2), ph.cr(fw, c * 3 + 2), ph.cr(fb, c)], writes=[(an, 0, 1)])
            S.op("dve", lambda e, c=c, Gt=Gt, at=at: e.scalar_tensor_tensor(out=at[:], in0=Gt[:, 1:T + 1], scalar=ph.c(fw, c * 3 + 1), in1=at[:],
                                                                            op0=ALU.mult, op1=ALU.add),
                 reads=[(gn, 1, T + 1), (an, 0, 1), ph.cr(fw, c * 3 + 1)], writes=[(an, 0, 1)])
            S.op("dve", lambda e, c=c, Gt=Gt, at=at: e.scalar_tensor_tensor(out=at[:], in0=Gt[:, 0:T], scalar=ph.c(fw, c * 3), in1=at[:],
                                                                           op0=ALU.mult, op1=ALU.add),
                 reads=[(gn, 0, T), (an, 0, 1), ph.cr(fw, c * 3)], writes=[(an, 0, 1)])
            S.op("act", lambda e, at=at, st_=st_: e.activation(out=st_[:], in_=at[:], func=AF.Silu),
                 reads=[(an, 0, 1)], writes=[(sn, 0, 1)])
            S.op("dve", lambda e, c=c, st_=st_, psu=psu: e.tensor_tensor(out=act[:, c, :], in0=psu[:, :T], in1=st_[:], op=ALU.mult),
                 reads=[(pun, 0, 1), (sn, 0, 1)], writes=[("act", c, c + 1)])
        for oc in range(8):
            pso, pon = ph.newps()
            for c in range(22):
                S.op("pe", lambda e, c=c, oc=oc, pso=pso: e.matmul(pso[:, :T], lhsT=wdn[:, c, oc * 128:(oc + 1) * 128], rhs=act[:, c, :],
                                                                   start=(c == 0), stop=(c == 21)),
                     reads=[("wdn", c, c + 1), ("act", c, c + 1)], writes=[(pon, 0, 1)])
            S.op("dve", lambda e, oc=oc, pso=pso: e.tensor_tensor(out=x[:, oc, :], in0=pso[:, :T], in1=x[:, oc, :], op=ALU.add),
                 reads=[(pon, 0, 1), ("x", oc, oc + 1)], writes=[("x", oc, oc + 1)])
        if t >= skip:
            S.dma("sp", dstv[:, :, (t - skip) * T:(t - skip + 1) * T], x[:], reads=[("x", 0, 8)])
    ph.close()


def attn_phase(nc, src, dst, pos, w_qkv, w_o, cdram, coff, TL, masked=False):
    T = 512
    NT = TL // T
    need = [("eps6", 1), ("halfpi", 1), ("zero", 1), ("hmask", 1), ("at_g", 8), ("b_q", 8), ("b_kd", 4), ("qn_g", 1), ("kn_g", 1), ("b_o", 8),
            ("invf", 1), ("bones", 128), ("ones", 128), ("rmT", 128), ("mP", 512), ("mC", 512), ("sinks", 2048)]
    ph = Phase(nc, cdram, coff, need)
    S = ph.S
    wq = ph.sb("wq", [128, 8, 1024], BF16)
    wkd = ph.sb("wkd", [128, 8, 4, 128], BF16)
    wvd = ph.sb("wvd", [128, 8, 4, 128], BF16)
    wo = ph.sb("wo", [128, 8, D], BF16)
    x = ph.sb("x", [128, 8, T], F32)
    hT = ph.sb("hT", [128, 8, T], BF16)
    sq = ph.sb("sq", [128, 8, T], BF16)
    rstd = ph.sb("rstd", [128, T], F32)
    ones_bf = ph.sb("ones_bf", [128, 128], BF16)
    mPb = ph.sb("mPb", [128, 512], BF16)
    mCb = ph.sb("mCb", [128, 512], BF16)
    esink = ph.sb("esink", [128, 2048], F32)
    COS = ph.sb("COS", [128, T], F32)
    SIN = ph.sb("SIN", [128, T], F32)
    posi = ph.sb("posi", [128, T], I32)
    ang = ph.sb("ang", [128, T], F32)
    nf = ph.sb("nf", [128, T], F32)
    QT = ph.sb("QT", [128, 8, T], BF16)
    KT = ph.sb("KT", [128, 4, 128 + T], BF16)
    VB = ph.sb("VB", [128, 5, 512], BF16)
    OT = ph.sb("OT", [128, 8, T], BF16)
    bv = ph.sb("bv", [1, 512], BF16)
    qraw = [ph.sb("qraw%d" % i, [128, T], F32) for i in range(2)]
    qsq = [ph.sb("qsq%d" % i, [128, T], F32) for i in range(2)]
    qr = [ph.sb("qr%d" % i, [128, T], F32) for i in range(2)]
    qn = [ph.sb("qn%d" % i, [128, T], F32) for i in range(2)]
    t1 = [ph.sb("t1%d" % i, [128, T], F32) for i in range(2)]
    t2 = [ph.sb("t2%d" % i, [128, T], F32) for i in range(2)]
    E = [ph.sb("E%d" % i, [128, 512], BF16) for i in range(4)]
    den = [ph.sb("den%d" % i, [128, 512], F32) for i in range(2)]
    load_w(S, "pool", wq, "wq", w_qkv[:, 0:1024], 8)
    load_w(S, "pool", wo, "wo", w_o, 8)
    wkv = ph.sb("wkv", [128, 8, 512], BF16)
    load_w(S, "pool", wkv, "wkv", w_qkv[:, 1024:1536], 8)
    for hk in range(4):
        for cp in range(2):
            S.op("pool", lambda e, hk=hk, cp=cp: e.tensor_copy(out=wkd[:, :, hk, cp * 64:(cp + 1) * 64], in_=wkv[:, :, hk * 64:(hk + 1) * 64]),
                 reads=[("wkv", 0, 8)], writes=[("wkd", hk * 2 + cp, hk * 2 + cp + 1)])
            S.op("dve", lambda e, hk=hk, cp=cp: e.tensor_copy(out=wvd[:, :, hk, cp * 64:(cp + 1) * 64], in_=wkv[:, :, 256 + hk * 64:256 + (hk + 1) * 64]),
                 reads=[("wkv", 0, 8)], writes=[("wvd", hk * 2 + cp, hk * 2 + cp + 1)])
    if 'bv' not in DBG['skip']:
        S.dma("pool", bv[:], nc_bv_dram[0], writes=[("bv", 0, 1)])
    S.op("dve", lambda e: e.tensor_copy(out=ones_bf[:], in_=ph.c("ones", 0, 128)), reads=[ph.cr("ones", 0, 128)], writes=[("ones_bf", 0, 1)])
    S.op("dve", lambda e: e.tensor_copy(out=mPb[:], in_=ph.c("mP", 0, 512)), reads=[ph.cr("mP", 0, 512)], writes=[("mPb", 0, 1)])
    S.op("dve", lambda e: e.tensor_copy(out=mCb[:], in_=ph.c("mC", 0, 512)), reads=[ph.cr("mC", 0, 512)], writes=[("mCb", 0, 1)])
    if 'esink' not in DBG['skip']:
      S.op("act", lambda e: e.activation(out=esink[:], in_=ph.c("sinks", 0, 2048), func=AF.Exp), reads=[ph.cr("sinks", 0, 2048)], writes=[("esink", 0, 1)])
    srcv = src.rearrange("(c p) t -> p c t", p=128)
    dstv = dst.rearrange("(c p) t -> p c t", p=128)
    ei = 0
    for t in range(NT):
        tc0 = t * T
        S.dma("sp", x[:], srcv[:, :, tc0:tc0 + T], writes=[("x", 0, 8)])
        rmsnorm(ph, x, "x", hT, "hT", sq, "sq", rstd, ones_bf, "at_g", T)
        S.dma("sp", posi[:], pos[0:1, tc0:tc0 + T].to_broadcast([128, T]), writes=[("posi", 0, 1)])
        S.op("dve", lambda e: e.tensor_copy(out=ang[:], in_=posi[:]), reads=[("posi", 0, 1)], writes=[("ang", 0, 1)])
        S.op("dve", lambda e: e.tensor_scalar(out=ang[:], in0=ang[:], scalar1=ph.c("invf"), scalar2=None, op0=ALU.mult),
             reads=[("ang", 0, 1), ph.cr("invf")], writes=[("ang", 0, 1)])
        for which, tab, tn in ((0, SIN, "SIN"), (1, COS, "COS")):
            if which == 1:
                S.op("dve", lambda e: e.tensor_scalar(out=ang[:], in0=ang[:], scalar1=PI / 2, scalar2=None, op0=ALU.add),
                     reads=[("ang", 0, 1)], writes=[("ang", 0, 1)])
            S.op("dve", lambda e: e.tensor_scalar(out=posi[:], in0=ang[:], scalar1=float(1.0 / (2 * PI)), scalar2=None, op0=ALU.mult),
                 reads=[("ang", 0, 1)], writes=[("posi", 0, 1)])
            S.op("dve", lambda e: e.tensor_copy(out=nf[:], in_=posi[:]), reads=[("posi", 0, 1)], writes=[("nf", 0, 1)])
            S.op("dve", lambda e: e.scalar_tensor_tensor(out=nf[:], in0=nf[:], scalar=float(-2 * PI), in1=ang[:], op0=ALU.mult, op1=ALU.add),
                 reads=[("nf", 0, 1), ("ang", 0, 1)], writes=[("nf", 0, 1)])
            S.op("dve", lambda e: e.tensor_scalar(out=nf[:], in0=nf[:], scalar1=PI, scalar2=-PI, op0=ALU.min, op1=ALU.max),
                 reads=[("nf", 0, 1)], writes=[("nf", 0, 1)])
            S.op("act", lambda e, tab=tab: e.activation(out=tab[:], in_=nf[:], func=AF.Sin), reads=[("nf", 0, 1)], writes=[(tn, 0, 1)])
        if t > 0:
            S.op("pool", lambda e: e.tensor_copy(out=KT[:, :, 0:128], in_=KT[:, :, T:T + 128]), reads=[("KT", 4, 5)], writes=[("KT", 0, 1)])
            S.op("pool", lambda e: e.tensor_copy(out=VB[:, 0, :], in_=VB[:, 4, :]), reads=[("VB", 4, 5)], writes=[("VB", 0, 1)])
        for b in range(4 if DBG['att'] >= 1 else 0):
            ps, pn = ph.newps()
            for k in range(8):
                S.op("pe", lambda e, k=k, b=b, ps=ps: e.matmul(ps[:, :], lhsT=hT[:, k, b * 128:(b + 1) * 128], rhs=wvd[:, k, :, :],
                                                               start=(k == 0), stop=False),
                     reads=[("hT", k, k + 1), ("wvd", 0, 8)], writes=[(pn, 0, 1)])
            S.op("pe", lambda e, ps=ps: e.matmul(ps[:, :], lhsT=ones_bf[0:1, :], rhs=bv[0:1, :], start=False, stop=True),
                 reads=[("ones_bf", 0, 1), ("bv", 0, 1)], writes=[(pn, 0, 1)])
            S.op("act", lambda e, b=b, ps=ps: e.activation(out=VB[:, 1 + b, :], in_=ps[:, :], func=AF.Identity),
                 reads=[(pn, 0, 1)], writes=[("VB", 1 + b, 2 + b)])
        for j in range(12 if DBG['att'] >= 2 else 0):
            i = j % 2
            isq = j < 8
            ps, pn = ph.newps()
            for k in range(8):
                if isq:
                    S.op("pe", lambda e, k=k, j=j, ps=ps: e.matmul(ps[:, :T], lhsT=wq[:, k, j * 128:(j + 1) * 128], rhs=hT[:, k, :],
                                                                   start=(k == 0), stop=(k == 7)),
                         reads=[("wq", k, k + 1), ("hT", k, k + 1)], writes=[(pn, 0, 1)])
                else:
                    S.op("pe", lambda e, k=k, j=j, ps=ps: e.matmul(ps[:, :T], lhsT=wkd[:, k, j - 8, :], rhs=hT[:, k, :],
                                                                   start=(k == 0), stop=(k == 7)),
                         reads=[("wkd", 0, 8), ("hT", k, k + 1)], writes=[(pn, 0, 1)])
            bias = ph.c("b_q", j) if isq else ph.c("b_kd", j - 8)
            bres = ph.cr("b_q", j) if isq else ph.cr("b_kd", j - 8)
            gcol, gres = (ph.c("qn_g"), ph.cr("qn_g")) if isq else (ph.c("kn_g"), ph.cr("kn_g"))
            qa, qs_, qrr, qnn, ta, tb = qraw[i], qsq[i], qr[i], qn[i], t1[i], t2[i]
            S.op("act", lambda e, ps=ps, qa=qa, bias=bias: e.activation(out=qa[:], in_=ps[:, :T], func=AF.Identity, bias=bias),
                 reads=[(pn, 0, 1), bres], writes=[("qraw%d" % i, 0, 1)])
            S.op("pool", lambda e, qa=qa, qs_=qs_: e.tensor_tensor(out=qs_[:], in0=qa[:], in1=qa[:], op=ALU.mult),
                 reads=[("qraw%d" % i, 0, 1)], writes=[("qsq%d" % i, 0, 1)])
            ps2, pn2 = ph.newps()
            S.op("pe", lambda e, ps2=ps2, qs_=qs_: e.matmul(ps2[:, :T], lhsT=ph.c("bones", 0, 128), rhs=qs_[:], start=True, stop=True),
                 reads=[ph.cr("bones", 0, 128), ("qsq%d" % i, 0, 1)], writes=[(pn2, 0, 1)])
            S.op("act", lambda e, ps2=ps2, qrr=qrr: e.activation(out=qrr[:], in_=ps2[:, :T], func=AF.Ln, scale=1.0 / 64, bias=ph.c("eps6")),
                 reads=[(pn2, 0, 1), ph.cr("eps6")], writes=[("qr%d" % i, 0, 1)])
            S.op("act", lambda e, qrr=qrr: e.activation(out=qrr[:], in_=qrr[:], func=AF.Exp, scale=-0.5),
                 reads=[("qr%d" % i, 0, 1)], writes=[("qr%d" % i, 0, 1)])
            S.op("dve", lambda e, qa=qa, qrr=qrr, qnn=qnn, gcol=gcol: e.scalar_tensor_tensor(out=qnn[:], in0=qa[:], scalar=gcol, in1=qrr[:],
                                                                                         op0=ALU.mult, op1=ALU.mult),
                 reads=[("qraw%d" % i, 0, 1), ("qr%d" % i, 0, 1), gres], writes=[("qn%d" % i, 0, 1)])
            ps3, pn3 = ph.newps()
            S.op("pe", lambda e, ps3=ps3, qnn=qnn: e.matmul(ps3[:, :T], lhsT=ph.c("rmT", 0, 128), rhs=qnn[:], start=True, stop=True),
                 reads=[ph.cr("rmT", 0, 128), ("qn%d" % i, 0, 1)], writes=[(pn3, 0, 1)])
            S.op("pool", lambda e, qnn=qnn, ta=ta: e.tensor_tensor(out=ta[:], in0=qnn[:], in1=COS[:, :], op=ALU.mult),
                 reads=[("qn%d" % i, 0, 1), ("COS", 0, 1)], writes=[("t1%d" % i, 0, 1)])
            S.op("dve", lambda e, ps3=ps3, tb=tb: e.tensor_tensor(out=tb[:], in0=ps3[:, :T], in1=SIN[:, :], op=ALU.mult),
                 reads=[(pn3, 0, 1), ("SIN", 0, 1)], writes=[("t2%d" % i, 0, 1)])
            if isq:
                S.op("pool", lambda e, ta=ta, tb=tb, j=j: e.tensor_tensor(out=QT[:, j, :], in0=ta[:], in1=tb[:], op=ALU.add),
                     reads=[("t1%d" % i, 0, 1), ("t2%d" % i, 0, 1)], writes=[("QT", j, j + 1)])
            else:
                S.op("pool", lambda e, ta=ta, tb=tb, j=j: e.tensor_tensor(out=KT[:, j - 8, 128:128 + T], in0=ta[:], in1=tb[:], op=ALU.add),
                     reads=[("t1%d" % i, 0, 1), ("t2%d" % i, 0, 1)], writes=[("KT", 1, 5)])
        for qb in range(4 if DBG['att'] >= 3 else 0):
            first = (t == 0 and qb == 0)
            for hk in range(4):
                kbs = [1] if first else [0, 1]
                Es = []
                for kb in kbs:
                    kc0 = (qb + kb) * 128
                    Et = E[ei % 4]
                    en = "E%d" % (ei % 4)
                    ei += 1
                    for par in range(2):
                        pss, psn = ph.newps()
                        hp = slice(par * 64, par * 64 + 64)
                        for pair in range(2):
                            col = pair * 128
                            S.op("pe", lambda e, pss=pss, hp=hp, col=col, kc0=kc0, hk=hk, pair=pair, qb=qb:
                                 e.matmul(pss[:, col:col + 128], lhsT=KT[hp, hk, kc0:kc0 + 128], rhs=QT[hp, 2 * hk + pair, qb * 128:(qb + 1) * 128],
                                          start=True, stop=True),
                                 reads=[("KT", 0, 5), ("QT", 2 * hk + pair, 2 * hk + pair + 1)], writes=[(psn, 0, 1)])
                        S.op("act", lambda e, Et=Et, pss=pss, par=par: e.activation(out=Et[:, par * 256:(par + 1) * 256], in_=pss[:, 0:256], func=AF.Exp, scale=0.125),
                             reads=[(psn, 0, 1)], writes=[(en, par, par + 1)])
                    mk, mkn = (mPb, "mPb") if kb == 0 else (mCb, "mCb")
                    S.op("pool" if kb == 0 else "dve", lambda e, Et=Et, mk=mk: e.tensor_tensor(out=Et[:], in0=Et[:], in1=mk[:], op=ALU.mult),
                         reads=[(en, 0, 2), (mkn, 0, 1)], writes=[(en, 0, 2)])
                    if masked and t == 1 and qb == 0 and kb == 0:
                        S.op("act", lambda e, Et=Et: e.activation(out=Et[:], in_=Et[:], func=AF.Identity, scale=ph.c("hmask")),
                             reads=[(en, 0, 2), ph.cr("hmask")], writes=[(en, 0, 2)])
                    Es.append((Et, en, kb))
                psd, pdn = ph.newps()
                pso, pon = ph.newps()
                for n_, (Et, en, kb) in enumerate(Es):
                    S.op("pe", lambda e, psd=psd, Et=Et, n_=n_: e.matmul(psd[:, :], lhsT=ones_bf[:], rhs=Et[:], start=(n_ == 0), stop=(n_ == len(Es) - 1)),
                         reads=[("ones_bf", 0, 1), (en, 0, 2)], writes=[(pdn, 0, 1)])
                for n_, (Et, en, kb) in enumerate(Es):
                    vb = qb + kb
                    S.op("pe", lambda e, pso=pso, Et=Et, n_=n_, vb=vb, hk=hk: e.matmul(pso[:, :], lhsT=VB[:, vb, hk * 128:(hk + 1) * 128], rhs=Et[:],
                                                                                     start=(n_ == 0), stop=(n_ == len(Es) - 1)),
                         reads=[("VB", vb, vb + 1), (en, 0, 2)], writes=[(pon, 0, 1)])
                dn = den[hk % 2]
                dnn = "den%d" % (hk % 2)
                S.op("dve", lambda e, dn=dn, psd=psd, hk=hk: e.tensor_tensor(out=dn[:], in0=psd[:, :], in1=esink[:, hk * 512:(hk + 1) * 512], op=ALU.add),
                     reads=[(pdn, 0, 1), ("esink", 0, 1)], writes=[(dnn, 0, 1)])
                S.op("act", lambda e, dn=dn: e.activation(out=dn[:], in_=dn[:], func=AF.Ln), reads=[(dnn, 0, 1)], writes=[(dnn, 0, 1)])
                S.op("act", lambda e, dn=dn: e.activation(out=dn[:], in_=dn[:], func=AF.Exp, scale=-1.0), reads=[(dnn, 0, 1)], writes=[(dnn, 0, 1)])
                for par in range(2):
                    hp = slice(par * 64, par * 64 + 64)
                    S.op("dve", lambda e, dn=dn, pso=pso, hp=hp, par=par, hk=hk, qb=qb:
                         e.tensor_tensor(out=OT[hp, 2 * hk:2 * hk + 2, qb * 128:(qb + 1) * 128],
                                         in0=pso[hp, par * 256:(par + 1) * 256].rearrange("p (a q) -> p a q", a=2),
                                         in1=dn[hp, par * 256:(par + 1) * 256].rearrange("p (a q) -> p a q", a=2), op=ALU.mult),
                         reads=[(pon, 0, 1), (dnn, 0, 1)], writes=[("OT", 2 * hk, 2 * hk + 2)])
        for oc in range(8):
            pso, pon = ph.newps()
            for k in range(8):
                S.op("pe", lambda e, k=k, oc=oc, pso=pso: e.matmul(pso[:, :T], lhsT=wo[:, k, oc * 128:(oc + 1) * 128], rhs=OT[:, k, :],
                                                                   start=(k == 0), stop=(k == 7)),
                     reads=[("wo", k, k + 1), ("OT", k, k + 1)], writes=[(pon, 0, 1)])
            S.op("dve", lambda e, oc=oc, pso=pso: e.scalar_tensor_tensor(out=x[:, oc, :], in0=pso[:, :T], scalar=ph.c("b_o", oc), in1=x[:, oc, :],
                                                                        op0=ALU.add, op1=ALU.add),
                 reads=[(pon, 0, 1), ("x", oc, oc + 1), ph.cr("b_o", oc)], writes=[("x", oc, oc + 1)])
        if masked and t == 0:
            for k in range(8):
                S.op("act", lambda e, k=k: e.activation(out=x[:, k, :], in_=x[:, k, :], func=AF.Identity, scale=ph.c("hmask")),
                     reads=[("x", k, k + 1), ph.cr("hmask")], writes=[("x", k, k + 1)])
        S.dma("sp", dstv[:, :, tc0:tc0 + T], x[:], reads=[("x", 0, 8)])
    ph.close()


nc_bv_dram = [None]


def mixer_phase(nc, src, dst, w_in, w_out, w2a2_d, g2_d, cdram, coff, TL, NPRE=0, masked=False):
    T = 256
    NT = TL // T
    NCH = T // 64
    need = [("eps6", 1), ("eps5", 1), ("epsgn", 1), ("hmask", 1), ("ab_g", 8), ("cin_b", 8), ("dw_w", 124), ("dw_b", 4), ("ln_g", 4), ("ln_b", 4),
            ("mu", 14), ("w0", 4), ("a0", 4), ("k_k", 4), ("k_a", 4), ("r_k", 4), ("gn_g", 4), ("gn_b", 4),
            ("ident", 128), ("bones", 128), ("ones", 128), ("m_su", 128), ("m_iu", 128), ("m_sl", 128)]
    ph = Phase(nc, cdram, coff, need)
    S = ph.S
    sb = ph.sb
    win = sb("win", [128, 8, 2816], BF16)
    wout = sb("wout", [128, 8, D], BF16)
    w2a2 = sb("w2a2", [128, 512], BF16)
    g2b = sb("g2b", [128, 512], BF16)
    x = sb("x", [128, 8, T], F32)
    hT = sb("hT", [128, 8, T], BF16)
    sq = sb("sq", [128, 8, T], BF16)
    rstd = sb("rstd", [128, T], F32)
    ones_bf = sb("ones_bf", [128, 128], BF16)
    omm = sb("omm", [128, 14], F32)
    GL = sb("GL", [128, 4, 30 + T], F32)
    cacc = sb("cacc", [128, 4, T], F32)
    csq = sb("csq", [128, 4, T], F32)
    sig = sb("sig", [128, T], F32)
    crs = sb("crs", [128, T], F32)
    catT = sb("catT", [128, 8, T], BF16)
    Pb = [sb("Pb%d" % i, [128, T + 1], F32) for i in range(2)]
    ptmp = [sb("ptmp%d" % i, [128, T], F32) for i in range(2)]
    pcarry = sb("pcarry", [128, 14], F32)
    rw12 = sb("rw12", [128, T], F32)
    rw13 = sb("rw13", [128, T], F32)
    twad = sb("twad", [128, T], BF16)
    sgd = sb("sgd", [128, T], BF16)
    lw4 = sb("lw4", [128, 4, T], F32)
    a4 = sb("a4", [128, 4, T], F32)
    gate4 = sb("gate4", [128, 4, T], F32)
    onesrow = sb("onesrow", [128, 64], F32)
    resetrow = sb("resetrow", [128, T], F32)
    names = ["r", "k", "v", "kk", "q1", "rn", "kkn", "k2", "bb", "bonus", "cum", "P", "Pinv", "Pprev", "y", "yc"]
    pers = ("P", "bonus", "y", "yc")
    st_ = {n: [sb("s_%s%d" % (n, i), [128, T], F32) for i in range(4 if n in pers else 2)] for n in names}
    bdn = ["a", "r", "b", "k", "v"]
    BD = {n: [sb("bd_%s%d" % (n, i), [128, NCH, 128], BF16) for i in range(4)] for n in bdn}
    mats = ["AT", "A", "YrbT", "XakT", "YrkT", "TT", "Vt", "Bt", "Kt", "W", "U"]
    M = {n: [sb("m_%s%d" % (n, i), [128, 128], BF16) for i in range(2)] for n in mats}
    A2 = [[sb("A2_%d_%d" % (j, i), [128, 128], BF16) for i in range(2)] for j in range(2)]
    A2T = [[sb("A2T_%d_%d" % (j, i), [128, 128], BF16) for i in range(2)] for j in range(2)]
    TT2 = [[sb("TT2_%d_%d" % (j, i), [128, 128], BF16) for i in range(2)] for j in range(2)]
    H = [[sb("H%d_%d" % (cc, i), [128, 128], F32) for i in range(2)] for cc in range(4)]
    hcur = [0, 0, 0, 0]
    Hb = [[sb("Hb%d_%d" % (cc, i), [128, 128], BF16) for i in range(2)] for cc in range(4)]
    ident_bf = sb("ident_bf", [128, 128], BF16)

    load_w(S, "pool", win, "win", w_in, 8)
    load_w(S, "pool", wout, "wout", w_out, 8)
    S.dma("pool", w2a2[:], w2a2_d, writes=[("w2a2", 0, 1)])
    S.dma("pool", g2b[:], g2_d, writes=[("g2b", 0, 1)])
    S.op("dve", lambda e: e.tensor_copy(out=ones_bf[:], in_=ph.c("ones", 0, 128)), reads=[ph.cr("ones", 0, 128)], writes=[("ones_bf", 0, 1)])
    S.op("dve", lambda e: e.tensor_copy(out=onesrow[:], in_=ph.c("ones", 0, 64)), reads=[ph.cr("ones", 0, 64)], writes=[("onesrow", 0, 1)])
    S.op("dve", lambda e: e.tensor_scalar(out=omm[:], in0=ph.c("mu", 0, 14), scalar1=-1.0, scalar2=1.0, op0=ALU.mult, op1=ALU.add),
         reads=[ph.cr("mu", 0, 14)], writes=[("omm", 0, 14)])
    S.op("pool", lambda e: e.memset(GL[:], 0.0), writes=[("GL", 0, 4 * 1000)])
    S.op("pool", lambda e: e.memset(resetrow[:], 1.0), writes=[("resetrow", 0, 1)])
    S.op("pool", lambda e: e.memset(resetrow[:].rearrange("p (n t) -> p n t", t=64)[:, :, 0:1], 0.0), writes=[("resetrow", 0, 1)])
    S.op("pool", lambda e: e.memset(pcarry[:], 0.0), writes=[("pcarry", 0, 14)])
    for n in bdn:
        for i in range(4):
            S.op("pool", lambda e, n=n, i=i: e.memset(BD[n][i][:], 0.0), writes=[("bd_%s%d" % (n, i), 0, NCH)])
    for cc in range(4):
        S.op("pool", lambda e, cc=cc: e.memset(H[cc][0][:], 0.0), writes=[("H%d_0" % cc, 0, 1)])
        S.op("pool", lambda e, cc=cc: e.memset(Hb[cc][0][:], 0.0), writes=[("Hb%d_0" % cc, 0, 1)])
    S.op("dve", lambda e: e.tensor_copy(out=ident_bf[:], in_=ph.c("ident", 0, 128)), reads=[ph.cr("ident", 0, 128)], writes=[("ident_bf", 0, 1)])

    ident = ph.c("ident", 0, 128)
    bones = ph.c("bones", 0, 128)
    ones32 = ph.c("ones", 0, 128)
    srcv = src.rearrange("(c p) t -> p c t", p=128)
    dstv = dst.rearrange("(c p) t -> p c t", p=128)

    def proj(pc):
        ps, pn = ph.newps()
        for k in range(8):
            S.op("pe", lambda e, k=k, ps=ps: e.matmul(ps[:, :T], lhsT=win[:, k, pc * 128:(pc + 1) * 128], rhs=hT[:, k, :],
                                                      start=(k == 0), stop=(k == 7)),
                 reads=[("win", k, k + 1), ("hT", k, k + 1)], writes=[(pn, 0, 1)])
        return ps, pn

    def shifted(ch, out, outname):
        i = ch % 2
        ps, pn = proj(8 + ch)
        pb, pbn, tm, tmn = Pb[i], "Pb%d" % i, ptmp[i], "ptmp%d" % i
        S.op("pool", lambda e: e.tensor_copy(out=pb[:, 0:1], in_=pcarry[:, ch:ch + 1]), reads=[("pcarry", ch, ch + 1)], writes=[(pbn, 0, 1)])
        S.op("act", lambda e: e.activation(out=pb[:, 1:T + 1], in_=ps[:, :T], func=AF.Identity), reads=[(pn, 0, 1)], writes=[(pbn, 1, T + 1)])
        S.op("pool", lambda e: e.tensor_copy(out=pcarry[:, ch:ch + 1], in_=pb[:, T:T + 1]), reads=[(pbn, T, T + 1)], writes=[("pcarry", ch, ch + 1)])
        S.op("act", lambda e: e.activation(out=tm[:], in_=pb[:, 0:T], func=AF.Identity, scale=ph.c("mu", ch)),
             reads=[(pbn, 0, T), ph.cr("mu", ch)], writes=[(tmn, 0, 1)])
        S.op("dve", lambda e: e.scalar_tensor_tensor(out=out[:], in0=pb[:, 1:T + 1], scalar=omm[:, ch:ch + 1], in1=tm[:], op0=ALU.mult, op1=ALU.add),
             reads=[(pbn, 1, T + 1), ("omm", ch, ch + 1), (tmn, 0, 1)], writes=[(outname, 0, 1)])

    def mm1(ps, pn, lhsT, rhs, reads, start=True, stop=True, n=128):
        S.op("pe", lambda e: e.matmul(ps[:, :n], lhsT=lhsT, rhs=rhs, start=start, stop=stop), reads=reads, writes=[(pn, 0, 1)])

    def rsqrt_ps(ps, pn, out, outn, scale, epsname, n=T):
        S.op("act", lambda e: e.activation(out=out, in_=ps[:, :n], func=AF.Ln, scale=scale, bias=ph.c(epsname)),
             reads=[(pn, 0, 1), ph.cr(epsname)], writes=[(outn, 0, 1)])
        S.op("act", lambda e: e.activation(out=out, in_=out, func=AF.Exp, scale=-0.5), reads=[(outn, 0, 1)], writes=[(outn, 0, 1)])

    for t in range(NT):
        full = t >= NPRE
        plast = (t == NPRE - 1)
        mtile = masked and (NPRE - 1 <= t < NPRE + 2)
        S.dma("sp", x[:], srcv[:, :, t * T:(t + 1) * T], writes=[("x", 0, 8)])
        rmsnorm(ph, x, "x", hT, "hT", sq, "sq", rstd, ones_bf, "ab_g", T)
        for cc in range(4 if (full or plast) else 0):
            psa, pan = proj(cc)
            psg, pgn = proj(4 + cc)
            S.op("act", lambda e, psg=psg, cc=cc: e.activation(out=sig[:], in_=psg[:, :T], func=AF.Sigmoid, bias=ph.c("cin_b", 4 + cc)),
                 reads=[(pgn, 0, 1), ph.cr("cin_b", 4 + cc)], writes=[("sig", 0, 1)])
            S.op("dve", lambda e, psa=psa, cc=cc: e.scalar_tensor_tensor(out=GL[:, cc, 30:30 + T], in0=psa[:, :T], scalar=ph.c("cin_b", cc), in1=sig[:],
                                                                        op0=ALU.add, op1=ALU.mult),
                 reads=[(pan, 0, 1), ("sig", 0, 1), ph.cr("cin_b", cc)], writes=[("GL", cc * 1000 + 30, cc * 1000 + 30 + T)])
            eng = "dve"
            if mtile:
                S.op("act", lambda e, cc=cc: e.activation(out=GL[:, cc, 30:30 + T], in_=GL[:, cc, 30:30 + T], func=AF.Identity, scale=ph.c("hmask")),
                     reads=[("GL", cc * 1000 + 30, cc * 1000 + 30 + T), ph.cr("hmask")], writes=[("GL", cc * 1000 + 30, cc * 1000 + 30 + T)])
            if full:
              S.op(eng, lambda e, cc=cc: e.tensor_scalar(out=cacc[:, cc, :], in0=GL[:, cc, 30:30 + T], scalar1=ph.c("dw_w", cc * 31 + 30), scalar2=ph.c("dw_b", cc),
                                                       op0=ALU.mult, op1=ALU.add),
                 reads=[("GL", cc * 1000, cc * 1000 + 30 + T), ph.cr("dw_w", cc * 31, 31), ph.cr("dw_b", cc)], writes=[("cacc", cc, cc + 1)])
            for j in range(30 if full else 0):
                S.op(eng, lambda e, cc=cc, j=j: e.scalar_tensor_tensor(out=cacc[:, cc, :], in0=GL[:, cc, j:j + T], scalar=ph.c("dw_w", cc * 31 + j), in1=cacc[:, cc, :],
                                                                      op0=ALU.mult, op1=ALU.add),
                     reads=[("GL", cc * 1000, cc * 1000 + 30 + T), ("cacc", cc, cc + 1)], writes=[("cacc", cc, cc + 1)])
            S.op(eng, lambda e, cc=cc: e.tensor_copy(out=GL[:, cc, 0:30], in_=GL[:, cc, T:T + 30]),
                 reads=[("GL", cc * 1000 + T, cc * 1000 + T + 30)], writes=[("GL", cc * 1000, cc * 1000 + 30)])
        psm, pmn = ph.newps()
        for cc in range(4 if full else 0):
            mm1(psm, pmn, ones32, cacc[:, cc, :], [ph.cr("ones", 0, 128), ("cacc", cc, cc + 1)], start=(cc == 0), stop=(cc == 3), n=T)
        for cc in range(4 if full else 0):
            S.op("dve", lambda e, cc=cc, psm=psm: e.scalar_tensor_tensor(out=cacc[:, cc, :], in0=psm[:, :T], scalar=-1.0 / 512, in1=cacc[:, cc, :],
                                                                        op0=ALU.mult, op1=ALU.add),
                 reads=[(pmn, 0, 1), ("cacc", cc, cc + 1)], writes=[("cacc", cc, cc + 1)])
            S.op("pool", lambda e, cc=cc: e.tensor_tensor(out=csq[:, cc, :], in0=cacc[:, cc, :], in1=cacc[:, cc, :], op=ALU.mult),
                 reads=[("cacc", cc, cc + 1)], writes=[("csq", cc, cc + 1)])
        psv, pvn = ph.newps()
        for cc in range(4 if full else 0):
            mm1(psv, pvn, ones32, csq[:, cc, :], [ph.cr("ones", 0, 128), ("csq", cc, cc + 1)], start=(cc == 0), stop=(cc == 3), n=T)
        if full:
            rsqrt_ps(psv, pvn, crs[:], "crs", 1.0 / 512, "eps5")
        for cc in range(4 if full else 0):
            S.op("dve", lambda e, cc=cc: e.tensor_tensor(out=cacc[:, cc, :], in0=cacc[:, cc, :], in1=crs[:], op=ALU.mult),
                 reads=[("cacc", cc, cc + 1), ("crs", 0, 1)], writes=[("cacc", cc, cc + 1)])
            S.op("act", lambda e, cc=cc: e.activation(out=catT[:, cc, :], in_=cacc[:, cc, :], func=AF.Silu, scale=ph.c("ln_g", cc), bias=ph.c("ln_b", cc)),
                 reads=[("cacc", cc, cc + 1), ph.cr("ln_g", cc), ph.cr("ln_b", cc)], writes=[("catT", cc, cc + 1)])
        shifted(12, rw12, "rw12")
        if full or plast:
            shifted(13, rw13, "rw13")
        S.op("act", lambda e: e.activation(out=twad[0:64, :], in_=rw12[0:64, :], func=AF.Tanh), reads=[("rw12", 0, 1)], writes=[("twad", 0, 1)])
        S.op("dve", lambda e: e.tensor_copy(out=twad[64:128, :], in_=rw12[64:128, :]), reads=[("rw12", 0, 1)], writes=[("twad", 1, 2)])
        if full:
            S.op("act", lambda e: e.activation(out=sgd[:], in_=rw13[:], func=AF.Sigmoid), reads=[("rw13", 0, 1)], writes=[("sgd", 0, 1)])
        for cc in range(4):
            ps, pn = ph.newps()
            mm1(ps, pn, w2a2[0:64, cc * 128:(cc + 1) * 128], twad[0:64, :], [("w2a2", 0, 1), ("twad", 0, 1)], n=T)
            S.op("act", lambda e, ps=ps, cc=cc: e.activation(out=lw4[:, cc, :], in_=ps[:, :T], func=AF.Sigmoid, bias=ph.c("w0", cc)),
                 reads=[(pn, 0, 1), ph.cr("w0", cc)], writes=[("lw4", cc, cc + 1)])
            S.op("act", lambda e, cc=cc: e.activation(out=lw4[:, cc, :], in_=lw4[:, cc, :], func=AF.Identity, scale=-float(np.exp(-0.5))),
                 reads=[("lw4", cc, cc + 1)], writes=[("lw4", cc, cc + 1)])
            ps, pn = ph.newps()
            mm1(ps, pn, w2a2[64:128, cc * 128:(cc + 1) * 128], twad[64:128, :], [("w2a2", 0, 1), ("twad", 1, 2)], n=T)
            S.op("act", lambda e, ps=ps, cc=cc: e.activation(out=a4[:, cc, :], in_=ps[:, :T], func=AF.Sigmoid, bias=ph.c("a0", cc)),
                 reads=[(pn, 0, 1), ph.cr("a0", cc)], writes=[("a4", cc, cc + 1)])
            if full:
                ps, pn = ph.newps()
                mm1(ps, pn, g2b[:, cc * 128:(cc + 1) * 128], sgd[:], [("g2b", 0, 1), ("sgd", 0, 1)], n=T)
                S.op("act", lambda e, ps=ps, cc=cc: e.activation(out=gate4[:, cc, :], in_=ps[:, :T], func=AF.Identity),
                     reads=[(pn, 0, 1)], writes=[("gate4", cc, cc + 1)])
        def prep(cc, cx):
            i = cc % 2
            s = {n: st_[n][cc if n in pers else i] for n in names}
            sn = {n: "s_%s%d" % (n, cc if n in pers else i) for n in names}
            bd = {n: BD[n][cc] for n in bdn}
            bn = {n: "bd_%s%d" % (n, cc) for n in bdn}
            if full or plast:
                shifted(cc, s["r"], sn["r"])
            shifted(4 + cc, s["k"], sn["k"])
            shifted(8 + cc, s["v"], sn["v"])
            lw = lw4[:, cc, :]
            av = a4[:, cc, :]
            yield

            def ew(eng, f, reads, writes):
                S.op(eng, f, reads=[(sn[r], 0, 1) if r in sn else r for r in reads], writes=[(sn[w], 0, 1) if w in sn else w for w in writes])
            ew("act", lambda e, s=s, cc=cc: e.activation(out=s["kk"][:], in_=s["k"][:], func=AF.Identity, scale=ph.c("k_k", cc)),
               ["k", ph.cr("k_k", cc)], ["kk"])
            ew("pool", lambda e, s=s: e.tensor_tensor(out=s["q1"][:], in0=s["kk"][:], in1=s["kk"][:], op=ALU.mult), ["kk"], ["q1"])
            ps, pn = ph.newps()
            mm1(ps, pn, bones, s["q1"][:], [ph.cr("bones", 0, 128), (sn["q1"], 0, 1)], n=T)
            yield
            ew("dve", lambda e, s=s, ps=ps: e.tensor_scalar(out=s["rn"][:], in0=ps[:, :T], scalar1=1e-24, scalar2=None, op0=ALU.max), [(pn, 0, 1)], ["rn"])
            ew("act", lambda e, s=s: e.activation(out=s["rn"][:], in_=s["rn"][:], func=AF.Ln), ["rn"], ["rn"])
            ew("act", lambda e, s=s: e.activation(out=s["rn"][:], in_=s["rn"][:], func=AF.Exp, scale=-0.5), ["rn"], ["rn"])
            yield
            ew("dve", lambda e, s=s: e.tensor_tensor(out=s["kkn"][:], in0=s["kk"][:], in1=s["rn"][:], op=ALU.mult), ["kk", "rn"], ["kkn"])
            ew("pool", lambda e, s=s, av=av, cc=cc: e.tensor_scalar(out=s["q1"][:], in0=av, scalar1=-1.0, scalar2=ph.c("k_a", cc), op0=ALU.add, op1=ALU.mult),
               [("a4", cc, cc + 1), ph.cr("k_a", cc)], ["q1"])
            ew("dve", lambda e, s=s: e.scalar_tensor_tensor(out=s["k2"][:], in0=s["q1"][:], scalar=1.0, in1=s["k"][:], op0=ALU.add, op1=ALU.mult),
               ["q1", "k"], ["k2"])
            ew("dve", lambda e, s=s, av=av: e.tensor_tensor(out=s["bb"][:], in0=s["kkn"][:], in1=av, op=ALU.mult), ["kkn", ("a4", cc, cc + 1)], ["bb"])
            if full:
                ew("dve", lambda e, s=s, cc=cc: e.scalar_tensor_tensor(out=s["q1"][:], in0=s["r"][:], scalar=ph.c("r_k", cc), in1=s["k2"][:], op0=ALU.mult, op1=ALU.mult),
                   ["r", "k2", ph.cr("r_k", cc)], ["q1"])
                ps, pn = ph.newps()
                mm1(ps, pn, bones, s["q1"][:], [ph.cr("bones", 0, 128), (sn["q1"], 0, 1)], n=T)
                ew("dve", lambda e, s=s, ps=ps: e.tensor_tensor(out=s["bonus"][:], in0=ps[:, :T], in1=s["v"][:], op=ALU.mult), [(pn, 0, 1), "v"], ["bonus"])
            yield
            ew("dve", lambda e, s=s, lw=lw: e.tensor_tensor_scan(out=s["cum"][:], data0=resetrow[:], data1=lw, initial=0.0, op0=ALU.mult, op1=ALU.add),
               [("lw4", cc, cc + 1), ("resetrow", 0, 1)], ["cum"])
            yield
            ew("act", lambda e, s=s: e.activation(out=s["P"][:], in_=s["cum"][:], func=AF.Exp), ["cum"], ["P"])
            ew("act", lambda e, s=s: e.activation(out=s["Pinv"][:], in_=s["cum"][:], func=AF.Exp, scale=-1.0), ["cum"], ["Pinv"])
            ew("pool", lambda e, s=s, lw=lw: e.tensor_tensor(out=s["Pprev"][:], in0=s["cum"][:], in1=lw, op=ALU.subtract), ["cum", ("lw4", cc, cc + 1)], ["Pprev"])
            ew("act", lambda e, s=s: e.activation(out=s["Pprev"][:], in_=s["Pprev"][:], func=AF.Exp), ["Pprev"], ["Pprev"])
            yield
            v3 = lambda ap: ap.rearrange("p (n t) -> p n t", t=64)
            for hh in range(2):
                hp = slice(hh * 64, hh * 64 + 64)
                hc = slice(hh * 64, hh * 64 + 64)
                eng = "dve" if hh == 0 else "pool"
                ew("dve", lambda e, s=s, bd=bd, hp=hp, hc=hc: e.scalar_tensor_tensor(out=bd["a"][hp, :, hc], in0=v3(s["kkn"][hp, :]), scalar=-1.0, in1=v3(s["Pprev"][hp, :]),
                                                                                   op0=ALU.mult, op1=ALU.mult), ["kkn", "Pprev"], [(bn["a"], 0, NCH)])
                if full:
                    ew(eng, lambda e, s=s, bd=bd, hp=hp, hc=hc: e.tensor_tensor(out=bd["r"][hp, :, hc], in0=v3(s["r"][hp, :]), in1=v3(s["P"][hp, :]), op=ALU.mult),
                       ["r", "P"], [(bn["r"], 0, NCH)])
                ew(eng, lambda e, s=s, bd=bd, hp=hp, hc=hc: e.tensor_tensor(out=bd["b"][hp, :, hc], in0=v3(s["bb"][hp, :]), in1=v3(s["Pinv"][hp, :]), op=ALU.mult),
                   ["bb", "Pinv"], [(bn["b"], 0, NCH)])
                ew(eng, lambda e, s=s, bd=bd, hp=hp, hc=hc: e.tensor_tensor(out=bd["k"][hp, :, hc], in0=v3(s["k2"][hp, :]), in1=v3(s["Pinv"][hp, :]), op=ALU.mult),
                   ["k2", "Pinv"], [(bn["k"], 0, NCH)])
                ew(eng, lambda e, s=s, bd=bd, hp=hp, hc=hc: e.tensor_copy(out=bd["v"][hp, :, hc], in_=v3(s["v"][hp, :])), ["v"], [(bn["v"], 0, NCH)])
            cx.update(dict(cc=cc, s=s, sn=sn, bd=bd, bn=bn, ew=ew))
            yield

        def unit(cx, n):
            cc, s, sn, bd, bn = cx["cc"], cx["s"], cx["sn"], cx["bd"], cx["bn"]
            i = cc % 2
            if True:
                m = {k_: M[k_][i] for k_ in mats}
                mn = {k_: "m_%s%d" % (k_, i) for k_ in mats}
                ba, br, bb_, bk, bv_ = (bd[q][:, n, :] for q in bdn)
                R = lambda q: (bn[q], n, n + 1)

                def sc(lq, rq, lhs, rhs, out, mask):
                    ps, pn = ph.newps()
                    mm1(ps, pn, lhs, rhs, [R(lq), R(rq)])
                    S.op("dve", lambda e, ps=ps, m=m: e.tensor_tensor(out=m[out][:], in0=ps[:, :128], in1=ph.c(mask, 0, 128), op=ALU.mult),
                         reads=[(pn, 0, 1), ph.cr(mask, 0, 128)], writes=[(mn[out], 0, 1)])
                sc("b", "a", bb_, ba, "AT", "m_su")
                sc("a", "b", ba, bb_, "A", "m_sl")
                if full:
                    sc("b", "r", bb_, br, "YrbT", "m_iu")
                sc("k", "a", bk, ba, "XakT", "m_su")
                if full:
                    sc("k", "r", bk, br, "YrkT", "m_iu")
                S.op("pool", lambda e, m=m: e.tensor_tensor(out=m["TT"][:], in0=m["AT"][:], in1=ident, op=ALU.add),
                     reads=[(mn["AT"], 0, 1), ph.cr("ident", 0, 128)], writes=[(mn["TT"], 0, 1)])
                cA, cAn, cAT, cATn, cTT, cTTn = m["A"], mn["A"], m["AT"], mn["AT"], m["TT"], mn["TT"]
                yield
                for d in range(5):
                    nA, nAn = A2[i][d % 2], "A2_%d_%d" % (i, d % 2)
                    nAT, nATn = A2T[i][d % 2], "A2T_%d_%d" % (i, d % 2)
                    nTT, nTTn = TT2[i][d % 2], "TT2_%d_%d" % (i, d % 2)
                    ps, pn = ph.newps()
                    mm1(ps, pn, cAT[:], cA[:], [(cATn, 0, 1), (cAn, 0, 1)])
                    S.op("act", lambda e, ps=ps, nA=nA: e.activation(out=nA[:], in_=ps[:, :128], func=AF.Identity), reads=[(pn, 0, 1)], writes=[(nAn, 0, 1)])
                    if d < 4:
                        ps2, pn2 = ph.newps()
                        mm1(ps2, pn2, cA[:], cAT[:], [(cATn, 0, 1), (cAn, 0, 1)])
                        S.op("dve", lambda e, ps2=ps2, nAT=nAT: e.tensor_copy(out=nAT[:], in_=ps2[:, :128]), reads=[(pn2, 0, 1)], writes=[(nATn, 0, 1)])
                    ps3, pn3 = ph.newps()
                    mm1(ps3, pn3, nA[:], cTT[:], [(nAn, 0, 1), (cTTn, 0, 1)])
                    S.op("dve", lambda e, ps3=ps3, nTT=nTT, cTT=cTT: e.tensor_tensor(out=nTT[:], in0=ps3[:, :128], in1=cTT[:], op=ALU.add),
                         reads=[(pn3, 0, 1), (cTTn, 0, 1)], writes=[(nTTn, 0, 1)])
                    cA, cAn, cAT, cATn, cTT, cTTn = nA, nAn, nAT, nATn, nTT, nTTn
                    yield
                for q, dst_ in (("v", "Vt"), ("b", "Bt"), ("k", "Kt")):
                    ps, pn = ph.newps()
                    mm1(ps, pn, bd[q][:, n, :], ident_bf[:], [R(q), ("ident_bf", 0, 1)])
                    S.op("act", lambda e, ps=ps, dst_=dst_, m=m: e.activation(out=m[dst_][:], in_=ps[:, :128], func=AF.Identity),
                         reads=[(pn, 0, 1)], writes=[(mn[dst_], 0, 1)])
                yield
                Hc, Hcn = H[cc][hcur[cc]], "H%d_%d" % (cc, hcur[cc])
                Hn, Hnn = H[cc][1 - hcur[cc]], "H%d_%d" % (cc, 1 - hcur[cc])
                Hbc, Hbcn = Hb[cc][hcur[cc]], "Hb%d_%d" % (cc, hcur[cc])
                Hbn, Hbnn = Hb[cc][1 - hcur[cc]], "Hb%d_%d" % (cc, 1 - hcur[cc])
                hcur[cc] = 1 - hcur[cc]
                ps, pn = ph.newps()
                mm1(ps, pn, ba, Hbc[:], [R("a"), (Hbcn, 0, 1)], start=True, stop=False)
                mm1(ps, pn, m["XakT"][:], m["Vt"][:], [(mn["XakT"], 0, 1), (mn["Vt"], 0, 1)], start=False, stop=True)
                S.op("act", lambda e, ps=ps, m=m: e.activation(out=m["W"][:], in_=ps[:, :128], func=AF.Identity), reads=[(pn, 0, 1)], writes=[(mn["W"], 0, 1)])
                yield
                ps, pn = ph.newps()
                mm1(ps, pn, cTT[:], m["W"][:], [(cTTn, 0, 1), (mn["W"], 0, 1)])
                S.op("dve", lambda e, ps=ps, m=m: e.tensor_copy(out=m["U"][:], in_=ps[:, :128]), reads=[(pn, 0, 1)], writes=[(mn["U"], 0, 1)])
                yield
                if full:
                    ps, pn = ph.newps()
                    mm1(ps, pn, Hbc[:], br, [(Hbcn, 0, 1), R("r")], start=True, stop=False)
                    mm1(ps, pn, m["U"][:], m["YrbT"][:], [(mn["U"], 0, 1), (mn["YrbT"], 0, 1)], start=False, stop=False)
                    mm1(ps, pn, m["Vt"][:], m["YrkT"][:], [(mn["Vt"], 0, 1), (mn["YrkT"], 0, 1)], start=False, stop=True)
                    for hh in range(2):
                        hp = slice(hh * 64, hh * 64 + 64)
                        S.op("act" if hh == 0 else "dve",
                             (lambda e, ps=ps, hp=hp, s=s, n=n: e.activation(out=s["y"][hp, n * 64:(n + 1) * 64], in_=ps[hp, hp], func=AF.Identity)) if hh == 0 else
                             (lambda e, ps=ps, hp=hp, s=s, n=n: e.tensor_copy(out=s["y"][hp, n * 64:(n + 1) * 64], in_=ps[hp, hp])),
                             reads=[(pn, 0, 1)], writes=[(sn["y"], 0, 1)])
                ps, pn = ph.newps()
                mm1(ps, pn, m["Bt"][:], m["U"][:], [(mn["Bt"], 0, 1), (mn["U"], 0, 1)], start=True, stop=False)
                mm1(ps, pn, m["Kt"][:], m["Vt"][:], [(mn["Kt"], 0, 1), (mn["Vt"], 0, 1)], start=False, stop=True)
                pc = s["P"][:, n * 64 + 63:n * 64 + 64]
                S.op("act", lambda e, Hn=Hn, Hc=Hc, pc=pc: e.activation(out=Hn[:], in_=Hc[:], func=AF.Identity, scale=pc),
                     reads=[(Hcn, 0, 1), (sn["P"], 0, 1)], writes=[(Hnn, 0, 1)])
                S.op("dve", lambda e, ps=ps, Hn=Hn, pc=pc: e.scalar_tensor_tensor(out=Hn[:], in0=ps[:, :128], scalar=pc, in1=Hn[:], op0=ALU.mult, op1=ALU.add),
                     reads=[(pn, 0, 1), (Hnn, 0, 1), (sn["P"], 0, 1)], writes=[(Hnn, 0, 1)])
                S.op("act", lambda e, Hn=Hn, Hbn=Hbn: e.activation(out=Hbn[:], in_=Hn[:], func=AF.Identity), reads=[(Hnn, 0, 1)], writes=[(Hbnn, 0, 1)])

        def post(cx):
            cc, s, sn, ew = cx["cc"], cx["s"], cx["sn"], cx["ew"]
            ps, pn = ph.newps()
            mm1(ps, pn, bones, s["y"][:], [ph.cr("bones", 0, 128), (sn["y"], 0, 1)], n=T)
            ew("dve", lambda e, s=s, ps=ps: e.scalar_tensor_tensor(out=s["yc"][:], in0=ps[:, :T], scalar=-1.0 / 64, in1=s["y"][:], op0=ALU.mult, op1=ALU.add),
               [(pn, 0, 1), "y"], ["yc"])
            ew("pool", lambda e, s=s: e.tensor_tensor(out=s["q1"][:], in0=s["yc"][:], in1=s["yc"][:], op=ALU.mult), ["yc"], ["q1"])
            ps, pn = ph.newps()
            mm1(ps, pn, bones, s["q1"][:], [ph.cr("bones", 0, 128), (sn["q1"], 0, 1)], n=T)
            rsqrt_ps(ps, pn, s["rn"][:], sn["rn"], 1.0 / 64, "epsgn")
            ew("dve", lambda e, s=s: e.tensor_tensor(out=s["yc"][:], in0=s["yc"][:], in1=s["rn"][:], op=ALU.mult), ["yc", "rn"], ["yc"])
            ew("act", lambda e, s=s, cc=cc: e.activation(out=s["yc"][:], in_=s["yc"][:], func=AF.Identity, scale=ph.c("gn_g", cc), bias=ph.c("gn_b", cc)),
               ["yc", ph.cr("gn_g", cc), ph.cr("gn_b", cc)], ["yc"])
            ew("pool", lambda e, s=s: e.tensor_tensor(out=s["yc"][:], in0=s["yc"][:], in1=s["bonus"][:], op=ALU.add), ["yc", "bonus"], ["yc"])
            ew("dve", lambda e, s=s, cc=cc: e.tensor_tensor(out=catT[:, 4 + cc, :], in0=s["yc"][:], in1=gate4[:, cc, :], op=ALU.mult),
               ["yc", ("gate4", cc, cc + 1)], [("catT", 4 + cc, 5 + cc)])
        def lockstep(gens):
            alive = True
            while alive:
                alive = False
                for g in gens:
                    try:
                        next(g)
                        alive = True
                    except StopIteration:
                        pass

        def chain(cx):
            for n in range(NCH):
                yield from unit(cx, n)

        cxs = [dict() for _ in range(4)]
        lockstep([prep(0, cxs[0]), prep(1, cxs[1])])
        lockstep([chain(cxs[0]), chain(cxs[1]), prep(2, cxs[2]), prep(3, cxs[3])])
        if full:
            post(cxs[0])
            post(cxs[1])
        lockstep([chain(cxs[2]), chain(cxs[3])])
        if full:
            post(cxs[2])
            post(cxs[3])
        if DBG['mix'] == 1:
            for k in range(8):
                S.op("dve", lambda e, k=k: e.tensor_copy(out=x[:, k, :], in_=catT[:, k, :]), reads=[("catT", k, k + 1)], writes=[("x", k, k + 1)])
        for oc in range(8 if (DBG['mix'] != 1 and full) else 0):
            pso, pon = ph.newps()
            for k in range(8):
                S.op("pe", lambda e, k=k, oc=oc, pso=pso: e.matmul(pso[:, :T], lhsT=wout[:, k, oc * 128:(oc + 1) * 128], rhs=catT[:, k, :],
                                                                   start=(k == 0), stop=(k == 7)),
                     reads=[("wout", k, k + 1), ("catT", k, k + 1)], writes=[(pon, 0, 1)])
            S.op("dve", lambda e, oc=oc, pso=pso: e.tensor_tensor(out=x[:, oc, :], in0=pso[:, :T], in1=x[:, oc, :], op=ALU.add),
                 reads=[(pon, 0, 1), ("x", oc, oc + 1)], writes=[("x", oc, oc + 1)])
        if full and mtile:
            for k in range(8):
                S.op("act", lambda e, k=k: e.activation(out=x[:, k, :], in_=x[:, k, :], func=AF.Identity, scale=ph.c("hmask")),
                     reads=[("x", k, k + 1), ph.cr("hmask")], writes=[("x", k, k + 1)])
        if full:
            S.dma("sp", dstv[:, :, (t - NPRE) * T:(t - NPRE + 1) * T], x[:], reads=[("x", 0, 8)])
    ph.close()


def build(TL, NC, phases=(0, 1, 2, 3), halo=False):
    nc = bass.Bass("TRN2", target_bir_lowering=False)
    dt = lambda n, s, d=F32, kind="ExternalInput": nc.dram_tensor(n, s, d, kind=kind).ap()
    TW, TH, TO = (8192, 2560, 2048) if halo else (TL, TL, TL)
    xT = dt("xT", [D, TW]); pos = dt("pos", [1, TH], I32); cst = dt("cst", [128, NC])
    w_in = dt("w_in", [D, 2816]); w_out = dt("w_out", [D, D]); w2a2 = dt("w2a2", [128, 512]); g2 = dt("g2", [128, 512])
    w_up = [dt("w_up%d" % l, [D, 2 * DFF]) for l in range(2)]
    w_dn = [dt("w_dn%d" % l, [DFF, D]) for l in range(2)]
    w_qkv = dt("w_qkv", [D, 1536]); w_o = dt("w_o", [D, D]); bvd = dt("bvd", [1, 512])
    nc_bv_dram[0] = bvd
    if len(phases) < 4:
        TO = TH
    yT = dt("yT", [D, TO], kind="ExternalOutput")
    scr = [dt("scr%d" % i, [D, TH], kind="Internal") for i in range(3)]
    coff = build.coff
    chain = [xT] + scr[:len(phases) - 1] + [yT]
    ci = 0
    for p in phases:
        s_, d_ = chain[ci], chain[ci + 1]
        ci += 1
        if p == 0:
            if halo:
                mixer_phase(nc, s_, d_, w_in, w_out, w2a2, g2, cst, coff, TW, NPRE=(TW - TH) // 256, masked=True)
            else:
                mixer_phase(nc, s_, d_, w_in, w_out, w2a2, g2, cst, coff, TL)
        elif p == 1:
            ffn_phase(nc, s_, d_, w_up[0], w_dn[0], cst, coff, 0, TH)
        elif p == 2:
            attn_phase(nc, s_, d_, pos, w_qkv, w_o, cst, coff, TH, masked=halo)
        else:
            ffn_phase(nc, s_, d_, w_up[1], w_dn[1], cst, coff, 1, TH, skip=(TH - TO) // 512)
    return nc


def host_inputs(inp, xb, posb, hmask=1.0, cache={}):
    key = id(inp)
    if key not in cache:
        P = make_consts(inp, 1.0)
        f = lambda a: np.ascontiguousarray(np.asarray(a, np.float32))
        bq = np.asarray(inp["attn_b_qkv"][0], np.float32)
        bvd = np.concatenate([np.concatenate([bq[1280 + h * 64:1344 + h * 64]] * 2) for h in range(4)])[None, :]
        shared = {"w_in": f(inp["ab_w_in"][0]), "w_out": f(inp["ab_w_out"][0]),
                  "w2a2": f(np.concatenate([inp["rwkv_w2"][0], inp["rwkv_a2"][0]], axis=0)), "g2": f(inp["rwkv_g2"][0]),
                  "w_up0": f(inp["ffn_w_up"][0]), "w_up1": f(inp["ffn_w_up"][1]), "w_dn0": f(inp["ffn_w_down"][0]), "w_dn1": f(inp["ffn_w_down"][1]),
                  "w_qkv": f(inp["attn_w_qkv"][0]), "w_o": f(inp["attn_w_o"][0]), "bvd": f(bvd)}
        cache.clear()
        cache[key] = (P.off, P.build(), shared)
    off, cst0, shared = cache[key]
    build.coff = off
    cst = cst0.copy()
    cst[:, off["hmask"]] = hmask
    m = dict(shared)
    m["xT"] = np.ascontiguousarray(np.asarray(xb, np.float32).T)
    m["pos"] = np.ascontiguousarray(np.asarray(posb, np.int32)[None, :])
    m["cst"] = cst
    return m, cst.shape[1]


def kernel(**inputs):
    x = np.asarray(inputs["x"], np.float32)
    pos = np.asarray(inputs["positions"])
    B, SEQ, _ = x.shape
    TW, TH, TO = 8192, 2560, 2048
    NQ = SEQ // TO
    maps = []
    for c in range(8):
        b, q = c // NQ, c % NQ
        end = (q + 1) * TO
        start = end - TW
        xw = np.zeros((TW, D), np.float32)
        xw[max(0, -start):] = x[b, max(0, start):end]
        hs = end - TH
        pw = np.zeros((TH,), np.int32)
        pw[max(0, -hs):] = pos[b, max(0, hs):end]
        m, NC = host_inputs(inputs, xw, pw, hmask=(0.0 if q == 0 else 1.0))
        maps.append(m)
    nc = build(SEQ, NC, halo=True)
    res = run_bass_kernel_spmd(nc, maps, core_ids=list(range(8)))
    out = np.zeros((B, SEQ, D), np.float32)
    for c in range(8):
        b, q = c // NQ, c % NQ
        out[b, q * TO:(q + 1) * TO] = res.results[c]["yT"].T
    return out
```

```python
import contextlib
import numpy as np
import concourse.bass as bass
import concourse.mybir as mybir
from concourse.bass_utils import run_bass_kernel_spmd

F32 = mybir.dt.float32
BF16 = mybir.dt.bfloat16
I32 = mybir.dt.int32
ALU = mybir.AluOpType
AF = mybir.ActivationFunctionType

ENGS = ["pe", "act", "dve", "pool", "sp"]
NDMASEM = 6
D = 1024
DFF = 2816
PI = float(np.pi)
DBG = {'att': 9, 'mix': 9, 'skip': ''}


class Sched:
    def __init__(self, nc, stack):
        self.nc = nc
        self.ops = []
        self.per_eng = {e: [] for e in ENGS}
        self.acc = {}
        self.dma_count = {e: 0 for e in ENGS}
        Sched.count = getattr(Sched, "count", 0) + 1
        sp_ = "q%d" % Sched.count
        self.sems = {e: stack.enter_context(nc.semaphore(sp_ + "s_" + e)) for e in ENGS if e != "sp"}
        self.dsems = {q: [stack.enter_context(nc.semaphore(sp_ + "d_%s%d" % (q, i))) for i in range(NDMASEM)]
                      for q in ("sp", "act", "pool")}

    def _deps(self, reads, writes, opid):
        deps = set()
        for (name, lo, hi) in reads:
            lst = self.acc.setdefault(name, [])
            for (l, h, k, o) in lst:
                if k == "w" and l < hi and lo < h:
                    deps.add(o)
            lst.append((lo, hi, "r", opid))
        for (name, lo, hi) in writes:
            lst = self.acc.setdefault(name, [])
            keep = []
            for (l, h, k, o) in lst:
                if l < hi and lo < h:
                    if o != opid:
                        deps.add(o)
                    if lo <= l and h <= hi:
                        continue
                keep.append((l, h, k, o))
            keep.append((lo, hi, "w", opid))
            self.acc[name] = keep
        return deps

    def op(self, eng, fn, reads=(), writes=()):
        opid = len(self.ops)
        deps = self._deps(reads, writes, opid)
        rec = dict(id=opid, eng=eng, fn=fn, deps=deps, dma=None, sig=False)
        self.ops.append(rec)
        self.per_eng[eng].append(rec)
        return opid

    def dma(self, q, out, in_, reads=(), writes=(), **kw):
        opid = len(self.ops)
        deps = self._deps(reads, writes, opid)
        n = self.dma_count[q]
        self.dma_count[q] += 1
        rec = dict(id=opid, eng=q, fn=None, deps=deps, dma=(q, n, out, in_, kw), sig=True)
        self.ops.append(rec)
        self.per_eng[q].append(rec)
        return opid

    def _finalize(self):
        for rec in self.ops:
            for d in rec["deps"]:
                p = self.ops[d]
                if p["dma"] is None and not (p["eng"] == "pe" and rec["eng"] == "pe" and rec["dma"] is None):
                    p["sig"] = True
        cnt = {e: 0 for e in ENGS}
        for rec in self.ops:
            if rec["dma"] is not None:
                q, n, _, _, _ = rec["dma"]
                rec["sem"] = self.dsems[q][n % NDMASEM]
                rec["val"] = 16 * (n // NDMASEM + 1)
            elif rec["sig"]:
                cnt[rec["eng"]] += 1
                rec["sem"] = self.sems[rec["eng"]]
                rec["val"] = cnt[rec["eng"]]

    def _emit_engine(self, ename, eng):
        seen = {}

        def wait(sem, val):
            key = id(sem)
            if seen.get(key, 0) >= val:
                return
            seen[key] = val
            eng.wait_ge(sem, val)

        for rec in self.per_eng[ename]:
            for d in sorted(rec["deps"]):
                p = self.ops[d]
                if p["dma"] is None and p["eng"] == "pe" and ename == "pe" and rec["dma"] is None:
                    continue
                wait(p["sem"], p["val"])
            if rec["dma"] is not None:
                q, n, out, in_, kw = rec["dma"]
                if n >= NDMASEM:
                    wait(rec["sem"], rec["val"] - 16)
                eng.dma_start(out=out, in_=in_, **kw).then_inc(rec["sem"], 16)
            else:
                ins = rec["fn"](eng)
                if rec["sig"]:
                    ins.then_inc(rec["sem"], 1)
        return wait

    def emit(self):
        self._finalize()
        with self.nc.Block() as block:
            @block.tensor
            def _(e):
                self._emit_engine("pe", e)

            @block.scalar
            def _(e):
                self._emit_engine("act", e)

            @block.vector
            def _(e):
                self._emit_engine("dve", e)

            @block.gpsimd
            def _(e):
                w = self._emit_engine("pool", e)
                n = self.dma_count["pool"]
                for i in range(NDMASEM):
                    k = len(range(i, n, NDMASEM))
                    if k:
                        w(self.dsems["pool"][i], 16 * k)

            @block.sync
            def _(e):
                w = self._emit_engine("sp", e)
                n = self.dma_count["sp"]
                for i in range(NDMASEM):
                    k = len(range(i, n, NDMASEM))
                    if k:
                        w(self.dsems["sp"][i], 16 * k)


def colvec(v):
    v = np.asarray(v, np.float32).reshape(-1, 128)
    return np.ascontiguousarray(v.T)


class Pack:
    def __init__(self):
        self.parts = []
        self.off = {}
        self.n = 0

    def add(self, name, arr):
        arr = np.asarray(arr, np.float32)
        if arr.ndim == 1:
            arr = arr[:, None]
        assert arr.shape[0] == 128, (name, arr.shape)
        arr = arr.reshape(128, -1)
        self.off[name] = self.n
        self.parts.append(arr)
        self.n += arr.shape[1]

    def build(self):
        return np.ascontiguousarray(np.concatenate(self.parts, axis=1))


def bd_mask(fn):
    m = np.zeros((128, 128), np.float32)
    i = np.arange(64)
    blk = fn(i[:, None], i[None, :]).astype(np.float32)
    m[:64, :64] = blk
    m[64:, 64:] = blk
    return m


def make_consts(inp, hmask=1.0):
    P = Pack()
    g = lambda n, l=0: np.asarray(inp[n][l], np.float32)
    P.add("eps6", np.full(128, 1e-6)); P.add("eps5", np.full(128, 1e-5)); P.add("epsgn", np.full(128, 64e-5))
    P.add("halfpi", np.full(128, PI / 2)); P.add("zero", np.zeros(128)); P.add("hmask", np.full(128, float(hmask)))
    P.add("ab_g", colvec(g("ab_norm_g")))
    P.add("cin_b", colvec(g("conv_in_b")))
    P.add("dw_w", np.stack([colvec(g("conv_dw_w")[j]) for j in range(31)], axis=2))
    P.add("dw_b", colvec(g("conv_dw_b"))); P.add("ln_g", colvec(g("conv_ln_g"))); P.add("ln_b", colvec(g("conv_ln_b")))
    P.add("mu", colvec(g("rwkv_mu"))); P.add("omm", colvec(1.0 - 0.0 * g("rwkv_mu")) * 0 + 0)
    P.add("w0", colvec(g("rwkv_w0"))); P.add("a0", colvec(g("rwkv_a0")))
    P.add("k_k", colvec(g("rwkv_k_k"))); P.add("k_a", colvec(g("rwkv_k_a")))
    P.add("r_k", colvec(g("rwkv_r_k").reshape(-1)))
    P.add("gn_g", colvec(g("rwkv_ln_g"))); P.add("gn_b", colvec(g("rwkv_ln_b")))
    for l in range(2):
        P.add("ffn_g%d" % l, colvec(g("ffn_norm_g", l)))
        P.add("fc_w%d" % l, np.stack([colvec(g("ffn_conv_w", l)[j]) for j in range(3)], axis=2))
        P.add("fc_b%d" % l, colvec(g("ffn_conv_b", l)))
    P.add("at_g", colvec(g("attn_norm_g")))
    bq = g("attn_b_qkv")
    P.add("b_q", colvec(bq[:1024]))
    P.add("b_kd", np.stack([np.concatenate([bq[1024 + h * 64:1088 + h * 64]] * 2) for h in range(4)], axis=1))
    P.add("qn_g", np.concatenate([g("attn_q_norm_g")] * 2)); P.add("kn_g", np.concatenate([g("attn_k_norm_g")] * 2))
    P.add("b_o", colvec(g("attn_b_o")))
    half = 8
    invf = (500000.0 ** (-(np.arange(half, dtype=np.float32) * 2.0) / 16)).astype(np.float32)
    iv = np.zeros(64, np.float32); iv[:8] = invf; iv[8:16] = invf
    P.add("invf", np.concatenate([iv, iv]))
    P.add("ident", np.eye(128, dtype=np.float32))
    P.add("bones", bd_mask(lambda a, b: a * 0 + b * 0 + 1))
    P.add("ones", np.ones((128, 128), np.float32))
    P.add("m_su", bd_mask(lambda s, t: s < t)); P.add("m_iu", bd_mask(lambda s, t: s <= t)); P.add("m_sl", bd_mask(lambda t, s: s < t))
    rm = np.zeros((64, 64), np.float32)
    for m in range(8):
        rm[m + 8, m] = -1.0
        rm[m, m + 8] = 1.0
    R2 = np.zeros((128, 128), np.float32); R2[:64, :64] = rm; R2[64:, 64:] = rm
    P.add("rmT", R2)
    kq = np.arange(128)
    mP = (kq[:, None] > kq[None, :]).astype(np.float32)
    mC = (kq[:, None] <= kq[None, :]).astype(np.float32)
    P.add("mP", np.tile(mP, (1, 4))); P.add("mC", np.tile(mC, (1, 4)))
    sk = g("attn_sinks")
    se = np.zeros((4, 2, 2, 128), np.float32)
    for hk in range(4):
        for par in range(2):
            for pair in range(2):
                se[hk, par, pair, :] = sk[4 * hk + 2 * pair + par]
    P.add("sinks", np.broadcast_to(se.reshape(1, -1), (128, 2048)))
    return P


class Phase:
    count = 0

    def __init__(self, nc, cdram, coff, need):
        self.nc = nc
        Phase.count += 1
        self.pfx = "p%d_" % Phase.count
        self.st = contextlib.ExitStack()
        self.S = Sched(nc, self.st)
        self.ps = [self.st.enter_context(nc.psum_tensor(self.pfx + "ps%d" % i, [128, 512], F32)) for i in range(8)]
        self.pi = 0
        self.coff = coff
        self.cmap = {}
        n = 0
        for name, w in need:
            self.cmap[name] = n
            n += w
        self.C = self.sb("C", [128, n], F32)
        for name, w in need:
            o = self.cmap[name]
            self.S.dma("sp", self.C[:, o:o + w], cdram[:, coff[name]:coff[name] + w], writes=[("C", o, o + w)], allow_slow_non_contiguous=True)

    def sb(self, name, shape, dt):
        return self.st.enter_context(self.nc.sbuf_tensor(self.pfx + name, shape, dt))

    def c(self, name, lo=0, w=1):
        o = self.cmap[name] + lo
        return self.C[:, o:o + w]

    def cr(self, name, lo=0, w=1):
        o = self.cmap[name] + lo
        return ("C", o, o + w)

    def newps(self):
        i = self.pi
        self.pi = (self.pi + 1) % 8
        return self.ps[i], "ps%d" % i

    def close(self):
        self.S.emit()
        self.st.close()


def rmsnorm(ph, x, xname, hT, hname, sq, sqname, rstd, ones_bf, gname, T):
    S = ph.S
    for k in range(8):
        S.op("act", lambda e, k=k: e.activation(out=sq[:, k, :T], in_=x[:, k, :T], func=AF.Square),
             reads=[(xname, k, k + 1)], writes=[(sqname, k, k + 1)])
    ps, pn = ph.newps()
    for k in range(8):
        S.op("pe", lambda e, k=k: e.matmul(ps[:, :T], lhsT=ones_bf[:], rhs=sq[:, k, :T], start=(k == 0), stop=(k == 7)),
             reads=[("ones_bf", 0, 1), (sqname, k, k + 1)], writes=[(pn, 0, 1)])
    S.op("act", lambda e: e.activation(out=rstd[:, :T], in_=ps[:, :T], func=AF.Ln, scale=1.0 / D, bias=ph.c("eps6")),
         reads=[(pn, 0, 1), ph.cr("eps6")], writes=[("rstd", 0, 1)])
    S.op("act", lambda e: e.activation(out=rstd[:, :T], in_=rstd[:, :T], func=AF.Exp, scale=-0.5),
         reads=[("rstd", 0, 1)], writes=[("rstd", 0, 1)])
    for k in range(8):
        S.op("dve",
             lambda e, k=k: e.scalar_tensor_tensor(out=hT[:, k, :T], in0=x[:, k, :T], scalar=ph.c(gname, k), in1=rstd[:, :T],
                                                   op0=ALU.mult, op1=ALU.mult),
             reads=[(xname, k, k + 1), ("rstd", 0, 1), ph.cr(gname, k)], writes=[(hname, k, k + 1)])


def load_w(S, q, dst, dname, src, nchunk):
    for k in range(nchunk):
        S.dma(q, dst[:, k, :], src[k * 128:(k + 1) * 128, :], writes=[(dname, k, k + 1)])


def ffn_phase(nc, src, dst, w_up, w_dn, cdram, coff, L, TL, skip=0):
    T = 512
    NT = TL // T
    need = [("eps6", 1), ("ffn_g%d" % L, 8), ("fc_w%d" % L, 66), ("fc_b%d" % L, 22), ("ones", 128)]
    ph = Phase(nc, cdram, coff, need)
    S = ph.S
    wup = ph.sb("wup", [128, 8, 2 * DFF], BF16)
    wdn = ph.sb("wdn", [128, 22, D], BF16)
    x = ph.sb("x", [128, 8, T], F32)
    hT = ph.sb("hT", [128, 8, T], BF16)
    act = ph.sb("act", [128, 22, T], BF16)
    rstd = ph.sb("rstd", [128, T], F32)
    ones_bf = ph.sb("ones_bf", [128, 128], BF16)
    G = [ph.sb("G%d" % i, [128, T + 2], F32) for i in range(2)]
    acc = [ph.sb("acc%d" % i, [128, T], F32) for i in range(2)]
    sg = [ph.sb("sg%d" % i, [128, T], F32) for i in range(2)]
    carry = ph.sb("carry", [128, 22, 2], F32)
    load_w(S, "pool", wup, "wup", w_up, 8)
    load_w(S, "pool", wdn, "wdn", w_dn, 22)
    S.op("dve", lambda e: e.tensor_copy(out=ones_bf[:], in_=ph.c("ones", 0, 128)), reads=[ph.cr("ones", 0, 128)], writes=[("ones_bf", 0, 1)])
    S.op("pool", lambda e: e.memset(carry[:], 0.0), writes=[("carry", 0, 22)])
    fw, fb, gname = "fc_w%d" % L, "fc_b%d" % L, "ffn_g%d" % L
    srcv = src.rearrange("(c p) t -> p c t", p=128)
    dstv = dst.rearrange("(c p) t -> p c t", p=128)
    for t in range(NT):
        S.dma("sp", x[:], srcv[:, :, t * T:(t + 1) * T], writes=[("x", 0, 8)])
        rmsnorm(ph, x, "x", hT, "hT", act, "act", rstd, ones_bf, gname, T)
        for c in range(22):
            i = c % 2
            Gt, at, st_ = G[i], acc[i], sg[i]
            gn, an, sn = "G%d" % i, "acc%d" % i, "sg%d" % i
            psg, pgn = ph.newps()
            psu, pun = ph.newps()
            for k in range(8):
                S.op("pe", lambda e, k=k, c=c, psg=psg: e.matmul(psg[:, :T], lhsT=wup[:, k, c * 128:(c + 1) * 128], rhs=hT[:, k, :],
                                                                 start=(k == 0), stop=(k == 7)),
                     reads=[("wup", k, k + 1), ("hT", k, k + 1)], writes=[(pgn, 0, 1)])
            for k in range(8):
                S.op("pe", lambda e, k=k, c=c, psu=psu: e.matmul(psu[:, :T], lhsT=wup[:, k, DFF + c * 128:DFF + (c + 1) * 128], rhs=hT[:, k, :],
                                                                 start=(k == 0), stop=(k == 7)),
                     reads=[("wup", k, k + 1), ("hT", k, k + 1)], writes=[(pun, 0, 1)])
            S.op("pool", lambda e, c=c, Gt=Gt: e.tensor_copy(out=Gt[:, 0:2], in_=carry[:, c, :]),
                 reads=[("carry", c, c + 1)], writes=[(gn, 0, 2)])
            S.op("act", lambda e, Gt=Gt, psg=psg: e.activation(out=Gt[:, 2:T + 2], in_=psg[:, :T], func=AF.Identity),
                 reads=[(pgn, 0, 1)], writes=[(gn, 2, T + 2)])
            S.op("pool", lambda e, c=c, Gt=Gt: e.tensor_copy(out=carry[:, c, :], in_=Gt[:, T:T + 2]),
                 reads=[(gn, T, T + 2)], writes=[("carry", c, c + 1)])
            S.op("dve", lambda e, c=c, Gt=Gt, at=at: e.tensor_scalar(out=at[:], in0=Gt[:, 2:T + 2], scalar1=ph.c(fw, c * 3 + 2), scalar2=ph.c(fb, c),
                                                                    op0=ALU.mult, op1=ALU.add),
                 reads=[(gn, 2, T + 2), ph.cr(fw, c * 3 + 2), ph.cr(fb, c)], writes=[(an, 0, 1)])
            S.op("dve", lambda e, c=c, Gt=Gt, at=at: e.scalar_tensor_tensor(out=at[:], in0=Gt[:, 1:T + 1], scalar=ph.c(fw, c * 3 + 1), in1=at[:],
                                                                            op0=ALU.mult, op1=ALU.add),
                 reads=[(gn, 1, T + 1), (an, 0, 1), ph.cr(fw, c * 3 + 1)], writes=[(an, 0, 1)])
            S.op("dve", lambda e, c=c, Gt=Gt, at=at: e.scalar_tensor_tensor(out=at[:], in0=Gt[:, 0:T], scalar=ph.c(fw, c * 3), in1=at[:],
                                                                           op0=ALU.mult, op1=ALU.add),
                 reads=[(gn, 0, T), (an, 0, 1), ph.cr(fw, c * 3)], writes=[(an, 0, 1)])
            S.op("act", lambda e, at=at, st_=st_: e.activation(out=st_[:], in_=at[:], func=AF.Silu),
                 reads=[(an, 0, 1)], writes=[(sn, 0, 1)])
            S.op("dve", lambda e, c=c, st_=st_, psu=psu: e.tensor_tensor(out=act[:, c, :], in0=psu[:, :T], in1=st_[:], op=ALU.mult),
                 reads=[(pun, 0, 1), (sn, 0, 1)], writes=[("act", c, c + 1)])
        for oc in range(8):
            pso, pon = ph.newps()
            for c in range(22):
                S.op("pe", lambda e, c=c, oc=oc, pso=pso: e.matmul(pso[:, :T], lhsT=wdn[:, c, oc * 128:(oc + 1) * 128], rhs=act[:, c, :],
                                                                   start=(c == 0), stop=(c == 21)),
                     reads=[("wdn", c, c + 1), ("act", c, c + 1)], writes=[(pon, 0, 1)])
            S.op("dve", lambda e, oc=oc, pso=pso: e.tensor_tensor(out=x[:, oc, :], in0=pso[:, :T], in1=x[:, oc, :], op=ALU.add),
                 reads=[(pon, 0, 1), ("x", oc, oc + 1)], writes=[("x", oc, oc + 1)])
        if t >= skip:
            S.dma("sp", dstv[:, :, (t - skip) * T:(t - skip + 1) * T], x[:], reads=[("x", 0, 8)])
    ph.close()


def attn_phase(nc, src, dst, pos, w_qkv, w_o, cdram, coff, TL, masked=False):
    T = 512
    NT = TL // T
    need = [("eps6", 1), ("halfpi", 1), ("zero", 1), ("hmask", 1), ("at_g", 8), ("b_q", 8), ("b_kd", 4), ("qn_g", 1), ("kn_g", 1), ("b_o", 8),
            ("invf", 1), ("bones", 128), ("ones", 128), ("rmT", 128), ("mP", 512), ("mC", 512), ("sinks", 2048)]
    ph = Phase(nc, cdram, coff, need)
    S = ph.S
    wq = ph.sb("wq", [128, 8, 1024], BF16)
    wkd = ph.sb("wkd", [128, 8, 4, 128], BF16)
    wvd = ph.sb("wvd", [128, 8, 4, 128], BF16)
    wo = ph.sb("wo", [128, 8, D], BF16)
    x = ph.sb("x", [128, 8, T], F32)
    hT = ph.sb("hT", [128, 8, T], BF16)
    sq = ph.sb("sq", [128, 8, T], BF16)
    rstd = ph.sb("rstd", [128, T], F32)
    ones_bf = ph.sb("ones_bf", [128, 128], BF16)
    mPb = ph.sb("mPb", [128, 512], BF16)
    mCb = ph.sb("mCb", [128, 512], BF16)
    esink = ph.sb("esink", [128, 2048], F32)
    COS = ph.sb("COS", [128, T], F32)
    SIN = ph.sb("SIN", [128, T], F32)
    posi = ph.sb("posi", [128, T], I32)
    ang = ph.sb("ang", [128, T], F32)
    nf = ph.sb("nf", [128, T], F32)
    QT = ph.sb("QT", [128, 8, T], BF16)
    KT = ph.sb("KT", [128, 4, 128 + T], BF16)
    VB = ph.sb("VB", [128, 5, 512], BF16)
    OT = ph.sb("OT", [128, 8, T], BF16)
    bv = ph.sb("bv", [1, 512], BF16)
    qraw = [ph.sb("qraw%d" % i, [128, T], F32) for i in range(2)]
    qsq = [ph.sb("qsq%d" % i, [128, T], F32) for i in range(2)]
    qr = [ph.sb("qr%d" % i, [128, T], F32) for i in range(2)]
    qn = [ph.sb("qn%d" % i, [128, T], F32) for i in range(2)]
    t1 = [ph.sb("t1%d" % i, [128, T], F32) for i in range(2)]
    t2 = [ph.sb("t2%d" % i, [128, T], F32) for i in range(2)]
    E = [ph.sb("E%d" % i, [128, 512], BF16) for i in range(4)]
    den = [ph.sb("den%d" % i, [128, 512], F32) for i in range(2)]
    load_w(S, "pool", wq, "wq", w_qkv[:, 0:1024], 8)
    load_w(S, "pool", wo, "wo", w_o, 8)
    wkv = ph.sb("wkv", [128, 8, 512], BF16)
    load_w(S, "pool", wkv, "wkv", w_qkv[:, 1024:1536], 8)
    for hk in range(4):
        for cp in range(2):
            S.op("pool", lambda e, hk=hk, cp=cp: e.tensor_copy(out=wkd[:, :, hk, cp * 64:(cp + 1) * 64], in_=wkv[:, :, hk * 64:(hk + 1) * 64]),
                 reads=[("wkv", 0, 8)], writes=[("wkd", hk * 2 + cp, hk * 2 + cp + 1)])
            S.op("dve", lambda e, hk=hk, cp=cp: e.tensor_copy(out=wvd[:, :, hk, cp * 64:(cp + 1) * 64], in_=wkv[:, :, 256 + hk * 64:256 + (hk + 1) * 64]),
                 reads=[("wkv", 0, 8)], writes=[("wvd", hk * 2 + cp, hk * 2 + cp + 1)])
    if 'bv' not in DBG['skip']:
        S.dma("pool", bv[:], nc_bv_dram[0], writes=[("bv", 0, 1)])
    S.op("dve", lambda e: e.tensor_copy(out=ones_bf[:], in_=ph.c("ones", 0, 128)), reads=[ph.cr("ones", 0, 128)], writes=[("ones_bf", 0, 1)])
    S.op("dve", lambda e: e.tensor_copy(out=mPb[:], in_=ph.c("mP", 0, 512)), reads=[ph.cr("mP", 0, 512)], writes=[("mPb", 0, 1)])
    S.op("dve", lambda e: e.tensor_copy(out=mCb[:], in_=ph.c("mC", 0, 512)), reads=[ph.cr("mC", 0, 512)], writes=[("mCb", 0, 1)])
    if 'esink' not in DBG['skip']:
      S.op("act", lambda e: e.activation(out=esink[:], in_=ph.c("sinks", 0, 2048), func=AF.Exp), reads=[ph.cr("sinks", 0, 2048)], writes=[("esink", 0, 1)])
    srcv = src.rearrange("(c p) t -> p c t", p=128)
    dstv = dst.rearrange("(c p) t -> p c t", p=128)
    ei = 0
    for t in range(NT):
        tc0 = t * T
        S.dma("sp", x[:], srcv[:, :, tc0:tc0 + T], writes=[("x", 0, 8)])
        rmsnorm(ph, x, "x", hT, "hT", sq, "sq", rstd, ones_bf, "at_g", T)
        S.dma("sp", posi[:], pos[0:1, tc0:tc0 + T].to_broadcast([128, T]), writes=[("posi", 0, 1)])
        S.op("dve", lambda e: e.tensor_copy(out=ang[:], in_=posi[:]), reads=[("posi", 0, 1)], writes=[("ang", 0, 1)])
        S.op("dve", lambda e: e.tensor_scalar(out=ang[:], in0=ang[:], scalar1=ph.c("invf"), scalar2=None, op0=ALU.mult),
             reads=[("ang", 0, 1), ph.cr("invf")], writes=[("ang", 0, 1)])
        for which, tab, tn in ((0, SIN, "SIN"), (1, COS, "COS")):
            if which == 1:
                S.op("dve", lambda e: e.tensor_scalar(out=ang[:], in0=ang[:], scalar1=PI / 2, scalar2=None, op0=ALU.add),
                     reads=[("ang", 0, 1)], writes=[("ang", 0, 1)])
            S.op("dve", lambda e: e.tensor_scalar(out=posi[:], in0=ang[:], scalar1=float(1.0 / (2 * PI)), scalar2=None, op0=ALU.mult),
                 reads=[("ang", 0, 1)], writes=[("posi", 0, 1)])
            S.op("dve", lambda e: e.tensor_copy(out=nf[:], in_=posi[:]), reads=[("posi", 0, 1)], writes=[("nf", 0, 1)])
            S.op("dve", lambda e: e.scalar_tensor_tensor(out=nf[:], in0=nf[:], scalar=float(-2 * PI), in1=ang[:], op0=ALU.mult, op1=ALU.add),
                 reads=[("nf", 0, 1), ("ang", 0, 1)], writes=[("nf", 0, 1)])
            S.op("dve", lambda e: e.tensor_scalar(out=nf[:], in0=nf[:], scalar1=PI, scalar2=-PI, op0=ALU.min, op1=ALU.max),
                 reads=[("nf", 0, 1)], writes=[("nf", 0, 1)])
            S.op("act", lambda e, tab=tab: e.activation(out=tab[:], in_=nf[:], func=AF.Sin), reads=[("nf", 0, 1)], writes=[(tn, 0, 1)])
        if t > 0:
            S.op("pool", lambda e: e.tensor_copy(out=KT[:, :, 0:128], in_=KT[:, :, T:T + 128]), reads=[("KT", 4, 5)], writes=[("KT", 0, 1)])
            S.op("pool", lambda e: e.tensor_copy(out=VB[:, 0, :], in_=VB[:, 4, :]), reads=[("VB", 4, 5)], writes=[("VB", 0, 1)])
        for b in range(4 if DBG['att'] >= 1 else 0):
            ps, pn = ph.newps()
            for k in range(8):
                S.op("pe", lambda e, k=k, b=b, ps=ps: e.matmul(ps[:, :], lhsT=hT[:, k, b * 128:(b + 1) * 128], rhs=wvd[:, k, :, :],
                                                               start=(k == 0), stop=False),
                     reads=[("hT", k, k + 1), ("wvd", 0, 8)], writes=[(pn, 0, 1)])
            S.op("pe", lambda e, ps=ps: e.matmul(ps[:, :], lhsT=ones_bf[0:1, :], rhs=bv[0:1, :], start=False, stop=True),
                 reads=[("ones_bf", 0, 1), ("bv", 0, 1)], writes=[(pn, 0, 1)])
            S.op("act", lambda e, b=b, ps=ps: e.activation(out=VB[:, 1 + b, :], in_=ps[:, :], func=AF.Identity),
                 reads=[(pn, 0, 1)], writes=[("VB", 1 + b, 2 + b)])
        for j in range(12 if DBG['att'] >= 2 else 0):
            i = j % 2
            isq = j < 8
            ps, pn = ph.newps()
            for k in range(8):
                if isq:
                    S.op("pe", lambda e, k=k, j=j, ps=ps: e.matmul(ps[:, :T], lhsT=wq[:, k, j * 128:(j + 1) * 128], rhs=hT[:, k, :],
                                                                   start=(k == 0), stop=(k == 7)),
                         reads=[("wq", k, k + 1), ("hT", k, k + 1)], writes=[(pn, 0, 1)])
                else:
                    S.op("pe", lambda e, k=k, j=j, ps=ps: e.matmul(ps[:, :T], lhsT=wkd[:, k, j - 8, :], rhs=hT[:, k, :],
                                                                   start=(k == 0), stop=(k == 7)),
                         reads=[("wkd", 0, 8), ("hT", k, k + 1)], writes=[(pn, 0, 1)])
            bias = ph.c("b_q", j) if isq else ph.c("b_kd", j - 8)
            bres = ph.cr("b_q", j) if isq else ph.cr("b_kd", j - 8)
            gcol, gres = (ph.c("qn_g"), ph.cr("qn_g")) if isq else (ph.c("kn_g"), ph.cr("kn_g"))
            qa, qs_, qrr, qnn, ta, tb = qraw[i], qsq[i], qr[i], qn[i], t1[i], t2[i]
            S.op("act", lambda e, ps=ps, qa=qa, bias=bias: e.activation(out=qa[:], in_=ps[:, :T], func=AF.Identity, bias=bias),
                 reads=[(pn, 0, 1), bres], writes=[("qraw%d" % i, 0, 1)])
            S.op("pool", lambda e, qa=qa, qs_=qs_: e.tensor_tensor(out=qs_[:], in0=qa[:], in1=qa[:], op=ALU.mult),
                 reads=[("qraw%d" % i, 0, 1)], writes=[("qsq%d" % i, 0, 1)])
            ps2, pn2 = ph.newps()
            S.op("pe", lambda e, ps2=ps2, qs_=qs_: e.matmul(ps2[:, :T], lhsT=ph.c("bones", 0, 128), rhs=qs_[:], start=True, stop=True),
                 reads=[ph.cr("bones", 0, 128), ("qsq%d" % i, 0, 1)], writes=[(pn2, 0, 1)])
            S.op("act", lambda e, ps2=ps2, qrr=qrr: e.activation(out=qrr[:], in_=ps2[:, :T], func=AF.Ln, scale=1.0 / 64, bias=ph.c("eps6")),
                 reads=[(pn2, 0, 1), ph.cr("eps6")], writes=[("qr%d" % i, 0, 1)])
            S.op("act", lambda e, qrr=qrr: e.activation(out=qrr[:], in_=qrr[:], func=AF.Exp, scale=-0.5),
                 reads=[("qr%d" % i, 0, 1)], writes=[("qr%d" % i, 0, 1)])
            S.op("dve", lambda e, qa=qa, qrr=qrr, qnn=qnn, gcol=gcol: e.scalar_tensor_tensor(out=qnn[:], in0=qa[:], scalar=gcol, in1=qrr[:],
                                                                                         op0=ALU.mult, op1=ALU.mult),
                 reads=[("qraw%d" % i, 0, 1), ("qr%d" % i, 0, 1), gres], writes=[("qn%d" % i, 0, 1)])
            ps3, pn3 = ph.newps()
            S.op("pe", lambda e, ps3=ps3, qnn=qnn: e.matmul(ps3[:, :T], lhsT=ph.c("rmT", 0, 128), rhs=qnn[:], start=True, stop=True),
                 reads=[ph.cr("rmT", 0, 128), ("qn%d" % i, 0, 1)], writes=[(pn3, 0, 1)])
            S.op("pool", lambda e, qnn=qnn, ta=ta: e.tensor_tensor(out=ta[:], in0=qnn[:], in1=COS[:, :], op=ALU.mult),
                 reads=[("qn%d" % i, 0, 1), ("COS", 0, 1)], writes=[("t1%d" % i, 0, 1)])
            S.op("dve", lambda e, ps3=ps3, tb=tb: e.tensor_tensor(out=tb[:], in0=ps3[:, :T], in1=SIN[:, :], op=ALU.mult),
                 reads=[(pn3, 0, 1), ("SIN", 0, 1)], writes=[("t2%d" % i, 0, 1)])
            if isq:
                S.op("pool", lambda e, ta=ta, tb=tb, j=j: e.tensor_tensor(out=QT[:, j, :], in0=ta[:], in1=tb[:], op=ALU.add),
                     reads=[("t1%d" % i, 0, 1), ("t2%d" % i, 0, 1)], writes=[("QT", j, j + 1)])
            else:
                S.op("pool", lambda e, ta=ta, tb=tb, j=j: e.tensor_tensor(out=KT[:, j - 8, 128:128 + T], in0=ta[:], in1=tb[:], op=ALU.add),
                     reads=[("t1%d" % i, 0, 1), ("t2%d" % i, 0, 1)], writes=[("KT", 1, 5)])
        for qb in range(4 if DBG['att'] >= 3 else 0):
            first = (t == 0 and qb == 0)
            for hk in range(4):
                kbs = [1] if first else [0, 1]
                Es = []
                for kb in kbs:
                    kc0 = (qb + kb) * 128
                    Et = E[ei % 4]
                    en = "E%d" % (ei % 4)
                    ei += 1
                    for par in range(2):
                        pss, psn = ph.newps()
                        hp = slice(par * 64, par * 64 + 64)
                        for pair in range(2):
                            col = pair * 128
                            S.op("pe", lambda e, pss=pss, hp=hp, col=col, kc0=kc0, hk=hk, pair=pair, qb=qb:
                                 e.matmul(pss[:, col:col + 128], lhsT=KT[hp, hk, kc0:kc0 + 128], rhs=QT[hp, 2 * hk + pair, qb * 128:(qb + 1) * 128],
                                          start=True, stop=True),
                                 reads=[("KT", 0, 5), ("QT", 2 * hk + pair, 2 * hk + pair + 1)], writes=[(psn, 0, 1)])
                        S.op("act", lambda e, Et=Et, pss=pss, par=par: e.activation(out=Et[:, par * 256:(par + 1) * 256], in_=pss[:, 0:256], func=AF.Exp, scale=0.125),
                             reads=[(psn, 0, 1)], writes=[(en, par, par + 1)])
                    mk, mkn = (mPb, "mPb") if kb == 0 else (mCb, "mCb")
                    S.op("pool" if kb == 0 else "dve", lambda e, Et=Et, mk=mk: e.tensor_tensor(out=Et[:], in0=Et[:], in1=mk[:], op=ALU.mult),
                         reads=[(en, 0, 2), (mkn, 0, 1)], writes=[(en, 0, 2)])
                    if masked and t == 1 and qb == 0 and kb == 0:
                        S.op("act", lambda e, Et=Et: e.activation(out=Et[:], in_=Et[:], func=AF.Identity, scale=ph.c("hmask")),
                             reads=[(en, 0, 2), ph.cr("hmask")], writes=[(en, 0, 2)])
                    Es.append((Et, en, kb))
                psd, pdn = ph.newps()
                pso, pon = ph.newps()
                for n_, (Et, en, kb) in enumerate(Es):
                    S.op("pe", lambda e, psd=psd, Et=Et, n_=n_: e.matmul(psd[:, :], lhsT=ones_bf[:], rhs=Et[:], start=(n_ == 0), stop=(n_ == len(Es) - 1)),
                         reads=[("ones_bf", 0, 1), (en, 0, 2)], writes=[(pdn, 0, 1)])
                for n_, (Et, en, kb) in enumerate(Es):
                    vb = qb + kb
                    S.op("pe", lambda e, pso=pso, Et=Et, n_=n_, vb=vb, hk=hk: e.matmul(pso[:, :], lhsT=VB[:, vb, hk * 128:(hk + 1) * 128], rhs=Et[:],
                                                                                     start=(n_ == 0), stop=(n_ == len(Es) - 1)),
                         reads=[("VB", vb, vb + 1), (en, 0, 2)], writes=[(pon, 0, 1)])
                dn = den[hk % 2]
                dnn = "den%d" % (hk % 2)
                S.op("dve", lambda e, dn=dn, psd=psd, hk=hk: e.tensor_tensor(out=dn[:], in0=psd[:, :], in1=esink[:, hk * 512:(hk + 1) * 512], op=ALU.add),
                     reads=[(pdn, 0, 1), ("esink", 0, 1)], writes=[(dnn, 0, 1)])
                S.op("act", lambda e, dn=dn: e.activation(out=dn[:], in_=dn[:], func=AF.Ln), reads=[(dnn, 0, 1)], writes=[(dnn, 0, 1)])
                S.op("act", lambda e, dn=dn: e.activation(out=dn[:], in_=dn[:], func=AF.Exp, scale=-1.0), reads=[(dnn, 0, 1)], writes=[(dnn, 0, 1)])
                for par in range(2):
                    hp = slice(par * 64, par * 64 + 64)
                    S.op("dve", lambda e, dn=dn, pso=pso, hp=hp, par=par, hk=hk, qb=qb:
                         e.tensor_tensor(out=OT[hp, 2 * hk:2 * hk + 2, qb * 128:(qb + 1) * 128],
                                         in0=pso[hp, par * 256:(par + 1) * 256].rearrange("p (a q) -> p a q", a=2),
                                         in1=dn[hp, par * 256:(par + 1) * 256].rearrange("p (a q) -> p a q", a=2), op=ALU.mult),
                         reads=[(pon, 0, 1), (dnn, 0, 1)], writes=[("OT", 2 * hk, 2 * hk + 2)])
        for oc in range(8):
            pso, pon = ph.newps()
            for k in range(8):
                S.op("pe", lambda e, k=k, oc=oc, pso=pso: e.matmul(pso[:, :T], lhsT=wo[:, k, oc * 128:(oc + 1) * 128], rhs=OT[:, k, :],
                                                                   start=(k == 0), stop=(k == 7)),
                     reads=[("wo", k, k + 1), ("OT", k, k + 1)], writes=[(pon, 0, 1)])
            S.op("dve", lambda e, oc=oc, pso=pso: e.scalar_tensor_tensor(out=x[:, oc, :], in0=pso[:, :T], scalar=ph.c("b_o", oc), in1=x[:, oc, :],
                                                                        op0=ALU.add, op1=ALU.add),
                 reads=[(pon, 0, 1), ("x", oc, oc + 1), ph.cr("b_o", oc)], writes=[("x", oc, oc + 1)])
        if masked and t == 0:
            for k in range(8):
                S.op("act", lambda e, k=k: e.activation(out=x[:, k, :], in_=x[:, k, :], func=AF.Identity, scale=ph.c("hmask")),
                     reads=[("x", k, k + 1), ph.cr("hmask")], writes=[("x", k, k + 1)])
        S.dma("sp", dstv[:, :, tc0:tc0 + T], x[:], reads=[("x", 0, 8)])
    ph.close()


nc_bv_dram = [None]


def mixer_phase(nc, src, dst, w_in, w_out, w2a2_d, g2_d, cdram, coff, TL, NPRE=0, masked=False):
    T = 256
    NT = TL // T
    NCH = T // 64
    need = [("eps6", 1), ("eps5", 1), ("epsgn", 1), ("hmask", 1), ("ab_g", 8), ("cin_b", 8), ("dw_w", 124), ("dw_b", 4), ("ln_g", 4), ("ln_b", 4),
            ("mu", 14), ("w0", 4), ("a0", 4), ("k_k", 4), ("k_a", 4), ("r_k", 4), ("gn_g", 4), ("gn_b", 4),
            ("ident", 128), ("bones", 128), ("ones", 128), ("m_su", 128), ("m_iu", 128), ("m_sl", 128)]
    ph = Phase(nc, cdram, coff, need)
    S = ph.S
    sb = ph.sb
    win = sb("win", [128, 8, 2816], BF16)
    wout = sb("wout", [128, 8, D], BF16)
    w2a2 = sb("w2a2", [128, 512], BF16)
    g2b = sb("g2b", [128, 512], BF16)
    x = sb("x", [128, 8, T], F32)
    hT = sb("hT", [128, 8, T], BF16)
    sq = sb("sq", [128, 8, T], BF16)
    rstd = sb("rstd", [128, T], F32)
    ones_bf = sb("ones_bf", [128, 128], BF16)
    omm = sb("omm", [128, 14], F32)
    GL = sb("GL", [128, 4, 30 + T], F32)
    cacc = sb("cacc", [128, 4, T], F32)
    csq = sb("csq", [128, 4, T], F32)
    sig = sb("sig", [128, T], F32)
    crs = sb("crs", [128, T], F32)
    catT = sb("catT", [128, 8, T], BF16)
    Pb = [sb("Pb%d" % i, [128, T + 1], F32) for i in range(2)]
    ptmp = [sb("ptmp%d" % i, [128, T], F32) for i in range(2)]
    pcarry = sb("pcarry", [128, 14], F32)
    rw12 = sb("rw12", [128, T], F32)
    rw13 = sb("rw13", [128, T], F32)
    twad = sb("twad", [128, T], BF16)
    sgd = sb("sgd", [128, T], BF16)
    lw4 = sb("lw4", [128, 4, T], F32)
    a4 = sb("a4", [128, 4, T], F32)
    gate4 = sb("gate4", [128, 4, T], F32)
    onesrow = sb("onesrow", [128, 64], F32)
    resetrow = sb("resetrow", [128, T], F32)
    names = ["r", "k", "v", "kk", "q1", "rn", "kkn", "k2", "bb", "bonus", "cum", "P", "Pinv", "Pprev", "y", "yc"]
    pers = ("P", "bonus", "y", "yc")
    st_ = {n: [sb("s_%s%d" % (n, i), [128, T], F32) for i in range(4 if n in pers else 2)] for n in names}
    bdn = ["a", "r", "b", "k", "v"]
    BD = {n: [sb("bd_%s%d" % (n, i), [128, NCH, 128], BF16) for i in range(4)] for n in bdn}
    mats = ["AT", "A", "YrbT", "XakT", "YrkT", "TT", "Vt", "Bt", "Kt", "W", "U"]
    M = {n: [sb("m_%s%d" % (n, i), [128, 128], BF16) for i in range(2)] for n in mats}
    A2 = [[sb("A2_%d_%d" % (j, i), [128, 128], BF16) for i in range(2)] for j in range(2)]
    A2T = [[sb("A2T_%d_%d" % (j, i), [128, 128], BF16) for i in range(2)] for j in range(2)]
    TT2 = [[sb("TT2_%d_%d" % (j, i), [128, 128], BF16) for i in range(2)] for j in range(2)]
    H = [[sb("H%d_%d" % (cc, i), [128, 128], F32) for i in range(2)] for cc in range(4)]
    hcur = [0, 0, 0, 0]
    Hb = [[sb("Hb%d_%d" % (cc, i), [128, 128], BF16) for i in range(2)] for cc in range(4)]
    ident_bf = sb("ident_bf", [128, 128], BF16)

    load_w(S, "pool", win, "win", w_in, 8)
    load_w(S, "pool", wout, "wout", w_out, 8)
    S.dma("pool", w2a2[:], w2a2_d, writes=[("w2a2", 0, 1)])
    S.dma("pool", g2b[:], g2_d, writes=[("g2b", 0, 1)])
    S.op("dve", lambda e: e.tensor_copy(out=ones_bf[:], in_=ph.c("ones", 0, 128)), reads=[ph.cr("ones", 0, 128)], writes=[("ones_bf", 0, 1)])
    S.op("dve", lambda e: e.tensor_copy(out=onesrow[:], in_=ph.c("ones", 0, 64)), reads=[ph.cr("ones", 0, 64)], writes=[("onesrow", 0, 1)])
    S.op("dve", lambda e: e.tensor_scalar(out=omm[:], in0=ph.c("mu", 0, 14), scalar1=-1.0, scalar2=1.0, op0=ALU.mult, op1=ALU.add),
         reads=[ph.cr("mu", 0, 14)], writes=[("omm", 0, 14)])
    S.op("pool", lambda e: e.memset(GL[:], 0.0), writes=[("GL", 0, 4 * 1000)])
    S.op("pool", lambda e: e.memset(resetrow[:], 1.0), writes=[("resetrow", 0, 1)])
    S.op("pool", lambda e: e.memset(resetrow[:].rearrange("p (n t) -> p n t", t=64)[:, :, 0:1], 0.0), writes=[("resetrow", 0, 1)])
    S.op("pool", lambda e: e.memset(pcarry[:], 0.0), writes=[("pcarry", 0, 14)])
    for n in bdn:
        for i in range(4):
            S.op("pool", lambda e, n=n, i=i: e.memset(BD[n][i][:], 0.0), writes=[("bd_%s%d" % (n, i), 0, NCH)])
    for cc in range(4):
        S.op("pool", lambda e, cc=cc: e.memset(H[cc][0][:], 0.0), writes=[("H%d_0" % cc, 0, 1)])
        S.op("pool", lambda e, cc=cc: e.memset(Hb[cc][0][:], 0.0), writes=[("Hb%d_0" % cc, 0, 1)])
    S.op("dve", lambda e: e.tensor_copy(out=ident_bf[:], in_=ph.c("ident", 0, 128)), reads=[ph.cr("ident", 0, 128)], writes=[("ident_bf", 0, 1)])

    ident = ph.c("ident", 0, 128)
    bones = ph.c("bones", 0, 128)
    ones32 = ph.c("ones", 0, 128)
    srcv = src.rearrange("(c p) t -> p c t", p=128)
    dstv = dst.rearrange("(c p) t -> p c t", p=128)

    def proj(pc):
        ps, pn = ph.newps()
        for k in range(8):
            S.op("pe", lambda e, k=k, ps=ps: e.matmul(ps[:, :T], lhsT=win[:, k, pc * 128:(pc + 1) * 128], rhs=hT[:, k, :],
                                                      start=(k == 0), stop=(k == 7)),
                 reads=[("win", k, k + 1), ("hT", k, k + 1)], writes=[(pn, 0, 1)])
        return ps, pn

    def shifted(ch, out, outname):
        i = ch % 2
        ps, pn = proj(8 + ch)
        pb, pbn, tm, tmn = Pb[i], "Pb%d" % i, ptmp[i], "ptmp%d" % i
        S.op("pool", lambda e: e.tensor_copy(out=pb[:, 0:1], in_=pcarry[:, ch:ch + 1]), reads=[("pcarry", ch, ch + 1)], writes=[(pbn, 0, 1)])
        S.op("act", lambda e: e.activation(out=pb[:, 1:T + 1], in_=ps[:, :T], func=AF.Identity), reads=[(pn, 0, 1)], writes=[(pbn, 1, T + 1)])
        S.op("pool", lambda e: e.tensor_copy(out=pcarry[:, ch:ch + 1], in_=pb[:, T:T + 1]), reads=[(pbn, T, T + 1)], writes=[("pcarry", ch, ch + 1)])
        S.op("act", lambda e: e.activation(out=tm[:], in_=pb[:, 0:T], func=AF.Identity, scale=ph.c("mu", ch)),
             reads=[(pbn, 0, T), ph.cr("mu", ch)], writes=[(tmn, 0, 1)])
        S.op("dve", lambda e: e.scalar_tensor_tensor(out=out[:], in0=pb[:, 1:T + 1], scalar=omm[:, ch:ch + 1], in1=tm[:], op0=ALU.mult, op1=ALU.add),
             reads=[(pbn, 1, T + 1), ("omm", ch, ch + 1), (tmn, 0, 1)], writes=[(outname, 0, 1)])

    def mm1(ps, pn, lhsT, rhs, reads, start=True, stop=True, n=128):
        S.op("pe", lambda e: e.matmul(ps[:, :n], lhsT=lhsT, rhs=rhs, start=start, stop=stop), reads=reads, writes=[(pn, 0, 1)])

    def rsqrt_ps(ps, pn, out, outn, scale, epsname, n=T):
        S.op("act", lambda e: e.activation(out=out, in_=ps[:, :n], func=AF.Ln, scale=scale, bias=ph.c(epsname)),
             reads=[(pn, 0, 1), ph.cr(epsname)], writes=[(outn, 0, 1)])
        S.op("act", lambda e: e.activation(out=out, in_=out, func=AF.Exp, scale=-0.5), reads=[(outn, 0, 1)], writes=[(outn, 0, 1)])

    for t in range(NT):
        full = t >= NPRE
        plast = (t == NPRE - 1)
        mtile = masked and (NPRE - 1 <= t < NPRE + 2)
        S.dma("sp", x[:], srcv[:, :, t * T:(t + 1) * T], writes=[("x", 0, 8)])
        rmsnorm(ph, x, "x", hT, "hT", sq, "sq", rstd, ones_bf, "ab_g", T)
        for cc in range(4 if (full or plast) else 0):
            psa, pan = proj(cc)
            psg, pgn = proj(4 + cc)
            S.op("act", lambda e, psg=psg, cc=cc: e.activation(out=sig[:], in_=psg[:, :T], func=AF.Sigmoid, bias=ph.c("cin_b", 4 + cc)),
                 reads=[(pgn, 0, 1), ph.cr("cin_b", 4 + cc)], writes=[("sig", 0, 1)])
            S.op("dve", lambda e, psa=psa, cc=cc: e.scalar_tensor_tensor(out=GL[:, cc, 30:30 + T], in0=psa[:, :T], scalar=ph.c("cin_b", cc), in1=sig[:],
                                                                        op0=ALU.add, op1=ALU.mult),
                 reads=[(pan, 0, 1), ("sig", 0, 1), ph.cr("cin_b", cc)], writes=[("GL", cc * 1000 + 30, cc * 1000 + 30 + T)])
            eng = "dve"
            if mtile:
                S.op("act", lambda e, cc=cc: e.activation(out=GL[:, cc, 30:30 + T], in_=GL[:, cc, 30:30 + T], func=AF.Identity, scale=ph.c("hmask")),
                     reads=[("GL", cc * 1000 + 30, cc * 1000 + 30 + T), ph.cr("hmask")], writes=[("GL", cc * 1000 + 30, cc * 1000 + 30 + T)])
            if not full:
                S.op("dve", lambda e, cc=cc: e.tensor_copy(out=GL[:, cc, 0:30], in_=GL[:, cc, T:T + 30]),
                     reads=[("GL", cc * 1000 + T, cc * 1000 + T + 30)], writes=[("GL", cc * 1000, cc * 1000 + 30)])
        def convgen():
            for cc in range(4 if full else 0):
                eng = "dve"
                S.op(eng, lambda e, cc=cc: e.tensor_scalar(out=cacc[:, cc, :], in0=GL[:, cc, 30:30 + T], scalar1=ph.c("dw_w", cc * 31 + 30), scalar2=ph.c("dw_b", cc),
                                                           op0=ALU.mult, op1=ALU.add),
                     reads=[("GL", cc * 1000, cc * 1000 + 30 + T), ph.cr("dw_w", cc * 31, 31), ph.cr("dw_b", cc)], writes=[("cacc", cc, cc + 1)])
                for j in range(30):
                    S.op(eng, lambda e, cc=cc, j=j: e.scalar_tensor_tensor(out=cacc[:, cc, :], in0=GL[:, cc, j:j + T], scalar=ph.c("dw_w", cc * 31 + j), in1=cacc[:, cc, :],
                                                                          op0=ALU.mult, op1=ALU.add),
                         reads=[("GL", cc * 1000, cc * 1000 + 30 + T), ("cacc", cc, cc + 1)], writes=[("cacc", cc, cc + 1)])
                    if j % 4 == 3:
                        yield
                S.op(eng, lambda e, cc=cc: e.tensor_copy(out=GL[:, cc, 0:30], in_=GL[:, cc, T:T + 30]),
                     reads=[("GL", cc * 1000 + T, cc * 1000 + T + 30)], writes=[("GL", cc * 1000, cc * 1000 + 30)])
                yield
            if not full:
                return
            psm, pmn = ph.newps()
            for cc in range(4):
                mm1(psm, pmn, ones32, cacc[:, cc, :], [ph.cr("ones", 0, 128), ("cacc", cc, cc + 1)], start=(cc == 0), stop=(cc == 3), n=T)
            for cc in range(4):
                S.op("dve", lambda e, cc=cc, psm=psm: e.scalar_tensor_tensor(out=cacc[:, cc, :], in0=psm[:, :T], scalar=-1.0 / 512, in1=cacc[:, cc, :],
                                                                            op0=ALU.mult, op1=ALU.add),
                     reads=[(pmn, 0, 1), ("cacc", cc, cc + 1)], writes=[("cacc", cc, cc + 1)])
                S.op("pool", lambda e, cc=cc: e.tensor_tensor(out=csq[:, cc, :], in0=cacc[:, cc, :], in1=cacc[:, cc, :], op=ALU.mult),
                     reads=[("cacc", cc, cc + 1)], writes=[("csq", cc, cc + 1)])
            yield
            psv, pvn = ph.newps()
            for cc in range(4):
                mm1(psv, pvn, ones32, csq[:, cc, :], [ph.cr("ones", 0, 128), ("csq", cc, cc + 1)], start=(cc == 0), stop=(cc == 3), n=T)
            rsqrt_ps(psv, pvn, crs[:], "crs", 1.0 / 512, "eps5")
            yield
            for cc in range(4):
                S.op("dve", lambda e, cc=cc: e.tensor_tensor(out=cacc[:, cc, :], in0=cacc[:, cc, :], in1=crs[:], op=ALU.mult),
                     reads=[("cacc", cc, cc + 1), ("crs", 0, 1)], writes=[("cacc", cc, cc + 1)])
                S.op("act", lambda e, cc=cc: e.activation(out=catT[:, cc, :], in_=cacc[:, cc, :], func=AF.Silu, scale=ph.c("ln_g", cc), bias=ph.c("ln_b", cc)),
                     reads=[("cacc", cc, cc + 1), ph.cr("ln_g", cc), ph.cr("ln_b", cc)], writes=[("catT", cc, cc + 1)])
            yield
        shifted(12, rw12, "rw12")
        if full or plast:
            shifted(13, rw13, "rw13")
        S.op("act", lambda e: e.activation(out=twad[0:64, :], in_=rw12[0:64, :], func=AF.Tanh), reads=[("rw12", 0, 1)], writes=[("twad", 0, 1)])
        S.op("dve", lambda e: e.tensor_copy(out=twad[64:128, :], in_=rw12[64:128, :]), reads=[("rw12", 0, 1)], writes=[("twad", 1, 2)])
        if full:
            S.op("act", lambda e: e.activation(out=sgd[:], in_=rw13[:], func=AF.Sigmoid), reads=[("rw13", 0, 1)], writes=[("sgd", 0, 1)])
        for cc in range(4):
            ps, pn = ph.newps()
            mm1(ps, pn, w2a2[0:64, cc * 128:(cc + 1) * 128], twad[0:64, :], [("w2a2", 0, 1), ("twad", 0, 1)], n=T)
            S.op("act", lambda e, ps=ps, cc=cc: e.activation(out=lw4[:, cc, :], in_=ps[:, :T], func=AF.Sigmoid, bias=ph.c("w0", cc)),
                 reads=[(pn, 0, 1), ph.cr("w0", cc)], writes=[("lw4", cc, cc + 1)])
            S.op("act", lambda e, cc=cc: e.activation(out=lw4[:, cc, :], in_=lw4[:, cc, :], func=AF.Identity, scale=-float(np.exp(-0.5))),
                 reads=[("lw4", cc, cc + 1)], writes=[("lw4", cc, cc + 1)])
            ps, pn = ph.newps()
            mm1(ps, pn, w2a2[64:128, cc * 128:(cc + 1) * 128], twad[64:128, :], [("w2a2", 0, 1), ("twad", 1, 2)], n=T)
            S.op("act", lambda e, ps=ps, cc=cc: e.activation(out=a4[:, cc, :], in_=ps[:, :T], func=AF.Sigmoid, bias=ph.c("a0", cc)),
                 reads=[(pn, 0, 1), ph.cr("a0", cc)], writes=[("a4", cc, cc + 1)])
            if full:
                ps, pn = ph.newps()
                mm1(ps, pn, g2b[:, cc * 128:(cc + 1) * 128], sgd[:], [("g2b", 0, 1), ("sgd", 0, 1)], n=T)
                S.op("act", lambda e, ps=ps, cc=cc: e.activation(out=gate4[:, cc, :], in_=ps[:, :T], func=AF.Identity),
                     reads=[(pn, 0, 1)], writes=[("gate4", cc, cc + 1)])
        def prep(cc, cx):
            i = cc % 2
            s = {n: st_[n][cc if n in pers else i] for n in names}
            sn = {n: "s_%s%d" % (n, cc if n in pers else i) for n in names}
            bd = {n: BD[n][cc] for n in bdn}
            bn = {n: "bd_%s%d" % (n, cc) for n in bdn}
            if full or plast:
                shifted(cc, s["r"], sn["r"])
            shifted(4 + cc, s["k"], sn["k"])
            shifted(8 + cc, s["v"], sn["v"])
            lw = lw4[:, cc, :]
            av = a4[:, cc, :]
            yield

            def ew(eng, f, reads, writes):
                S.op(eng, f, reads=[(sn[r], 0, 1) if r in sn else r for r in reads], writes=[(sn[w], 0, 1) if w in sn else w for w in writes])
            ew("act", lambda e, s=s, cc=cc: e.activation(out=s["kk"][:], in_=s["k"][:], func=AF.Identity, scale=ph.c("k_k", cc)),
               ["k", ph.cr("k_k", cc)], ["kk"])
            ew("pool", lambda e, s=s: e.tensor_tensor(out=s["q1"][:], in0=s["kk"][:], in1=s["kk"][:], op=ALU.mult), ["kk"], ["q1"])
            ps, pn = ph.newps()
            mm1(ps, pn, bones, s["q1"][:], [ph.cr("bones", 0, 128), (sn["q1"], 0, 1)], n=T)
            yield
            ew("dve", lambda e, s=s, ps=ps: e.tensor_scalar(out=s["rn"][:], in0=ps[:, :T], scalar1=1e-24, scalar2=None, op0=ALU.max), [(pn, 0, 1)], ["rn"])
            ew("act", lambda e, s=s: e.activation(out=s["rn"][:], in_=s["rn"][:], func=AF.Ln), ["rn"], ["rn"])
            ew("act", lambda e, s=s: e.activation(out=s["rn"][:], in_=s["rn"][:], func=AF.Exp, scale=-0.5), ["rn"], ["rn"])
            yield
            ew("dve", lambda e, s=s: e.tensor_tensor(out=s["kkn"][:], in0=s["kk"][:], in1=s["rn"][:], op=ALU.mult), ["kk", "rn"], ["kkn"])
            ew("pool", lambda e, s=s, av=av, cc=cc: e.tensor_scalar(out=s["q1"][:], in0=av, scalar1=-1.0, scalar2=ph.c("k_a", cc), op0=ALU.add, op1=ALU.mult),
               [("a4", cc, cc + 1), ph.cr("k_a", cc)], ["q1"])
            ew("dve", lambda e, s=s: e.scalar_tensor_tensor(out=s["k2"][:], in0=s["q1"][:], scalar=1.0, in1=s["k"][:], op0=ALU.add, op1=ALU.mult),
               ["q1", "k"], ["k2"])
            ew("dve", lambda e, s=s, av=av: e.tensor_tensor(out=s["bb"][:], in0=s["kkn"][:], in1=av, op=ALU.mult), ["kkn", ("a4", cc, cc + 1)], ["bb"])
            if full:
                ew("dve", lambda e, s=s, cc=cc: e.scalar_tensor_tensor(out=s["q1"][:], in0=s["r"][:], scalar=ph.c("r_k", cc), in1=s["k2"][:], op0=ALU.mult, op1=ALU.mult),
                   ["r", "k2", ph.cr("r_k", cc)], ["q1"])
                ps, pn = ph.newps()
                mm1(ps, pn, bones, s["q1"][:], [ph.cr("bones", 0, 128), (sn["q1"], 0, 1)], n=T)
                ew("dve", lambda e, s=s, ps=ps: e.tensor_tensor(out=s["bonus"][:], in0=ps[:, :T], in1=s["v"][:], op=ALU.mult), [(pn, 0, 1), "v"], ["bonus"])
            yield
            ew("dve", lambda e, s=s, lw=lw: e.tensor_tensor_scan(out=s["cum"][:], data0=resetrow[:], data1=lw, initial=0.0, op0=ALU.mult, op1=ALU.add),
               [("lw4", cc, cc + 1), ("resetrow", 0, 1)], ["cum"])
            yield
            ew("act", lambda e, s=s: e.activation(out=s["P"][:], in_=s["cum"][:], func=AF.Exp), ["cum"], ["P"])
            ew("act", lambda e, s=s: e.activation(out=s["Pinv"][:], in_=s["cum"][:], func=AF.Exp, scale=-1.0), ["cum"], ["Pinv"])
            ew("pool", lambda e, s=s, lw=lw: e.tensor_tensor(out=s["Pprev"][:], in0=s["cum"][:], in1=lw, op=ALU.subtract), ["cum", ("lw4", cc, cc + 1)], ["Pprev"])
            ew("act", lambda e, s=s: e.activation(out=s["Pprev"][:], in_=s["Pprev"][:], func=AF.Exp), ["Pprev"], ["Pprev"])
            yield
            v3 = lambda ap: ap.rearrange("p (n t) -> p n t", t=64)
            for hh in range(2):
                hp = slice(hh * 64, hh * 64 + 64)
                hc = slice(hh * 64, hh * 64 + 64)
                eng = "dve" if hh == 0 else "pool"
                ew("dve", lambda e, s=s, bd=bd, hp=hp, hc=hc: e.scalar_tensor_tensor(out=bd["a"][hp, :, hc], in0=v3(s["kkn"][hp, :]), scalar=-1.0, in1=v3(s["Pprev"][hp, :]),
                                                                                   op0=ALU.mult, op1=ALU.mult), ["kkn", "Pprev"], [(bn["a"], 0, NCH)])
                if full:
                    ew(eng, lambda e, s=s, bd=bd, hp=hp, hc=hc: e.tensor_tensor(out=bd["r"][hp, :, hc], in0=v3(s["r"][hp, :]), in1=v3(s["P"][hp, :]), op=ALU.mult),
                       ["r", "P"], [(bn["r"], 0, NCH)])
                ew(eng, lambda e, s=s, bd=bd, hp=hp, hc=hc: e.tensor_tensor(out=bd["b"][hp, :, hc], in0=v3(s["bb"][hp, :]), in1=v3(s["Pinv"][hp, :]), op=ALU.mult),
                   ["bb", "Pinv"], [(bn["b"], 0, NCH)])
                ew(eng, lambda e, s=s, bd=bd, hp=hp, hc=hc: e.tensor_tensor(out=bd["k"][hp, :, hc], in0=v3(s["k2"][hp, :]), in1=v3(s["Pinv"][hp, :]), op=ALU.mult),
                   ["k2", "Pinv"], [(bn["k"], 0, NCH)])
                ew(eng, lambda e, s=s, bd=bd, hp=hp, hc=hc: e.tensor_copy(out=bd["v"][hp, :, hc], in_=v3(s["v"][hp, :])), ["v"], [(bn["v"], 0, NCH)])
            cx.update(dict(cc=cc, s=s, sn=sn, bd=bd, bn=bn, ew=ew))
            yield

        def unit(cx, n):
            cc, s, sn, bd, bn = cx["cc"], cx["s"], cx["sn"], cx["bd"], cx["bn"]
            i = cc % 2
            if True:
                m = {k_: M[k_][i] for k_ in mats}
                mn = {k_: "m_%s%d" % (k_, i) for k_ in mats}
                ba, br, bb_, bk, bv_ = (bd[q][:, n, :] for q in bdn)
                R = lambda q: (bn[q], n, n + 1)

                def sc(lq, rq, lhs, rhs, out, mask):
                    ps, pn = ph.newps()
                    mm1(ps, pn, lhs, rhs, [R(lq), R(rq)])
                    S.op("dve", lambda e, ps=ps, m=m: e.tensor_tensor(out=m[out][:], in0=ps[:, :128], in1=ph.c(mask, 0, 128), op=ALU.mult),
                         reads=[(pn, 0, 1), ph.cr(mask, 0, 128)], writes=[(mn[out], 0, 1)])
                sc("b", "a", bb_, ba, "AT", "m_su")
                sc("a", "b", ba, bb_, "A", "m_sl")
                if full:
                    sc("b", "r", bb_, br, "YrbT", "m_iu")
                sc("k", "a", bk, ba, "XakT", "m_su")
                if full:
                    sc("k", "r", bk, br, "YrkT", "m_iu")
                S.op("pool", lambda e, m=m: e.tensor_tensor(out=m["TT"][:], in0=m["AT"][:], in1=ident, op=ALU.add),
                     reads=[(mn["AT"], 0, 1), ph.cr("ident", 0, 128)], writes=[(mn["TT"], 0, 1)])
                cA, cAn, cAT, cATn, cTT, cTTn = m["A"], mn["A"], m["AT"], mn["AT"], m["TT"], mn["TT"]
                yield
                for d in range(5):
                    nA, nAn = A2[i][d % 2], "A2_%d_%d" % (i, d % 2)
                    nAT, nATn = A2T[i][d % 2], "A2T_%d_%d" % (i, d % 2)
                    nTT, nTTn = TT2[i][d % 2], "TT2_%d_%d" % (i, d % 2)
                    ps, pn = ph.newps()
                    mm1(ps, pn, cAT[:], cA[:], [(cATn, 0, 1), (cAn, 0, 1)])
                    S.op("act", lambda e, ps=ps, nA=nA: e.activation(out=nA[:], in_=ps[:, :128], func=AF.Identity), reads=[(pn, 0, 1)], writes=[(nAn, 0, 1)])
                    if d < 4:
                        ps2, pn2 = ph.newps()
                        mm1(ps2, pn2, cA[:], cAT[:], [(cATn, 0, 1), (cAn, 0, 1)])
                        S.op("dve", lambda e, ps2=ps2, nAT=nAT: e.tensor_copy(out=nAT[:], in_=ps2[:, :128]), reads=[(pn2, 0, 1)], writes=[(nATn, 0, 1)])
                    ps3, pn3 = ph.newps()
                    mm1(ps3, pn3, nA[:], cTT[:], [(nAn, 0, 1), (cTTn, 0, 1)])
                    S.op("dve", lambda e, ps3=ps3, nTT=nTT, cTT=cTT: e.tensor_tensor(out=nTT[:], in0=ps3[:, :128], in1=cTT[:], op=ALU.add),
                         reads=[(pn3, 0, 1), (cTTn, 0, 1)], writes=[(nTTn, 0, 1)])
                    cA, cAn, cAT, cATn, cTT, cTTn = nA, nAn, nAT, nATn, nTT, nTTn
                    yield
                for q, dst_ in (("v", "Vt"), ("b", "Bt"), ("k", "Kt")):
                    ps, pn = ph.newps()
                    mm1(ps, pn, bd[q][:, n, :], ident_bf[:], [R(q), ("ident_bf", 0, 1)])
                    S.op("act", lambda e, ps=ps, dst_=dst_, m=m: e.activation(out=m[dst_][:], in_=ps[:, :128], func=AF.Identity),
                         reads=[(pn, 0, 1)], writes=[(mn[dst_], 0, 1)])
                yield
                Hc, Hcn = H[cc][hcur[cc]], "H%d_%d" % (cc, hcur[cc])
                Hn, Hnn = H[cc][1 - hcur[cc]], "H%d_%d" % (cc, 1 - hcur[cc])
                Hbc, Hbcn = Hb[cc][hcur[cc]], "Hb%d_%d" % (cc, hcur[cc])
                Hbn, Hbnn = Hb[cc][1 - hcur[cc]], "Hb%d_%d" % (cc, 1 - hcur[cc])
                hcur[cc] = 1 - hcur[cc]
                ps, pn = ph.newps()
                mm1(ps, pn, ba, Hbc[:], [R("a"), (Hbcn, 0, 1)], start=True, stop=False)
                mm1(ps, pn, m["XakT"][:], m["Vt"][:], [(mn["XakT"], 0, 1), (mn["Vt"], 0, 1)], start=False, stop=True)
                S.op("act", lambda e, ps=ps, m=m: e.activation(out=m["W"][:], in_=ps[:, :128], func=AF.Identity), reads=[(pn, 0, 1)], writes=[(mn["W"], 0, 1)])
                yield
                ps, pn = ph.newps()
                mm1(ps, pn, cTT[:], m["W"][:], [(cTTn, 0, 1), (mn["W"], 0, 1)])
                S.op("dve", lambda e, ps=ps, m=m: e.tensor_copy(out=m["U"][:], in_=ps[:, :128]), reads=[(pn, 0, 1)], writes=[(mn["U"], 0, 1)])
                yield
                if full:
                    ps, pn = ph.newps()
                    mm1(ps, pn, Hbc[:], br, [(Hbcn, 0, 1), R("r")], start=True, stop=False)
                    mm1(ps, pn, m["U"][:], m["YrbT"][:], [(mn["U"], 0, 1), (mn["YrbT"], 0, 1)], start=False, stop=False)
                    mm1(ps, pn, m["Vt"][:], m["YrkT"][:], [(mn["Vt"], 0, 1), (mn["YrkT"], 0, 1)], start=False, stop=True)
                    for hh in range(2):
                        hp = slice(hh * 64, hh * 64 + 64)
                        S.op("act" if hh == 0 else "dve",
                             (lambda e, ps=ps, hp=hp, s=s, n=n: e.activation(out=s["y"][hp, n * 64:(n + 1) * 64], in_=ps[hp, hp], func=AF.Identity)) if hh == 0 else
                             (lambda e, ps=ps, hp=hp, s=s, n=n: e.tensor_copy(out=s["y"][hp, n * 64:(n + 1) * 64], in_=ps[hp, hp])),
                             reads=[(pn, 0, 1)], writes=[(sn["y"], 0, 1)])
                ps, pn = ph.newps()
                mm1(ps, pn, m["Bt"][:], m["U"][:], [(mn["Bt"], 0, 1), (mn["U"], 0, 1)], start=True, stop=False)
                mm1(ps, pn, m["Kt"][:], m["Vt"][:], [(mn["Kt"], 0, 1), (mn["Vt"], 0, 1)], start=False, stop=True)
                pc = s["P"][:, n * 64 + 63:n * 64 + 64]
                S.op("act", lambda e, Hn=Hn, Hc=Hc, pc=pc: e.activation(out=Hn[:], in_=Hc[:], func=AF.Identity, scale=pc),
                     reads=[(Hcn, 0, 1), (sn["P"], 0, 1)], writes=[(Hnn, 0, 1)])
                S.op("dve", lambda e, ps=ps, Hn=Hn, pc=pc: e.scalar_tensor_tensor(out=Hn[:], in0=ps[:, :128], scalar=pc, in1=Hn[:], op0=ALU.mult, op1=ALU.add),
                     reads=[(pn, 0, 1), (Hnn, 0, 1), (sn["P"], 0, 1)], writes=[(Hnn, 0, 1)])
                S.op("act", lambda e, Hn=Hn, Hbn=Hbn: e.activation(out=Hbn[:], in_=Hn[:], func=AF.Identity), reads=[(Hnn, 0, 1)], writes=[(Hbnn, 0, 1)])

        def post(cx):
            cc, s, sn, ew = cx["cc"], cx["s"], cx["sn"], cx["ew"]
            ps, pn = ph.newps()
            mm1(ps, pn, bones, s["y"][:], [ph.cr("bones", 0, 128), (sn["y"], 0, 1)], n=T)
            ew("dve", lambda e, s=s, ps=ps: e.scalar_tensor_tensor(out=s["yc"][:], in0=ps[:, :T], scalar=-1.0 / 64, in1=s["y"][:], op0=ALU.mult, op1=ALU.add),
               [(pn, 0, 1), "y"], ["yc"])
            yield
            ew("pool", lambda e, s=s: e.tensor_tensor(out=s["q1"][:], in0=s["yc"][:], in1=s["yc"][:], op=ALU.mult), ["yc"], ["q1"])
            ps, pn = ph.newps()
            mm1(ps, pn, bones, s["q1"][:], [ph.cr("bones", 0, 128), (sn["q1"], 0, 1)], n=T)
            rsqrt_ps(ps, pn, s["rn"][:], sn["rn"], 1.0 / 64, "epsgn")
            yield
            ew("dve", lambda e, s=s: e.tensor_tensor(out=s["yc"][:], in0=s["yc"][:], in1=s["rn"][:], op=ALU.mult), ["yc", "rn"], ["yc"])
            ew("act", lambda e, s=s, cc=cc: e.activation(out=s["yc"][:], in_=s["yc"][:], func=AF.Identity, scale=ph.c("gn_g", cc), bias=ph.c("gn_b", cc)),
               ["yc", ph.cr("gn_g", cc), ph.cr("gn_b", cc)], ["yc"])
            yield
            ew("pool", lambda e, s=s: e.tensor_tensor(out=s["yc"][:], in0=s["yc"][:], in1=s["bonus"][:], op=ALU.add), ["yc", "bonus"], ["yc"])
            ew("dve", lambda e, s=s, cc=cc: e.tensor_tensor(out=catT[:, 4 + cc, :], in0=s["yc"][:], in1=gate4[:, cc, :], op=ALU.mult),
               ["yc", ("gate4", cc, cc + 1)], [("catT", 4 + cc, 5 + cc)])
        def lockstep(gens):
            alive = True
            while alive:
                alive = False
                for g in gens:
                    try:
                        next(g)
                        alive = True
                    except StopIteration:
                        pass

        def chain(cx):
            for n in range(NCH):
                yield from unit(cx, n)

        cxs = [dict() for _ in range(4)]
        lockstep([prep(0, cxs[0]), prep(1, cxs[1])])
        lockstep([chain(cxs[0]), chain(cxs[1]), prep(2, cxs[2]), prep(3, cxs[3]), convgen()])
        lockstep([chain(cxs[2]), chain(cxs[3])] + ([post(cxs[0]), post(cxs[1])] if full else []))
        if full:
            lockstep([post(cxs[2]), post(cxs[3])])
        if DBG['mix'] == 1:
            for k in range(8):
                S.op("dve", lambda e, k=k: e.tensor_copy(out=x[:, k, :], in_=catT[:, k, :]), reads=[("catT", k, k + 1)], writes=[("x", k, k + 1)])
        for oc in range(8 if (DBG['mix'] != 1 and full) else 0):
            pso, pon = ph.newps()
            for k in range(8):
                S.op("pe", lambda e, k=k, oc=oc, pso=pso: e.matmul(pso[:, :T], lhsT=wout[:, k, oc * 128:(oc + 1) * 128], rhs=catT[:, k, :],
                                                                   start=(k == 0), stop=(k == 7)),
                     reads=[("wout", k, k + 1), ("catT", k, k + 1)], writes=[(pon, 0, 1)])
            S.op("dve", lambda e, oc=oc, pso=pso: e.tensor_tensor(out=x[:, oc, :], in0=pso[:, :T], in1=x[:, oc, :], op=ALU.add),
                 reads=[(pon, 0, 1), ("x", oc, oc + 1)], writes=[("x", oc, oc + 1)])
        if full and mtile:
            for k in range(8):
                S.op("act", lambda e, k=k: e.activation(out=x[:, k, :], in_=x[:, k, :], func=AF.Identity, scale=ph.c("hmask")),
                     reads=[("x", k, k + 1), ph.cr("hmask")], writes=[("x", k, k + 1)])
        if full:
            S.dma("sp", dstv[:, :, (t - NPRE) * T:(t - NPRE + 1) * T], x[:], reads=[("x", 0, 8)])
    ph.close()


def build(TL, NC, phases=(0, 1, 2, 3), halo=False):
    nc = bass.Bass("TRN2", target_bir_lowering=False)
    dt = lambda n, s, d=F32, kind="ExternalInput": nc.dram_tensor(n, s, d, kind=kind).ap()
    TW, TH, TO = (8192, 2560, 2048) if halo else (TL, TL, TL)
    xT = dt("xT", [D, TW]); pos = dt("pos", [1, TH], I32); cst = dt("cst", [128, NC])
    w_in = dt("w_in", [D, 2816]); w_out = dt("w_out", [D, D]); w2a2 = dt("w2a2", [128, 512]); g2 = dt("g2", [128, 512])
    w_up = [dt("w_up%d" % l, [D, 2 * DFF]) for l in range(2)]
    w_dn = [dt("w_dn%d" % l, [DFF, D]) for l in range(2)]
    w_qkv = dt("w_qkv", [D, 1536]); w_o = dt("w_o", [D, D]); bvd = dt("bvd", [1, 512])
    nc_bv_dram[0] = bvd
    if len(phases) < 4:
        TO = TH
    yT = dt("yT", [D, TO], kind="ExternalOutput")
    scr = [dt("scr%d" % i, [D, TH], kind="Internal") for i in range(3)]
    coff = build.coff
    chain = [xT] + scr[:len(phases) - 1] + [yT]
    ci = 0
    for p in phases:
        s_, d_ = chain[ci], chain[ci + 1]
        ci += 1
        if p == 0:
            if halo:
                mixer_phase(nc, s_, d_, w_in, w_out, w2a2, g2, cst, coff, TW, NPRE=(TW - TH) // 256, masked=True)
            else:
                mixer_phase(nc, s_, d_, w_in, w_out, w2a2, g2, cst, coff, TL)
        elif p == 1:
            ffn_phase(nc, s_, d_, w_up[0], w_dn[0], cst, coff, 0, TH)
        elif p == 2:
            attn_phase(nc, s_, d_, pos, w_qkv, w_o, cst, coff, TH, masked=halo)
        else:
            ffn_phase(nc, s_, d_, w_up[1], w_dn[1], cst, coff, 1, TH, skip=(TH - TO) // 512)
    return nc


def host_inputs(inp, xb, posb, hmask=1.0, cache={}):
    key = id(inp)
    if key not in cache:
        P = make_consts(inp, 1.0)
        f = lambda a: np.ascontiguousarray(np.asarray(a, np.float32))
        bq = np.asarray(inp["attn_b_qkv"][0], np.float32)
        bvd = np.concatenate([np.concatenate([bq[1280 + h * 64:1344 + h * 64]] * 2) for h in range(4)])[None, :]
        shared = {"w_in": f(inp["ab_w_in"][0]), "w_out": f(inp["ab_w_out"][0]),
                  "w2a2": f(np.concatenate([inp["rwkv_w2"][0], inp["rwkv_a2"][0]], axis=0)), "g2": f(inp["rwkv_g2"][0]),
                  "w_up0": f(inp["ffn_w_up"][0]), "w_up1": f(inp["ffn_w_up"][1]), "w_dn0": f(inp["ffn_w_down"][0]), "w_dn1": f(inp["ffn_w_down"][1]),
                  "w_qkv": f(inp["attn_w_qkv"][0]), "w_o": f(inp["attn_w_o"][0]), "bvd": f(bvd)}
        cache.clear()
        cache[key] = (P.off, P.build(), shared)
    off, cst0, shared = cache[key]
    build.coff = off
    cst = cst0.copy()
    cst[:, off["hmask"]] = hmask
    m = dict(shared)
    m["xT"] = np.ascontiguousarray(np.asarray(xb, np.float32).T)
    m["pos"] = np.ascontiguousarray(np.asarray(posb, np.int32)[None, :])
    m["cst"] = cst
    return m, cst.shape[1]


def kernel(**inputs):
    x = np.asarray(inputs["x"], np.float32)
    pos = np.asarray(inputs["positions"])
    B, SEQ, _ = x.shape
    TW, TH, TO = 8192, 2560, 2048
    NQ = SEQ // TO
    maps = []
    for c in range(8):
        b, q = c // NQ, c % NQ
        end = (q + 1) * TO
        start = end - TW
        xw = np.zeros((TW, D), np.float32)
        xw[max(0, -start):] = x[b, max(0, start):end]
        hs = end - TH
        pw = np.zeros((TH,), np.int32)
        pw[max(0, -hs):] = pos[b, max(0, hs):end]
        m, NC = host_inputs(inputs, xw, pw, hmask=(0.0 if q == 0 else 1.0))
        maps.append(m)
    nc = build(SEQ, NC, halo=True)
    res = run_bass_kernel_spmd(nc, maps, core_ids=list(range(8)))
    out = np.zeros((B, SEQ, D), np.float32)
    for c in range(8):
        b, q = c // NQ, c % NQ
        out[b, q * TO:(q + 1) * TO] = res.results[c]["yT"].T
    return out
```

```python
import contextlib
import numpy as np
import concourse.bass as bass
import concourse.mybir as mybir
from concourse.bass_utils import run_bass_kernel_spmd

F32 = mybir.dt.float32
BF16 = mybir.dt.bfloat16
I32 = mybir.dt.int32
ALU = mybir.AluOpType
AF = mybir.ActivationFunctionType

ENGS = ["pe", "act", "dve", "pool", "sp"]
NDMASEM = 6
D = 1024
DFF = 2816
PI = float(np.pi)
DBG = {'att': 9, 'mix': 9, 'skip': ''}


class Sched:
    def __init__(self, nc, stack):
        self.nc = nc
        self.ops = []
        self.per_eng = {e: [] for e in ENGS}
        self.acc = {}
        self.dma_count = {e: 0 for e in ENGS}
        Sched.count = getattr(Sched, "count", 0) + 1
        sp_ = "q%d" % Sched.count
        self.sems = {e: stack.enter_context(nc.semaphore(sp_ + "s_" + e)) for e in ENGS if e != "sp"}
        self.dsems = {q: [stack.enter_context(nc.semaphore(sp_ + "d_%s%d" % (q, i))) for i in range(NDMASEM)]
                      for q in ("sp", "act", "pool")}

    def _deps(self, reads, writes, opid):
        deps = set()
        for (name, lo, hi) in reads:
            lst = self.acc.setdefault(name, [])
            for (l, h, k, o) in lst:
                if k == "w" and l < hi and lo < h:
                    deps.add(o)
            lst.append((lo, hi, "r", opid))
        for (name, lo, hi) in writes:
            lst = self.acc.setdefault(name, [])
            keep = []
            for (l, h, k, o) in lst:
                if l < hi and lo < h:
                    if o != opid:
                        deps.add(o)
                    if lo <= l and h <= hi:
                        continue
                keep.append((l, h, k, o))
            keep.append((lo, hi, "w", opid))
            self.acc[name] = keep
        return deps

    def op(self, eng, fn, reads=(), writes=()):
        opid = len(self.ops)
        deps = self._deps(reads, writes, opid)
        rec = dict(id=opid, eng=eng, fn=fn, deps=deps, dma=None, sig=False)
        self.ops.append(rec)
        self.per_eng[eng].append(rec)
        return opid

    def dma(self, q, out, in_, reads=(), writes=(), **kw):
        opid = len(self.ops)
        deps = self._deps(reads, writes, opid)
        n = self.dma_count[q]
        self.dma_count[q] += 1
        rec = dict(id=opid, eng=q, fn=None, deps=deps, dma=(q, n, out, in_, kw), sig=True)
        self.ops.append(rec)
        self.per_eng[q].append(rec)
        return opid

    def _finalize(self):
        for rec in self.ops:
            for d in rec["deps"]:
                p = self.ops[d]
                if p["dma"] is None and not (p["eng"] == "pe" and rec["eng"] == "pe" and rec["dma"] is None):
                    p["sig"] = True
        cnt = {e: 0 for e in ENGS}
        for rec in self.ops:
            if rec["dma"] is not None:
                q, n, _, _, _ = rec["dma"]
                rec["sem"] = self.dsems[q][n % NDMASEM]
                rec["val"] = 16 * (n // NDMASEM + 1)
            elif rec["sig"]:
                cnt[rec["eng"]] += 1
                rec["sem"] = self.sems[rec["eng"]]
                rec["val"] = cnt[rec["eng"]]

    def _emit_engine(self, ename, eng):
        seen = {}

        def wait(sem, val):
            key = id(sem)
            if seen.get(key, 0) >= val:
                return
            seen[key] = val
            eng.wait_ge(sem, val)

        for rec in self.per_eng[ename]:
            for d in sorted(rec["deps"]):
                p = self.ops[d]
                if p["dma"] is None and p["eng"] == "pe" and ename == "pe" and rec["dma"] is None:
                    continue
                wait(p["sem"], p["val"])
            if rec["dma"] is not None:
                q, n, out, in_, kw = rec["dma"]
                if n >= NDMASEM:
                    wait(rec["sem"], rec["val"] - 16)
                eng.dma_start(out=out, in_=in_, **kw).then_inc(rec["sem"], 16)
            else:
                ins = rec["fn"](eng)
                if rec["sig"]:
                    ins.then_inc(rec["sem"], 1)
        return wait

    def emit(self):
        self._finalize()
        with self.nc.Block() as block:
            @block.tensor
            def _(e):
                self._emit_engine("pe", e)

            @block.scalar
            def _(e):
                self._emit_engine("act", e)

            @block.vector
            def _(e):
                self._emit_engine("dve", e)

            @block.gpsimd
            def _(e):
                w = self._emit_engine("pool", e)
                n = self.dma_count["pool"]
                for i in range(NDMASEM):
                    k = len(range(i, n, NDMASEM))
                    if k:
                        w(self.dsems["pool"][i], 16 * k)

            @block.sync
            def _(e):
                w = self._emit_engine("sp", e)
                n = self.dma_count["sp"]
                for i in range(NDMASEM):
                    k = len(range(i, n, NDMASEM))
                    if k:
                        w(self.dsems["sp"][i], 16 * k)


def colvec(v):
    v = np.asarray(v, np.float32).reshape(-1, 128)
    return np.ascontiguousarray(v.T)


class Pack:
    def __init__(self):
        self.parts = []
        self.off = {}
        self.n = 0

    def add(self, name, arr):
        arr = np.asarray(arr, np.float32)
        if arr.ndim == 1:
            arr = arr[:, None]
        assert arr.shape[0] == 128, (name, arr.shape)
        arr = arr.reshape(128, -1)
        self.off[name] = self.n
        self.parts.append(arr)
        self.n += arr.shape[1]

    def build(self):
        return np.ascontiguousarray(np.concatenate(self.parts, axis=1))


def bd_mask(fn):
    m = np.zeros((128, 128), np.float32)
    i = np.arange(64)
    blk = fn(i[:, None], i[None, :]).astype(np.float32)
    m[:64, :64] = blk
    m[64:, 64:] = blk
    return m


def make_consts(inp, hmask=1.0):
    P = Pack()
    g = lambda n, l=0: np.asarray(inp[n][l], np.float32)
    P.add("eps6", np.full(128, 1e-6)); P.add("eps5", np.full(128, 1e-5)); P.add("epsgn", np.full(128, 64e-5))
    P.add("halfpi", np.full(128, PI / 2)); P.add("zero", np.zeros(128)); P.add("hmask", np.full(128, float(hmask)))
    P.add("ab_g", colvec(g("ab_norm_g")))
    P.add("cin_b", colvec(g("conv_in_b")))
    P.add("dw_w", np.stack([colvec(g("conv_dw_w")[j]) for j in range(31)], axis=2))
    P.add("dw_b", colvec(g("conv_dw_b"))); P.add("ln_g", colvec(g("conv_ln_g"))); P.add("ln_b", colvec(g("conv_ln_b")))
    P.add("mu", colvec(g("rwkv_mu"))); P.add("omm", colvec(1.0 - 0.0 * g("rwkv_mu")) * 0 + 0)
    P.add("w0", colvec(g("rwkv_w0"))); P.add("a0", colvec(g("rwkv_a0")))
    P.add("k_k", colvec(g("rwkv_k_k"))); P.add("k_a", colvec(g("rwkv_k_a")))
    P.add("r_k", colvec(g("rwkv_r_k").reshape(-1)))
    P.add("gn_g", colvec(g("rwkv_ln_g"))); P.add("gn_b", colvec(g("rwkv_ln_b")))
    for l in range(2):
        P.add("ffn_g%d" % l, colvec(g("ffn_norm_g", l)))
        P.add("fc_w%d" % l, np.stack([colvec(g("ffn_conv_w", l)[j]) for j in range(3)], axis=2))
        P.add("fc_b%d" % l, colvec(g("ffn_conv_b", l)))
    P.add("at_g", colvec(g("attn_norm_g")))
    bq = g("attn_b_qkv")
    P.add("b_q", colvec(bq[:1024]))
    P.add("b_kd", np.stack([np.concatenate([bq[1024 + h * 64:1088 + h * 64]] * 2) for h in range(4)], axis=1))
    P.add("qn_g", np.concatenate([g("attn_q_norm_g")] * 2)); P.add("kn_g", np.concatenate([g("attn_k_norm_g")] * 2))
    P.add("b_o", colvec(g("attn_b_o")))
    half = 8
    invf = (500000.0 ** (-(np.arange(half, dtype=np.float32) * 2.0) / 16)).astype(np.float32)
    iv = np.zeros(64, np.float32); iv[:8] = invf; iv[8:16] = invf
    P.add("invf", np.concatenate([iv, iv]))
    P.add("ident", np.eye(128, dtype=np.float32))
    P.add("bones", bd_mask(lambda a, b: a * 0 + b * 0 + 1))
    P.add("ones", np.ones((128, 128), np.float32))
    P.add("m_su", bd_mask(lambda s, t: s < t)); P.add("m_iu", bd_mask(lambda s, t: s <= t)); P.add("m_sl", bd_mask(lambda t, s: s < t))
    rm = np.zeros((64, 64), np.float32)
    for m in range(8):
        rm[m + 8, m] = -1.0
        rm[m, m + 8] = 1.0
    R2 = np.zeros((128, 128), np.float32); R2[:64, :64] = rm; R2[64:, 64:] = rm
    P.add("rmT", R2)
    kq = np.arange(128)
    mP = (kq[:, None] > kq[None, :]).astype(np.float32)
    mC = (kq[:, None] <= kq[None, :]).astype(np.float32)
    P.add("mP", np.tile(mP, (1, 4))); P.add("mC", np.tile(mC, (1, 4)))
    sk = g("attn_sinks")
    se = np.zeros((4, 2, 2, 128), np.float32)
    for hk in range(4):
        for par in range(2):
            for pair in range(2):
                se[hk, par, pair, :] = sk[4 * hk + 2 * pair + par]
    P.add("sinks", np.broadcast_to(se.reshape(1, -1), (128, 2048)))
    return P


class Phase:
    count = 0

    def __init__(self, nc, cdram, coff, need):
        self.nc = nc
        Phase.count += 1
        self.pfx = "p%d_" % Phase.count
        self.st = contextlib.ExitStack()
        self.S = Sched(nc, self.st)
        self.ps = [self.st.enter_context(nc.psum_tensor(self.pfx + "ps%d" % i, [128, 512], F32)) for i in range(8)]
        self.pi = 0
        self.coff = coff
        self.cmap = {}
        n = 0
        for name, w in need:
            self.cmap[name] = n
            n += w
        self.C = self.sb("C", [128, n], F32)
        for name, w in need:
            o = self.cmap[name]
            self.S.dma("sp", self.C[:, o:o + w], cdram[:, coff[name]:coff[name] + w], writes=[("C", o, o + w)], allow_slow_non_contiguous=True)

    def sb(self, name, shape, dt):
        return self.st.enter_context(self.nc.sbuf_tensor(self.pfx + name, shape, dt))

    def c(self, name, lo=0, w=1):
        o = self.cmap[name] + lo
        return self.C[:, o:o + w]

    def cr(self, name, lo=0, w=1):
        o = self.cmap[name] + lo
        return ("C", o, o + w)

    def newps(self):
        i = self.pi
        self.pi = (self.pi + 1) % 8
        return self.ps[i], "ps%d" % i

    def close(self):
        self.S.emit()
        self.st.close()


def rmsnorm(ph, x, xname, hT, hname, sq, sqname, rstd, ones_bf, gname, T):
    S = ph.S
    for k in range(8):
        S.op("act", lambda e, k=k: e.activation(out=sq[:, k, :T], in_=x[:, k, :T], func=AF.Square),
             reads=[(xname, k, k + 1)], writes=[(sqname, k, k + 1)])
    ps, pn = ph.newps()
    for k in range(8):
        S.op("pe", lambda e, k=k: e.matmul(ps[:, :T], lhsT=ones_bf[:], rhs=sq[:, k, :T], start=(k == 0), stop=(k == 7)),
             reads=[("ones_bf", 0, 1), (sqname, k, k + 1)], writes=[(pn, 0, 1)])
    S.op("act", lambda e: e.activation(out=rstd[:, :T], in_=ps[:, :T], func=AF.Ln, scale=1.0 / D, bias=ph.c("eps6")),
         reads=[(pn, 0, 1), ph.cr("eps6")], writes=[("rstd", 0, 1)])
    S.op("act", lambda e: e.activation(out=rstd[:, :T], in_=rstd[:, :T], func=AF.Exp, scale=-0.5),
         reads=[("rstd", 0, 1)], writes=[("rstd", 0, 1)])
    for k in range(8):
        S.op("dve",
             lambda e, k=k: e.scalar_tensor_tensor(out=hT[:, k, :T], in0=x[:, k, :T], scalar=ph.c(gname, k), in1=rstd[:, :T],
                                                   op0=ALU.mult, op1=ALU.mult),
             reads=[(xname, k, k + 1), ("rstd", 0, 1), ph.cr(gname, k)], writes=[(hname, k, k + 1)])


def load_w(S, q, dst, dname, src, nchunk):
    for k in range(nchunk):
        S.dma(q, dst[:, k, :], src[k * 128:(k + 1) * 128, :], writes=[(dname, k, k + 1)])


def ffn_phase(nc, src, dst, w_up, w_dn, cdram, coff, L, TL, skip=0):
    T = 512
    NT = TL // T
    need = [("eps6", 1), ("ffn_g%d" % L, 8), ("fc_w%d" % L, 66), ("fc_b%d" % L, 22), ("ones", 128)]
    ph = Phase(nc, cdram, coff, need)
    S = ph.S
    wup = ph.sb("wup", [128, 8, 2 * DFF], BF16)
    wdn = ph.sb("wdn", [128, 22, D], BF16)
    x = ph.sb("x", [128, 8, T], F32)
    hT = ph.sb("hT", [128, 8, T], BF16)
    act = ph.sb("act", [128, 22, T], BF16)
    rstd = ph.sb("rstd", [128, T], F32)
    ones_bf = ph.sb("ones_bf", [128, 128], BF16)
    G = [ph.sb("G%d" % i, [128, T + 2], F32) for i in range(2)]
    acc = [ph.sb("acc%d" % i, [128, T], F32) for i in range(2)]
    sg = [ph.sb("sg%d" % i, [128, T], F32) for i in range(2)]
    carry = ph.sb("carry", [128, 22, 2], F32)
    load_w(S, "pool", wup, "wup", w_up, 8)
    load_w(S, "pool", wdn, "wdn", w_dn, 22)
    S.op("dve", lambda e: e.tensor_copy(out=ones_bf[:], in_=ph.c("ones", 0, 128)), reads=[ph.cr("ones", 0, 128)], writes=[("ones_bf", 0, 1)])
    S.op("pool", lambda e: e.memset(carry[:], 0.0), writes=[("carry", 0, 22)])
    fw, fb, gname = "fc_w%d" % L, "fc_b%d" % L, "ffn_g%d" % L
    srcv = src.rearrange("(c p) t -> p c t", p=128)
    dstv = dst.rearrange("(c p) t -> p c t", p=128)
    for t in range(NT):
        S.dma("sp", x[:], srcv[:, :, t * T:(t + 1) * T], writes=[("x", 0, 8)])
        rmsnorm(ph, x, "x", hT, "hT", act, "act", rstd, ones_bf, gname, T)
        for c in range(22):
            i = c % 2
            Gt, at, st_ = G[i], acc[i], sg[i]
            gn, an, sn = "G%d" % i, "acc%d" % i, "sg%d" % i
            psg, pgn = ph.newps()
            psu, pun = ph.newps()
            for k in range(8):
                S.op("pe", lambda e, k=k, c=c, psg=psg: e.matmul(psg[:, :T], lhsT=wup[:, k, c * 128:(c + 1) * 128], rhs=hT[:, k, :],
                                                                 start=(k == 0), stop=(k == 7)),
                     reads=[("wup", k, k + 1), ("hT", k, k + 1)], writes=[(pgn, 0, 1)])
            for k in range(8):
                S.op("pe", lambda e, k=k, c=c, psu=psu: e.matmul(psu[:, :T], lhsT=wup[:, k, DFF + c * 128:DFF + (c + 1) * 128], rhs=hT[:, k, :],
                                                                 start=(k == 0), stop=(k == 7)),
                     reads=[("wup", k, k + 1), ("hT", k, k + 1)], writes=[(pun, 0, 1)])
            S.op("pool", lambda e, c=c, Gt=Gt: e.tensor_copy(out=Gt[:, 0:2], in_=carry[:, c, :]),
                 reads=[("carry", c, c + 1)], writes=[(gn, 0, 2)])
            S.op("act", lambda e, Gt=Gt, psg=psg: e.activation(out=Gt[:, 2:T + 2], in_=psg[:, :T], func=AF.Identity),
                 reads=[(pgn, 0, 1)], writes=[(gn, 2, T + 2)])
            S.op("pool", lambda e, c=c, Gt=Gt: e.tensor_copy(out=carry[:, c, :], in_=Gt[:, T:T + 2]),
                 reads=[(gn, T, T + 2)], writes=[("carry", c, c + 1)])
            S.op("dve", lambda e, c=c, Gt=Gt, at=at: e.tensor_scalar(out=at[:], in0=Gt[:, 2:T + 2], scalar1=ph.c(fw, c * 3 + 2), scalar2=ph.c(fb, c),
                                                                    op0=ALU.mult, op1=ALU.add),
                 reads=[(gn, 2, T + 2), ph.cr(fw, c * 3 + 2), ph.cr(fb, c)], writes=[(an, 0, 1)])
            S.op("dve", lambda e, c=c, Gt=Gt, at=at: e.scalar_tensor_tensor(out=at[:], in0=Gt[:, 1:T + 1], scalar=ph.c(fw, c * 3 + 1), in1=at[:],
                                                                            op0=ALU.mult, op1=ALU.add),
                 reads=[(gn, 1, T + 1), (an, 0, 1), ph.cr(fw, c * 3 + 1)], writes=[(an, 0, 1)])
            S.op("dve", lambda e, c=c, Gt=Gt, at=at: e.scalar_tensor_tensor(out=at[:], in0=Gt[:, 0:T], scalar=ph.c(fw, c * 3), in1=at[:],
                                                                           op0=ALU.mult, op1=ALU.add),
                 reads=[(gn, 0, T), (an, 0, 1), ph.cr(fw, c * 3)], writes=[(an, 0, 1)])
            S.op("act", lambda e, at=at, st_=st_: e.activation(out=st_[:], in_=at[:], func=AF.Silu),
                 reads=[(an, 0, 1)], writes=[(sn, 0, 1)])
            S.op("dve", lambda e, c=c, st_=st_, psu=psu: e.tensor_tensor(out=act[:, c, :], in0=psu[:, :T], in1=st_[:], op=ALU.mult),
                 reads=[(pun, 0, 1), (sn, 0, 1)], writes=[("act", c, c + 1)])
        for oc in range(8):
            pso, pon = ph.newps()
            for c in range(22):
                S.op("pe", lambda e, c=c, oc=oc, pso=pso: e.matmul(pso[:, :T], lhsT=wdn[:, c, oc * 128:(oc + 1) * 128], rhs=act[:, c, :],
                                                                   start=(c == 0), stop=(c == 21)),
                     reads=[("wdn", c, c + 1), ("act", c, c + 1)], writes=[(pon, 0, 1)])
            S.op("dve", lambda e, oc=oc, pso=pso: e.tensor_tensor(out=x[:, oc, :], in0=pso[:, :T], in1=x[:, oc, :], op=ALU.add),
                 reads=[(pon, 0, 1), ("x", oc, oc + 1)], writes=[("x", oc, oc + 1)])
        if t >= skip:
            S.dma("sp", dstv[:, :, (t - skip) * T:(t - skip + 1) * T], x[:], reads=[("x", 0, 8)])
    ph.close()


def attn_phase(nc, src, dst, pos, w_qkv, w_o, cdram, coff, TL, masked=False):
    T = 512
    NT = TL // T
    need = [("eps6", 1), ("halfpi", 1), ("zero", 1), ("hmask", 1), ("at_g", 8), ("b_q", 8), ("b_kd", 4), ("qn_g", 1), ("kn_g", 1), ("b_o", 8),
            ("invf", 1), ("bones", 128), ("ones", 128), ("rmT", 128), ("mP", 512), ("mC", 512), ("sinks", 2048)]
    ph = Phase(nc, cdram, coff, need)
    S = ph.S
    wq = ph.sb("wq", [128, 8, 1024], BF16)
    wkd = ph.sb("wkd", [128, 8, 4, 128], BF16)
    wvd = ph.sb("wvd", [128, 8, 4, 128], BF16)
    wo = ph.sb("wo", [128, 8, D], BF16)
    x = ph.sb("x", [128, 8, T], F32)
    hT = ph.sb("hT", [128, 8, T], BF16)
    sq = ph.sb("sq", [128, 8, T], BF16)
    rstd = ph.sb("rstd", [128, T], F32)
    ones_bf = ph.sb("ones_bf", [128, 128], BF16)
    mPb = ph.sb("mPb", [128, 512], BF16)
    mCb = ph.sb("mCb", [128, 512], BF16)
    esink = ph.sb("esink", [128, 2048], F32)
    COS = ph.sb("COS", [128, T], F32)
    SIN = ph.sb("SIN", [128, T], F32)
    posi = ph.sb("posi", [128, T], I32)
    ang = ph.sb("ang", [128, T], F32)
    nf = ph.sb("nf", [128, T], F32)
    QT = ph.sb("QT", [128, 8, T], BF16)
    KT = ph.sb("KT", [128, 4, 128 + T], BF16)
    VB = ph.sb("VB", [128, 5, 512], BF16)
    OT = ph.sb("OT", [128, 8, T], BF16)
    bv = ph.sb("bv", [1, 512], BF16)
    qraw = [ph.sb("qraw%d" % i, [128, T], F32) for i in range(3)]
    qsq = [ph.sb("qsq%d" % i, [128, T], F32) for i in range(3)]
    qr = [ph.sb("qr%d" % i, [128, T], F32) for i in range(3)]
    qn = [ph.sb("qn%d" % i, [128, T], F32) for i in range(3)]
    t1 = [ph.sb("t1%d" % i, [128, T], F32) for i in range(3)]
    t2 = [ph.sb("t2%d" % i, [128, T], F32) for i in range(3)]
    E = [ph.sb("E%d" % i, [128, 512], BF16) for i in range(4)]
    den = [ph.sb("den%d" % i, [128, 512], F32) for i in range(2)]
    load_w(S, "pool", wq, "wq", w_qkv[:, 0:1024], 8)
    load_w(S, "pool", wo, "wo", w_o, 8)
    wkv = ph.sb("wkv", [128, 8, 512], BF16)
    load_w(S, "pool", wkv, "wkv", w_qkv[:, 1024:1536], 8)
    for hk in range(4):
        for cp in range(2):
            S.op("pool", lambda e, hk=hk, cp=cp: e.tensor_copy(out=wkd[:, :, hk, cp * 64:(cp + 1) * 64], in_=wkv[:, :, hk * 64:(hk + 1) * 64]),
                 reads=[("wkv", 0, 8)], writes=[("wkd", hk * 2 + cp, hk * 2 + cp + 1)])
            S.op("dve", lambda e, hk=hk, cp=cp: e.tensor_copy(out=wvd[:, :, hk, cp * 64:(cp + 1) * 64], in_=wkv[:, :, 256 + hk * 64:256 + (hk + 1) * 64]),
                 reads=[("wkv", 0, 8)], writes=[("wvd", hk * 2 + cp, hk * 2 + cp + 1)])
    if 'bv' not in DBG['skip']:
        S.dma("pool", bv[:], nc_bv_dram[0], writes=[("bv", 0, 1)])
    S.op("dve", lambda e: e.tensor_copy(out=ones_bf[:], in_=ph.c("ones", 0, 128)), reads=[ph.cr("ones", 0, 128)], writes=[("ones_bf", 0, 1)])
    S.op("dve", lambda e: e.tensor_copy(out=mPb[:], in_=ph.c("mP", 0, 512)), reads=[ph.cr("mP", 0, 512)], writes=[("mPb", 0, 1)])
    S.op("dve", lambda e: e.tensor_copy(out=mCb[:], in_=ph.c("mC", 0, 512)), reads=[ph.cr("mC", 0, 512)], writes=[("mCb", 0, 1)])
    if 'esink' not in DBG['skip']:
      S.op("act", lambda e: e.activation(out=esink[:], in_=ph.c("sinks", 0, 2048), func=AF.Exp), reads=[ph.cr("sinks", 0, 2048)], writes=[("esink", 0, 1)])
    srcv = src.rearrange("(c p) t -> p c t", p=128)
    dstv = dst.rearrange("(c p) t -> p c t", p=128)
    ei = 0
    for t in range(NT):
        tc0 = t * T
        S.dma("sp", x[:], srcv[:, :, tc0:tc0 + T], writes=[("x", 0, 8)])
        rmsnorm(ph, x, "x", hT, "hT", sq, "sq", rstd, ones_bf, "at_g", T)
        S.dma("sp", posi[:], pos[0:1, tc0:tc0 + T].to_broadcast([128, T]), writes=[("posi", 0, 1)])
        S.op("dve", lambda e: e.tensor_copy(out=ang[:], in_=posi[:]), reads=[("posi", 0, 1)], writes=[("ang", 0, 1)])
        S.op("dve", lambda e: e.tensor_scalar(out=ang[:], in0=ang[:], scalar1=ph.c("invf"), scalar2=None, op0=ALU.mult),
             reads=[("ang", 0, 1), ph.cr("invf")], writes=[("ang", 0, 1)])
        for which, tab, tn in ((0, SIN, "SIN"), (1, COS, "COS")):
            if which == 1:
                S.op("dve", lambda e: e.tensor_scalar(out=ang[:], in0=ang[:], scalar1=PI / 2, scalar2=None, op0=ALU.add),
                     reads=[("ang", 0, 1)], writes=[("ang", 0, 1)])
            S.op("dve", lambda e: e.tensor_scalar(out=posi[:], in0=ang[:], scalar1=float(1.0 / (2 * PI)), scalar2=None, op0=ALU.mult),
                 reads=[("ang", 0, 1)], writes=[("posi", 0, 1)])
            S.op("dve", lambda e: e.tensor_copy(out=nf[:], in_=posi[:]), reads=[("posi", 0, 1)], writes=[("nf", 0, 1)])
            S.op("dve", lambda e: e.scalar_tensor_tensor(out=nf[:], in0=nf[:], scalar=float(-2 * PI), in1=ang[:], op0=ALU.mult, op1=ALU.add),
                 reads=[("nf", 0, 1), ("ang", 0, 1)], writes=[("nf", 0, 1)])
            S.op("dve", lambda e: e.tensor_scalar(out=nf[:], in0=nf[:], scalar1=PI, scalar2=-PI, op0=ALU.min, op1=ALU.max),
                 reads=[("nf", 0, 1)], writes=[("nf", 0, 1)])
            S.op("act", lambda e, tab=tab: e.activation(out=tab[:], in_=nf[:], func=AF.Sin), reads=[("nf", 0, 1)], writes=[(tn, 0, 1)])
        if t > 0:
            S.op("pool", lambda e: e.tensor_copy(out=KT[:, :, 0:128], in_=KT[:, :, T:T + 128]), reads=[("KT", 4, 5)], writes=[("KT", 0, 1)])
            S.op("pool", lambda e: e.tensor_copy(out=VB[:, 0, :], in_=VB[:, 4, :]), reads=[("VB", 4, 5)], writes=[("VB", 0, 1)])
        for b in range(4 if DBG['att'] >= 1 else 0):
            ps, pn = ph.newps()
            for k in range(8):
                S.op("pe", lambda e, k=k, b=b, ps=ps: e.matmul(ps[:, :], lhsT=hT[:, k, b * 128:(b + 1) * 128], rhs=wvd[:, k, :, :],
                                                               start=(k == 0), stop=False),
                     reads=[("hT", k, k + 1), ("wvd", 0, 8)], writes=[(pn, 0, 1)])
            S.op("pe", lambda e, ps=ps: e.matmul(ps[:, :], lhsT=ones_bf[0:1, :], rhs=bv[0:1, :], start=False, stop=True),
                 reads=[("ones_bf", 0, 1), ("bv", 0, 1)], writes=[(pn, 0, 1)])
            S.op("act", lambda e, b=b, ps=ps: e.activation(out=VB[:, 1 + b, :], in_=ps[:, :], func=AF.Identity),
                 reads=[(pn, 0, 1)], writes=[("VB", 1 + b, 2 + b)])
        for j in range(12 if DBG['att'] >= 2 else 0):
            i = j % 3
            isq = j < 8
            ps, pn = ph.newps()
            for k in range(8):
                if isq:
                    S.op("pe", lambda e, k=k, j=j, ps=ps: e.matmul(ps[:, :T], lhsT=wq[:, k, j * 128:(j + 1) * 128], rhs=hT[:, k, :],
                                                                   start=(k == 0), stop=(k == 7)),
                         reads=[("wq", k, k + 1), ("hT", k, k + 1)], writes=[(pn, 0, 1)])
                else:
                    S.op("pe", lambda e, k=k, j=j, ps=ps: e.matmul(ps[:, :T], lhsT=wkd[:, k, j - 8, :], rhs=hT[:, k, :],
                                                                   start=(k == 0), stop=(k == 7)),
                         reads=[("wkd", 0, 8), ("hT", k, k + 1)], writes=[(pn, 0, 1)])
            bias = ph.c("b_q", j) if isq else ph.c("b_kd", j - 8)
            bres = ph.cr("b_q", j) if isq else ph.cr("b_kd", j - 8)
            gcol, gres = (ph.c("qn_g"), ph.cr("qn_g")) if isq else (ph.c("kn_g"), ph.cr("kn_g"))
            qa, qs_, qrr, qnn, ta, tb = qraw[i], qsq[i], qr[i], qn[i], t1[i], t2[i]
            S.op("act", lambda e, ps=ps, qa=qa, bias=bias: e.activation(out=qa[:], in_=ps[:, :T], func=AF.Identity, bias=bias),
                 reads=[(pn, 0, 1), bres], writes=[("qraw%d" % i, 0, 1)])
            S.op("pool", lambda e, qa=qa, qs_=qs_: e.tensor_tensor(out=qs_[:], in0=qa[:], in1=qa[:], op=ALU.mult),
                 reads=[("qraw%d" % i, 0, 1)], writes=[("qsq%d" % i, 0, 1)])
            ps2, pn2 = ph.newps()
            S.op("pe", lambda e, ps2=ps2, qs_=qs_: e.matmul(ps2[:, :T], lhsT=ph.c("bones", 0, 128), rhs=qs_[:], start=True, stop=True),
                 reads=[ph.cr("bones", 0, 128), ("qsq%d" % i, 0, 1)], writes=[(pn2, 0, 1)])
            S.op("act", lambda e, ps2=ps2, qrr=qrr: e.activation(out=qrr[:], in_=ps2[:, :T], func=AF.Ln, scale=1.0 / 64, bias=ph.c("eps6")),
                 reads=[(pn2, 0, 1), ph.cr("eps6")], writes=[("qr%d" % i, 0, 1)])
            S.op("act", lambda e, qrr=qrr: e.activation(out=qrr[:], in_=qrr[:], func=AF.Exp, scale=-0.5),
                 reads=[("qr%d" % i, 0, 1)], writes=[("qr%d" % i, 0, 1)])
            S.op("dve", lambda e, qa=qa, qrr=qrr, qnn=qnn, gcol=gcol: e.scalar_tensor_tensor(out=qnn[:], in0=qa[:], scalar=gcol, in1=qrr[:],
                                                                                         op0=ALU.mult, op1=ALU.mult),
                 reads=[("qraw%d" % i, 0, 1), ("qr%d" % i, 0, 1), gres], writes=[("qn%d" % i, 0, 1)])
            ps3, pn3 = ph.newps()
            S.op("pe", lambda e, ps3=ps3, qnn=qnn: e.matmul(ps3[:, :T], lhsT=ph.c("rmT", 0, 128), rhs=qnn[:], start=True, stop=True),
                 reads=[ph.cr("rmT", 0, 128), ("qn%d" % i, 0, 1)], writes=[(pn3, 0, 1)])
            S.op("pool", lambda e, qnn=qnn, ta=ta: e.tensor_tensor(out=ta[:], in0=qnn[:], in1=COS[:, :], op=ALU.mult),
                 reads=[("qn%d" % i, 0, 1), ("COS", 0, 1)], writes=[("t1%d" % i, 0, 1)])
            S.op("dve", lambda e, ps3=ps3, tb=tb: e.tensor_tensor(out=tb[:], in0=ps3[:, :T], in1=SIN[:, :], op=ALU.mult),
                 reads=[(pn3, 0, 1), ("SIN", 0, 1)], writes=[("t2%d" % i, 0, 1)])
            if isq:
                S.op("pool", lambda e, ta=ta, tb=tb, j=j: e.tensor_tensor(out=QT[:, j, :], in0=ta[:], in1=tb[:], op=ALU.add),
                     reads=[("t1%d" % i, 0, 1), ("t2%d" % i, 0, 1)], writes=[("QT", j, j + 1)])
            else:
                S.op("pool", lambda e, ta=ta, tb=tb, j=j: e.tensor_tensor(out=KT[:, j - 8, 128:128 + T], in0=ta[:], in1=tb[:], op=ALU.add),
                     reads=[("t1%d" % i, 0, 1), ("t2%d" % i, 0, 1)], writes=[("KT", 1, 5)])
        for qb in range(4 if DBG['att'] >= 3 else 0):
            first = (t == 0 and qb == 0)
            for hk in range(4):
                kbs = [1] if first else [0, 1]
                Es = []
                for kb in kbs:
                    kc0 = (qb + kb) * 128
                    Et = E[ei % 4]
                    en = "E%d" % (ei % 4)
                    ei += 1
                    for par in range(2):
                        pss, psn = ph.newps()
                        hp = slice(par * 64, par * 64 + 64)
                        for pair in range(2):
                            col = pair * 128
                            S.op("pe", lambda e, pss=pss, hp=hp, col=col, kc0=kc0, hk=hk, pair=pair, qb=qb:
                                 e.matmul(pss[:, col:col + 128], lhsT=KT[hp, hk, kc0:kc0 + 128], rhs=QT[hp, 2 * hk + pair, qb * 128:(qb + 1) * 128],
                                          start=True, stop=True),
                                 reads=[("KT", 0, 5), ("QT", 2 * hk + pair, 2 * hk + pair + 1)], writes=[(psn, 0, 1)])
                        S.op("act", lambda e, Et=Et, pss=pss, par=par: e.activation(out=Et[:, par * 256:(par + 1) * 256], in_=pss[:, 0:256], func=AF.Exp, scale=0.125),
                             reads=[(psn, 0, 1)], writes=[(en, par, par + 1)])
                    mk, mkn = (mPb, "mPb") if kb == 0 else (mCb, "mCb")
                    S.op("pool" if kb == 0 else "dve", lambda e, Et=Et, mk=mk: e.tensor_tensor(out=Et[:], in0=Et[:], in1=mk[:], op=ALU.mult),
                         reads=[(en, 0, 2), (mkn, 0, 1)], writes=[(en, 0, 2)])
                    if masked and t == 1 and qb == 0 and kb == 0:
                        S.op("act", lambda e, Et=Et: e.activation(out=Et[:], in_=Et[:], func=AF.Identity, scale=ph.c("hmask")),
                             reads=[(en, 0, 2), ph.cr("hmask")], writes=[(en, 0, 2)])
                    Es.append((Et, en, kb))
                psd, pdn = ph.newps()
                pso, pon = ph.newps()
                for n_, (Et, en, kb) in enumerate(Es):
                    S.op("pe", lambda e, psd=psd, Et=Et, n_=n_: e.matmul(psd[:, :], lhsT=ones_bf[:], rhs=Et[:], start=(n_ == 0), stop=(n_ == len(Es) - 1)),
                         reads=[("ones_bf", 0, 1), (en, 0, 2)], writes=[(pdn, 0, 1)])
                for n_, (Et, en, kb) in enumerate(Es):
                    vb = qb + kb
                    S.op("pe", lambda e, pso=pso, Et=Et, n_=n_, vb=vb, hk=hk: e.matmul(pso[:, :], lhsT=VB[:, vb, hk * 128:(hk + 1) * 128], rhs=Et[:],
                                                                                     start=(n_ == 0), stop=(n_ == len(Es) - 1)),
                         reads=[("VB", vb, vb + 1), (en, 0, 2)], writes=[(pon, 0, 1)])
                dn = den[hk % 2]
                dnn = "den%d" % (hk % 2)
                S.op("dve", lambda e, dn=dn, psd=psd, hk=hk: e.tensor_tensor(out=dn[:], in0=psd[:, :], in1=esink[:, hk * 512:(hk + 1) * 512], op=ALU.add),
                     reads=[(pdn, 0, 1), ("esink", 0, 1)], writes=[(dnn, 0, 1)])
                S.op("act", lambda e, dn=dn: e.activation(out=dn[:], in_=dn[:], func=AF.Ln), reads=[(dnn, 0, 1)], writes=[(dnn, 0, 1)])
                S.op("act", lambda e, dn=dn: e.activation(out=dn[:], in_=dn[:], func=AF.Exp, scale=-1.0), reads=[(dnn, 0, 1)], writes=[(dnn, 0, 1)])
                for par in range(2):
                    hp = slice(par * 64, par * 64 + 64)
                    S.op("dve", lambda e, dn=dn, pso=pso, hp=hp, par=par, hk=hk, qb=qb:
                         e.tensor_tensor(out=OT[hp, 2 * hk:2 * hk + 2, qb * 128:(qb + 1) * 128],
                                         in0=pso[hp, par * 256:(par + 1) * 256].rearrange("p (a q) -> p a q", a=2),
                                         in1=dn[hp, par * 256:(par + 1) * 256].rearrange("p (a q) -> p a q", a=2), op=ALU.mult),
                         reads=[(pon, 0, 1), (dnn, 0, 1)], writes=[("OT", 2 * hk, 2 * hk + 2)])
        for oc in range(8):
            pso, pon = ph.newps()
            for k in range(8):
                S.op("pe", lambda e, k=k, oc=oc, pso=pso: e.matmul(pso[:, :T], lhsT=wo[:, k, oc * 128:(oc + 1) * 128], rhs=OT[:, k, :],
                                                                   start=(k == 0), stop=(k == 7)),
                     reads=[("wo", k, k + 1), ("OT", k, k + 1)], writes=[(pon, 0, 1)])
            S.op("dve", lambda e, oc=oc, pso=pso: e.scalar_tensor_tensor(out=x[:, oc, :], in0=pso[:, :T], scalar=ph.c("b_o", oc), in1=x[:, oc, :],
                                                                        op0=ALU.add, op1=ALU.add),
                 reads=[(pon, 0, 1), ("x", oc, oc + 1), ph.cr("b_o", oc)], writes=[("x", oc, oc + 1)])
        if masked and t == 0:
            for k in range(8):
                S.op("act", lambda e, k=k: e.activation(out=x[:, k, :], in_=x[:, k, :], func=AF.Identity, scale=ph.c("hmask")),
                     reads=[("x", k, k + 1), ph.cr("hmask")], writes=[("x", k, k + 1)])
        S.dma("sp", dstv[:, :, tc0:tc0 + T], x[:], reads=[("x", 0, 8)])
    ph.close()


nc_bv_dram = [None]


def mixer_phase(nc, src, dst, w_in, w_out, w2a2_d, g2_d, cdram, coff, TL, NPRE=0, masked=False):
    T = 256
    NT = TL // T
    NCH = T // 64
    need = [("eps6", 1), ("eps5", 1), ("epsgn", 1), ("hmask", 1), ("ab_g", 8), ("cin_b", 8), ("dw_w", 124), ("dw_b", 4), ("ln_g", 4), ("ln_b", 4),
            ("mu", 14), ("w0", 4), ("a0", 4), ("k_k", 4), ("k_a", 4), ("r_k", 4), ("gn_g", 4), ("gn_b", 4),
            ("ident", 128), ("bones", 128), ("ones", 128), ("m_su", 128), ("m_iu", 128), ("m_sl", 128)]
    ph = Phase(nc, cdram, coff, need)
    S = ph.S
    sb = ph.sb
    win = sb("win", [128, 8, 2816], BF16)
    wout = sb("wout", [128, 8, D], BF16)
    w2a2 = sb("w2a2", [128, 512], BF16)
    g2b = sb("g2b", [128, 512], BF16)
    x = sb("x", [128, 8, T], F32)
    hT = sb("hT", [128, 8, T], BF16)
    sq = sb("sq", [128, 8, T], BF16)
    rstd = sb("rstd", [128, T], F32)
    ones_bf = sb("ones_bf", [128, 128], BF16)
    omm = sb("omm", [128, 14], F32)
    GL = sb("GL", [128, 4, 30 + T], F32)
    cacc = sb("cacc", [128, 4, T], F32)
    csq = sb("csq", [128, 4, T], F32)
    sig = sb("sig", [128, T], F32)
    crs = sb("crs", [128, T], F32)
    catT = sb("catT", [128, 8, T], BF16)
    Pb = [sb("Pb%d" % i, [128, T + 1], F32) for i in range(2)]
    ptmp = [sb("ptmp%d" % i, [128, T], F32) for i in range(2)]
    pcarry = sb("pcarry", [128, 14], F32)
    rw12 = sb("rw12", [128, T], F32)
    rw13 = sb("rw13", [128, T], F32)
    twad = sb("twad", [128, T], BF16)
    sgd = sb("sgd", [128, T], BF16)
    lw4 = sb("lw4", [128, 4, T], F32)
    a4 = sb("a4", [128, 4, T], F32)
    gate4 = sb("gate4", [128, 4, T], F32)
    onesrow = sb("onesrow", [128, 64], F32)
    resetrow = sb("resetrow", [128, T], F32)
    names = ["r", "k", "v", "kk", "q1", "rn", "kkn", "k2", "bb", "bonus", "cum", "P", "Pinv", "Pprev", "y", "yc"]
    pers = ("P", "bonus", "y", "yc")
    st_ = {n: [sb("s_%s%d" % (n, i), [128, T], F32) for i in range(4 if n in pers else 2)] for n in names}
    bdn = ["a", "r", "b", "k", "v"]
    BD = {n: [sb("bd_%s%d" % (n, i), [128, NCH, 128], BF16) for i in range(4)] for n in bdn}
    mats = ["AT", "A", "YrbT", "XakT", "YrkT", "TT", "Vt", "Bt", "Kt", "W", "U"]
    M = {n: [sb("m_%s%d" % (n, i), [128, 128], BF16) for i in range(2)] for n in mats}
    A2 = [[sb("A2_%d_%d" % (j, i), [128, 128], BF16) for i in range(2)] for j in range(2)]
    A2T = [[sb("A2T_%d_%d" % (j, i), [128, 128], BF16) for i in range(2)] for j in range(2)]
    TT2 = [[sb("TT2_%d_%d" % (j, i), [128, 128], BF16) for i in range(2)] for j in range(2)]
    H = [[sb("H%d_%d" % (cc, i), [128, 128], F32) for i in range(2)] for cc in range(4)]
    hcur = [0, 0, 0, 0]
    Hb = [[sb("Hb%d_%d" % (cc, i), [128, 128], BF16) for i in range(2)] for cc in range(4)]
    ident_bf = sb("ident_bf", [128, 128], BF16)

    load_w(S, "pool", win, "win", w_in, 8)
    load_w(S, "pool", wout, "wout", w_out, 8)
    S.dma("pool", w2a2[:], w2a2_d, writes=[("w2a2", 0, 1)])
    S.dma("pool", g2b[:], g2_d, writes=[("g2b", 0, 1)])
    S.op("dve", lambda e: e.tensor_copy(out=ones_bf[:], in_=ph.c("ones", 0, 128)), reads=[ph.cr("ones", 0, 128)], writes=[("ones_bf", 0, 1)])
    S.op("dve", lambda e: e.tensor_copy(out=onesrow[:], in_=ph.c("ones", 0, 64)), reads=[ph.cr("ones", 0, 64)], writes=[("onesrow", 0, 1)])
    S.op("dve", lambda e: e.tensor_scalar(out=omm[:], in0=ph.c("mu", 0, 14), scalar1=-1.0, scalar2=1.0, op0=ALU.mult, op1=ALU.add),
         reads=[ph.cr("mu", 0, 14)], writes=[("omm", 0, 14)])
    S.op("pool", lambda e: e.memset(GL[:], 0.0), writes=[("GL", 0, 4 * 1000)])
    S.op("pool", lambda e: e.memset(resetrow[:], 1.0), writes=[("resetrow", 0, 1)])
    S.op("pool", lambda e: e.memset(resetrow[:].rearrange("p (n t) -> p n t", t=64)[:, :, 0:1], 0.0), writes=[("resetrow", 0, 1)])
    S.op("pool", lambda e: e.memset(pcarry[:], 0.0), writes=[("pcarry", 0, 14)])
    for n in bdn:
        for i in range(4):
            S.op("pool", lambda e, n=n, i=i: e.memset(BD[n][i][:], 0.0), writes=[("bd_%s%d" % (n, i), 0, NCH)])
    for cc in range(4):
        S.op("pool", lambda e, cc=cc: e.memset(H[cc][0][:], 0.0), writes=[("H%d_0" % cc, 0, 1)])
        S.op("pool", lambda e, cc=cc: e.memset(Hb[cc][0][:], 0.0), writes=[("Hb%d_0" % cc, 0, 1)])
    S.op("dve", lambda e: e.tensor_copy(out=ident_bf[:], in_=ph.c("ident", 0, 128)), reads=[ph.cr("ident", 0, 128)], writes=[("ident_bf", 0, 1)])

    ident = ph.c("ident", 0, 128)
    bones = ph.c("bones", 0, 128)
    ones32 = ph.c("ones", 0, 128)
    srcv = src.rearrange("(c p) t -> p c t", p=128)
    dstv = dst.rearrange("(c p) t -> p c t", p=128)

    def proj(pc):
        ps, pn = ph.newps()
        for k in range(8):
            S.op("pe", lambda e, k=k, ps=ps: e.matmul(ps[:, :T], lhsT=win[:, k, pc * 128:(pc + 1) * 128], rhs=hT[:, k, :],
                                                      start=(k == 0), stop=(k == 7)),
                 reads=[("win", k, k + 1), ("hT", k, k + 1)], writes=[(pn, 0, 1)])
        return ps, pn

    def shifted(ch, out, outname):
        i = ch % 2
        ps, pn = proj(8 + ch)
        pb, pbn, tm, tmn = Pb[i], "Pb%d" % i, ptmp[i], "ptmp%d" % i
        S.op("pool", lambda e: e.tensor_copy(out=pb[:, 0:1], in_=pcarry[:, ch:ch + 1]), reads=[("pcarry", ch, ch + 1)], writes=[(pbn, 0, 1)])
        S.op("act", lambda e: e.activation(out=pb[:, 1:T + 1], in_=ps[:, :T], func=AF.Identity), reads=[(pn, 0, 1)], writes=[(pbn, 1, T + 1)])
        S.op("pool", lambda e: e.tensor_copy(out=pcarry[:, ch:ch + 1], in_=pb[:, T:T + 1]), reads=[(pbn, T, T + 1)], writes=[("pcarry", ch, ch + 1)])
        S.op("act", lambda e: e.activation(out=tm[:], in_=pb[:, 0:T], func=AF.Identity, scale=ph.c("mu", ch)),
             reads=[(pbn, 0, T), ph.cr("mu", ch)], writes=[(tmn, 0, 1)])
        S.op("dve", lambda e: e.scalar_tensor_tensor(out=out[:], in0=pb[:, 1:T + 1], scalar=omm[:, ch:ch + 1], in1=tm[:], op0=ALU.mult, op1=ALU.add),
             reads=[(pbn, 1, T + 1), ("omm", ch, ch + 1), (tmn, 0, 1)], writes=[(outname, 0, 1)])

    def mm1(ps, pn, lhsT, rhs, reads, start=True, stop=True, n=128):
        S.op("pe", lambda e: e.matmul(ps[:, :n], lhsT=lhsT, rhs=rhs, start=start, stop=stop), reads=reads, writes=[(pn, 0, 1)])

    def rsqrt_ps(ps, pn, out, outn, scale, epsname, n=T):
        S.op("act", lambda e: e.activation(out=out, in_=ps[:, :n], func=AF.Ln, scale=scale, bias=ph.c(epsname)),
             reads=[(pn, 0, 1), ph.cr(epsname)], writes=[(outn, 0, 1)])
        S.op("act", lambda e: e.activation(out=out, in_=out, func=AF.Exp, scale=-0.5), reads=[(outn, 0, 1)], writes=[(outn, 0, 1)])

    for t in range(NT):
        full = t >= NPRE
        plast = (t == NPRE - 1)
        mtile = masked and (NPRE - 1 <= t < NPRE + 2)
        S.dma("sp", x[:], srcv[:, :, t * T:(t + 1) * T], writes=[("x", 0, 8)])
        rmsnorm(ph, x, "x", hT, "hT", sq, "sq", rstd, ones_bf, "ab_g", T)
        for cc in range(4 if (full or plast) else 0):
            psa, pan = proj(cc)
            psg, pgn = proj(4 + cc)
            S.op("act", lambda e, psg=psg, cc=cc: e.activation(out=sig[:], in_=psg[:, :T], func=AF.Sigmoid, bias=ph.c("cin_b", 4 + cc)),
                 reads=[(pgn, 0, 1), ph.cr("cin_b", 4 + cc)], writes=[("sig", 0, 1)])
            S.op("dve", lambda e, psa=psa, cc=cc: e.scalar_tensor_tensor(out=GL[:, cc, 30:30 + T], in0=psa[:, :T], scalar=ph.c("cin_b", cc), in1=sig[:],
                                                                        op0=ALU.add, op1=ALU.mult),
                 reads=[(pan, 0, 1), ("sig", 0, 1), ph.cr("cin_b", cc)], writes=[("GL", cc * 1000 + 30, cc * 1000 + 30 + T)])
            eng = "dve"
            if mtile:
                S.op("act", lambda e, cc=cc: e.activation(out=GL[:, cc, 30:30 + T], in_=GL[:, cc, 30:30 + T], func=AF.Identity, scale=ph.c("hmask")),
                     reads=[("GL", cc * 1000 + 30, cc * 1000 + 30 + T), ph.cr("hmask")], writes=[("GL", cc * 1000 + 30, cc * 1000 + 30 + T)])
            if not full:
                S.op("dve", lambda e, cc=cc: e.tensor_copy(out=GL[:, cc, 0:30], in_=GL[:, cc, T:T + 30]),
                     reads=[("GL", cc * 1000 + T, cc * 1000 + T + 30)], writes=[("GL", cc * 1000, cc * 1000 + 30)])
        def convgen():
            for cc in range(4 if full else 0):
                eng = "dve"
                S.op(eng, lambda e, cc=cc: e.tensor_scalar(out=cacc[:, cc, :], in0=GL[:, cc, 30:30 + T], scalar1=ph.c("dw_w", cc * 31 + 30), scalar2=ph.c("dw_b", cc),
                                                           op0=ALU.mult, op1=ALU.add),
                     reads=[("GL", cc * 1000, cc * 1000 + 30 + T), ph.cr("dw_w", cc * 31, 31), ph.cr("dw_b", cc)], writes=[("cacc", cc, cc + 1)])
                for j in range(30):
                    S.op(eng, lambda e, cc=cc, j=j: e.scalar_tensor_tensor(out=cacc[:, cc, :], in0=GL[:, cc, j:j + T], scalar=ph.c("dw_w", cc * 31 + j), in1=cacc[:, cc, :],
                                                                          op0=ALU.mult, op1=ALU.add),
                         reads=[("GL", cc * 1000, cc * 1000 + 30 + T), ("cacc", cc, cc + 1)], writes=[("cacc", cc, cc + 1)])
                    if j % 4 == 3:
                        yield
                S.op(eng, lambda e, cc=cc: e.tensor_copy(out=GL[:, cc, 0:30], in_=GL[:, cc, T:T + 30]),
                     reads=[("GL", cc * 1000 + T, cc * 1000 + T + 30)], writes=[("GL", cc * 1000, cc * 1000 + 30)])
                yield
            if not full:
                return
            psm, pmn = ph.newps()
            for cc in range(4):
                mm1(psm, pmn, ones32, cacc[:, cc, :], [ph.cr("ones", 0, 128), ("cacc", cc, cc + 1)], start=(cc == 0), stop=(cc == 3), n=T)
            for cc in range(4):
                S.op("dve", lambda e, cc=cc, psm=psm: e.scalar_tensor_tensor(out=cacc[:, cc, :], in0=psm[:, :T], scalar=-1.0 / 512, in1=cacc[:, cc, :],
                                                                            op0=ALU.mult, op1=ALU.add),
                     reads=[(pmn, 0, 1), ("cacc", cc, cc + 1)], writes=[("cacc", cc, cc + 1)])
                S.op("pool", lambda e, cc=cc: e.tensor_tensor(out=csq[:, cc, :], in0=cacc[:, cc, :], in1=cacc[:, cc, :], op=ALU.mult),
                     reads=[("cacc", cc, cc + 1)], writes=[("csq", cc, cc + 1)])
            yield
            psv, pvn = ph.newps()
            for cc in range(4):
                mm1(psv, pvn, ones32, csq[:, cc, :], [ph.cr("ones", 0, 128), ("csq", cc, cc + 1)], start=(cc == 0), stop=(cc == 3), n=T)
            rsqrt_ps(psv, pvn, crs[:], "crs", 1.0 / 512, "eps5")
            yield
            for cc in range(4):
                S.op("dve", lambda e, cc=cc: e.tensor_tensor(out=cacc[:, cc, :], in0=cacc[:, cc, :], in1=crs[:], op=ALU.mult),
                     reads=[("cacc", cc, cc + 1), ("crs", 0, 1)], writes=[("cacc", cc, cc + 1)])
                S.op("act", lambda e, cc=cc: e.activation(out=catT[:, cc, :], in_=cacc[:, cc, :], func=AF.Silu, scale=ph.c("ln_g", cc), bias=ph.c("ln_b", cc)),
                     reads=[("cacc", cc, cc + 1), ph.cr("ln_g", cc), ph.cr("ln_b", cc)], writes=[("catT", cc, cc + 1)])
            yield
        shifted(12, rw12, "rw12")
        if full or plast:
            shifted(13, rw13, "rw13")
        S.op("act", lambda e: e.activation(out=twad[0:64, :], in_=rw12[0:64, :], func=AF.Tanh), reads=[("rw12", 0, 1)], writes=[("twad", 0, 1)])
        S.op("dve", lambda e: e.tensor_copy(out=twad[64:128, :], in_=rw12[64:128, :]), reads=[("rw12", 0, 1)], writes=[("twad", 1, 2)])
        if full:
            S.op("act", lambda e: e.activation(out=sgd[:], in_=rw13[:], func=AF.Sigmoid), reads=[("rw13", 0, 1)], writes=[("sgd", 0, 1)])
        for cc in range(4):
            ps, pn = ph.newps()
            mm1(ps, pn, w2a2[0:64, cc * 128:(cc + 1) * 128], twad[0:64, :], [("w2a2", 0, 1), ("twad", 0, 1)], n=T)
            S.op("act", lambda e, ps=ps, cc=cc: e.activation(out=lw4[:, cc, :], in_=ps[:, :T], func=AF.Sigmoid, bias=ph.c("w0", cc)),
                 reads=[(pn, 0, 1), ph.cr("w0", cc)], writes=[("lw4", cc, cc + 1)])
            S.op("act", lambda e, cc=cc: e.activation(out=lw4[:, cc, :], in_=lw4[:, cc, :], func=AF.Identity, scale=-float(np.exp(-0.5))),
                 reads=[("lw4", cc, cc + 1)], writes=[("lw4", cc, cc + 1)])
            ps, pn = ph.newps()
            mm1(ps, pn, w2a2[64:128, cc * 128:(cc + 1) * 128], twad[64:128, :], [("w2a2", 0, 1), ("twad", 1, 2)], n=T)
            S.op("act", lambda e, ps=ps, cc=cc: e.activation(out=a4[:, cc, :], in_=ps[:, :T], func=AF.Sigmoid, bias=ph.c("a0", cc)),
                 reads=[(pn, 0, 1), ph.cr("a0", cc)], writes=[("a4", cc, cc + 1)])
            if full:
                ps, pn = ph.newps()
                mm1(ps, pn, g2b[:, cc * 128:(cc + 1) * 128], sgd[:], [("g2b", 0, 1), ("sgd", 0, 1)], n=T)
                S.op("act", lambda e, ps=ps, cc=cc: e.activation(out=gate4[:, cc, :], in_=ps[:, :T], func=AF.Identity),
                     reads=[(pn, 0, 1)], writes=[("gate4", cc, cc + 1)])
        def prep(cc, cx):
            i = cc % 2
            s = {n: st_[n][cc if n in pers else i] for n in names}
            sn = {n: "s_%s%d" % (n, cc if n in pers else i) for n in names}
            bd = {n: BD[n][cc] for n in bdn}
            bn = {n: "bd_%s%d" % (n, cc) for n in bdn}
            if full or plast:
                shifted(cc, s["r"], sn["r"])
            shifted(4 + cc, s["k"], sn["k"])
            shifted(8 + cc, s["v"], sn["v"])
            lw = lw4[:, cc, :]
            av = a4[:, cc, :]
            yield

            def ew(eng, f, reads, writes):
                S.op(eng, f, reads=[(sn[r], 0, 1) if r in sn else r for r in reads], writes=[(sn[w], 0, 1) if w in sn else w for w in writes])
            ew("act", lambda e, s=s, cc=cc: e.activation(out=s["kk"][:], in_=s["k"][:], func=AF.Identity, scale=ph.c("k_k", cc)),
               ["k", ph.cr("k_k", cc)], ["kk"])
            ew("pool", lambda e, s=s: e.tensor_tensor(out=s["q1"][:], in0=s["kk"][:], in1=s["kk"][:], op=ALU.mult), ["kk"], ["q1"])
            ps, pn = ph.newps()
            mm1(ps, pn, bones, s["q1"][:], [ph.cr("bones", 0, 128), (sn["q1"], 0, 1)], n=T)
            yield
            ew("dve", lambda e, s=s, ps=ps: e.tensor_scalar(out=s["rn"][:], in0=ps[:, :T], scalar1=1e-24, scalar2=None, op0=ALU.max), [(pn, 0, 1)], ["rn"])
            ew("act", lambda e, s=s: e.activation(out=s["rn"][:], in_=s["rn"][:], func=AF.Ln), ["rn"], ["rn"])
            ew("act", lambda e, s=s: e.activation(out=s["rn"][:], in_=s["rn"][:], func=AF.Exp, scale=-0.5), ["rn"], ["rn"])
            yield
            ew("dve", lambda e, s=s: e.tensor_tensor(out=s["kkn"][:], in0=s["kk"][:], in1=s["rn"][:], op=ALU.mult), ["kk", "rn"], ["kkn"])
            ew("pool", lambda e, s=s, av=av, cc=cc: e.tensor_scalar(out=s["q1"][:], in0=av, scalar1=-1.0, scalar2=ph.c("k_a", cc), op0=ALU.add, op1=ALU.mult),
               [("a4", cc, cc + 1), ph.cr("k_a", cc)], ["q1"])
            ew("dve", lambda e, s=s: e.scalar_tensor_tensor(out=s["k2"][:], in0=s["q1"][:], scalar=1.0, in1=s["k"][:], op0=ALU.add, op1=ALU.mult),
               ["q1", "k"], ["k2"])
            ew("dve", lambda e, s=s, av=av: e.tensor_tensor(out=s["bb"][:], in0=s["kkn"][:], in1=av, op=ALU.mult), ["kkn", ("a4", cc, cc + 1)], ["bb"])
            if full:
                ew("dve", lambda e, s=s, cc=cc: e.scalar_tensor_tensor(out=s["q1"][:], in0=s["r"][:], scalar=ph.c("r_k", cc), in1=s["k2"][:], op0=ALU.mult, op1=ALU.mult),
                   ["r", "k2", ph.cr("r_k", cc)], ["q1"])
                ps, pn = ph.newps()
                mm1(ps, pn, bones, s["q1"][:], [ph.cr("bones", 0, 128), (sn["q1"], 0, 1)], n=T)
                ew("dve", lambda e, s=s, ps=ps: e.tensor_tensor(out=s["bonus"][:], in0=ps[:, :T], in1=s["v"][:], op=ALU.mult), [(pn, 0, 1), "v"], ["bonus"])
            yield
            ew("dve", lambda e, s=s, lw=lw: e.tensor_tensor_scan(out=s["cum"][:], data0=resetrow[:], data1=lw, initial=0.0, op0=ALU.mult, op1=ALU.add),
               [("lw4", cc, cc + 1), ("resetrow", 0, 1)], ["cum"])
            yield
            ew("act", lambda e, s=s: e.activation(out=s["P"][:], in_=s["cum"][:], func=AF.Exp), ["cum"], ["P"])
            ew("act", lambda e, s=s: e.activation(out=s["Pinv"][:], in_=s["cum"][:], func=AF.Exp, scale=-1.0), ["cum"], ["Pinv"])
            ew("pool", lambda e, s=s, lw=lw: e.tensor_tensor(out=s["Pprev"][:], in0=s["cum"][:], in1=lw, op=ALU.subtract), ["cum", ("lw4", cc, cc + 1)], ["Pprev"])
            ew("act", lambda e, s=s: e.activation(out=s["Pprev"][:], in_=s["Pprev"][:], func=AF.Exp), ["Pprev"], ["Pprev"])
            yield
            v3 = lambda ap: ap.rearrange("p (n t) -> p n t", t=64)
            for hh in range(2):
                hp = slice(hh * 64, hh * 64 + 64)
                hc = slice(hh * 64, hh * 64 + 64)
                eng = "dve" if hh == 0 else "pool"
                ew("dve", lambda e, s=s, bd=bd, hp=hp, hc=hc: e.scalar_tensor_tensor(out=bd["a"][hp, :, hc], in0=v3(s["kkn"][hp, :]), scalar=-1.0, in1=v3(s["Pprev"][hp, :]),
                                                                                   op0=ALU.mult, op1=ALU.mult), ["kkn", "Pprev"], [(bn["a"], 0, NCH)])
                if full:
                    ew(eng, lambda e, s=s, bd=bd, hp=hp, hc=hc: e.tensor_tensor(out=bd["r"][hp, :, hc], in0=v3(s["r"][hp, :]), in1=v3(s["P"][hp, :]), op=ALU.mult),
                       ["r", "P"], [(bn["r"], 0, NCH)])
                ew(eng, lambda e, s=s, bd=bd, hp=hp, hc=hc: e.tensor_tensor(out=bd["b"][hp, :, hc], in0=v3(s["bb"][hp, :]), in1=v3(s["Pinv"][hp, :]), op=ALU.mult),
                   ["bb", "Pinv"], [(bn["b"], 0, NCH)])
                ew(eng, lambda e, s=s, bd=bd, hp=hp, hc=hc: e.tensor_tensor(out=bd["k"][hp, :, hc], in0=v3(s["k2"][hp, :]), in1=v3(s["Pinv"][hp, :]), op=ALU.mult),
                   ["k2", "Pinv"], [(bn["k"], 0, NCH)])
                ew(eng, lambda e, s=s, bd=bd, hp=hp, hc=hc: e.tensor_copy(out=bd["v"][hp, :, hc], in_=v3(s["v"][hp, :])), ["v"], [(bn["v"], 0, NCH)])
            cx.update(dict(cc=cc, s=s, sn=sn, bd=bd, bn=bn, ew=ew))
            yield

        def unit(cx, n):
            cc, s, sn, bd, bn = cx["cc"], cx["s"], cx["sn"], cx["bd"], cx["bn"]
            i = cc % 2
            if True:
                m = {k_: M[k_][i] for k_ in mats}
                mn = {k_: "m_%s%d" % (k_, i) for k_ in mats}
                ba, br, bb_, bk, bv_ = (bd[q][:, n, :] for q in bdn)
                R = lambda q: (bn[q], n, n + 1)

                def sc(lq, rq, lhs, rhs, out, mask):
                    ps, pn = ph.newps()
                    mm1(ps, pn, lhs, rhs, [R(lq), R(rq)])
                    S.op("dve", lambda e, ps=ps, m=m: e.tensor_tensor(out=m[out][:], in0=ps[:, :128], in1=ph.c(mask, 0, 128), op=ALU.mult),
                         reads=[(pn, 0, 1), ph.cr(mask, 0, 128)], writes=[(mn[out], 0, 1)])
                sc("b", "a", bb_, ba, "AT", "m_su")
                sc("a", "b", ba, bb_, "A", "m_sl")
                if full:
                    sc("b", "r", bb_, br, "YrbT", "m_iu")
                sc("k", "a", bk, ba, "XakT", "m_su")
                if full:
                    sc("k", "r", bk, br, "YrkT", "m_iu")
                S.op("pool", lambda e, m=m: e.tensor_tensor(out=m["TT"][:], in0=m["AT"][:], in1=ident, op=ALU.add),
                     reads=[(mn["AT"], 0, 1), ph.cr("ident", 0, 128)], writes=[(mn["TT"], 0, 1)])
                cA, cAn, cAT, cATn, cTT, cTTn = m["A"], mn["A"], m["AT"], mn["AT"], m["TT"], mn["TT"]
                yield
                for d in range(5):
                    nA, nAn = A2[i][d % 2], "A2_%d_%d" % (i, d % 2)
                    nAT, nATn = A2T[i][d % 2], "A2T_%d_%d" % (i, d % 2)
                    nTT, nTTn = TT2[i][d % 2], "TT2_%d_%d" % (i, d % 2)
                    ps, pn = ph.newps()
                    mm1(ps, pn, cAT[:], cA[:], [(cATn, 0, 1), (cAn, 0, 1)])
                    S.op("act", lambda e, ps=ps, nA=nA: e.activation(out=nA[:], in_=ps[:, :128], func=AF.Identity), reads=[(pn, 0, 1)], writes=[(nAn, 0, 1)])
                    if d < 4:
                        ps2, pn2 = ph.newps()
                        mm1(ps2, pn2, cA[:], cAT[:], [(cATn, 0, 1), (cAn, 0, 1)])
                        S.op("dve", lambda e, ps2=ps2, nAT=nAT: e.tensor_copy(out=nAT[:], in_=ps2[:, :128]), reads=[(pn2, 0, 1)], writes=[(nATn, 0, 1)])
                    ps3, pn3 = ph.newps()
                    mm1(ps3, pn3, nA[:], cTT[:], [(nAn, 0, 1), (cTTn, 0, 1)])
                    S.op("dve", lambda e, ps3=ps3, nTT=nTT, cTT=cTT: e.tensor_tensor(out=nTT[:], in0=ps3[:, :128], in1=cTT[:], op=ALU.add),
                         reads=[(pn3, 0, 1), (cTTn, 0, 1)], writes=[(nTTn, 0, 1)])
                    cA, cAn, cAT, cATn, cTT, cTTn = nA, nAn, nAT, nATn, nTT, nTTn
                    yield
                for q, dst_ in (("v", "Vt"), ("b", "Bt"), ("k", "Kt")):
                    ps, pn = ph.newps()
                    mm1(ps, pn, bd[q][:, n, :], ident_bf[:], [R(q), ("ident_bf", 0, 1)])
                    S.op("act", lambda e, ps=ps, dst_=dst_, m=m: e.activation(out=m[dst_][:], in_=ps[:, :128], func=AF.Identity),
                         reads=[(pn, 0, 1)], writes=[(mn[dst_], 0, 1)])
                yield
                Hc, Hcn = H[cc][hcur[cc]], "H%d_%d" % (cc, hcur[cc])
                Hn, Hnn = H[cc][1 - hcur[cc]], "H%d_%d" % (cc, 1 - hcur[cc])
                Hbc, Hbcn = Hb[cc][hcur[cc]], "Hb%d_%d" % (cc, hcur[cc])
                Hbn, Hbnn = Hb[cc][1 - hcur[cc]], "Hb%d_%d" % (cc, 1 - hcur[cc])
                hcur[cc] = 1 - hcur[cc]
                ps, pn = ph.newps()
                mm1(ps, pn, ba, Hbc[:], [R("a"), (Hbcn, 0, 1)], start=True, stop=False)
                mm1(ps, pn, m["XakT"][:], m["Vt"][:], [(mn["XakT"], 0, 1), (mn["Vt"], 0, 1)], start=False, stop=True)
                S.op("act", lambda e, ps=ps, m=m: e.activation(out=m["W"][:], in_=ps[:, :128], func=AF.Identity), reads=[(pn, 0, 1)], writes=[(mn["W"], 0, 1)])
                yield
                ps, pn = ph.newps()
                mm1(ps, pn, cTT[:], m["W"][:], [(cTTn, 0, 1), (mn["W"], 0, 1)])
                S.op("dve", lambda e, ps=ps, m=m: e.tensor_copy(out=m["U"][:], in_=ps[:, :128]), reads=[(pn, 0, 1)], writes=[(mn["U"], 0, 1)])
                yield
                if full:
                    ps, pn = ph.newps()
                    mm1(ps, pn, Hbc[:], br, [(Hbcn, 0, 1), R("r")], start=True, stop=False)
                    mm1(ps, pn, m["U"][:], m["YrbT"][:], [(mn["U"], 0, 1), (mn["YrbT"], 0, 1)], start=False, stop=False)
                    mm1(ps, pn, m["Vt"][:], m["YrkT"][:], [(mn["Vt"], 0, 1), (mn["YrkT"], 0, 1)], start=False, stop=True)
                    for hh in range(2):
                        hp = slice(hh * 64, hh * 64 + 64)
                        S.op("act" if hh == 0 else "dve",
                             (lambda e, ps=ps, hp=hp, s=s, n=n: e.activation(out=s["y"][hp, n * 64:(n + 1) * 64], in_=ps[hp, hp], func=AF.Identity)) if hh == 0 else
                             (lambda e, ps=ps, hp=hp, s=s, n=n: e.tensor_copy(out=s["y"][hp, n * 64:(n + 1) * 64], in_=ps[hp, hp])),
                             reads=[(pn, 0, 1)], writes=[(sn["y"], 0, 1)])
                ps, pn = ph.newps()
                mm1(ps, pn, m["Bt"][:], m["U"][:], [(mn["Bt"], 0, 1), (mn["U"], 0, 1)], start=True, stop=False)
                mm1(ps, pn, m["Kt"][:], m["Vt"][:], [(mn["Kt"], 0, 1), (mn["Vt"], 0, 1)], start=False, stop=True)
                pc = s["P"][:, n * 64 + 63:n * 64 + 64]
                S.op("act", lambda e, Hn=Hn, Hc=Hc, pc=pc: e.activation(out=Hn[:], in_=Hc[:], func=AF.Identity, scale=pc),
                     reads=[(Hcn, 0, 1), (sn["P"], 0, 1)], writes=[(Hnn, 0, 1)])
                S.op("dve", lambda e, ps=ps, Hn=Hn, pc=pc: e.scalar_tensor_tensor(out=Hn[:], in0=ps[:, :128], scalar=pc, in1=Hn[:], op0=ALU.mult, op1=ALU.add),
                     reads=[(pn, 0, 1), (Hnn, 0, 1), (sn["P"], 0, 1)], writes=[(Hnn, 0, 1)])
                S.op("act", lambda e, Hn=Hn, Hbn=Hbn: e.activation(out=Hbn[:], in_=Hn[:], func=AF.Identity), reads=[(Hnn, 0, 1)], writes=[(Hbnn, 0, 1)])

        def post(cx):
            cc, s, sn, ew = cx["cc"], cx["s"], cx["sn"], cx["ew"]
            ps, pn = ph.newps()
            mm1(ps, pn, bones, s["y"][:], [ph.cr("bones", 0, 128), (sn["y"], 0, 1)], n=T)
            ew("dve", lambda e, s=s, ps=ps: e.scalar_tensor_tensor(out=s["yc"][:], in0=ps[:, :T], scalar=-1.0 / 64, in1=s["y"][:], op0=ALU.mult, op1=ALU.add),
               [(pn, 0, 1), "y"], ["yc"])
            yield
            ew("pool", lambda e, s=s: e.tensor_tensor(out=s["q1"][:], in0=s["yc"][:], in1=s["yc"][:], op=ALU.mult), ["yc"], ["q1"])
            ps, pn = ph.newps()
            mm1(ps, pn, bones, s["q1"][:], [ph.cr("bones", 0, 128), (sn["q1"], 0, 1)], n=T)
            rsqrt_ps(ps, pn, s["rn"][:], sn["rn"], 1.0 / 64, "epsgn")
            yield
            ew("dve", lambda e, s=s: e.tensor_tensor(out=s["yc"][:], in0=s["yc"][:], in1=s["rn"][:], op=ALU.mult), ["yc", "rn"], ["yc"])
            ew("act", lambda e, s=s, cc=cc: e.activation(out=s["yc"][:], in_=s["yc"][:], func=AF.Identity, scale=ph.c("gn_g", cc), bias=ph.c("gn_b", cc)),
               ["yc", ph.cr("gn_g", cc), ph.cr("gn_b", cc)], ["yc"])
            yield
            ew("pool", lambda e, s=s: e.tensor_tensor(out=s["yc"][:], in0=s["yc"][:], in1=s["bonus"][:], op=ALU.add), ["yc", "bonus"], ["yc"])
            ew("dve", lambda e, s=s, cc=cc: e.tensor_tensor(out=catT[:, 4 + cc, :], in0=s["yc"][:], in1=gate4[:, cc, :], op=ALU.mult),
               ["yc", ("gate4", cc, cc + 1)], [("catT", 4 + cc, 5 + cc)])
        def lockstep(gens):
            alive = True
            while alive:
                alive = False
                for g in gens:
                    try:
                        next(g)
                        alive = True
                    except StopIteration:
                        pass

        def chain(cx):
            for n in range(NCH):
                yield from unit(cx, n)

        cxs = [dict() for _ in range(4)]
        lockstep([prep(0, cxs[0]), prep(1, cxs[1])])
        lockstep([chain(cxs[0]), chain(cxs[1]), prep(2, cxs[2]), prep(3, cxs[3]), convgen()])
        lockstep([chain(cxs[2]), chain(cxs[3])] + ([post(cxs[0]), post(cxs[1])] if full else []))
        if full:
            lockstep([post(cxs[2]), post(cxs[3])])
        if DBG['mix'] == 1:
            for k in range(8):
                S.op("dve", lambda e, k=k: e.tensor_copy(out=x[:, k, :], in_=catT[:, k, :]), reads=[("catT", k, k + 1)], writes=[("x", k, k + 1)])
        for oc in range(8 if (DBG['mix'] != 1 and full) else 0):
            pso, pon = ph.newps()
            for k in range(8):
                S.op("pe", lambda e, k=k, oc=oc, pso=pso: e.matmul(pso[:, :T], lhsT=wout[:, k, oc * 128:(oc + 1) * 128], rhs=catT[:, k, :],
                                                                   start=(k == 0), stop=(k == 7)),
                     reads=[("wout", k, k + 1), ("catT", k, k + 1)], writes=[(pon, 0, 1)])
            S.op("dve", lambda e, oc=oc, pso=pso: e.tensor_tensor(out=x[:, oc, :], in0=pso[:, :T], in1=x[:, oc, :], op=ALU.add),
                 reads=[(pon, 0, 1), ("x", oc, oc + 1)], writes=[("x", oc, oc + 1)])
        if full and mtile:
            for k in range(8):
                S.op("act", lambda e, k=k: e.activation(out=x[:, k, :], in_=x[:, k, :], func=AF.Identity, scale=ph.c("hmask")),
                     reads=[("x", k, k + 1), ph.cr("hmask")], writes=[("x", k, k + 1)])
        if full:
            S.dma("sp", dstv[:, :, (t - NPRE) * T:(t - NPRE + 1) * T], x[:], reads=[("x", 0, 8)])
    ph.close()


def build(TL, NC, phases=(0, 1, 2, 3), halo=False):
    nc = bass.Bass("TRN2", target_bir_lowering=False)
    dt = lambda n, s, d=F32, kind="ExternalInput": nc.dram_tensor(n, s, d, kind=kind).ap()
    TW, TH, TO = (8192, 2560, 2048) if halo else (TL, TL, TL)
    xT = dt("xT", [D, TW]); pos = dt("pos", [1, TH], I32); cst = dt("cst", [128, NC])
    w_in = dt("w_in", [D, 2816]); w_out = dt("w_out", [D, D]); w2a2 = dt("w2a2", [128, 512]); g2 = dt("g2", [128, 512])
    w_up = [dt("w_up%d" % l, [D, 2 * DFF]) for l in range(2)]
    w_dn = [dt("w_dn%d" % l, [DFF, D]) for l in range(2)]
    w_qkv = dt("w_qkv", [D, 1536]); w_o = dt("w_o", [D, D]); bvd = dt("bvd", [1, 512])
    nc_bv_dram[0] = bvd
    if len(phases) < 4:
        TO = TH
    yT = dt("yT", [D, TO], kind="ExternalOutput")
    scr = [dt("scr%d" % i, [D, TH], kind="Internal") for i in range(3)]
    coff = build.coff
    chain = [xT] + scr[:len(phases) - 1] + [yT]
    ci = 0
    for p in phases:
        s_, d_ = chain[ci], chain[ci + 1]
        ci += 1
        if p == 0:
            if halo:
                mixer_phase(nc, s_, d_, w_in, w_out, w2a2, g2, cst, coff, TW, NPRE=(TW - TH) // 256, masked=True)
            else:
                mixer_phase(nc, s_, d_, w_in, w_out, w2a2, g2, cst, coff, TL)
        elif p == 1:
            ffn_phase(nc, s_, d_, w_up[0], w_dn[0], cst, coff, 0, TH)
        elif p == 2:
            attn_phase(nc, s_, d_, pos, w_qkv, w_o, cst, coff, TH, masked=halo)
        else:
            ffn_phase(nc, s_, d_, w_up[1], w_dn[1], cst, coff, 1, TH, skip=(TH - TO) // 512)
    return nc


def host_inputs(inp, xb, posb, hmask=1.0, cache={}):
    key = id(inp)
    if key not in cache:
        P = make_consts(inp, 1.0)
        f = lambda a: np.ascontiguousarray(np.asarray(a, np.float32))
        bq = np.asarray(inp["attn_b_qkv"][0], np.float32)
        bvd = np.concatenate([np.concatenate([bq[1280 + h * 64:1344 + h * 64]] * 2) for h in range(4)])[None, :]
        shared = {"w_in": f(inp["ab_w_in"][0]), "w_out": f(inp["ab_w_out"][0]),
                  "w2a2": f(np.concatenate([inp["rwkv_w2"][0], inp["rwkv_a2"][0]], axis=0)), "g2": f(inp["rwkv_g2"][0]),
                  "w_up0": f(inp["ffn_w_up"][0]), "w_up1": f(inp["ffn_w_up"][1]), "w_dn0": f(inp["ffn_w_down"][0]), "w_dn1": f(inp["ffn_w_down"][1]),
                  "w_qkv": f(inp["attn_w_qkv"][0]), "w_o": f(inp["attn_w_o"][0]), "bvd": f(bvd)}
        cache.clear()
        cache[key] = (P.off, P.build(), shared)
    off, cst0, shared = cache[key]
    build.coff = off
    cst = cst0.copy()
    cst[:, off["hmask"]] = hmask
    m = dict(shared)
    m["xT"] = np.ascontiguousarray(np.asarray(xb, np.float32).T)
    m["pos"] = np.ascontiguousarray(np.asarray(posb, np.int32)[None, :])
    m["cst"] = cst
    return m, cst.shape[1]


def kernel(**inputs):
    x = np.asarray(inputs["x"], np.float32)
    pos = np.asarray(inputs["positions"])
    B, SEQ, _ = x.shape
    TW, TH, TO = 8192, 2560, 2048
    NQ = SEQ // TO
    maps = []
    for c in range(8):
        b, q = c // NQ, c % NQ
        end = (q + 1) * TO
        start = end - TW
        xw = np.zeros((TW, D), np.float32)
        xw[max(0, -start):] = x[b, max(0, start):end]
        hs = end - TH
        pw = np.zeros((TH,), np.int32)
        pw[max(0, -hs):] = pos[b, max(0, hs):end]
        m, NC = host_inputs(inputs, xw, pw, hmask=(0.0 if q == 0 else 1.0))
        maps.append(m)
    nc = build(SEQ, NC, halo=True)
    res = run_bass_kernel_spmd(nc, maps, core_ids=list(range(8)))
    out = np.zeros((B, SEQ, D), np.float32)
    for c in range(8):
        b, q = c // NQ, c % NQ
        out[b, q * TO:(q + 1) * TO] = res.results[c]["yT"].T
    return out
```
